# Optimizing a Trainium2 kernel written in Bass

```python
import jax
import jax.numpy as jnp
from jax import lax
import numpy as np

D_MODEL = 2048
BATCH = 2
SEQ = 16384
DEPTH = 2

HEAD_DIM = 64
CHUNK = 128
BRANCH_DIM = D_MODEL // 2
N_BRANCH = 3
M_INNER = BRANCH_DIM
M_HEADDIM = HEAD_DIM
M_HEADS = M_INNER // M_HEADDIM
M_GROUPS = 4
M_HPG = M_HEADS // M_GROUPS
M_STATE = 128
M_CONV = 4
M_CONV_DIM = M_INNER + 2 * M_GROUPS * M_STATE
M_COLS = M_INNER + M_CONV_DIM + M_HEADS
M_NORM_EPS = 1e-5
RW_DIM = BRANCH_DIM
RW_HEADDIM = HEAD_DIM
RW_HEADS = RW_DIM // RW_HEADDIM
LORA_W = 96
LORA_A = 96
LORA_G = 256
RW_COLS = 3 * RW_DIM + LORA_W + LORA_A + LORA_G
RW_LN_EPS = 64e-5
RET_HEADS = 8
RET_DK = HEAD_DIM
RET_DV = BRANCH_DIM // RET_HEADS
RET_COLS = 2 * RET_HEADS * RET_DK + 2 * BRANCH_DIM
RET_EPS = 1e-6
ROPE_BASE = 10000.0
IN_COLS = M_COLS + RW_COLS + RET_COLS
FFN_DIM = 5504
FFN_CONV = 3
NORM_EPS = 1e-6

kernel_name = 'hybrid_ssd_rwkv7_retention_block'

F32 = jnp.float32


def split_cols(t, sizes):
    return jnp.split(t, np.cumsum(sizes)[:-1].tolist(), axis=-1)


def rms_norm(t, g, eps=NORM_EPS):
    tf = t.astype(F32)
    tf = tf * lax.rsqrt(jnp.mean(tf * tf, axis=-1, keepdims=True) + eps)
    return (tf * g.astype(F32)).astype(t.dtype)


def causal_dwconv(t, w, b):
    k, ch = w.shape
    y = lax.conv_general_dilated(t, w[:, None, :].astype(t.dtype), window_strides=(1,),
                                 padding=[(k - 1, 0)], dimension_numbers=('NWC', 'WIO', 'NWC'),
                                 feature_group_count=ch)
    return y + b.astype(t.dtype)


def token_shift(t):
    return jnp.pad(t, ((0, 0), (1, 0), (0, 0)))[:, :-1]


def to_chunks(t):
    b, s = t.shape[:2]
    return jnp.moveaxis(t.reshape(b, s // CHUNK, CHUNK, *t.shape[2:]), 1, 0)


def from_chunks(t):
    t = jnp.moveaxis(t, 0, 1)
    return t.reshape(t.shape[0], -1, *t.shape[3:])


def rotary(t, positions):
    half = t.shape[-1] // 2
    inv_freq = ROPE_BASE ** (-jnp.arange(half, dtype=F32) / half)
    ang = positions.astype(F32)[..., None] * inv_freq
    cos = jnp.cos(ang)[:, :, None, :]
    sin = jnp.sin(ang)[:, :, None, :]
    t1, t2 = t[..., :half], t[..., half:]
    return jnp.concatenate([t1 * cos - t2 * sin, t1 * sin + t2 * cos], axis=-1)


def mamba2_branch(proj, conv_w, conv_b, head_p, norm_g):
    bsz, s, _ = proj.shape
    z, xbc, dt_raw = split_cols(proj, [M_INNER, M_CONV_DIM, M_HEADS])
    xbc = jax.nn.silu(causal_dwconv(xbc, conv_w, conv_b))
    xs, bm, cm = split_cols(xbc, [M_INNER, M_GROUPS * M_STATE, M_GROUPS * M_STATE])
    dt_bias, a_log, d_skip = head_p[0], head_p[1], head_p[2]
    dt = jax.nn.softplus(dt_raw.astype(F32) + dt_bias.astype(F32))
    a_dt = (dt * -jnp.exp(a_log.astype(F32))).reshape(bsz, s, M_GROUPS, M_HPG)
    xs = xs.astype(F32).reshape(bsz, s, M_GROUPS, M_HPG, M_HEADDIM)
    xdt = xs * dt.reshape(bsz, s, M_GROUPS, M_HPG)[..., None]
    bm = bm.astype(F32).reshape(bsz, s, M_GROUPS, M_STATE)
    cm = cm.astype(F32).reshape(bsz, s, M_GROUPS, M_STATE)
    causal = jnp.tril(jnp.ones((CHUNK, CHUNK), dtype=bool))[None, :, :, None, None]

    def step(state, inp):
        xc, ac, bc, cc = inp
        cum = jnp.cumsum(ac, axis=1)
        seg = cum[:, :, None] - cum[:, None, :]
        decay = jnp.exp(jnp.where(causal, seg, -jnp.inf))
        cb = jnp.einsum('bign,bjgn->bijg', cc, bc)
        y = jnp.einsum('bijg,bijgr,bjgrp->bigrp', cb, decay, xc)
        y = y + jnp.einsum('bign,bgrpn->bigrp', cc, state) * jnp.exp(cum)[..., None]
        to_end = jnp.exp(cum[:, -1:] - cum)
        state = (state * jnp.exp(cum[:, -1])[..., None, None]
                 + jnp.einsum('bjgn,bjgr,bjgrp->bgrpn', bc, to_end, xc))
        return state, y

    state0 = jnp.zeros((bsz, M_GROUPS, M_HPG, M_HEADDIM, M_STATE), F32)
    _, ys = lax.scan(step, state0, (to_chunks(xdt), to_chunks(a_dt), to_chunks(bm), to_chunks(cm)))
    y = from_chunks(ys) + xs * d_skip.astype(F32).reshape(M_GROUPS, M_HPG)[:, :, None]
    y = y.reshape(bsz, s, M_INNER) * jax.nn.silu(z.astype(F32))
    y = rms_norm(y.reshape(bsz, s, M_GROUPS, M_INNER // M_GROUPS),
                 norm_g.reshape(M_GROUPS, M_INNER // M_GROUPS), M_NORM_EPS)
    return y.reshape(bsz, s, M_INNER).astype(proj.dtype)


def rwkv7_branch(proj, mu, w2, a2, g2, vec):
    bsz, s, _ = proj.shape
    proj = proj + (token_shift(proj) - proj) * mu
    r, k, v, wd, ad, gd = split_cols(proj, [RW_DIM, RW_DIM, RW_DIM, LORA_W, LORA_A, LORA_G])
    w0, a0, k_k, k_a, r_k, ln_w, ln_b = [vec[i] for i in range(7)]
    log_w = -jax.nn.softplus(-(w0 + jnp.tanh(wd) @ w2).astype(F32)) - 0.5
    decay = jnp.exp(-jnp.exp(log_w))
    a = jax.nn.sigmoid((a0 + ad @ a2).astype(F32))
    g = (jax.nn.sigmoid(gd) @ g2).astype(F32)

    def heads(t):
        return t.astype(F32).reshape(bsz, s, RW_HEADS, RW_HEADDIM)

    kk = heads(k * k_k)
    kk = kk * lax.rsqrt(jnp.maximum(jnp.sum(kk * kk, axis=-1, keepdims=True), 1e-24))
    k_h = heads(k.astype(F32) * (1.0 + (a - 1.0) * k_a.astype(F32)))
    r_h, v_h, a_h, w_h = heads(r), heads(v), heads(a), heads(decay)

    def step(state, inp):
        r_t, w_t, k_t, v_t, kk_t, b_t = inp
        sa = jnp.einsum('bhvk,bhk->bhv', state, kk_t)
        state = (state * w_t[:, :, None, :] - sa[..., None] * b_t[:, :, None, :]
                 + v_t[..., None] * k_t[:, :, None, :])
        return state, jnp.einsum('bhvk,bhk->bhv', state, r_t)

    seq_major = tuple(jnp.moveaxis(t, 1, 0) for t in (r_h, w_h, k_h, v_h, kk, kk * a_h))
    state0 = jnp.zeros((bsz, RW_HEADS, RW_HEADDIM, RW_HEADDIM), F32)
    _, ys = lax.scan(step, state0, seq_major)
    y = jnp.moveaxis(ys, 0, 1)
    mean = jnp.mean(y, axis=-1, keepdims=True)
    var = jnp.mean(jnp.square(y - mean), axis=-1, keepdims=True)
    y = ((y - mean) * lax.rsqrt(var + RW_LN_EPS) * ln_w.astype(F32).reshape(RW_HEADS, RW_HEADDIM)
         + ln_b.astype(F32).reshape(RW_HEADS, RW_HEADDIM))
    bonus = jnp.sum(r_h * k_h * r_k.astype(F32).reshape(RW_HEADS, RW_HEADDIM), axis=-1, keepdims=True) * v_h
    y = (y + bonus).reshape(bsz, s, RW_DIM) * g
    return y.astype(proj.dtype)


def retention_branch(proj, positions):
    bsz, s, _ = proj.shape
    q, k, v, g = split_cols(proj, [RET_HEADS * RET_DK, RET_HEADS * RET_DK, BRANCH_DIM, BRANCH_DIM])
    q = rotary(q.astype(F32).reshape(bsz, s, RET_HEADS, RET_DK), positions)
    k = rotary(k.astype(F32).reshape(bsz, s, RET_HEADS, RET_DK), positions) * (RET_DK ** -0.5)
    v = v.astype(F32).reshape(bsz, s, RET_HEADS, RET_DV)
    log_gamma = jnp.log1p(-jnp.exp2(-5.0 - jnp.arange(RET_HEADS, dtype=F32)))
    idx = jnp.arange(CHUNK, dtype=F32)
    rel = (idx[:, None] - idx[None, :])[None]
    dmask = jnp.exp(jnp.where(rel >= 0, rel * log_gamma[:, None, None], -jnp.inf))
    q_dec = jnp.exp((idx[:, None] + 1.0) * log_gamma)
    k_dec = jnp.exp((CHUNK - 1.0 - idx)[:, None] * log_gamma)
    c_dec = jnp.exp(CHUNK * log_gamma)

    def step(state, inp):
        qc, kc, vc = inp
        scores = jnp.einsum('bihd,bjhd->bhij', qc, kc) * dmask
        y = jnp.einsum('bhij,bjhe->bihe', scores, vc)
        y = y + jnp.einsum('bihd,bhde->bihe', qc * q_dec[:, :, None], state)
        state = state * c_dec[:, None, None] + jnp.einsum('bjhd,bjhe->bhde', kc * k_dec[:, :, None], vc)
        return state, y

    state0 = jnp.zeros((bsz, RET_HEADS, RET_DK, RET_DV), F32)
    _, ys = lax.scan(step, state0, (to_chunks(q), to_chunks(k), to_chunks(v)))
    y = from_chunks(ys)
    y = y * lax.rsqrt(jnp.mean(y * y, axis=-1, keepdims=True) + RET_EPS)
    y = jax.nn.silu(g.astype(F32)) * y.reshape(bsz, s, BRANCH_DIM)
    return y.astype(proj.dtype)


def hybrid_mixer(h, positions, w_in, m_conv_w, m_conv_b, m_head, m_norm_g, rwkv_mu, rwkv_w2,
                 rwkv_a2, rwkv_g2, rwkv_vec, w_gate, w_branch, w_out):
    proj = h @ w_in
    m_proj, rw_proj, ret_proj = split_cols(proj, [M_COLS, RW_COLS, RET_COLS])
    y_ssd = mamba2_branch(m_proj, m_conv_w, m_conv_b, m_head, m_norm_g)
    y_rwkv = rwkv7_branch(rw_proj, rwkv_mu, rwkv_w2, rwkv_a2, rwkv_g2, rwkv_vec)
    y_ret = retention_branch(ret_proj, positions)
    merged = jax.nn.sigmoid(h @ w_gate[0]) * (y_ssd @ w_branch[0])
    merged = merged + jax.nn.sigmoid(h @ w_gate[1]) * (y_rwkv @ w_branch[1])
    merged = merged + jax.nn.sigmoid(h @ w_gate[2]) * (y_ret @ w_branch[2])
    return merged @ w_out


def conv_geglu(h, w_up, conv_w, conv_b, w_down):
    u = causal_dwconv(h @ w_up, conv_w, conv_b)
    gate, up = jnp.split(u, 2, axis=-1)
    return (jax.nn.gelu(gate, approximate=True) * up) @ w_down


def setup_inputs(seed: int = 0) -> dict:
    key = jax.random.key(seed)
    ks = jax.random.split(key, 32)

    def nrm(k, shape, scale):
        return jax.random.normal(k, shape, F32) * scale

    def unif(k, shape, lo, hi):
        return jax.random.uniform(k, shape, F32, lo, hi)

    x = nrm(ks[0], (BATCH, SEQ, D_MODEL), 1.0)
    c = nrm(ks[1], (BATCH, D_MODEL), 1.0)
    positions = jnp.broadcast_to(jnp.arange(SEQ, dtype=jnp.int32), (BATCH, SEQ))
    w_ada = nrm(ks[2], (DEPTH, D_MODEL, 6 * D_MODEL), 0.5 * D_MODEL ** -0.5)
    b_ada = nrm(ks[3], (DEPTH, 6 * D_MODEL), 0.02)
    norm_g = 1.0 + nrm(ks[4], (DEPTH, 4, D_MODEL), 0.02)
    w_in = nrm(ks[5], (DEPTH, D_MODEL, IN_COLS), D_MODEL ** -0.5)
    m_conv_w = nrm(ks[6], (DEPTH, M_CONV, M_CONV_DIM), 0.5)
    m_conv_b = nrm(ks[7], (DEPTH, M_CONV_DIM), 0.02)
    dt0 = jnp.exp(unif(ks[8], (DEPTH, M_HEADS), float(np.log(1e-3)), float(np.log(1e-1))))
    dt_bias = dt0 + jnp.log(-jnp.expm1(-dt0))
    a_log = jnp.log(unif(ks[9], (DEPTH, M_HEADS), 1.0, 16.0))
    d_skip = 1.0 + nrm(ks[10], (DEPTH, M_HEADS), 0.02)
    m_head = jnp.stack([dt_bias, a_log, d_skip], axis=1)
    m_norm_g = 1.0 + nrm(ks[11], (DEPTH, M_INNER), 0.02)
    rwkv_mu = unif(ks[12], (DEPTH, RW_COLS), 0.0, 1.0)
    rwkv_w2 = nrm(ks[13], (DEPTH, LORA_W, RW_DIM), 0.1 * LORA_W ** -0.5)
    rwkv_a2 = nrm(ks[14], (DEPTH, LORA_A, RW_DIM), 0.1 * LORA_A ** -0.5)
    rwkv_g2 = nrm(ks[15], (DEPTH, LORA_G, RW_DIM), LORA_G ** -0.5)
    rw_w0 = unif(ks[16], (DEPTH, RW_DIM), -6.5, -1.5)
    rw_a0 = nrm(ks[17], (DEPTH, RW_DIM), 0.1)
    rw_kk = 0.85 + nrm(ks[18], (DEPTH, RW_DIM), 0.05)
    rw_ka = 1.0 + nrm(ks[19], (DEPTH, RW_DIM), 0.05)
    rw_rk = nrm(ks[20], (DEPTH, RW_DIM), 0.1)
    rw_lnw = 1.0 + nrm(ks[21], (DEPTH, RW_DIM), 0.02)
    rw_lnb = nrm(ks[22], (DEPTH, RW_DIM), 0.02)
    rwkv_vec = jnp.stack([rw_w0, rw_a0, rw_kk, rw_ka, rw_rk, rw_lnw, rw_lnb], axis=1)
    w_gate = nrm(ks[23], (DEPTH, N_BRANCH, D_MODEL, D_MODEL), D_MODEL ** -0.5)
    w_branch = nrm(ks[24], (DEPTH, N_BRANCH, BRANCH_DIM, D_MODEL), BRANCH_DIM ** -0.5)
    w_out = nrm(ks[25], (DEPTH, D_MODEL, D_MODEL), D_MODEL ** -0.5)
    w_up = nrm(ks[26], (DEPTH, D_MODEL, 2 * FFN_DIM), D_MODEL ** -0.5)
    f_conv_w = nrm(ks[27], (DEPTH, FFN_CONV, 2 * FFN_DIM), FFN_CONV ** -0.5)
    f_conv_b = nrm(ks[28], (DEPTH, 2 * FFN_DIM), 0.02)
    w_down = nrm(ks[29], (DEPTH, FFN_DIM, D_MODEL), FFN_DIM ** -0.5)
    return {'x': x, 'c': c, 'positions': positions, 'w_ada': w_ada, 'b_ada': b_ada,
            'norm_g': norm_g, 'w_in': w_in, 'm_conv_w': m_conv_w, 'm_conv_b': m_conv_b,
            'm_head': m_head, 'm_norm_g': m_norm_g, 'rwkv_mu': rwkv_mu, 'rwkv_w2': rwkv_w2,
            'rwkv_a2': rwkv_a2, 'rwkv_g2': rwkv_g2, 'rwkv_vec': rwkv_vec, 'w_gate': w_gate,
            'w_branch': w_branch, 'w_out': w_out, 'w_up': w_up, 'f_conv_w': f_conv_w,
            'f_conv_b': f_conv_b, 'w_down': w_down}


def reference(x, c, positions, w_ada, b_ada, norm_g, w_in, m_conv_w, m_conv_b, m_head, m_norm_g,
              rwkv_mu, rwkv_w2, rwkv_a2, rwkv_g2, rwkv_vec, w_gate, w_branch, w_out,
              w_up, f_conv_w, f_conv_b, w_down):
    for l in range(DEPTH):
        mod = (jax.nn.silu(c) @ w_ada[l] + b_ada[l])[:, None, :]
        sh_m, sc_m, gt_m, sh_f, sc_f, gt_f = jnp.split(mod, 6, axis=-1)
        h = rms_norm(x, norm_g[l, 0]) * (1.0 + sc_m) + sh_m
        y = hybrid_mixer(h, positions, w_in[l], m_conv_w[l], m_conv_b[l], m_head[l], m_norm_g[l],
                         rwkv_mu[l], rwkv_w2[l], rwkv_a2[l], rwkv_g2[l], rwkv_vec[l],
                         w_gate[l], w_branch[l], w_out[l])
        x = x + gt_m * rms_norm(y, norm_g[l, 1])
        h = rms_norm(x, norm_g[l, 2]) * (1.0 + sc_f) + sh_f
        y = conv_geglu(h, w_up[l], f_conv_w[l], f_conv_b[l], w_down[l])
        x = x + gt_f * rms_norm(y, norm_g[l, 3])
    return x
```

```python
import numpy as np
import concourse.bass as bass
import concourse.mybir as mybir
from concourse.bass_utils import run_bass_kernel_spmd

F32 = mybir.dt.float32
BF16 = mybir.dt.bfloat16
I32 = mybir.dt.int32
AF = mybir.ActivationFunctionType
ALU = mybir.AluOpType
AX = mybir.AxisListType
EPOCH = 30000
D = 2048


class Tracker:
    ENGS = ('pe', 'act', 'dve', 'pool', 'sp')

    def __init__(self, nc):
        self.nc = nc
        self.ops = {e: [] for e in self.ENGS}
        self.cur_sem = {}
        self.cnt = {}
        self.nsem = 0
        for e in self.ENGS:
            self.cur_sem[e] = self._new_sem(e)
            self.cnt[e] = 0
        self.known = {e: {} for e in self.ENGS}
        self.last_w = {}
        self.readers = {}
        self.dma_sems = {}
        self.dma_cnt = {}
        self.nops = 0
        self._old_epochs = []

    def _new_sem(self, tag):
        self.nsem += 1
        return self.nc.alloc_semaphore(f"s{self.nsem}_{tag}")

    def _waits_for(self, eng, reads, writes, is_dma):
        need = {}

        def add(ev):
            sem, val, src, src_dma = ev
            if src == eng and eng == 'pe' and not src_dma and not is_dma:
                return
            k = id(sem)
            if self.known[eng].get(k, 0) >= val:
                return
            if k not in need or need[k][1] < val:
                need[k] = (sem, val)

        for r in reads:
            ev = self.last_w.get(r)
            if ev is not None:
                add(ev)
        for w in writes:
            ev = self.last_w.get(w)
            if ev is not None:
                add(ev)
            rd = self.readers.get(w)
            if rd:
                for ev in rd.values():
                    add(ev)
        out = list(need.values())
        for sem, val in out:
            self.known[eng][id(sem)] = val
        return out

    def _commit(self, ev, reads, writes):
        for r in reads:
            self.readers.setdefault(r, {})[id(ev[0])] = ev
        for w in writes:
            self.last_w[w] = ev
            self.readers[w] = {}

    def op(self, eng, fn, reads=(), writes=()):
        pr = tuple(r for r in reads if isinstance(r, str) and r.startswith('pb'))
        if pr and eng != 'pe':
            writes = tuple(writes) + pr
        waits = self._waits_for(eng, reads, writes, False)
        if self.cnt[eng] >= EPOCH:
            self._old_epochs.append((self.cur_sem[eng], self.cnt[eng]))
            self.cur_sem[eng] = self._new_sem(eng)
            self.cnt[eng] = 0
        self.cnt[eng] += 1
        sem = self.cur_sem[eng]
        ev = (sem, self.cnt[eng], eng, False)
        self.ops[eng].append((waits, fn, sem, 1))
        self._commit(ev, reads, writes)
        self.nops += 1
        return ev

    def dma(self, eng, fn, key, reads=(), writes=(), inc=16):
        if key not in self.dma_sems:
            self.dma_sems[key] = self._new_sem('d' + str(key))
            self.dma_cnt[key] = 0
        sem = self.dma_sems[key]
        chan = ('__chan__', key)
        waits = self._waits_for(eng, tuple(reads), tuple(writes) + (chan,), True)
        self.dma_cnt[key] += inc
        ev = (sem, self.dma_cnt[key], eng, True)
        self.ops[eng].append((waits, fn, sem, inc))
        self._commit(ev, reads, tuple(writes) + (chan,))
        self.nops += 1
        return ev

    def barrier(self):
        latest = {}
        for e in self.ENGS:
            for (waits, fn, sem, inc) in ():
                pass
        for e in self.ENGS:
            if self.cnt[e] > 0:
                latest[id(self.cur_sem[e])] = (self.cur_sem[e], self.cnt[e])
        for k, sem in self.dma_sems.items():
            if self.dma_cnt[k] > 0:
                latest[id(sem)] = (sem, self.dma_cnt[k])
        for (sem, val) in self._old_epochs:
            latest[id(sem)] = (sem, val)
        for e in self.ENGS:
            waits = []
            for k, (sem, val) in latest.items():
                if self.known[e].get(k, 0) < val:
                    waits.append((sem, val))
                    self.known[e][k] = val
            if waits:
                self.ops[e].append((waits, None, None, 0))

    def wait_all(self, eng, resources):
        waits = self._waits_for(eng, resources, (), True)
        self.ops[eng].append((waits, None, None, 0))

    def emit(self):
        nc = self.nc
        ops = self.ops
        with nc.Block() as block:
            def run(e, lst):
                for waits, fn, sem, inc in lst:
                    for s, v in waits:
                        e.wait_ge(s, v)
                    if fn is not None:
                        fn(e).then_inc(sem, inc)

            @block.tensor
            def _(e):
                run(e, ops['pe'])

            @block.scalar
            def _(e):
                run(e, ops['act'])

            @block.vector
            def _(e):
                run(e, ops['dve'])

            @block.gpsimd
            def _(e):
                run(e, ops['pool'])

            @block.sync
            def _(e):
                run(e, ops['sp'])


class KB:
    def __init__(self, name="k"):
        self.nc = bass.Bass("TRN2", target_bir_lowering=False)
        self.nc.allow_low_precision("bf16 matmul operands with fp32 PSUM accumulation")
        self.tr = Tracker(self.nc)
        self._n = 0
        self.outs = []

    def din(self, name, shape, dt=F32):
        return self.nc.dram_tensor(name, list(shape), dt, kind="ExternalInput").ap()

    def dout(self, name, shape, dt=F32):
        self.outs.append(name)
        return self.nc.dram_tensor(name, list(shape), dt, kind="ExternalOutput").ap()

    def dscratch(self, name, shape, dt=F32):
        return self.nc.dram_tensor(name, list(shape), dt, kind="Internal").ap()

    def sb(self, name, shape, dt=F32):
        return self.nc.alloc_sbuf_tensor(name, list(shape), dt)

    def ps(self, name, shape, dt=F32):
        return self.nc.alloc_psum_tensor(name, list(shape), dt)

    def dma(self, out, in_, key, r=(), w=(), eng='sp'):
        self.tr.dma(eng, lambda e: e.dma_start(out=out, in_=in_), key, reads=r, writes=w)

    def mm(self, out, lhsT, rhs, start, stop, r, w):
        self.tr.op('pe', lambda e: e.matmul(out, lhsT=lhsT, rhs=rhs, start=start, stop=stop), reads=r, writes=w)

    def tp(self, out, in_, ident, r, w):
        self.tr.op('pe', lambda e: e.transpose(out=out, in_=in_, identity=ident), reads=r, writes=w)

    def act(self, out, in_, func, r, w, bias=None, scale=None, accum=None):
        kw = {}
        if bias is not None:
            kw['bias'] = bias
        if scale is not None:
            kw['scale'] = scale
        if accum is not None:
            kw['accum_out'] = accum
        self.tr.op('act', lambda e: e.activation(out=out, in_=in_, func=func, **kw), reads=r, writes=w)

    def tt(self, eng, out, a, b, op, r, w):
        self.tr.op(eng, lambda e: e.tensor_tensor(out=out, in0=a, in1=b, op=op), reads=r, writes=w)

    def ts(self, eng, out, a, s1, op0, r, w, s2=None, op1=None, accum=None):
        kw = {}
        if op1 is not None:
            kw['op1'] = op1
        if accum is not None:
            kw['accum_out'] = accum
        self.tr.op(eng, lambda e: e.tensor_scalar(out=out, in0=a, scalar1=s1, scalar2=s2, op0=op0, **kw),
                   reads=r, writes=w)

    def stt(self, eng, out, a, s, b, op0, op1, r, w):
        eng = 'dve'
        self.tr.op(eng, lambda e: e.scalar_tensor_tensor(out=out, in0=a, scalar=s, in1=b, op0=op0, op1=op1),
                   reads=r, writes=w)

    def cp(self, eng, out, in_, r, w):
        if eng == 'act':
            self.tr.op('act', lambda e: e.copy(out=out, in_=in_), reads=r, writes=w)
        else:
            self.tr.op(eng, lambda e: e.tensor_copy(out=out, in_=in_), reads=r, writes=w)

    def memset(self, eng, out, val, w):
        self.tr.op(eng, lambda e: e.memset(out, val), writes=w)

    def recip(self, out, in_, r, w):
        self.tr.op('dve', lambda e: e.reciprocal(out=out, in_=in_), reads=r, writes=w)

    def scan(self, out, d0, d1, r, w):
        self.tr.op('dve', lambda e: e.tensor_tensor_scan(out=out, data0=d0, data1=d1, initial=0.0,
                                                         op0=ALU.mult, op1=ALU.add), reads=r, writes=w)

    def finish(self, out_resources):
        self.tr.wait_all('sp', out_resources)
        self.tr.emit()
        return self.nc


import numpy as np


def p1_consts():
    C = np.zeros((128, 2048), np.float32)
    i = np.arange(128)
    C[:, 0:128] = np.eye(128)
    C[:, 128:256] = (i[None, :] >= i[:, None])
    C[:, 256:384] = 1.0
    blk = (i[:, None] // 64 == i[None, :] // 64).astype(np.float32)
    C[:, 384:512] = blk
    C[:, 512:640] = blk / 64.0
    C[:, 640:768] = blk * (i[:, None] < i[None, :])
    C[:, 768:896] = blk * (i[:, None] > i[None, :])
    C[:, 896:1024] = blk * (i[:, None] <= i[None, :])
    rm = np.ones(512, np.float32)
    rm[::64] = 0
    C[:, 1024:1536] = rm[None, :]
    return C


def p1_core_consts(g):
    Cb = np.zeros((128, 1024), np.float32)
    p = np.arange(128)
    hl = p // 64
    d = p % 64
    heads = 2 * g + hl
    lg = np.log1p(-np.exp2(-5.0 - heads.astype(np.float64)))
    inv_freq = 10000.0 ** (-(d % 32).astype(np.float64) / 32.0)
    Cb[:, 0] = inv_freq
    Cb[:, 1] = np.where(d < 32, -1.0, 1.0)
    Cb[:, 2] = np.exp(128.0 * lg)
    idx = np.arange(128, dtype=np.float64)
    Cb[:, 128:256] = np.exp((idx[None, :] + 1.0) * lg[:, None])
    Cb[:, 256:384] = np.exp((127.0 - idx)[:, None] * lg[None, :])
    for h2 in range(2):
        lgh = np.log1p(-np.exp2(-5.0 - (2 * g + h2)))
        rel = idx[None, :] - idx[:, None]
        Cb[:, 384 + h2 * 128:384 + (h2 + 1) * 128] = np.where(rel >= 0, np.exp(rel * lgh), 0.0)
    return Cb


M_COLS = 3088
RW_COLS = 3520


def p1_core_inputs(inp, l, b, g):
    w_in = inp['w_in'][l]
    o = {}
    zc = np.arange(256 * g, 256 * g + 256)
    xsc = 1024 + np.arange(256 * g, 256 * g + 256)
    Bc = 2048 + np.arange(128 * g, 128 * g + 128)
    Cc = 2560 + np.arange(128 * g, 128 * g + 128)
    dtc = 3072 + np.arange(4 * g, 4 * g + 4)
    o['w_mfm'] = np.ascontiguousarray(w_in[:, np.concatenate([xsc, Bc, Cc])])
    o['w_mtm'] = np.ascontiguousarray(w_in[:, np.concatenate([zc, dtc])])
    convc = np.concatenate([xsc, Bc, Cc]) - 1024
    cw = np.concatenate([inp['m_conv_w'][l][:, convc], inp['m_conv_b'][l][None, convc]], axis=0)
    o['m_cw'] = np.ascontiguousarray(cw.T.reshape(4, 128, 5).transpose(1, 0, 2))
    o['m_hd'] = np.ascontiguousarray(inp['m_head'][l][:, 4 * g:4 * g + 4].reshape(1, 12))
    o['m_ng'] = np.ascontiguousarray(inp['m_norm_g'][l][None, 256 * g:256 * g + 256])
    r0 = M_COLS
    ch = np.arange(256 * g, 256 * g + 256)
    cols = np.concatenate([r0 + ch, r0 + 1024 + ch, r0 + 2048 + ch, r0 + 3072 + np.arange(96),
                           r0 + 3168 + np.arange(96), r0 + 3264 + np.arange(256)])
    o['w_wfm'] = np.ascontiguousarray(w_in[:, cols])
    mu = inp['rwkv_mu'][l][cols - r0]
    mut = np.zeros((128, 10), np.float32)
    offs = [0, 128, 256, 384, 512, 640, 768, 864, 960, 1088]
    sizes = [128, 128, 128, 128, 128, 128, 96, 96, 128, 128]
    for q, (of, sz) in enumerate(zip(offs, sizes)):
        mut[:sz, q] = mu[of:of + sz]
    o['rw_mu'] = mut
    vec = inp['rwkv_vec'][l][:, ch]
    o['rw_vec'] = np.ascontiguousarray(vec.T.reshape(2, 128, 7).transpose(1, 0, 2))
    o['rw_w2'] = np.ascontiguousarray(inp['rwkv_w2'][l][:, ch])
    o['rw_a2'] = np.ascontiguousarray(inp['rwkv_a2'][l][:, ch])
    o['rw_g2'] = np.ascontiguousarray(inp['rwkv_g2'][l][:, ch])
    t0 = M_COLS + RW_COLS
    hd = np.arange(128 * g, 128 * g + 128)
    d = hd % 64
    partner = np.where(d < 32, hd + 32, hd - 32)
    cols = np.concatenate([t0 + hd, t0 + partner, t0 + 512 + hd, t0 + 512 + partner])
    o['w_rfm'] = np.ascontiguousarray(w_in[:, cols])
    cols = np.concatenate([t0 + 1024 + ch, t0 + 2048 + ch])
    o['w_rtm'] = np.ascontiguousarray(w_in[:, cols])
    o['c_col'] = np.ascontiguousarray(inp['c'][b].reshape(16, 128).T)
    o['w_ada'] = np.ascontiguousarray(inp['w_ada'][l][:, :4096])
    o['b_ada'] = np.ascontiguousarray(inp['b_ada'][l][None, :4096])
    o['norm_g'] = inp['norm_g'][l]
    o['pos'] = np.ascontiguousarray(inp['positions'][b][None, :]).astype(np.int32)
    o['cst'] = p1_consts()
    o['cstb'] = p1_core_consts(g)
    return o


def p2_core_inputs(inp, l, b, q, Tq, x_b, yT_b):
    o = {}
    t0 = q * Tq
    NT = 128 + Tq
    if x_b is not None:
        xh = np.zeros((NT, 2048), np.float32)
        yh = np.zeros((3, 1024, NT), yT_b.dtype)
        if q == 0:
            xh[128:] = x_b[0:Tq]
            yh[:, :, 128:] = yT_b[:, :, 0:Tq]
        else:
            xh[:] = x_b[t0 - 128:t0 + Tq]
            yh[:] = yT_b[:, :, t0 - 128:t0 + Tq]
        o['xh'] = xh
        o['yTh'] = yh
    o['hmask'] = np.full((128, 1), 0.0 if q == 0 else 1.0, np.float32)
    o['c_col'] = np.ascontiguousarray(inp['c'][b].reshape(16, 128).T)
    o['w_ada'] = inp['w_ada'][l]
    o['b_ada'] = inp['b_ada'][l][None, :]
    ng = inp['norm_g'][l]
    o['norm_g'] = ng
    o['ng_col'] = np.ascontiguousarray(ng.reshape(4, 16, 128).transpose(2, 0, 1).reshape(128, 64))
    o['w_gate'] = inp['w_gate'][l]
    o['w_branch'] = inp['w_branch'][l]
    o['w_out'] = inp['w_out'][l]
    o['w_up'] = inp['w_up'][l]
    fcv = np.concatenate([inp['f_conv_w'][l], inp['f_conv_b'][l][None, :]], axis=0)
    o['f_cv'] = np.ascontiguousarray(fcv.T.reshape(86, 128, 4).transpose(1, 0, 2))
    o['w_down'] = inp['w_down'][l]
    o['cst'] = p1_consts()
    return o


from contextlib import ExitStack
import numpy as np


MAGIC = 12582912.0
TWO_PI = float(2 * np.pi)


class Scope:
    _n = 0

    def __init__(self, kb):
        self.kb = kb
        self.st = ExitStack()
        Scope._n += 1
        self.tag = f"_sc{Scope._n}"

    def sb(self, name, shape, dt=F32):
        return self.st.enter_context(self.kb.nc.sbuf_tensor(name + self.tag, list(shape), dt))

    def close(self):
        self.kb.tr.barrier()
        self.st.close()


def load_cast_weights(kb, sc, wdram, ncols, name, stage_cols=128):
    W = sc.sb(name, [128, 16, ncols], BF16)
    stg = [sc.sb(f"{name}_stg{i}", [128, 16, stage_cols], F32) for i in range(2)]
    wv = wdram.rearrange("(k p) n -> p k n", p=128)
    i = 0
    for c0 in range(0, ncols, stage_cols):
        c1 = min(ncols, c0 + stage_cols)
        s = stg[i % 2]
        rn = f"{name}_stg{i % 2}"
        kb.dma(s[:, :, 0:c1 - c0], wv[:, :, c0:c1], rn, w=[rn])
        kb.cp('pool' if i % 2 else 'dve', W[:, :, c0:c1], s[:, :, 0:c1 - c0], r=[rn], w=[name])
        i += 1
    return W


def emit_mod(kb, sc, c_col, w_ada, b_ada, vec_ids, ones_row, pb, tag):
    cc = sc.sb(f"{tag}_cc", [128, 16])
    scl = sc.sb(f"{tag}_sc", [128, 16])
    kb.dma(cc[:], c_col, f"{tag}_cc", w=[f"{tag}_cc"])
    kb.act(scl[:], cc[:], AF.Silu, r=[f"{tag}_cc"], w=[f"{tag}_sc"])
    stg = [sc.sb(f"{tag}_wa{i}", [128, 16, 128], F32) for i in range(2)]
    wv = w_ada.rearrange("(k p) n -> p k n", p=128)
    rows = {}
    i = 0
    for vid in vec_ids:
        row = sc.sb(f"{tag}_row{vid}", [1, 2048])
        brow = sc.sb(f"{tag}_brow{vid}", [1, 2048])
        rn = f"{tag}_row{vid}"
        kb.dma(brow[:], b_ada[0:1, vid * 2048:(vid + 1) * 2048], f"{tag}_brow{vid}", w=[f"{tag}_brow{vid}"])
        for cch in range(16):
            c0 = vid * 2048 + cch * 128
            s = stg[i % 2]
            sn = f"{tag}_wa{i % 2}"
            kb.dma(s[:], wv[:, :, c0:c0 + 128], sn, w=[sn])
            pr = "pb7" if i % 2 == 0 else "pb4"
            pcols = (pb[7] if i % 2 == 0 else pb[4])[0:1, 0:128]
            for k in range(16):
                kb.mm(pcols, lhsT=scl[:, k:k + 1], rhs=s[:, k, :], start=(k == 0), stop=(k == 15),
                      r=[sn, f"{tag}_sc"], w=[pr])
            kb.tt('dve', row[0:1, cch * 128:(cch + 1) * 128], pcols, brow[0:1, cch * 128:(cch + 1) * 128], ALU.add,
                  r=[pr, f"{tag}_brow{vid}"], w=[rn])
            i += 1
        rows[vid] = (row, rn)
    return rows


def bcast_row(kb, row, rn, ones_row, pb, emit_chunk):
    for c in range(4):
        pr = "pb6" if c % 2 == 0 else "pb5"
        pa = pb[6][:, 0:512] if c % 2 == 0 else pb[5][:, 0:512]
        kb.mm(pa, lhsT=ones_row[0:1, 0:128], rhs=row[0:1, c * 512:(c + 1) * 512], start=True, stop=True,
              r=[rn, 'C'], w=[pr])
        emit_chunk(c, pa, pr)


P1_PARAM_SHAPES = {
    "w_mfm": ([D, 512], F32), "w_mtm": ([D, 260], F32), "m_cw": ([128, 4, 5], F32), "m_hd": ([1, 12], F32),
    "m_ng": ([1, 256], F32), "w_rfm": ([D, 512], F32), "w_rtm": ([D, 512], F32), "w_wfm": ([D, 1216], F32),
    "rw_mu": ([128, 10], F32), "rw_vec": ([128, 2, 7], F32), "rw_w2": ([96, 256], F32), "rw_a2": ([96, 256], F32),
    "rw_g2": ([256, 256], F32),
}


def setup_globals(kb, cst):
    G = {}
    G['pb'] = [kb.ps(f"pb{i}", [128, 512]) for i in range(8)]
    C = kb.sb("C", [128, 2048])
    kb.dma(C[:], cst, "C", w=["C"])
    Cb = kb.sb("Cb", [128, 256], BF16)
    kb.cp('dve', Cb[:, 0:128], C[:, 0:128], r=["C"], w=["Cb"])
    G['C'] = C
    G['Cb'] = Cb
    return G


def build_p1(S, parts=('h', 'mamba', 'rwkv', 'ret'), env=None):
    NM = S // 512
    if env is None:
        kb = KB()
        e = {}
        e['x'] = kb.din("x", [S, D])
        e['c_col'] = kb.din("c_col", [128, 16])
        e['w_ada'] = kb.din("w_ada", [D, 2 * D])
        e['b_ada'] = kb.din("b_ada", [1, 2 * D])
        e['norm_g'] = kb.din("norm_g", [4, D])
        e['pos'] = kb.din("pos", [1, S], I32)
        cst = kb.din("cst", [128, 2048])
        e['cstb'] = kb.din("cstb", [128, 1024])
        for k, (shp, dt_) in P1_PARAM_SHAPES.items():
            e[k] = kb.din(k, shp, dt_)
        e['yT'] = kb.dout("yT", [768, S], BF16)
        e['hTd'] = kb.dscratch("hTd", [NM, 128, 16, 512], BF16)
        e['hTo'] = e['hTd']
        e['NH'] = S // 128
        G = setup_globals(kb, cst)
    else:
        kb = env['kb']
        e = env
        G = env['G']
    nc = kb.nc
    x, c_col, w_ada, b_ada, norm_g, pos, cstb = (e[k] for k in ('x', 'c_col', 'w_ada', 'b_ada', 'norm_g', 'pos', 'cstb'))
    w_mfm, w_mtm, m_cw, m_hd, m_ng, w_rfm, w_rtm, w_wfm = (e[k] for k in ('w_mfm', 'w_mtm', 'm_cw', 'm_hd', 'm_ng', 'w_rfm', 'w_rtm', 'w_wfm'))
    rw_mu, rw_vec, rw_w2, rw_a2, rw_g2 = (e[k] for k in ('rw_mu', 'rw_vec', 'rw_w2', 'rw_a2', 'rw_g2'))
    yT, hTd, hTo, NH = e['yT'], e['hTd'], e['hTo'], e['NH']

    def _load_hT(dst, m, key):
        kb.dma(dst[:], hTd[m], key, r=[f"hTd{m}"], w=[key])

    def _yT_ap(row0, nrows, m):
        return yT[row0:row0 + nrows, m * 512:(m + 1) * 512]

    load_hT = e.get('load_hT', _load_hT)
    yT_ap = e.get('yT_ap', _yT_ap)
    after_y = e.get('after_y', lambda i, m, res: None)
    pb = G['pb']
    C, Cb = G['C'], G['Cb']
    ident_f = C[:, 0:128]
    U_f = C[:, 128:256]
    ones_f = C[:, 256:384]
    blk_f = C[:, 384:512]
    blk64_f = C[:, 512:640]
    SU_f = C[:, 640:768]
    SL_f = C[:, 768:896]
    UI_f = C[:, 896:1024]
    rmask = C[:, 1024:1536]
    ident_b = Cb[:, 0:128]
    ones_row = C[0:1, 256:384]

    if 'h' in parts:
        sc = Scope(kb)
        rows = emit_mod(kb, sc, c_col, w_ada, b_ada, [0, 1], ones_row, pb, "m0")
        Gbc = sc.sb("Gbc", [128, 2048])
        SHbc = sc.sb("SHbc", [128, 2048])
        kb.dma(Gbc[:], norm_g[0:1, :].partition_broadcast(128), "Gbc", w=["Gbc"])

        def g_chunk(c, pa, pr):
            kb.stt('dve', Gbc[:, c * 512:(c + 1) * 512], pa, 1.0, Gbc[:, c * 512:(c + 1) * 512], ALU.add, ALU.mult,
                   r=[pr, "Gbc"], w=["Gbc"])

        def sh_chunk(c, pa, pr):
            kb.cp('act', SHbc[:, c * 512:(c + 1) * 512], pa, r=[pr], w=["SHbc"])

        bcast_row(kb, rows[1][0], rows[1][1], ones_row, pb, g_chunk)
        bcast_row(kb, rows[0][0], rows[0][1], ones_row, pb, sh_chunk)
        xt = [sc.sb(f"xt{i}", [128, 2048]) for i in range(2)]
        junk = sc.sb("junk", [128, 2048], BF16)
        tmp = sc.sb("htmp", [128, 2048])
        hb = sc.sb("hb", [128, 2048], BF16)
        ss = sc.sb("ss", [128, 1])
        rstd = sc.sb("rstd", [128, 1])
        hst = [sc.sb(f"hst{i}", [128, 16, 512], BF16) for i in range(2)]
        for i in range(NH):
            m, sub = divmod(i, 4)
            xb = xt[i % 2]
            xn = f"xt{i % 2}"
            kb.dma(xb[:], x[i * 128:(i + 1) * 128, :], xn, w=[xn])
            kb.act(junk[:], xb[:], AF.Square, r=[xn], w=["junk", "ss"], accum=ss[:])
            kb.act(rstd[:], ss[:], AF.Sqrt, r=["ss"], w=["rstd"], bias=1e-6, scale=1.0 / D)
            kb.recip(rstd[:], rstd[:], r=["rstd"], w=["rstd"])
            kb.stt('dve', tmp[:], xb[:], rstd[:, 0:1], Gbc[:], ALU.mult, ALU.mult, r=[xn, "rstd", "Gbc"], w=["htmp"])
            kb.tt('pool', hb[:], tmp[:], SHbc[:], ALU.add, r=["htmp", "SHbc"], w=["hb"])
            hs = hst[m % 2]
            hn = f"hst{m % 2}"
            for q in range(4):
                pr = f"pb{q}"
                for kk in range(4):
                    k = q * 4 + kk
                    kb.mm(pb[q][:, kk * 128:(kk + 1) * 128], lhsT=hb[:, k * 128:(k + 1) * 128], rhs=ident_b,
                          start=True, stop=True, r=["hb", "Cb"], w=[pr])
                eng = 'act' if q % 2 == 0 else 'dve'
                kb.cp(eng, hs[:, q * 4:(q + 1) * 4, sub * 128:(sub + 1) * 128],
                      pb[q][:, :].rearrange("p (k t) -> p k t", k=4), r=[pr], w=[hn])
            if sub == 3:
                kb.dma(hTo[m], hs[:], hn, r=[hn], w=[f"hTd{m}"])
                if 'after_h' in e:
                    e['after_h'](m)
        sc.close()

    if 'mamba' in parts:
        sc = Scope(kb)
        Wfm = load_cast_weights(kb, sc, w_mfm, 512, "Wfm")
        Wtm = load_cast_weights(kb, sc, w_mtm, 260, "Wtm")
        cw = sc.sb("cw", [128, 4, 5])
        kb.dma(cw[:], m_cw, "cw", w=["cw"])
        mh = sc.sb("mh", [128, 12])
        kb.dma(mh[:], m_hd.partition_broadcast(128), "mh", w=["mh"])
        negA = sc.sb("negA", [128, 4])
        kb.act(negA[:], mh[:, 4:8], AF.Exp, r=["mh"], w=["negA"])
        kb.ts('dve', negA[:], negA[:], -1.0, ALU.mult, r=["negA"], w=["negA"])
        dsk = sc.sb("dsk", [128, 256])
        for h in range(4):
            kb.ts('dve', dsk[:, h * 64:(h + 1) * 64], ones_f[:, 0:64], mh[:, 8 + h:9 + h], ALU.mult,
                  r=["C", "mh"], w=["dsk"])
        mng = sc.sb("mng", [128, 256])
        kb.dma(mng[:], m_ng.partition_broadcast(128), "mng", w=["mng"])
        hT = [sc.sb(f"hT{i}", [128, 16, 512], BF16) for i in range(2)]
        raw = [sc.sb(f"raw{j}", [128, 515]) for j in range(4)]
        acc = [sc.sb(f"acc{j}", [128, 512]) for j in range(2)]
        xs_act = [sc.sb(f"xsa{j}", [128, 512]) for j in range(2)]
        B_act = sc.sb("B_act", [128, 512])
        BT_bf = sc.sb("BT_bf", [128, 512], BF16)
        CT_bf = sc.sb("CT_bf", [128, 512], BF16)
        zs = sc.sb("zs", [128, 256])
        dtp = sc.sb("dtp", [128, 4])
        dt = sc.sb("dt", [128, 4])
        adt = sc.sb("adt", [128, 4])
        xs_sb = sc.sb("xs_sb", [128, 256])
        B_tm = sc.sb("B_tm", [128, 128], BF16)
        xdt_bf = sc.sb("xdt_bf", [128, 256], BF16)
        xte_bf = sc.sb("xte_bf", [128, 256], BF16)
        cum_sb = sc.sb("cum_sb", [128, 4])
        rh = sc.sb("rh", [128, 4, 128])
        CBm = sc.sb("CBm", [128, 128])
        seg = sc.sb("seg", [128, 4, 128])
        dec = sc.sb("dec", [128, 4, 128])
        MT_bf = sc.sb("MT_bf", [128, 4, 128], BF16)
        ecum = sc.sb("ecum", [128, 4])
        tte = sc.sb("tte", [128, 4])
        ecl = sc.sb("ecl", [128, 4])
        yi_sb = sc.sb("yi_sb", [128, 256])
        y = sc.sb("y", [128, 256])
        y2 = sc.sb("y2", [128, 256])
        state = sc.sb("state", [128, 256])
        state_bf = sc.sb("state_bf", [128, 256], BF16)
        mjunk = sc.sb("mjunk", [128, 256])
        mss = sc.sb("mss", [128, 1])
        mrstd = sc.sb("mrstd", [128, 1])
        yo_bf = sc.sb("yo_bf", [128, 256], BF16)
        yst = [sc.sb(f"yst{i}", [128, 2, 512], BF16) for i in range(2)]
        kb.memset('pool', state[:], 0.0, w=["state"])
        kb.memset('pool', state_bf[:], 0.0, w=["state_bf"])
        for j in range(4):
            kb.memset('pool', raw[j][:], 0.0, w=[f"raw{j}"])
        for m in range(NM):
            h_ = hT[m % 2]
            hn = f"hT{m % 2}"
            load_hT(h_, m, hn)
            ys = yst[m % 2]
            ysn = f"yst{m % 2}"
            for j in range(4):
                pr = f"pb{j % 2}"
                pa = pb[j % 2]
                for k in range(16):
                    kb.mm(pa[:, :], lhsT=Wfm[:, k, j * 128:(j + 1) * 128], rhs=h_[:, k, :], start=(k == 0),
                          stop=(k == 15), r=["Wfm", hn], w=[pr])
                rj = f"raw{j}"
                kb.cp('pool', raw[j][:, 0:3], raw[j][:, 512:515], r=[rj], w=[rj])
                kb.cp('act', raw[j][:, 3:515], pa[:, :], r=[pr], w=[rj])
                a_ = acc[j % 2]
                an = f"acc{j % 2}"
                kb.ts('dve', a_[:], raw[j][:, 3:515], cw[:, j, 3:4], ALU.mult, r=[rj, "cw"], w=[an],
                      s2=cw[:, j, 4:5], op1=ALU.add)
                for tpi, eng in ((2, 'pool'), (1, 'dve'), (0, 'pool')):
                    kb.stt(eng, a_[:], raw[j][:, tpi:tpi + 512], cw[:, j, tpi:tpi + 1], a_[:], ALU.mult, ALU.add,
                           r=[rj, "cw", an], w=[an])
                if j < 2:
                    kb.act(xs_act[j][:], a_[:], AF.Silu, r=[an], w=[f"xsa{j}"])
                elif j == 2:
                    kb.act(B_act[:], a_[:], AF.Silu, r=[an], w=["B_act"])
                    kb.cp('pool', BT_bf[:], B_act[:], r=["B_act"], w=["BT_bf"])
                else:
                    kb.act(CT_bf[:], a_[:], AF.Silu, r=[an], w=["CT_bf"])
            import os as _os
            _nch = int(_os.environ.get("MB_NCH", "4"))
            for c in range(_nch):
                cs = slice(c * 128, (c + 1) * 128)
                _sect = _os.environ.get('MB_SECT', '123456')
                if '1' in _sect:
                    for k in range(16):
                        kb.mm(pb[2][:, 0:260], lhsT=h_[:, k, cs], rhs=Wtm[:, k, :], start=(k == 0), stop=(k == 15),
                              r=[hn, "Wtm"], w=["pb2"])
                    kb.act(zs[:], pb[2][:, 0:256], AF.Silu, r=["pb2"], w=["zs"])
                    kb.tt('dve', dtp[:], pb[2][:, 256:260], mh[:, 0:4], ALU.add, r=["pb2", "mh"], w=["dtp"])
                    kb.act(dtp[:], dtp[:], AF.Exp, r=["dtp"], w=["dtp"])
                    kb.act(dt[:], dtp[:], AF.Ln, r=["dtp"], w=["dt"], bias=1.0)
                    kb.tt('dve', adt[:], dt[:], negA[:], ALU.mult, r=["dt", "negA"], w=["adt"])
                if '2' in _sect:
                    for j in range(2):
                        kb.mm(pb[3][:, j * 128:(j + 1) * 128], lhsT=xs_act[j][:, cs], rhs=ident_f, start=True, stop=True,
                              r=[f"xsa{j}", "C"], w=["pb3"])
                    kb.mm(pb[3][:, 256:384], lhsT=B_act[:, cs], rhs=ident_f, start=True, stop=True,
                          r=["B_act", "C"], w=["pb3"])
                    kb.cp('act', xs_sb[:], pb[3][:, 0:256], r=["pb3"], w=["xs_sb"])
                    kb.cp('dve', B_tm[:], pb[3][:, 256:384], r=["pb3"], w=["B_tm"])
                    for h in range(4):
                        kb.ts('pool', xdt_bf[:, h * 64:(h + 1) * 64], xs_sb[:, h * 64:(h + 1) * 64], dt[:, h:h + 1],
                              ALU.mult, r=["xs_sb", "dt"], w=["xdt_bf"])
                if '3' in _sect:
                    kb.mm(pb[4][:, 0:4], lhsT=U_f, rhs=adt[:], start=True, stop=True, r=["C", "adt"], w=["pb4"])
                    kb.cp('act', cum_sb[:], pb[4][:, 0:4], r=["pb4"], w=["cum_sb"])
                    for h in range(4):
                        kb.ts('pool', rh[:, h, :], U_f, adt[:, h:h + 1], ALU.mult, r=["C", "adt"], w=["rh"])
                    for h in range(4):
                        kb.mm(pb[5][:, h * 128:(h + 1) * 128], lhsT=ones_f, rhs=rh[:, h, :], start=True, stop=True,
                              r=["C", "rh"], w=["pb5"])
                    kb.mm(pb[4][:, 128:256], lhsT=BT_bf[:, cs], rhs=CT_bf[:, cs], start=True, stop=True,
                          r=["BT_bf", "CT_bf"], w=["pb4"])
                    kb.tt('dve', CBm[:], pb[4][:, 128:256], U_f, ALU.mult, r=["pb4", "C"], w=["CBm"])
                    for h in range(4):
                        kb.ts('dve', seg[:, h, :], pb[5][:, h * 128:(h + 1) * 128], cum_sb[:, h:h + 1], ALU.subtract,
                              r=["pb5", "cum_sb"], w=["seg"], s2=0.0, op1=ALU.min)
                    kb.act(dec[:], seg[:], AF.Exp, r=["seg"], w=["dec"])
                    for h in range(4):
                        kb.tt('pool', MT_bf[:, h, :], dec[:, h, :], CBm[:], ALU.mult, r=["dec", "CBm"], w=["MT_bf"])
                    kb.act(ecum[:], cum_sb[:], AF.Exp, r=["cum_sb"], w=["ecum"])
                    last = pb[5][:, :].rearrange("p (h i) -> p h i", h=4)[:, :, 127:128]
                    kb.tt('dve', tte[:].rearrange("p (h o) -> p h o", o=1), last,
                          cum_sb[:].rearrange("p (h o) -> p h o", o=1), ALU.subtract, r=["pb5", "cum_sb"], w=["tte"])
                    kb.act(tte[:], tte[:], AF.Exp, r=["tte"], w=["tte"])
                    kb.act(ecl[:].rearrange("p (h o) -> p h o", o=1), last, AF.Exp, r=["pb5"], w=["ecl"])
                if '4' in _sect:
                    for h in range(4):
                        kb.mm(pb[6][:, h * 64:(h + 1) * 64], lhsT=MT_bf[:, h, :], rhs=xdt_bf[:, h * 64:(h + 1) * 64],
                              start=True, stop=True, r=["MT_bf", "xdt_bf"], w=["pb6"])
                    kb.mm(pb[6][:, 256:512], lhsT=CT_bf[:, cs], rhs=state_bf[:], start=True, stop=True,
                          r=["CT_bf", "state_bf"], w=["pb6"])
                    kb.cp('act', yi_sb[:], pb[6][:, 0:256], r=["pb6"], w=["yi_sb"])
                    for h in range(4):
                        hs_ = slice(h * 64, (h + 1) * 64)
                        kb.stt('dve', y[:, hs_], pb[6][:, 256 + h * 64:256 + (h + 1) * 64], ecum[:, h:h + 1], yi_sb[:, hs_],
                               ALU.mult, ALU.add, r=["pb6", "ecum", "yi_sb"], w=["y"])
                if '5' in _sect:
                    for h in range(4):
                        hs_ = slice(h * 64, (h + 1) * 64)
                        kb.ts('pool', xte_bf[:, hs_], xdt_bf[:, hs_], tte[:, h:h + 1], ALU.mult, r=["xdt_bf", "tte"],
                              w=["xte_bf"])
                    kb.mm(pb[7][:, 0:256], lhsT=B_tm[:], rhs=xte_bf[:], start=True, stop=True, r=["B_tm", "xte_bf"],
                          w=["pb7"])
                    for h in range(4):
                        hs_ = slice(h * 64, (h + 1) * 64)
                        kb.stt('dve', state[:, hs_], state[:, hs_], ecl[:, h:h + 1], pb[7][:, hs_], ALU.mult, ALU.add,
                               r=["state", "ecl", "pb7"], w=["state"])
                    kb.cp('pool', state_bf[:], state[:], r=["state"], w=["state_bf"])
                if '6' in _sect:
                    kb.tt('pool', y2[:], xs_sb[:], dsk[:], ALU.mult, r=["xs_sb", "dsk"], w=["y2"])
                    kb.tt('pool', y[:], y[:], y2[:], ALU.add, r=["y", "y2"], w=["y"])
                    kb.tt('pool', y[:], y[:], zs[:], ALU.mult, r=["y", "zs"], w=["y"])
                    kb.act(mjunk[:], y[:], AF.Square, r=["y"], w=["mjunk", "mss"], accum=mss[:])
                    kb.act(mrstd[:], mss[:], AF.Sqrt, r=["mss"], w=["mrstd"], bias=1e-5, scale=1.0 / 256)
                    kb.recip(mrstd[:], mrstd[:], r=["mrstd"], w=["mrstd"])
                    kb.stt('dve', yo_bf[:], y[:], mrstd[:, 0:1], mng[:], ALU.mult, ALU.mult, r=["y", "mrstd", "mng"],
                           w=["yo_bf"])
                    for t in range(2):
                        kb.mm(pb[7][:, 256 + t * 128:256 + (t + 1) * 128], lhsT=yo_bf[:, t * 128:(t + 1) * 128],
                              rhs=ident_b, start=True, stop=True, r=["yo_bf", "Cb"], w=["pb7"])
                    kb.cp('act', ys[:, :, cs], pb[7][:, 256:512].rearrange("p (t s) -> p t s", t=2), r=["pb7"], w=[ysn])
            kb.dma(yT_ap(0, 256, m).rearrange("(t p) s -> p t s", p=128), ys[:], ysn, r=[ysn],
                   w=["yT_m"])
            after_y(0, m, "yT_m")
        sc.close()

    if 'ret' in parts:
        sc = Scope(kb)
        Wfm = load_cast_weights(kb, sc, w_rfm, 512, "Wfm")
        Wtm = load_cast_weights(kb, sc, w_rtm, 512, "Wtm")
        CB = sc.sb("CB", [128, 1024])
        kb.dma(CB[:], cstb, "CB", w=["CB"])
        invf = CB[:, 0:1]
        sgn = CB[:, 1:2]
        cdec = CB[:, 2:3]
        QD = CB[:, 128:256]
        KDEC = CB[:, 256:384]
        DMT = CB[:, 384:640]
        hT = [sc.sb(f"hT{i}", [128, 16, 512], BF16) for i in range(2)]
        pi_ = sc.sb("pi", [128, 512], I32)
        ang = sc.sb("ang", [128, 512])
        kq = sc.sb("kq", [128, 512])
        rr = sc.sb("rr", [128, 512])
        sinT = sc.sb("sinT", [128, 512])
        cosT = sc.sb("cosT", [128, 512])
        t1 = sc.sb("t1", [128, 512])
        t2 = sc.sb("t2", [128, 512])
        qr_bf = sc.sb("qr_bf", [128, 512], BF16)
        kr_bf = sc.sb("kr_bf", [128, 512], BF16)
        krz = [sc.sb(f"krz{h}", [128, 512], BF16) for h in range(2)]
        qdz = [sc.sb(f"qdz{h}", [128, 512], BF16) for h in range(2)]
        v_bf = sc.sb("v_bf", [128, 256], BF16)
        gs = sc.sb("gs", [128, 256])
        ST_bf = sc.sb("ST_bf", [128, 256], BF16)
        kd_bf = sc.sb("kd_bf", [128, 128], BF16)
        rstate = sc.sb("rstate", [128, 128])
        rstate_bf = sc.sb("rstate_bf", [128, 128], BF16)
        rjunk = sc.sb("rjunk", [128, 128])
        rss = sc.sb("rss", [128, 2])
        rrs = sc.sb("rrs", [128, 2])
        yo_bf = sc.sb("ryo_bf", [128, 256], BF16)
        yst = [sc.sb(f"yst{i}", [128, 2, 512], BF16) for i in range(2)]
        kb.memset('pool', rstate[:], 0.0, w=["rstate"])
        kb.memset('pool', rstate_bf[:], 0.0, w=["rstate_bf"])
        for h in range(2):
            kb.memset('pool', krz[h][:], 0.0, w=[f"krz{h}"])
            kb.memset('pool', qdz[h][:], 0.0, w=[f"qdz{h}"])
        for m in range(NM):
            h_ = hT[m % 2]
            hn = f"hT{m % 2}"
            load_hT(h_, m, hn)
            ys = yst[m % 2]
            ysn = f"yst{m % 2}"
            kb.dma(pi_[:], pos[0:1, m * 512:(m + 1) * 512].partition_broadcast(128), "pi", w=["pi"])
            kb.cp('dve', ang[:], pi_[:], r=["pi"], w=["ang"])
            kb.ts('dve', ang[:], ang[:], invf, ALU.mult, r=["ang", "CB"], w=["ang"])
            for which, dst, shift in (("s", sinT, 0.0), ("c", cosT, float(np.pi / 2))):
                if shift != 0.0:
                    kb.ts('pool', rr[:], ang[:], shift, ALU.add, r=["ang"], w=["rr"])
                    src, srn = rr, "rr"
                else:
                    src, srn = ang, "ang"
                kb.ts('dve', kq[:], src[:], float(1.0 / TWO_PI), ALU.mult, r=[srn], w=["kq"], s2=MAGIC, op1=ALU.add)
                kb.ts('pool', kq[:], kq[:], -MAGIC, ALU.add, r=["kq"], w=["kq"])
                kb.stt('dve', rr[:], kq[:], -TWO_PI, src[:], ALU.mult, ALU.add, r=["kq", srn], w=["rr"])
                kb.ts('pool', rr[:], rr[:], 3.14159, ALU.min, r=["rr"], w=["rr"], s2=-3.14159, op1=ALU.max)
                if which == "s":
                    kb.act(dst[:], rr[:], AF.Sin, r=["rr", "CB"], w=["sinT"], scale=sgn)
                else:
                    kb.act(dst[:], rr[:], AF.Sin, r=["rr"], w=["cosT"])
            for j in range(4):
                pr = f"pb{j % 2}"
                pa = pb[j % 2]
                for k in range(16):
                    kb.mm(pa[:, :], lhsT=Wfm[:, k, j * 128:(j + 1) * 128], rhs=h_[:, k, :], start=(k == 0),
                          stop=(k == 15), r=["Wfm", hn], w=[pr])
                if j % 2 == 0:
                    kb.tt('dve', t1[:], pa[:, :], cosT[:], ALU.mult, r=[pr, "cosT"], w=["t1"])
                else:
                    kb.tt('dve', t2[:], pa[:, :], sinT[:], ALU.mult, r=[pr, "sinT"], w=["t2"])
                    if j == 1:
                        kb.tt('pool', qr_bf[:], t1[:], t2[:], ALU.add, r=["t1", "t2"], w=["qr_bf"])
                    else:
                        kb.tt('pool', t1[:], t1[:], t2[:], ALU.add, r=["t1", "t2"], w=["t1"])
                        kb.ts('pool', kr_bf[:], t1[:], 0.125, ALU.mult, r=["t1"], w=["kr_bf"])
            for h in range(2):
                ph = slice(h * 64, (h + 1) * 64)
                kb.cp('pool', krz[h][ph, :], kr_bf[ph, :], r=["kr_bf"], w=[f"krz{h}"])
                for c in range(4):
                    cs = slice(c * 128, (c + 1) * 128)
                    kb.tt('dve' if c % 2 else 'pool', qdz[h][ph, cs], qr_bf[ph, cs], QD[ph, :], ALU.mult,
                          r=["qr_bf", "CB"], w=[f"qdz{h}"])
            for c in range(4):
                cs = slice(c * 128, (c + 1) * 128)
                for k in range(16):
                    kb.mm(pb[2][:, :], lhsT=h_[:, k, cs], rhs=Wtm[:, k, :], start=(k == 0), stop=(k == 15),
                          r=[hn, "Wtm"], w=["pb2"])
                kb.cp('act', v_bf[:], pb[2][:, 0:256], r=["pb2"], w=["v_bf"])
                kb.act(gs[:], pb[2][:, 256:512], AF.Silu, r=["pb2"], w=["gs"])
                for h in range(2):
                    kb.mm(pb[3][:, h * 128:(h + 1) * 128], lhsT=krz[h][:, cs], rhs=qr_bf[:, cs], start=True, stop=True,
                          r=[f"krz{h}", "qr_bf"], w=["pb3"])
                kb.tt('dve', ST_bf[:], pb[3][:, 0:256], DMT, ALU.mult, r=["pb3", "CB"], w=["ST_bf"])
                kb.mm(pb[4][:, 0:128], lhsT=kr_bf[:, cs], rhs=ident_b, start=True, stop=True, r=["kr_bf", "Cb"],
                      w=["pb4"])
                kb.tt('dve', kd_bf[:], pb[4][:, 0:128], KDEC, ALU.mult, r=["pb4", "CB"], w=["kd_bf"])
                for h in range(2):
                    hs_ = slice(h * 128, (h + 1) * 128)
                    kb.mm(pb[5][:, hs_], lhsT=ST_bf[:, hs_], rhs=v_bf[:, hs_], start=True, stop=False,
                          r=["ST_bf", "v_bf"], w=["pb5"])
                    kb.mm(pb[5][:, hs_], lhsT=qdz[h][:, cs], rhs=rstate_bf[:], start=False, stop=True,
                          r=[f"qdz{h}", "rstate_bf"], w=["pb5"])
                kb.mm(pb[6][:, 0:256], lhsT=kd_bf[:], rhs=v_bf[:], start=True, stop=True, r=["kd_bf", "v_bf"],
                      w=["pb6"])
                for h in range(2):
                    ph = slice(h * 64, (h + 1) * 64)
                    kb.stt('dve', rstate[ph, :], rstate[ph, :], cdec[ph, :], pb[6][ph, h * 128:(h + 1) * 128],
                           ALU.mult, ALU.add, r=["rstate", "CB", "pb6"], w=["rstate"])
                kb.cp('pool', rstate_bf[:], rstate[:], r=["rstate"], w=["rstate_bf"])
                for h in range(2):
                    kb.act(rjunk[:], pb[5][:, h * 128:(h + 1) * 128], AF.Square, r=["pb5"], w=["rjunk", "rss"],
                           accum=rss[:, h:h + 1])
                kb.act(rrs[:], rss[:], AF.Sqrt, r=["rss"], w=["rrs"], bias=1e-6, scale=1.0 / 128)
                kb.recip(rrs[:], rrs[:], r=["rrs"], w=["rrs"])
                for h in range(2):
                    hs_ = slice(h * 128, (h + 1) * 128)
                    kb.stt('dve', yo_bf[:, hs_], pb[5][:, hs_], rrs[:, h:h + 1], gs[:, hs_], ALU.mult, ALU.mult,
                           r=["pb5", "rrs", "gs"], w=["ryo_bf"])
                for t in range(2):
                    kb.mm(pb[7][:, 256 + t * 128:256 + (t + 1) * 128], lhsT=yo_bf[:, t * 128:(t + 1) * 128],
                          rhs=ident_b, start=True, stop=True, r=["ryo_bf", "Cb"], w=["pb7"])
                kb.cp('act', ys[:, :, cs], pb[7][:, 256:512].rearrange("p (t s) -> p t s", t=2), r=["pb7"], w=[ysn])
            kb.dma(yT_ap(512, 256, m).rearrange("(t p) s -> p t s", p=128), ys[:], ysn, r=[ysn],
                   w=["yT_r"])
            after_y(2, m, "yT_r")
        sc.close()

    if 'rwkv' in parts:
        sc = Scope(kb)
        Wr = load_cast_weights(kb, sc, w_wfm, 1216, "Wr")
        offs = [0, 128, 256, 384, 512, 640, 768, 864, 960, 1088]
        sizes = [128, 128, 128, 128, 128, 128, 96, 96, 128, 128]
        mu = sc.sb("mu", [128, 10])
        kb.dma(mu[:], rw_mu, "mu", w=["mu"])
        vec = sc.sb("vec", [128, 2, 7])
        kb.dma(vec[:], rw_vec, "vec", w=["vec"])
        omka = sc.sb("omka", [128, 2])
        kb.ts('dve', omka[:].rearrange("p (c o) -> p c o", o=1), vec[:, :, 3:4], -1.0, ALU.mult, r=["vec"], w=["omka"],
              s2=1.0, op1=ALU.add)
        lstg = sc.sb("lstg", [128, 2, 256])
        w2_bf = sc.sb("w2_bf", [96, 256], BF16)
        a2_bf = sc.sb("a2_bf", [96, 256], BF16)
        g2_bf = sc.sb("g2_bf", [128, 2, 256], BF16)
        kb.dma(lstg[0:96, 0, :], rw_w2, "lstg", w=["lstg"])
        kb.cp('dve', w2_bf[:], lstg[0:96, 0, :], r=["lstg"], w=["w2_bf"])
        kb.dma(lstg[0:96, 1, :], rw_a2, "lstg", r=["lstg"], w=["lstg"])
        kb.cp('dve', a2_bf[:], lstg[0:96, 1, :], r=["lstg"], w=["a2_bf"])
        kb.dma(lstg[:], rw_g2.rearrange("(k p) n -> p k n", p=128), "lstg", r=["lstg"], w=["lstg"])
        kb.cp('dve', g2_bf[:], lstg[:], r=["lstg"], w=["g2_bf"])
        M3 = sc.sb("M3", [128, 384])
        kb.cp('pool', M3[:, 0:128], SU_f, r=["C"], w=["M3"])
        kb.cp('pool', M3[:, 128:256], SL_f, r=["C"], w=["M3"])
        kb.cp('pool', M3[:, 256:384], SU_f, r=["C"], w=["M3"])
        M2 = sc.sb("M2", [128, 256])
        kb.cp('pool', M2[:, 0:128], UI_f, r=["C"], w=["M2"])
        kb.cp('pool', M2[:, 128:256], UI_f, r=["C"], w=["M2"])
        hT = sc.sb("hT0", [128, 16, 512], BF16)
        halo = sc.sb("halo", [128, 10])
        kb.memset('pool', halo[:], 0.0, w=["halo"])
        rawt = [sc.sb(f"rawt{i}", [128, 513]) for i in range(2)]
        dlt = [sc.sb(f"dlt{i}", [128, 512]) for i in range(2)]
        sh = [sc.sb(f"sh{q}", [128, 512]) for q in range(10)]
        tw = sc.sb("tw", [128, 512], BF16)
        ad_bf = sc.sb("ad_bf", [128, 512], BF16)
        sg = [sc.sb(f"sg{i}", [128, 512], BF16) for i in range(2)]
        names = ["a", "g_sb", "kkraw", "sq", "rn", "kk", "kmul", "kh", "bb", "lw", "cwm", "Winc", "Winv", "Wexc",
                 "tmpw", "prod", "bv", "yraw", "yc", "sq2", "rs2", "yn"]
        T = {n: sc.sb("rw_" + n, [128, 512]) for n in names}
        bdn = ["Abd", "Bbd", "Kbd", "Rbd", "Vbd"]
        BD = {n: sc.sb(n, [128, 8, 128], BF16) for n in bdn}
        for n in bdn:
            kb.memset('pool', BD[n][:], 0.0, w=[n])
        II = sc.sb("II", [128, 128], BF16)
        kb.cp('dve', II[:], ident_f, r=["C"], w=["II"])
        tm_bf = sc.sb("tm_bf", [128, 384], BF16)
        gr_bf = sc.sb("gr_bf", [128, 384], BF16)
        pr_bf = sc.sb("pr_bf", [128, 256], BF16)
        pw_bf = [sc.sb(f"pw_bf{i}", [128, 256], BF16) for i in range(5)]
        G_bf = [sc.sb(f"G_bf{i}", [128, 128], BF16) for i in range(2)]
        Xn_bf = sc.sb("Xn_bf", [128, 128], BF16)
        U_bf = sc.sb("U_bf", [128, 128], BF16)
        ST_f = [sc.sb(f"ST_f{ct}", [128, 128]) for ct in range(2)]
        STt = sc.sb("STt", [128, 128])
        STb = [sc.sb(f"STb{ct}", [128, 128], BF16) for ct in range(2)]
        yst = [sc.sb(f"wyst{i}", [128, 512], BF16) for i in range(2)]
        for ct in range(2):
            kb.memset('pool', ST_f[ct][:], 0.0, w=[f"ST_f{ct}"])
            kb.memset('pool', STb[ct][:], 0.0, w=[f"STb{ct}"])
        v3 = lambda ap: ap.rearrange("p (c t) -> p c t", c=8)
        for m in range(NM):
            load_hT(hT, m, "hT0")
            for q in range(10):
                sz = sizes[q]
                pr = f"pb{q % 2}"
                pa = pb[q % 2]
                for k in range(16):
                    kb.mm(pa[0:sz, :], lhsT=Wr[:, k, offs[q]:offs[q] + sz], rhs=hT[:, k, :], start=(k == 0),
                          stop=(k == 15), r=["Wr", "hT0"], w=[pr])
                rw_ = rawt[q % 2]
                rn_ = f"rawt{q % 2}"
                kb.cp('pool', rw_[0:sz, 0:1], halo[0:sz, q:q + 1], r=["halo"], w=[rn_])
                kb.cp('act', rw_[0:sz, 1:513], pa[0:sz, :], r=[pr], w=[rn_])
                kb.cp('pool', halo[0:sz, q:q + 1], rw_[0:sz, 512:513], r=[rn_], w=["halo"])
                d_ = dlt[q % 2]
                dn_ = f"dlt{q % 2}"
                kb.tt('pool', d_[0:sz, :], rw_[0:sz, 0:512], rw_[0:sz, 1:513], ALU.subtract, r=[rn_], w=[dn_])
                kb.stt('dve', sh[q][0:sz, :], d_[0:sz, :], mu[0:sz, q:q + 1], rw_[0:sz, 1:513], ALU.mult, ALU.add,
                       r=[dn_, "mu", rn_], w=[f"sh{q}"])
            kb.act(tw[0:96, :], sh[6][0:96, :], AF.Tanh, r=["sh6"], w=["tw"])
            kb.cp('pool', ad_bf[0:96, :], sh[7][0:96, :], r=["sh7"], w=["ad_bf"])
            for i in range(2):
                kb.act(sg[i][:], sh[8 + i][:], AF.Sigmoid, r=[f"sh{8 + i}"], w=[f"sg{i}"])
            for ct in range(2):
                cts = slice(ct * 128, (ct + 1) * 128)
                r_, k_, v_ = sh[ct], sh[2 + ct], sh[4 + ct]
                rn_r, rn_k, rn_v = f"sh{ct}", f"sh{2 + ct}", f"sh{4 + ct}"
                kb.mm(pb[2][:, :], lhsT=w2_bf[:, cts], rhs=tw[0:96, :], start=True, stop=True, r=["w2_bf", "tw"],
                      w=["pb2"])
                kb.act(T["lw"][:], pb[2][:, :], AF.Sigmoid, r=["pb2", "vec"], w=["rw_lw"], bias=vec[:, ct, 0:1])
                kb.ts('dve', T["lw"][:], T["lw"][:], -0.6065306597126334, ALU.mult, r=["rw_lw"], w=["rw_lw"])
                kb.mm(pb[3][:, :], lhsT=a2_bf[:, cts], rhs=ad_bf[0:96, :], start=True, stop=True, r=["a2_bf", "ad_bf"],
                      w=["pb3"])
                kb.act(T["a"][:], pb[3][:, :], AF.Sigmoid, r=["pb3", "vec"], w=["rw_a"], bias=vec[:, ct, 1:2])
                for i in range(2):
                    kb.mm(pb[2][:, :], lhsT=g2_bf[:, i, cts], rhs=sg[i][:], start=(i == 0), stop=(i == 1),
                          r=["g2_bf", f"sg{i}"], w=["pb2"])
                kb.cp('act', T["g_sb"][:], pb[2][:, :], r=["pb2"], w=["rw_g_sb"])
                kb.ts('pool', T["kkraw"][:], k_[:], vec[:, ct, 2:3], ALU.mult, r=[rn_k, "vec"], w=["rw_kkraw"])
                kb.tt('pool', T["sq"][:], T["kkraw"][:], T["kkraw"][:], ALU.mult, r=["rw_kkraw"], w=["rw_sq"])
                kb.mm(pb[3][:, :], lhsT=blk_f, rhs=T["sq"][:], start=True, stop=True, r=["C", "rw_sq"], w=["pb3"])
                kb.act(T["rn"][:], pb[3][:, :], AF.Sqrt, r=["pb3"], w=["rw_rn"])
                kb.ts('dve', T["rn"][:], T["rn"][:], 1e-12, ALU.max, r=["rw_rn"], w=["rw_rn"])
                kb.recip(T["rn"][:], T["rn"][:], r=["rw_rn"], w=["rw_rn"])
                kb.tt('pool', T["kk"][:], T["kkraw"][:], T["rn"][:], ALU.mult, r=["rw_kkraw", "rw_rn"], w=["rw_kk"])
                kb.ts('dve', T["kmul"][:], T["a"][:], vec[:, ct, 3:4], ALU.mult, r=["rw_a", "vec", "omka"],
                      w=["rw_kmul"], s2=omka[:, ct:ct + 1], op1=ALU.add)
                kb.tt('pool', T["kh"][:], k_[:], T["kmul"][:], ALU.mult, r=[rn_k, "rw_kmul"], w=["rw_kh"])
                kb.tt('pool', T["bb"][:], T["kk"][:], T["a"][:], ALU.mult, r=["rw_kk", "rw_a"], w=["rw_bb"])
                kb.scan(T["cwm"][:], rmask, T["lw"][:], r=["C", "rw_lw"], w=["rw_cwm"])
                kb.act(T["Winc"][:], T["cwm"][:], AF.Exp, r=["rw_cwm"], w=["rw_Winc"])
                kb.act(T["Winv"][:], T["cwm"][:], AF.Exp, r=["rw_cwm"], w=["rw_Winv"], scale=-1.0)
                kb.tt('pool', T["tmpw"][:], T["cwm"][:], T["lw"][:], ALU.subtract, r=["rw_cwm", "rw_lw"], w=["rw_tmpw"])
                kb.act(T["Wexc"][:], T["tmpw"][:], AF.Exp, r=["rw_tmpw"], w=["rw_Wexc"])
                pairs = (("Abd", T["kk"], "rw_kk", T["Wexc"], "rw_Wexc"), ("Bbd", T["bb"], "rw_bb", T["Winv"], "rw_Winv"),
                         ("Kbd", T["kh"], "rw_kh", T["Winv"], "rw_Winv"), ("Rbd", r_, rn_r, T["Winc"], "rw_Winc"))
                ei = 0
                for (bn, a0, an0, a1, an1) in pairs:
                    for hh in range(2):
                        ph = slice(hh * 64, (hh + 1) * 64)
                        kb.tt('dve' if ei % 2 == 0 else 'pool', BD[bn][ph, :, hh * 64:(hh + 1) * 64], v3(a0[ph, :]),
                              v3(a1[ph, :]), ALU.mult, r=[an0, an1, bn], w=[bn])
                        ei += 1
                for hh in range(2):
                    ph = slice(hh * 64, (hh + 1) * 64)
                    kb.cp('pool', BD["Vbd"][ph, :, hh * 64:(hh + 1) * 64], v3(v_[ph, :]), r=[rn_v, "Vbd"], w=["Vbd"])
                kb.stt('dve', T["prod"][:], r_[:], vec[:, ct, 4:5], T["kh"][:], ALU.mult, ALU.mult,
                       r=[rn_r, "vec", "rw_kh"], w=["rw_prod"])
                kb.mm(pb[3][:, :], lhsT=blk_f, rhs=T["prod"][:], start=True, stop=True, r=["C", "rw_prod"], w=["pb3"])
                kb.tt('dve', T["bv"][:], pb[3][:, :], v_[:], ALU.mult, r=["pb3", rn_v], w=["rw_bv"])
                STf, STn = ST_f[ct], f"ST_f{ct}"
                STbf, STbn = STb[ct], f"STb{ct}"
                for c in range(8):
                    A_c, B_c, K_c, R_c, V_c = (BD[n][:, c, :] for n in bdn)
                    for i, (X_c, xn) in enumerate(((B_c, "Bbd"), (K_c, "Kbd"), (V_c, "Vbd"))):
                        kb.mm(pb[4][:, i * 128:(i + 1) * 128], lhsT=X_c, rhs=ident_b, start=True, stop=True,
                              r=[xn, "Cb"], w=["pb4"])
                    kb.cp('act', tm_bf[:], pb[4][:, 0:384], r=["pb4"], w=["tm_bf"])
                    btm, ktm, vtm = tm_bf[:, 0:128], tm_bf[:, 128:256], tm_bf[:, 256:384]
                    kb.mm(pb[5][:, 0:128], lhsT=B_c, rhs=A_c, start=True, stop=True, r=["Bbd", "Abd"], w=["pb5"])
                    kb.mm(pb[5][:, 128:256], lhsT=A_c, rhs=B_c, start=True, stop=True, r=["Bbd", "Abd"], w=["pb5"])
                    kb.mm(pb[5][:, 256:384], lhsT=K_c, rhs=A_c, start=True, stop=True, r=["Kbd", "Abd"], w=["pb5"])
                    kb.tt('dve', gr_bf[:], pb[5][:, 0:384], M3[:], ALU.mult, r=["pb5", "M3"], w=["gr_bf"])
                    kb.mm(pb[6][:, 0:128], lhsT=B_c, rhs=R_c, start=True, stop=True, r=["Bbd", "Rbd"], w=["pb6"])
                    kb.mm(pb[6][:, 128:256], lhsT=K_c, rhs=R_c, start=True, stop=True, r=["Kbd", "Rbd"], w=["pb6"])
                    kb.tt('dve', pr_bf[:], pb[6][:, 0:256], M2[:], ALU.mult, r=["pb6", "M2"], w=["pr_bf"])
                    Nn, Tt, TakT = gr_bf[:, 0:128], gr_bf[:, 128:256], gr_bf[:, 256:384]
                    PrbT, PrkT = pr_bf[:, 0:128], pr_bf[:, 128:256]
                    kb.tt('pool', G_bf[0][:], II[:], Nn, ALU.subtract, r=["II", "gr_bf"], w=["G_bf0"])
                    gcur = 0
                    Ncur, Tcur, ncn = Nn, Tt, "gr_bf"
                    for lv in range(5):
                        kb.mm(pb[6][:, 256:384], lhsT=Tcur, rhs=Ncur, start=True, stop=True, r=[ncn], w=["pb6"])
                        kb.mm(pb[6][:, 384:512], lhsT=Ncur, rhs=Tcur, start=True, stop=True, r=[ncn], w=["pb6"])
                        kb.cp('act', pw_bf[lv][:], pb[6][:, 256:512], r=["pb6"], w=[f"pw_bf{lv}"])
                        Ncur, Tcur, ncn = pw_bf[lv][:, 0:128], pw_bf[lv][:, 128:256], f"pw_bf{lv}"
                        kb.mm(pb[4][:, 384:512], lhsT=Tcur, rhs=G_bf[gcur][:], start=True, stop=True,
                              r=[ncn, f"G_bf{gcur}"], w=["pb4"])
                        kb.tt('dve', G_bf[1 - gcur][:], pb[4][:, 384:512], G_bf[gcur][:], ALU.add,
                              r=["pb4", f"G_bf{gcur}"], w=[f"G_bf{1 - gcur}"])
                        gcur = 1 - gcur
                    Gf, Gn = G_bf[gcur], f"G_bf{gcur}"
                    kb.mm(pb[7][:, 0:128], lhsT=A_c, rhs=STbf[:], start=True, stop=False, r=["Abd", STbn], w=["pb7"])
                    kb.mm(pb[7][:, 0:128], lhsT=TakT, rhs=vtm, start=False, stop=True, r=["gr_bf", "tm_bf"], w=["pb7"])
                    kb.act(Xn_bf[:], pb[7][:, 0:128], AF.Copy, r=["pb7"], w=["Xn_bf"], scale=-1.0)
                    kb.mm(pb[7][:, 128:256], lhsT=Gf[:], rhs=Xn_bf[:], start=True, stop=True, r=[Gn, "Xn_bf"], w=["pb7"])
                    kb.cp('act', U_bf[:], pb[7][:, 128:256], r=["pb7"], w=["U_bf"])
                    kb.mm(pb[7][:, 256:384], lhsT=STbf[:], rhs=R_c, start=True, stop=False, r=[STbn, "Rbd"], w=["pb7"])
                    kb.mm(pb[7][:, 256:384], lhsT=U_bf[:], rhs=PrbT, start=False, stop=False, r=["U_bf", "pr_bf"],
                          w=["pb7"])
                    kb.mm(pb[7][:, 256:384], lhsT=vtm, rhs=PrkT, start=False, stop=True, r=["tm_bf", "pr_bf"], w=["pb7"])
                    for hh in range(2):
                        ph = slice(hh * 64, (hh + 1) * 64)
                        kb.cp('act', T["yraw"][ph, c * 64:(c + 1) * 64], pb[7][ph, 256 + hh * 64:256 + (hh + 1) * 64],
                              r=["pb7"], w=["rw_yraw"])
                    kb.mm(pb[7][:, 384:512], lhsT=btm, rhs=U_bf[:], start=True, stop=False, r=["tm_bf", "U_bf"], w=["pb7"])
                    kb.mm(pb[7][:, 384:512], lhsT=ktm, rhs=vtm, start=False, stop=True, r=["tm_bf"], w=["pb7"])
                    WL = T["Winc"][:, c * 64 + 63:c * 64 + 64]
                    kb.ts('pool', STt[:], STf[:], WL, ALU.mult, r=[STn, "rw_Winc"], w=["STt"])
                    kb.stt('dve', STf[:], pb[7][:, 384:512], WL, STt[:], ALU.mult, ALU.add, r=["pb7", "rw_Winc", "STt"],
                           w=[STn])
                    kb.cp('pool', STbf[:], STf[:], r=[STn], w=[STbn])
                kb.mm(pb[2][:, :], lhsT=blk64_f, rhs=T["yraw"][:], start=True, stop=True, r=["C", "rw_yraw"], w=["pb2"])
                kb.tt('dve', T["yc"][:], T["yraw"][:], pb[2][:, :], ALU.subtract, r=["rw_yraw", "pb2"], w=["rw_yc"])
                kb.tt('pool', T["sq2"][:], T["yc"][:], T["yc"][:], ALU.mult, r=["rw_yc"], w=["rw_sq2"])
                kb.mm(pb[3][:, :], lhsT=blk64_f, rhs=T["sq2"][:], start=True, stop=True, r=["C", "rw_sq2"], w=["pb3"])
                kb.act(T["rs2"][:], pb[3][:, :], AF.Sqrt, r=["pb3"], w=["rw_rs2"], bias=64e-5)
                kb.recip(T["rs2"][:], T["rs2"][:], r=["rw_rs2"], w=["rw_rs2"])
                kb.tt('pool', T["yn"][:], T["yc"][:], T["rs2"][:], ALU.mult, r=["rw_yc", "rw_rs2"], w=["rw_yn"])
                kb.ts('dve', T["yn"][:], T["yn"][:], vec[:, ct, 5:6], ALU.mult, r=["rw_yn", "vec"], w=["rw_yn"],
                      s2=vec[:, ct, 6:7], op1=ALU.add)
                kb.tt('pool', T["yn"][:], T["yn"][:], T["bv"][:], ALU.add, r=["rw_yn", "rw_bv"], w=["rw_yn"])
                ys, ysn = yst[ct], f"wyst{ct}"
                kb.tt('dve', ys[:], T["yn"][:], T["g_sb"][:], ALU.mult, r=["rw_yn", "rw_g_sb"], w=[ysn])
                kb.dma(yT_ap(256 + ct * 128, 128, m), ys[:], ysn, r=[ysn], w=["yT_w"])
                if ct == 1:
                    after_y(1, m, "yT_w")
        sc.close()

    outs = ["yT_m", "yT_r", "yT_w"]
    if env is not None:
        return None
    return kb, outs


def finish_p1(kb, outs):
    return kb.finish(outs)


import numpy as np


FF = 5504
NJ = 43


P2_PARAM_SHAPES = {
    "w_ada": ([D, 6 * D], F32), "b_ada": ([1, 6 * D], F32), "ng_col": ([128, 64], F32), "norm_g": ([4, D], F32),
    "w_gate": ([3, D, D], F32), "w_branch": ([3, 1024, D], F32), "w_out": ([D, D], F32), "w_up": ([D, 2 * FF], F32),
    "f_cv": ([128, 86, 4], F32), "w_down": ([FF, D], F32),
}


def p2_scratch(kb, sfx=""):
    return {
        'Wg_d': kb.dscratch("Wg_d" + sfx, [3, 16, 128, 16, 128], BF16),
        'Wb_d': kb.dscratch("Wb_d" + sfx, [3, 16, 128, 8, 128], BF16),
        'Wo_d': kb.dscratch("Wo_d" + sfx, [16, 128, D], BF16),
        'Wu_d': kb.dscratch("Wu_d" + sfx, [NJ, 128, 16, 256], BF16),
        'Wd_d': kb.dscratch("Wd_d" + sfx, [NJ, 128, D], BF16),
    }


def build_p2(Tq, env=None):
    NT = 128 + Tq
    if env is None:
        kb = KB()
        e = {}
        xh = kb.din("xh", [NT, D])
        yTh = kb.din("yTh", [3, 1024, NT], BF16)
        e['hmask'] = kb.din("hmask", [128, 1])
        e['c_col'] = kb.din("c_col", [128, 16])
        for k, (shp, dt_) in P2_PARAM_SHAPES.items():
            e[k] = kb.din(k, shp, dt_)
        cst = kb.din("cst", [128, 2048])
        xo = kb.dout("xo", [Tq, D])
        e.update(p2_scratch(kb))
        G = setup_globals(kb, cst)

        def load_x(dst, rn, row0, is_halo):
            kb.dma(dst, xh[row0:row0 + 128, :], rn, w=[rn])

        def load_y(ybf, t0, ntok, is_halo):
            kb.dma(ybf[:, :, :, 0:ntok], yTh[:, :, t0:t0 + ntok].rearrange("i (k p) t -> p i k t", p=128), "ybf",
                   w=["ybf"])

        def store_x(src, rn, o0):
            kb.dma(xo[o0:o0 + 128, :], src, rn, r=[rn], w=["xo"])
        e['load_x'], e['load_y'], e['store_x'] = load_x, load_y, store_x
    else:
        kb = env['kb']
        e = env
        G = env['G']
    nc = kb.nc
    hmask, c_col, w_ada, b_ada, ng_col, norm_g = (e[k] for k in ('hmask', 'c_col', 'w_ada', 'b_ada', 'ng_col', 'norm_g'))
    w_gate, w_branch, w_out, w_up, f_cv, w_down = (e[k] for k in ('w_gate', 'w_branch', 'w_out', 'w_up', 'f_cv', 'w_down'))
    Wg_d, Wb_d, Wo_d, Wu_d, Wd_d = (e[k] for k in ('Wg_d', 'Wb_d', 'Wo_d', 'Wu_d', 'Wd_d'))
    load_x, load_y, store_x = e['load_x'], e['load_y'], e['store_x']
    pb = G['pb']
    C, Cb = G['C'], G['Cb']
    ident_f = C[:, 0:128]
    ones_row = C[0:1, 256:384]
    ident_b = Cb[:, 0:128]

    sc = Scope(kb)
    stg = [sc.sb(f"stg{i}", [128, 2048]) for i in range(3)]
    stb = [sc.sb(f"stb{i}", [128, 2048], BF16) for i in range(3)]
    cnt = [0]

    def cast_block(src_ap, dst_ap, shape):
        i = cnt[0] % 3
        cnt[0] += 1
        n = int(np.prod(shape))
        sv = stg[i][:, 0:n]
        bv = stb[i][:, 0:n]
        if len(shape) == 2:
            sv = sv.rearrange("p (a b) -> p a b", a=shape[0])
            bv = bv.rearrange("p (a b) -> p a b", a=shape[0])
        kb.dma(sv, src_ap, f"stg{i}", w=[f"stg{i}"])
        eng = ('dve', 'pool', 'act')[i]
        kb.cp(eng, stb[i][:, 0:n], stg[i][:, 0:n], r=[f"stg{i}"], w=[f"stb{i}"])
        kb.dma(dst_ap, bv, f"stb{i}", r=[f"stb{i}"], w=["Wscratch"])

    for i in range(3):
        wv = w_gate[i].rearrange("(k p) n -> p k n", p=128)
        for nt in range(16):
            cast_block(wv[:, :, nt * 128:(nt + 1) * 128], Wg_d[i, nt], (16, 128))
        wv = w_branch[i].rearrange("(k p) n -> p k n", p=128)
        for nt in range(16):
            cast_block(wv[:, :, nt * 128:(nt + 1) * 128], Wb_d[i, nt], (8, 128))
    for kn in range(16):
        cast_block(w_out[kn * 128:(kn + 1) * 128, :], Wo_d[kn], (2048,))
    wv = w_up.rearrange("(k p) n -> p k n", p=128)
    for j in range(NJ):
        cast_block(wv[:, :, j * 128:(j + 1) * 128], Wu_d[j, :, :, 0:128], (16, 128))
        cast_block(wv[:, :, FF + j * 128:FF + (j + 1) * 128], Wu_d[j, :, :, 128:256], (16, 128))
        cast_block(w_down[j * 128:(j + 1) * 128, :], Wd_d[j], (2048,))
    sc.close()

    scm = Scope(kb)
    GTm = scm.sb("GTm", [128, 2048])
    GTf = scm.sb("GTf", [128, 2048])
    cols = scm.sb("cols", [128, 64])
    ngc = scm.sb("ngc", [128, 64])
    kb.dma(ngc[:], ng_col, "ngc", w=["ngc"])
    sc = Scope(kb)
    rows = emit_mod(kb, sc, c_col, w_ada, b_ada, [0, 1, 2, 3, 4, 5], ones_row, pb, "m2")
    kb.dma(GTm[:], norm_g[1:2, :].partition_broadcast(128), "GTm", w=["GTm"])
    kb.dma(GTf[:], norm_g[3:4, :].partition_broadcast(128), "GTf", w=["GTf"])

    def mk_gt(dst, dn):
        def f(c, pa, pr):
            kb.tt('dve', dst[:, c * 512:(c + 1) * 512], pa, dst[:, c * 512:(c + 1) * 512], ALU.mult, r=[pr, dn], w=[dn])
        return f

    bcast_row(kb, rows[2][0], rows[2][1], ones_row, pb, mk_gt(GTm, "GTm"))
    bcast_row(kb, rows[5][0], rows[5][1], ones_row, pb, mk_gt(GTf, "GTf"))
    for slot, vid in enumerate((1, 0, 4, 3)):
        row, rn = rows[vid]
        for k in range(16):
            kb.mm(pb[3][:, slot * 16 + k:slot * 16 + k + 1], lhsT=row[0:1, k * 128:(k + 1) * 128],
                  rhs=ones_row[0:1, 0:1], start=True, stop=True, r=[rn, "C"], w=["pb3"])
    kb.cp('act', cols[:], pb[3][:, 0:64], r=["pb3"], w=["cols"])
    for slot, gsl in ((0, 0), (2, 2)):
        kb.stt('dve', cols[:, slot * 16:(slot + 1) * 16], cols[:, slot * 16:(slot + 1) * 16], 1.0,
               ngc[:, gsl * 16:(gsl + 1) * 16], ALU.add, ALU.mult, r=["cols", "ngc"], w=["cols"])
    sc.close()

    xres = [scm.sb(f"xres{i}", [128, 2048]) for i in range(2)]
    htmp = scm.sb("htmp", [128, 2048])
    hb = scm.sb("hb", [128, 2048], BF16)
    ss = scm.sb("ss", [128, 1])
    rstd = scm.sb("rstd", [128, 1])
    hT = scm.sb("hT", [128, 16, 256], BF16)
    ybf = scm.sb("ybf", [128, 3, 8, 256], BF16)
    mT = scm.sb("mT", [128, 16, 256], BF16)
    sig = [scm.sb(f"sig{i}", [128, 256]) for i in range(2)]
    macc = scm.sb("macc", [128, 256])
    mtmp = scm.sb("mtmp", [128, 256])
    Wg_s = [scm.sb(f"Wg_s{i}", [128, 16, 128], BF16) for i in range(3)]
    Wb_s = [scm.sb(f"Wb_s{i}", [128, 8, 128], BF16) for i in range(3)]
    Wo_s = [scm.sb(f"Wo_s{i}", [128, 2048], BF16) for i in range(2)]
    Wu_s = [scm.sb(f"Wu_s{i}", [128, 16, 256], BF16) for i in range(2)]
    Wd_s = [scm.sb(f"Wd_s{i}", [128, 2048], BF16) for i in range(2)]
    ymix = scm.sb("ymix", [128, 2048])
    rawf = [scm.sb(f"rawf{i}", [128, 258]) for i in range(2)]
    facc = [scm.sb(f"facc{i}", [128, 256]) for i in range(2)]
    gg = scm.sb("gg", [128, 256])
    aT = scm.sb("aT", [128, NJ, 256], BF16)
    fhalo = scm.sb("fhalo", [128, 86, 2])
    fc = scm.sb("fc", [128, 86, 4])
    hm = scm.sb("hm", [128, 1])
    kb.dma(fc[:], f_cv, "fc", w=["fc"])
    kb.dma(hm[:], hmask, "hm", w=["hm"])
    kb.memset('pool', fhalo[:], 0.0, w=["fhalo"])
    if 'extra_alloc' in e:
        e['extra_alloc'](scm, dict(ymix=ymix))
    slab_ctr = {"g": 0, "b": 0, "o": 0, "u": 0, "d": 0}

    def emit_h(nsub, ntok, gslot):
        for sub in range(nsub):
            xn = f"xres{sub}"
            kb.act(htmp[:], xres[sub][:], AF.Square, r=[xn], w=["htmp", "ss"], accum=ss[:])
            kb.act(rstd[:], ss[:], AF.Sqrt, r=["ss"], w=["rstd"], bias=1e-6, scale=1.0 / D)
            kb.recip(rstd[:], rstd[:], r=["rstd"], w=["rstd"])
            kb.ts('dve', hb[:], xres[sub][:], rstd[:, 0:1], ALU.mult, r=[xn, "rstd"], w=["hb"])
            for q in range(4):
                bank = 6 + (q % 2)
                for kk in range(4):
                    k = q * 4 + kk
                    kb.mm(pb[bank][:, kk * 128:(kk + 1) * 128], lhsT=hb[:, k * 128:(k + 1) * 128], rhs=ident_b,
                          start=True, stop=True, r=["hb", "Cb"], w=[f"pb{bank}"])
                for kk in range(4):
                    k = q * 4 + kk
                    gcol = cols[:, gslot * 16 + k:gslot * 16 + k + 1]
                    scol = cols[:, (gslot + 1) * 16 + k:(gslot + 1) * 16 + k + 1]
                    if kk % 2 == 0:
                        kb.act(hT[:, k, sub * 128:(sub + 1) * 128], pb[bank][:, kk * 128:(kk + 1) * 128], AF.Identity,
                               r=[f"pb{bank}", "cols"], w=["hT"], bias=scol, scale=gcol)
                    else:
                        kb.ts('dve', hT[:, k, sub * 128:(sub + 1) * 128], pb[bank][:, kk * 128:(kk + 1) * 128], gcol,
                              ALU.mult, r=[f"pb{bank}", "cols"], w=["hT"], s2=scol, op1=ALU.add)

    def post_norm(nsub_i, src_ps_banks, GT, gtn, sub, store_ap):
        xn = f"xres{sub}"
        kb.act(htmp[:], ymix[:], AF.Square, r=["ymix"], w=["htmp", "ss"], accum=ss[:])
        kb.act(rstd[:], ss[:], AF.Sqrt, r=["ss"], w=["rstd"], bias=1e-6, scale=1.0 / D)
        kb.recip(rstd[:], rstd[:], r=["rstd"], w=["rstd"])
        kb.stt('dve', htmp[:], ymix[:], rstd[:, 0:1], GT[:], ALU.mult, ALU.mult, r=["ymix", "rstd", gtn], w=["htmp"])
        kb.tt('pool', xres[sub][:], xres[sub][:], htmp[:], ALU.add, r=[xn, "htmp"], w=[xn])
        if store_ap is not None:
            store_x(xres[sub][:], xn, store_ap)

    def tile(t0, ntok, is_halo):
        nsub = ntok // 128
        tsl = slice(0, ntok)
        for sub in range(nsub):
            xn = f"xres{sub}"
            load_x(xres[sub][:], xn, t0 + sub * 128, is_halo)
        load_y(ybf, t0, ntok, is_halo)
        emit_h(nsub, ntok, 0)
        for nt in range(16):
            for i in range(3):
                gi = slab_ctr["g"] % 3
                slab_ctr["g"] += 1
                kb.dma(Wg_s[gi][:], Wg_d[i, nt], f"Wg_s{gi}", r=["Wscratch"], w=[f"Wg_s{gi}"])
                kb.dma(Wb_s[gi][:], Wb_d[i, nt], f"Wb_s{gi}", r=["Wscratch"], w=[f"Wb_s{gi}"])
                for k in range(16):
                    kb.mm(pb[4][:, tsl], lhsT=Wg_s[gi][:, k, :], rhs=hT[:, k, tsl], start=(k == 0), stop=(k == 15),
                          r=[f"Wg_s{gi}", "hT"], w=["pb4"])
                for k in range(8):
                    kb.mm(pb[5][:, tsl], lhsT=Wb_s[gi][:, k, :], rhs=ybf[:, i, k, tsl], start=(k == 0), stop=(k == 7),
                          r=[f"Wb_s{gi}", "ybf"], w=["pb5"])
                sg_ = sig[i % 2]
                sgn_ = f"sig{i % 2}"
                kb.act(sg_[:, tsl], pb[4][:, tsl], AF.Sigmoid, r=["pb4"], w=[sgn_])
                if i == 0:
                    kb.tt('dve', macc[:, tsl], pb[5][:, tsl], sg_[:, tsl], ALU.mult, r=["pb5", sgn_], w=["macc"])
                else:
                    kb.tt('dve', mtmp[:, tsl], pb[5][:, tsl], sg_[:, tsl], ALU.mult, r=["pb5", sgn_], w=["mtmp"])
                    if i == 1:
                        kb.tt('pool', macc[:, tsl], macc[:, tsl], mtmp[:, tsl], ALU.add, r=["macc", "mtmp"], w=["macc"])
                    else:
                        kb.tt('pool', mT[:, nt, tsl], macc[:, tsl], mtmp[:, tsl], ALU.add, r=["macc", "mtmp"],
                              w=["mT"])
        for sub in range(nsub):
            for kn in range(16):
                oi = slab_ctr["o"] % 2
                slab_ctr["o"] += 1
                kb.dma(Wo_s[oi][:], Wo_d[kn], f"Wo_s{oi}", r=["Wscratch"], w=[f"Wo_s{oi}"])
                for c in range(4):
                    kb.mm(pb[c][:, :], lhsT=mT[:, kn, sub * 128:(sub + 1) * 128], rhs=Wo_s[oi][:, c * 512:(c + 1) * 512],
                          start=(kn == 0), stop=(kn == 15), r=["mT", f"Wo_s{oi}"], w=[f"pb{c}"])
            for c in range(4):
                kb.cp('act', ymix[:, c * 512:(c + 1) * 512], pb[c][:, :], r=[f"pb{c}"], w=["ymix"])
            post_norm(nsub, None, GTm, "GTm", sub, None)
        emit_h(nsub, ntok, 2)
        for j in range(NJ):
            ui = slab_ctr["u"] % 2
            slab_ctr["u"] += 1
            kb.dma(Wu_s[ui][:], Wu_d[j], f"Wu_s{ui}", r=["Wscratch"], w=[f"Wu_s{ui}"])
            for half in range(2):
                bank = 4 + half
                jj = half * NJ + j
                for k in range(16):
                    kb.mm(pb[bank][:, tsl], lhsT=Wu_s[ui][:, k, half * 128:(half + 1) * 128], rhs=hT[:, k, tsl],
                          start=(k == 0), stop=(k == 15), r=[f"Wu_s{ui}", "hT"], w=[f"pb{bank}"])
                rw_ = rawf[half]
                rn_ = f"rawf{half}"
                kb.cp('pool', rw_[:, 0:2], fhalo[:, jj, :], r=["fhalo"], w=[rn_])
                kb.cp('act', rw_[:, 2:2 + ntok], pb[bank][:, tsl], r=[f"pb{bank}"], w=[rn_])
                if is_halo:
                    kb.ts('pool', fhalo[:, jj, :], rw_[:, ntok:ntok + 2], hm[:, 0:1], ALU.mult, r=[rn_, "hm"],
                          w=["fhalo"])
                else:
                    kb.cp('pool', fhalo[:, jj, :], rw_[:, ntok:ntok + 2], r=[rn_], w=["fhalo"])
                fa = facc[half]
                fan = f"facc{half}"
                kb.ts('dve', fa[:, tsl], rw_[:, 2:2 + ntok], fc[:, jj, 2:3], ALU.mult, r=[rn_, "fc"], w=[fan],
                      s2=fc[:, jj, 3:4], op1=ALU.add)
                kb.stt('dve', fa[:, tsl], rw_[:, 1:1 + ntok], fc[:, jj, 1:2], fa[:, tsl], ALU.mult, ALU.add,
                       r=[rn_, "fc", fan], w=[fan])
                kb.stt('dve', fa[:, tsl], rw_[:, 0:ntok], fc[:, jj, 0:1], fa[:, tsl], ALU.mult, ALU.add,
                       r=[rn_, "fc", fan], w=[fan])
            if not is_halo:
                kb.act(gg[:, tsl], facc[0][:, tsl], AF.Gelu_apprx_tanh, r=["facc0"], w=["gg"])
                kb.tt('pool', aT[:, j, tsl], gg[:, tsl], facc[1][:, tsl], ALU.mult, r=["gg", "facc1"], w=["aT"])
        if is_halo:
            return
        for sub in range(nsub):
            for j in range(NJ):
                di = slab_ctr["d"] % 2
                slab_ctr["d"] += 1
                kb.dma(Wd_s[di][:], Wd_d[j], f"Wd_s{di}", r=["Wscratch"], w=[f"Wd_s{di}"])
                for c in range(4):
                    kb.mm(pb[c][:, :], lhsT=aT[:, j, sub * 128:(sub + 1) * 128], rhs=Wd_s[di][:, c * 512:(c + 1) * 512],
                          start=(j == 0), stop=(j == NJ - 1), r=["aT", f"Wd_s{di}"], w=[f"pb{c}"])
            for c in range(4):
                kb.cp('act', ymix[:, c * 512:(c + 1) * 512], pb[c][:, :], r=[f"pb{c}"], w=["ymix"])
            o0 = t0 - 128 + sub * 128
            post_norm(nsub, None, GTf, "GTf", sub, o0)

    tile(0, 128, True)
    t = 128
    while t < NT:
        tile(t, 256, False)
        t += 256
    scm.close()
    if env is not None:
        return None
    return kb, ["xo"]


import numpy as np


RG = [[0, 1, 2, 3], [4, 5, 6, 7]]


def build_fused(S, depth=2):
    Tq = S // 4
    NM = S // 512
    NMq = NM // 4
    NT = 128 + Tq
    kb = KB()
    cst = kb.din("cst", [128, 2048])
    cstb = kb.din("cstb", [128, 1024])
    c_col = kb.din("c_col", [128, 16])
    pos = kb.din("pos", [1, S], I32)
    xh = kb.din("xh", [NT, D])
    hmask = kb.din("hmask", [128, 1])
    selv = kb.din("selv", [128, 8])
    xo = kb.dout("xo", [Tq, D])
    L = []
    for l in range(depth):
        e = {}
        for k, (shp, dt_) in P1_PARAM_SHAPES.items():
            e[k] = kb.din(f"{k}_{l}", shp, dt_)
        for k, (shp, dt_) in P2_PARAM_SHAPES.items():
            e[k] = kb.din(f"{k}_{l}", shp, dt_)
        L.append(e)
    G = setup_globals(kb, cst)
    CT = min(2048, Tq)
    NCH = S // CT
    MPC = CT // 512
    hT_own = kb.dscratch("hT_own", [NMq, 128, 16, 512], BF16)
    hTg = kb.dscratch("hTg", [2 * NMq, 4 * 64, 8192], BF16)
    ysc = [kb.dscratch(f"ysc{i}", [NCH, 256, CT], BF16) for i in range(3)]
    yall = [kb.dscratch(f"yall{i}", [NCH, 1024, CT], BF16) for i in range(3)]
    xs1 = kb.dscratch("xs1", [Tq, D])
    xlast = kb.dscratch("xlast", [128, D])
    xl_all = kb.dscratch("xl_all", [4 * 128, D])
    w2s = p2_scratch(kb)
    sel = kb.sb("sel", [128, 8])
    kb.dma(sel[:], selv, "sel", w=["sel"])

    def allgather(src2d, dst2d, key, reads, writes):
        kb.tr.dma('pool', lambda e_: e_.collective_compute("AllGather", ALU.bypass, replica_groups=RG,
                                                            ins=[src2d.opt()], outs=[dst2d.opt()]),
                  key, reads=reads, writes=writes, inc=1)

    for l in range(depth):
        P = L[l]
        x_own = xh[128:NT, :] if l == 0 else xs1
        def after_h(m):
            for half in range(2):
                allgather(hT_own[m, half * 64:(half + 1) * 64].rearrange("p k t -> p (k t)"), hTg[2 * m + half], "ag",
                          reads=[f"hTd{m}"], writes=[f"hTg{2 * m + half}"])

        def load_hT(dst, m, key):
            r_, ml = divmod(m, NMq)
            for half in range(2):
                kb.dma(dst[half * 64:(half + 1) * 64, :, :],
                       hTg[2 * ml + half, r_ * 64:(r_ + 1) * 64, :].rearrange("p (k t) -> p k t", k=16),
                       f"{key}_{half}", r=[f"hTg{2 * ml + half}"], w=[key])

        def yT_ap(row0, nrows, m):
            i, rr = divmod(row0, 256)
            c, mm = divmod(m, MPC)
            return ysc[i][c, rr:rr + nrows, mm * 512:(mm + 1) * 512]

        def after_y(i, m, res):
            if (m + 1) % MPC == 0:
                c = m // MPC
                allgather(ysc[i][c], yall[i][c], "ag", reads=[res], writes=[f"yall{i}_{c}"])

        env1 = dict(P)
        env1.update(kb=kb, G=G, x=x_own, NH=Tq // 128, hTo=hT_own, hTd=None, yT=None, c_col=c_col, pos=pos, cstb=cstb,
                    after_h=after_h, load_hT=load_hT, yT_ap=yT_ap, after_y=after_y)
        build_p1(S, parts=('h',), env=env1)
        build_p1(S, parts=('mamba', 'rwkv', 'ret'), env=env1)
        if l > 0:
            allgather(xlast, xl_all, "ag", reads=["xlast"], writes=["xlall"])

        X = {}

        def extra_alloc(scm, bufs, X=X):
            X['cand'] = scm.sb("ycand", [128, 3, 8, 256], BF16)
            X['ymix'] = bufs['ymix']

        def load_x(dst, rn, row0, is_halo, l=l, X=X):
            if l == 0:
                kb.dma(dst, xh[row0:row0 + 128, :], rn, w=[rn])
            elif not is_halo:
                kb.dma(dst, xs1[row0 - 128:row0, :], rn, r=["xs1"], w=[rn])
            else:
                for q in range(4):
                    kb.dma(X['ymix'][:], xl_all[q * 128:(q + 1) * 128, :], "ymix", r=["xlall"], w=["ymix"])
                    if q == 0:
                        kb.ts('dve', dst, X['ymix'][:], sel[:, 4:5], ALU.mult, r=["ymix", "sel"], w=[rn])
                    else:
                        kb.stt('dve', dst, X['ymix'][:], sel[:, 4 + q:5 + q], dst, ALU.mult, ALU.add,
                               r=["ymix", "sel", rn], w=[rn])

        def load_y(ybf, t0, ntok, is_halo, X=X):
            cand = X['cand']
            for q in range(4):
                gt0 = q * Tq + t0 - 128
                if gt0 < 0:
                    kb.memset('pool', cand[:, :, :, 0:ntok], 0.0, w=[f"ycand{g}" for g in range(3)])
                else:
                    c, off = divmod(gt0, CT)
                    for g in range(3):
                        kb.dma(cand[:, g, :, 0:ntok],
                               yall[g][c, :, off:off + ntok].rearrange("(k p) t -> p k t", p=128),
                               f"ycand{g}", r=[f"yall{g}_{c}"], w=[f"ycand{g}"])
                if q == 0:
                    kb.ts('dve', ybf[:, :, :, 0:ntok], cand[:, :, :, 0:ntok], sel[:, 0:1], ALU.mult,
                          r=["ycand0", "ycand1", "ycand2", "sel"], w=["ybf"])
                else:
                    kb.stt('dve', ybf[:, :, :, 0:ntok], cand[:, :, :, 0:ntok], sel[:, q:q + 1], ybf[:, :, :, 0:ntok],
                           ALU.mult, ALU.add, r=["ycand0", "ycand1", "ycand2", "sel", "ybf"], w=["ybf"])

        def store_x(src, rn, o0, l=l):
            if l < depth - 1:
                kb.dma(xs1[o0:o0 + 128, :], src, rn, r=[rn], w=["xs1"])
                if o0 == Tq - 128:
                    kb.dma(xlast, src, rn, r=[rn], w=["xlast"])
            else:
                kb.dma(xo[o0:o0 + 128, :], src, rn, r=[rn], w=["xo"])

        env2 = dict(P)
        env2.update(w2s)
        env2.update(kb=kb, G=G, hmask=hmask, c_col=c_col, load_x=load_x, load_y=load_y, store_x=store_x,
                    extra_alloc=extra_alloc)
        build_p2(Tq, env=env2)
    nc = kb.finish(["xo"])
    return kb, nc


def fused_core_inputs(inp, b, r, S, depth):
    Tq = S // 4
    o = {}
    for l in range(depth):
        p1 = p1_core_inputs(inp, l, b, r)
        for k in P1_PARAM_SHAPES:
            o[f"{k}_{l}"] = p1[k]
        p2 = p2_core_inputs(inp, l, b, r, Tq, None, None)
        for k in P2_PARAM_SHAPES:
            o[f"{k}_{l}"] = p2[k]
    o['cst'] = p1_consts()
    o['cstb'] = p1_core_consts(r)
    o['c_col'] = np.ascontiguousarray(inp['c'][b].reshape(16, 128).T)
    o['pos'] = np.ascontiguousarray(inp['positions'][b][None, :]).astype(np.int32)
    x_b = inp['x'][b]
    xh = np.zeros((128 + Tq, 2048), np.float32)
    if r == 0:
        xh[128:] = x_b[0:Tq]
    else:
        xh[:] = x_b[r * Tq - 128:(r + 1) * Tq]
    o['xh'] = xh
    o['hmask'] = np.full((128, 1), 0.0 if r == 0 else 1.0, np.float32)
    sv = np.zeros((128, 8), np.float32)
    sv[:, r] = 1.0
    if r > 0:
        sv[:, 4 + r - 1] = 1.0
    o['selv'] = sv
    return o


_CACHE = {}


def kernel(**inputs):
    inp = {k: np.asarray(v) for k, v in inputs.items()}
    inp['x'] = np.ascontiguousarray(inp['x'], dtype=np.float32)
    B, S, _ = inp['x'].shape
    depth = inp['w_in'].shape[0]
    Tq = S // 4
    key = (S, depth)
    if key not in _CACHE:
        _CACHE[key] = build_fused(S, depth)[1]
    nc = _CACHE[key]
    in_maps = []
    for core in range(8):
        b, r = divmod(core, 4)
        in_maps.append(fused_core_inputs(inp, b, r, S, depth))
    res = run_bass_kernel_spmd(nc, in_maps, core_ids=list(range(8)))
    out = np.zeros((B, S, 2048), np.float32)
    for core in range(8):
        b, r = divmod(core, 4)
        out[b, r * Tq:(r + 1) * Tq] = np.asarray(res.results[core]['xo'])
    return out
```

```python
import numpy as np
import concourse.bass as bass
import concourse.mybir as mybir
from concourse.bass_utils import run_bass_kernel_spmd

F32 = mybir.dt.float32
BF16 = mybir.dt.bfloat16
I32 = mybir.dt.int32
AF = mybir.ActivationFunctionType
ALU = mybir.AluOpType
AX = mybir.AxisListType
EPOCH = 30000
import os as _os
SAME_ENGINE_ORDERED = tuple(_os.environ.get('SEO', 'pe').split(','))
D = 2048


class Tracker:
    ENGS = ('pe', 'act', 'dve', 'pool', 'sp')

    def __init__(self, nc):
        self.nc = nc
        self.ops = {e: [] for e in self.ENGS}
        self.cur_sem = {}
        self.cnt = {}
        self.nsem = 0
        for e in self.ENGS:
            self.cur_sem[e] = self._new_sem(e)
            self.cnt[e] = 0
        self.known = {e: {} for e in self.ENGS}
        self.last_w = {}
        self.readers = {}
        self.dma_sems = {}
        self.dma_cnt = {}
        self.nops = 0
        self._old_epochs = []

    def _new_sem(self, tag):
        self.nsem += 1
        return self.nc.alloc_semaphore(f"s{self.nsem}_{tag}")

    def _waits_for(self, eng, reads, writes, is_dma):
        need = {}

        def add(ev):
            sem, val, src, src_dma = ev
            if src == eng and eng in SAME_ENGINE_ORDERED and not src_dma and not is_dma:
                return
            k = id(sem)
            if self.known[eng].get(k, 0) >= val:
                return
            if k not in need or need[k][1] < val:
                need[k] = (sem, val)

        for r in reads:
            ev = self.last_w.get(r)
            if ev is not None:
                add(ev)
        for w in writes:
            ev = self.last_w.get(w)
            if ev is not None:
                add(ev)
            rd = self.readers.get(w)
            if rd:
                for ev in rd.values():
                    add(ev)
        out = list(need.values())
        for sem, val in out:
            self.known[eng][id(sem)] = val
        return out

    def _commit(self, ev, reads, writes):
        for r in reads:
            self.readers.setdefault(r, {})[id(ev[0])] = ev
        for w in writes:
            self.last_w[w] = ev
            self.readers[w] = {}

    def begin_capture(self):
        self._cap = []

    def end_capture(self):
        c, self._cap = self._cap, None
        return c

    def replay_interleaved(self, caps):
        idx = [0] * len(caps)
        while True:
            done = True
            for j, c in enumerate(caps):
                if idx[j] < len(c):
                    kind, args = c[idx[j]]
                    idx[j] += 1
                    done = False
                    if kind == 'op':
                        self.op(*args)
                    else:
                        self.dma(*args)
            if done:
                break

    def op(self, eng, fn, reads=(), writes=()):
        if getattr(self, '_cap', None) is not None:
            self._cap.append(('op', (eng, fn, tuple(reads), tuple(writes))))
            return None
        pr = tuple(r for r in reads if isinstance(r, str) and r.startswith('pb'))
        if pr and eng != 'pe':
            writes = tuple(writes) + pr
        waits = self._waits_for(eng, reads, writes, False)
        if self.cnt[eng] >= EPOCH:
            self._old_epochs.append((self.cur_sem[eng], self.cnt[eng]))
            self.cur_sem[eng] = self._new_sem(eng)
            self.cnt[eng] = 0
        self.cnt[eng] += 1
        sem = self.cur_sem[eng]
        ev = (sem, self.cnt[eng], eng, False)
        self.ops[eng].append((waits, fn, sem, 1))
        self._commit(ev, reads, writes)
        self.nops += 1
        return ev

    def dma(self, eng, fn, key, reads=(), writes=(), inc=16):
        if getattr(self, '_cap', None) is not None:
            self._cap.append(('dma', (eng, fn, key, tuple(reads), tuple(writes), inc)))
            return None
        if key not in self.dma_sems:
            self.dma_sems[key] = self._new_sem('d' + str(key))
            self.dma_cnt[key] = 0
        sem = self.dma_sems[key]
        chan = ('__chan__', key)
        waits = self._waits_for(eng, tuple(reads), tuple(writes) + (chan,), True)
        self.dma_cnt[key] += inc
        ev = (sem, self.dma_cnt[key], eng, True)
        self.ops[eng].append((waits, fn, sem, inc))
        self._commit(ev, reads, tuple(writes) + (chan,))
        self.nops += 1
        return ev

    def barrier(self):
        latest = {}
        for e in self.ENGS:
            for (waits, fn, sem, inc) in ():
                pass
        for e in self.ENGS:
            if self.cnt[e] > 0:
                latest[id(self.cur_sem[e])] = (self.cur_sem[e], self.cnt[e])
        for k, sem in self.dma_sems.items():
            if self.dma_cnt[k] > 0:
                latest[id(sem)] = (sem, self.dma_cnt[k])
        for (sem, val) in self._old_epochs:
            latest[id(sem)] = (sem, val)
        for e in self.ENGS:
            waits = []
            for k, (sem, val) in latest.items():
                if self.known[e].get(k, 0) < val:
                    waits.append((sem, val))
                    self.known[e][k] = val
            if waits:
                self.ops[e].append((waits, None, None, 0))

    def wait_all(self, eng, resources):
        waits = self._waits_for(eng, resources, (), True)
        self.ops[eng].append((waits, None, None, 0))

    def emit(self):
        nc = self.nc
        ops = self.ops
        with nc.Block() as block:
            def run(e, lst):
                for waits, fn, sem, inc in lst:
                    for s, v in waits:
                        e.wait_ge(s, v)
                    if fn is not None:
                        fn(e).then_inc(sem, inc)

            @block.tensor
            def _(e):
                run(e, ops['pe'])

            @block.scalar
            def _(e):
                run(e, ops['act'])

            @block.vector
            def _(e):
                run(e, ops['dve'])

            @block.gpsimd
            def _(e):
                run(e, ops['pool'])

            @block.sync
            def _(e):
                run(e, ops['sp'])


class KB:
    def __init__(self, name="k"):
        self.nc = bass.Bass("TRN2", target_bir_lowering=False)
        self.nc.allow_low_precision("bf16 matmul operands with fp32 PSUM accumulation")
        self.tr = Tracker(self.nc)
        self._n = 0
        self.outs = []

    def din(self, name, shape, dt=F32):
        return self.nc.dram_tensor(name, list(shape), dt, kind="ExternalInput").ap()

    def dout(self, name, shape, dt=F32):
        self.outs.append(name)
        return self.nc.dram_tensor(name, list(shape), dt, kind="ExternalOutput").ap()

    def dscratch(self, name, shape, dt=F32):
        return self.nc.dram_tensor(name, list(shape), dt, kind="Internal").ap()

    def sb(self, name, shape, dt=F32):
        return self.nc.alloc_sbuf_tensor(name, list(shape), dt)

    def ps(self, name, shape, dt=F32):
        return self.nc.alloc_psum_tensor(name, list(shape), dt)

    def dma(self, out, in_, key, r=(), w=(), eng='sp'):
        self.tr.dma(eng, lambda e: e.dma_start(out=out, in_=in_), key, reads=r, writes=w)

    def mm(self, out, lhsT, rhs, start, stop, r, w):
        self.tr.op('pe', lambda e: e.matmul(out, lhsT=lhsT, rhs=rhs, start=start, stop=stop), reads=r, writes=w)

    def tp(self, out, in_, ident, r, w):
        self.tr.op('pe', lambda e: e.transpose(out=out, in_=in_, identity=ident), reads=r, writes=w)

    def act(self, out, in_, func, r, w, bias=None, scale=None, accum=None):
        kw = {}
        if bias is not None:
            kw['bias'] = bias
        if scale is not None:
            kw['scale'] = scale
        if accum is not None:
            kw['accum_out'] = accum
        self.tr.op('act', lambda e: e.activation(out=out, in_=in_, func=func, **kw), reads=r, writes=w)

    def tt(self, eng, out, a, b, op, r, w):
        self.tr.op(eng, lambda e: e.tensor_tensor(out=out, in0=a, in1=b, op=op), reads=r, writes=w)

    def ts(self, eng, out, a, s1, op0, r, w, s2=None, op1=None, accum=None):
        kw = {}
        if op1 is not None:
            kw['op1'] = op1
        if accum is not None:
            kw['accum_out'] = accum
        self.tr.op(eng, lambda e: e.tensor_scalar(out=out, in0=a, scalar1=s1, scalar2=s2, op0=op0, **kw),
                   reads=r, writes=w)

    def stt(self, eng, out, a, s, b, op0, op1, r, w):
        eng = 'dve'
        self.tr.op(eng, lambda e: e.scalar_tensor_tensor(out=out, in0=a, scalar=s, in1=b, op0=op0, op1=op1),
                   reads=r, writes=w)

    def cp(self, eng, out, in_, r, w):
        if eng == 'act':
            self.tr.op('act', lambda e: e.copy(out=out, in_=in_), reads=r, writes=w)
        else:
            self.tr.op(eng, lambda e: e.tensor_copy(out=out, in_=in_), reads=r, writes=w)

    def memset(self, eng, out, val, w):
        self.tr.op(eng, lambda e: e.memset(out, val), writes=w)

    def recip(self, out, in_, r, w):
        self.tr.op('dve', lambda e: e.reciprocal(out=out, in_=in_), reads=r, writes=w)

    def scan(self, out, d0, d1, r, w):
        self.tr.op('dve', lambda e: e.tensor_tensor_scan(out=out, data0=d0, data1=d1, initial=0.0,
                                                         op0=ALU.mult, op1=ALU.add), reads=r, writes=w)

    def finish(self, out_resources):
        self.tr.wait_all('sp', out_resources)
        self.tr.emit()
        return self.nc


import numpy as np


def p1_consts():
    C = np.zeros((128, 2048), np.float32)
    i = np.arange(128)
    C[:, 0:128] = np.eye(128)
    C[:, 128:256] = (i[None, :] >= i[:, None])
    C[:, 256:384] = 1.0
    blk = (i[:, None] // 64 == i[None, :] // 64).astype(np.float32)
    C[:, 384:512] = blk
    C[:, 512:640] = blk / 64.0
    C[:, 640:768] = blk * (i[:, None] < i[None, :])
    C[:, 768:896] = blk * (i[:, None] > i[None, :])
    C[:, 896:1024] = blk * (i[:, None] <= i[None, :])
    rm = np.ones(512, np.float32)
    rm[::64] = 0
    C[:, 1024:1536] = rm[None, :]
    return C


def p1_core_consts(g):
    Cb = np.zeros((128, 1024), np.float32)
    p = np.arange(128)
    hl = p // 64
    d = p % 64
    heads = 2 * g + hl
    lg = np.log1p(-np.exp2(-5.0 - heads.astype(np.float64)))
    inv_freq = 10000.0 ** (-(d % 32).astype(np.float64) / 32.0)
    Cb[:, 0] = inv_freq
    Cb[:, 1] = np.where(d < 32, -1.0, 1.0)
    Cb[:, 2] = np.exp(128.0 * lg)
    idx = np.arange(128, dtype=np.float64)
    Cb[:, 128:256] = np.exp((idx[None, :] + 1.0) * lg[:, None])
    Cb[:, 256:384] = np.exp((127.0 - idx)[:, None] * lg[None, :])
    for h2 in range(2):
        lgh = np.log1p(-np.exp2(-5.0 - (2 * g + h2)))
        rel = idx[None, :] - idx[:, None]
        Cb[:, 384 + h2 * 128:384 + (h2 + 1) * 128] = np.where(rel >= 0, np.exp(rel * lgh), 0.0)
    return Cb


M_COLS = 3088
RW_COLS = 3520


def p1_core_inputs(inp, l, b, g):
    w_in = inp['w_in'][l]
    o = {}
    zc = np.arange(256 * g, 256 * g + 256)
    xsc = 1024 + np.arange(256 * g, 256 * g + 256)
    Bc = 2048 + np.arange(128 * g, 128 * g + 128)
    Cc = 2560 + np.arange(128 * g, 128 * g + 128)
    dtc = 3072 + np.arange(4 * g, 4 * g + 4)
    o['w_mfm'] = np.ascontiguousarray(w_in[:, np.concatenate([xsc, Bc, Cc])])
    o['w_mtm'] = np.ascontiguousarray(w_in[:, np.concatenate([zc, dtc])])
    convc = np.concatenate([xsc, Bc, Cc]) - 1024
    cw = np.concatenate([inp['m_conv_w'][l][:, convc], inp['m_conv_b'][l][None, convc]], axis=0)
    o['m_cw'] = np.ascontiguousarray(cw.T.reshape(4, 128, 5).transpose(1, 0, 2))
    o['m_hd'] = np.ascontiguousarray(inp['m_head'][l][:, 4 * g:4 * g + 4].reshape(1, 12))
    o['m_ng'] = np.ascontiguousarray(inp['m_norm_g'][l][None, 256 * g:256 * g + 256])
    r0 = M_COLS
    ch = np.arange(256 * g, 256 * g + 256)
    cols = np.concatenate([r0 + ch, r0 + 1024 + ch, r0 + 2048 + ch, r0 + 3072 + np.arange(96),
                           r0 + 3168 + np.arange(96), r0 + 3264 + np.arange(256)])
    o['w_wfm'] = np.ascontiguousarray(w_in[:, cols])
    mu = inp['rwkv_mu'][l][cols - r0]
    mut = np.zeros((128, 10), np.float32)
    offs = [0, 128, 256, 384, 512, 640, 768, 864, 960, 1088]
    sizes = [128, 128, 128, 128, 128, 128, 96, 96, 128, 128]
    for q, (of, sz) in enumerate(zip(offs, sizes)):
        mut[:sz, q] = mu[of:of + sz]
    o['rw_mu'] = mut
    vec = inp['rwkv_vec'][l][:, ch]
    o['rw_vec'] = np.ascontiguousarray(vec.T.reshape(2, 128, 7).transpose(1, 0, 2))
    o['rw_w2'] = np.ascontiguousarray(inp['rwkv_w2'][l][:, ch])
    o['rw_a2'] = np.ascontiguousarray(inp['rwkv_a2'][l][:, ch])
    o['rw_g2'] = np.ascontiguousarray(inp['rwkv_g2'][l][:, ch])
    t0 = M_COLS + RW_COLS
    hd = np.arange(128 * g, 128 * g + 128)
    d = hd % 64
    partner = np.where(d < 32, hd + 32, hd - 32)
    cols = np.concatenate([t0 + hd, t0 + partner, t0 + 512 + hd, t0 + 512 + partner])
    o['w_rfm'] = np.ascontiguousarray(w_in[:, cols])
    cols = np.concatenate([t0 + 1024 + ch, t0 + 2048 + ch])
    o['w_rtm'] = np.ascontiguousarray(w_in[:, cols])
    o['c_col'] = np.ascontiguousarray(inp['c'][b].reshape(16, 128).T)
    o['w_ada'] = np.ascontiguousarray(inp['w_ada'][l][:, :4096])
    o['b_ada'] = np.ascontiguousarray(inp['b_ada'][l][None, :4096])
    o['norm_g'] = inp['norm_g'][l]
    o['pos'] = np.ascontiguousarray(inp['positions'][b][None, :]).astype(np.int32)
    o['cst'] = p1_consts()
    o['cstb'] = p1_core_consts(g)
    return o


def p2_core_inputs(inp, l, b, q, Tq, x_b, yT_b):
    o = {}
    t0 = q * Tq
    NT = 128 + Tq
    if x_b is not None:
        xh = np.zeros((NT, 2048), np.float32)
        yh = np.zeros((3, 1024, NT), yT_b.dtype)
        if q == 0:
            xh[128:] = x_b[0:Tq]
            yh[:, :, 128:] = yT_b[:, :, 0:Tq]
        else:
            xh[:] = x_b[t0 - 128:t0 + Tq]
            yh[:] = yT_b[:, :, t0 - 128:t0 + Tq]
        o['xh'] = xh
        o['yTh'] = yh
    o['hmask'] = np.full((128, 1), 0.0 if q == 0 else 1.0, np.float32)
    o['c_col'] = np.ascontiguousarray(inp['c'][b].reshape(16, 128).T)
    o['w_ada'] = inp['w_ada'][l]
    o['b_ada'] = inp['b_ada'][l][None, :]
    ng = inp['norm_g'][l]
    o['norm_g'] = ng
    o['ng_col'] = np.ascontiguousarray(ng.reshape(4, 16, 128).transpose(2, 0, 1).reshape(128, 64))
    o['w_gate'] = inp['w_gate'][l]
    o['w_branch'] = inp['w_branch'][l]
    o['w_out'] = inp['w_out'][l]
    o['w_up'] = inp['w_up'][l]
    fcv = np.concatenate([inp['f_conv_w'][l], inp['f_conv_b'][l][None, :]], axis=0)
    o['f_cv'] = np.ascontiguousarray(fcv.T.reshape(86, 128, 4).transpose(1, 0, 2))
    o['w_down'] = inp['w_down'][l]
    o['cst'] = p1_consts()
    return o


from contextlib import ExitStack
import numpy as np


MAGIC = 12582912.0
TWO_PI = float(2 * np.pi)


class Scope:
    _n = 0

    def __init__(self, kb):
        self.kb = kb
        self.st = ExitStack()
        Scope._n += 1
        self.tag = f"_sc{Scope._n}"

    def sb(self, name, shape, dt=F32):
        return self.st.enter_context(self.kb.nc.sbuf_tensor(name + self.tag, list(shape), dt))

    def close(self):
        self.kb.tr.barrier()
        self.st.close()


def load_cast_weights(kb, sc, wdram, ncols, name, stage_cols=128):
    W = sc.sb(name, [128, 16, ncols], BF16)
    stg = [sc.sb(f"{name}_stg{i}", [128, 16, stage_cols], F32) for i in range(2)]
    wv = wdram.rearrange("(k p) n -> p k n", p=128)
    i = 0
    for c0 in range(0, ncols, stage_cols):
        c1 = min(ncols, c0 + stage_cols)
        s = stg[i % 2]
        rn = f"{name}_stg{i % 2}"
        kb.dma(s[:, :, 0:c1 - c0], wv[:, :, c0:c1], rn, w=[rn])
        kb.cp('pool' if i % 2 else 'dve', W[:, :, c0:c1], s[:, :, 0:c1 - c0], r=[rn], w=[name])
        i += 1
    return W


def emit_mod(kb, sc, c_col, w_ada, b_ada, vec_ids, ones_row, pb, tag):
    cc = sc.sb(f"{tag}_cc", [128, 16])
    scl = sc.sb(f"{tag}_sc", [128, 16])
    kb.dma(cc[:], c_col, f"{tag}_cc", w=[f"{tag}_cc"])
    kb.act(scl[:], cc[:], AF.Silu, r=[f"{tag}_cc"], w=[f"{tag}_sc"])
    stg = [sc.sb(f"{tag}_wa{i}", [128, 16, 128], F32) for i in range(2)]
    wv = w_ada.rearrange("(k p) n -> p k n", p=128)
    rows = {}
    i = 0
    for vid in vec_ids:
        row = sc.sb(f"{tag}_row{vid}", [1, 2048])
        brow = sc.sb(f"{tag}_brow{vid}", [1, 2048])
        rn = f"{tag}_row{vid}"
        kb.dma(brow[:], b_ada[0:1, vid * 2048:(vid + 1) * 2048], f"{tag}_brow{vid}", w=[f"{tag}_brow{vid}"])
        for cch in range(16):
            c0 = vid * 2048 + cch * 128
            s = stg[i % 2]
            sn = f"{tag}_wa{i % 2}"
            kb.dma(s[:], wv[:, :, c0:c0 + 128], sn, w=[sn])
            pr = "pb7" if i % 2 == 0 else "pb4"
            pcols = (pb[7] if i % 2 == 0 else pb[4])[0:1, 0:128]
            for k in range(16):
                kb.mm(pcols, lhsT=scl[:, k:k + 1], rhs=s[:, k, :], start=(k == 0), stop=(k == 15),
                      r=[sn, f"{tag}_sc"], w=[pr])
            kb.tt('dve', row[0:1, cch * 128:(cch + 1) * 128], pcols, brow[0:1, cch * 128:(cch + 1) * 128], ALU.add,
                  r=[pr, f"{tag}_brow{vid}"], w=[rn])
            i += 1
        rows[vid] = (row, rn)
    return rows


def bcast_row(kb, row, rn, ones_row, pb, emit_chunk):
    for c in range(4):
        pr = "pb6" if c % 2 == 0 else "pb5"
        pa = pb[6][:, 0:512] if c % 2 == 0 else pb[5][:, 0:512]
        kb.mm(pa, lhsT=ones_row[0:1, 0:128], rhs=row[0:1, c * 512:(c + 1) * 512], start=True, stop=True,
              r=[rn, 'C'], w=[pr])
        emit_chunk(c, pa, pr)


P1_PARAM_SHAPES = {
    "w_mfm": ([D, 512], F32), "w_mtm": ([D, 260], F32), "m_cw": ([128, 4, 5], F32), "m_hd": ([1, 12], F32),
    "m_ng": ([1, 256], F32), "w_rfm": ([D, 512], F32), "w_rtm": ([D, 512], F32), "w_wfm": ([D, 1216], F32),
    "rw_mu": ([128, 10], F32), "rw_vec": ([128, 2, 7], F32), "rw_w2": ([96, 256], F32), "rw_a2": ([96, 256], F32),
    "rw_g2": ([256, 256], F32),
}


def setup_globals(kb, cst):
    G = {}
    G['pb'] = [kb.ps(f"pb{i}", [128, 512]) for i in range(8)]
    C = kb.sb("C", [128, 2048])
    kb.dma(C[:], cst, "C", w=["C"])
    Cb = kb.sb("Cb", [128, 256], BF16)
    kb.cp('dve', Cb[:, 0:128], C[:, 0:128], r=["C"], w=["Cb"])
    G['C'] = C
    G['Cb'] = Cb
    return G


def build_p1(S, parts=('h', 'mamba', 'rwkv', 'ret'), env=None):
    NM = S // 512
    if env is None:
        kb = KB()
        e = {}
        e['x'] = kb.din("x", [S, D])
        e['c_col'] = kb.din("c_col", [128, 16])
        e['w_ada'] = kb.din("w_ada", [D, 2 * D])
        e['b_ada'] = kb.din("b_ada", [1, 2 * D])
        e['norm_g'] = kb.din("norm_g", [4, D])
        e['pos'] = kb.din("pos", [1, S], I32)
        cst = kb.din("cst", [128, 2048])
        e['cstb'] = kb.din("cstb", [128, 1024])
        for k, (shp, dt_) in P1_PARAM_SHAPES.items():
            e[k] = kb.din(k, shp, dt_)
        e['yT'] = kb.dout("yT", [768, S], BF16)
        e['hTd'] = kb.dscratch("hTd", [NM, 128, 16, 512], BF16)
        e['hTo'] = e['hTd']
        e['NH'] = S // 128
        G = setup_globals(kb, cst)
    else:
        kb = env['kb']
        e = env
        G = env['G']
    nc = kb.nc
    x, c_col, w_ada, b_ada, norm_g, pos, cstb = (e[k] for k in ('x', 'c_col', 'w_ada', 'b_ada', 'norm_g', 'pos', 'cstb'))
    w_mfm, w_mtm, m_cw, m_hd, m_ng, w_rfm, w_rtm, w_wfm = (e[k] for k in ('w_mfm', 'w_mtm', 'm_cw', 'm_hd', 'm_ng', 'w_rfm', 'w_rtm', 'w_wfm'))
    rw_mu, rw_vec, rw_w2, rw_a2, rw_g2 = (e[k] for k in ('rw_mu', 'rw_vec', 'rw_w2', 'rw_a2', 'rw_g2'))
    yT, hTd, hTo, NH = e['yT'], e['hTd'], e['hTo'], e['NH']

    def _load_hT(dst, m, key):
        kb.dma(dst[:], hTd[m], key, r=[f"hTd{m}"], w=[key])

    def _yT_ap(row0, nrows, m):
        return yT[row0:row0 + nrows, m * 512:(m + 1) * 512]

    load_hT = e.get('load_hT', _load_hT)
    yT_ap = e.get('yT_ap', _yT_ap)
    after_y = e.get('after_y', lambda i, m, res: None)
    pb = G['pb']
    C, Cb = G['C'], G['Cb']
    ident_f = C[:, 0:128]
    U_f = C[:, 128:256]
    ones_f = C[:, 256:384]
    blk_f = C[:, 384:512]
    blk64_f = C[:, 512:640]
    SU_f = C[:, 640:768]
    SL_f = C[:, 768:896]
    UI_f = C[:, 896:1024]
    rmask = C[:, 1024:1536]
    ident_b = Cb[:, 0:128]
    ones_row = C[0:1, 256:384]

    if 'h' in parts:
        sc = Scope(kb)
        rows = emit_mod(kb, sc, c_col, w_ada, b_ada, [0, 1], ones_row, pb, "m0")
        Gbc = sc.sb("Gbc", [128, 2048])
        SHbc = sc.sb("SHbc", [128, 2048])
        kb.dma(Gbc[:], norm_g[0:1, :].partition_broadcast(128), "Gbc", w=["Gbc"])

        def g_chunk(c, pa, pr):
            kb.stt('dve', Gbc[:, c * 512:(c + 1) * 512], pa, 1.0, Gbc[:, c * 512:(c + 1) * 512], ALU.add, ALU.mult,
                   r=[pr, "Gbc"], w=["Gbc"])

        def sh_chunk(c, pa, pr):
            kb.cp('act', SHbc[:, c * 512:(c + 1) * 512], pa, r=[pr], w=["SHbc"])

        bcast_row(kb, rows[1][0], rows[1][1], ones_row, pb, g_chunk)
        bcast_row(kb, rows[0][0], rows[0][1], ones_row, pb, sh_chunk)
        xt = [sc.sb(f"xt{i}", [128, 2048]) for i in range(2)]
        junk = sc.sb("junk", [128, 2048], BF16)
        tmp = sc.sb("htmp", [128, 2048])
        hb = sc.sb("hb", [128, 2048], BF16)
        ss = sc.sb("ss", [128, 1])
        rstd = sc.sb("rstd", [128, 1])
        hst = [sc.sb(f"hst{i}", [128, 16, 512], BF16) for i in range(2)]
        for i in range(NH):
            m, sub = divmod(i, 4)
            xb = xt[i % 2]
            xn = f"xt{i % 2}"
            kb.dma(xb[:], x[i * 128:(i + 1) * 128, :], xn, w=[xn])
            kb.act(junk[:], xb[:], AF.Square, r=[xn], w=["junk", "ss"], accum=ss[:])
            kb.act(rstd[:], ss[:], AF.Sqrt, r=["ss"], w=["rstd"], bias=1e-6, scale=1.0 / D)
            kb.recip(rstd[:], rstd[:], r=["rstd"], w=["rstd"])
            kb.stt('dve', tmp[:], xb[:], rstd[:, 0:1], Gbc[:], ALU.mult, ALU.mult, r=[xn, "rstd", "Gbc"], w=["htmp"])
            kb.tt('pool', hb[:], tmp[:], SHbc[:], ALU.add, r=["htmp", "SHbc"], w=["hb"])
            hs = hst[m % 2]
            hn = f"hst{m % 2}"
            for q in range(4):
                pr = f"pb{q}"
                for kk in range(4):
                    k = q * 4 + kk
                    kb.mm(pb[q][:, kk * 128:(kk + 1) * 128], lhsT=hb[:, k * 128:(k + 1) * 128], rhs=ident_b,
                          start=True, stop=True, r=["hb", "Cb"], w=[pr])
                eng = 'act' if q % 2 == 0 else 'dve'
                kb.cp(eng, hs[:, q * 4:(q + 1) * 4, sub * 128:(sub + 1) * 128],
                      pb[q][:, :].rearrange("p (k t) -> p k t", k=4), r=[pr], w=[hn])
            if sub == 3:
                kb.dma(hTo[m], hs[:], hn, r=[hn], w=[f"hTd{m}"])
                if 'after_h' in e:
                    e['after_h'](m)
        sc.close()

    if 'mamba' in parts:
        sc = Scope(kb)
        Wfm = load_cast_weights(kb, sc, w_mfm, 512, "Wfm")
        Wtm = load_cast_weights(kb, sc, w_mtm, 260, "Wtm")
        cw = sc.sb("cw", [128, 4, 5])
        kb.dma(cw[:], m_cw, "cw", w=["cw"])
        mh = sc.sb("mh", [128, 12])
        kb.dma(mh[:], m_hd.partition_broadcast(128), "mh", w=["mh"])
        negA = sc.sb("negA", [128, 4])
        kb.act(negA[:], mh[:, 4:8], AF.Exp, r=["mh"], w=["negA"])
        kb.ts('dve', negA[:], negA[:], -1.0, ALU.mult, r=["negA"], w=["negA"])
        dsk = sc.sb("dsk", [128, 256])
        for h in range(4):
            kb.ts('dve', dsk[:, h * 64:(h + 1) * 64], ones_f[:, 0:64], mh[:, 8 + h:9 + h], ALU.mult,
                  r=["C", "mh"], w=["dsk"])
        mng = sc.sb("mng", [128, 256])
        kb.dma(mng[:], m_ng.partition_broadcast(128), "mng", w=["mng"])
        hT = [sc.sb(f"hT{i}", [128, 16, 512], BF16) for i in range(2)]
        raw = [sc.sb(f"raw{j}", [128, 515]) for j in range(4)]
        acc = [sc.sb(f"acc{j}", [128, 512]) for j in range(2)]
        xs_act = [sc.sb(f"xsa{j}", [128, 512]) for j in range(2)]
        B_act = sc.sb("B_act", [128, 512])
        BT_bf = sc.sb("BT_bf", [128, 512], BF16)
        CT_bf = sc.sb("CT_bf", [128, 512], BF16)
        zs = sc.sb("zs", [128, 256])
        dtp = sc.sb("dtp", [128, 4])
        dt = sc.sb("dt", [128, 4])
        adt = sc.sb("adt", [128, 4])
        xs_sb = sc.sb("xs_sb", [128, 256])
        B_tm = sc.sb("B_tm", [128, 128], BF16)
        xdt_bf = sc.sb("xdt_bf", [128, 256], BF16)
        xte_bf = sc.sb("xte_bf", [128, 256], BF16)
        cum_sb = sc.sb("cum_sb", [128, 4])
        rh = sc.sb("rh", [128, 4, 128])
        CBm = sc.sb("CBm", [128, 128])
        seg = sc.sb("seg", [128, 4, 128])
        dec = sc.sb("dec", [128, 4, 128])
        MT_bf = sc.sb("MT_bf", [128, 4, 128], BF16)
        ecum = sc.sb("ecum", [128, 4])
        tte = sc.sb("tte", [128, 4])
        ecl = sc.sb("ecl", [128, 4])
        yi_sb = sc.sb("yi_sb", [128, 256])
        y = sc.sb("y", [128, 256])
        y2 = sc.sb("y2", [128, 256])
        state = sc.sb("state", [128, 256])
        state_bf = sc.sb("state_bf", [128, 256], BF16)
        mjunk = sc.sb("mjunk", [128, 256])
        mss = sc.sb("mss", [128, 1])
        mrstd = sc.sb("mrstd", [128, 1])
        yo_bf = sc.sb("yo_bf", [128, 256], BF16)
        yst = [sc.sb(f"yst{i}", [128, 2, 512], BF16) for i in range(2)]
        kb.memset('pool', state[:], 0.0, w=["state"])
        kb.memset('pool', state_bf[:], 0.0, w=["state_bf"])
        for j in range(4):
            kb.memset('pool', raw[j][:], 0.0, w=[f"raw{j}"])
        for m in range(NM):
            h_ = hT[m % 2]
            hn = f"hT{m % 2}"
            load_hT(h_, m, hn)
            ys = yst[m % 2]
            ysn = f"yst{m % 2}"
            for j in range(4):
                pr = f"pb{j % 2}"
                pa = pb[j % 2]
                for k in range(16):
                    kb.mm(pa[:, :], lhsT=Wfm[:, k, j * 128:(j + 1) * 128], rhs=h_[:, k, :], start=(k == 0),
                          stop=(k == 15), r=["Wfm", hn], w=[pr])
                rj = f"raw{j}"
                kb.cp('pool', raw[j][:, 0:3], raw[j][:, 512:515], r=[rj], w=[rj])
                kb.cp('act', raw[j][:, 3:515], pa[:, :], r=[pr], w=[rj])
                a_ = acc[j % 2]
                an = f"acc{j % 2}"
                kb.ts('dve', a_[:], raw[j][:, 3:515], cw[:, j, 3:4], ALU.mult, r=[rj, "cw"], w=[an],
                      s2=cw[:, j, 4:5], op1=ALU.add)
                for tpi, eng in ((2, 'pool'), (1, 'dve'), (0, 'pool')):
                    kb.stt(eng, a_[:], raw[j][:, tpi:tpi + 512], cw[:, j, tpi:tpi + 1], a_[:], ALU.mult, ALU.add,
                           r=[rj, "cw", an], w=[an])
                if j < 2:
                    kb.act(xs_act[j][:], a_[:], AF.Silu, r=[an], w=[f"xsa{j}"])
                elif j == 2:
                    kb.act(B_act[:], a_[:], AF.Silu, r=[an], w=["B_act"])
                    kb.cp('pool', BT_bf[:], B_act[:], r=["B_act"], w=["BT_bf"])
                else:
                    kb.act(CT_bf[:], a_[:], AF.Silu, r=[an], w=["CT_bf"])
            import os as _os
            _nch = int(_os.environ.get("MB_NCH", "4"))
            for c in range(_nch):
                cs = slice(c * 128, (c + 1) * 128)
                _sect = _os.environ.get('MB_SECT', '123456')
                if '1' in _sect:
                    for k in range(16):
                        kb.mm(pb[2][:, 0:260], lhsT=h_[:, k, cs], rhs=Wtm[:, k, :], start=(k == 0), stop=(k == 15),
                              r=[hn, "Wtm"], w=["pb2"])
                    kb.act(zs[:], pb[2][:, 0:256], AF.Silu, r=["pb2"], w=["zs"])
                    kb.tt('dve', dtp[:], pb[2][:, 256:260], mh[:, 0:4], ALU.add, r=["pb2", "mh"], w=["dtp"])
                    kb.act(dtp[:], dtp[:], AF.Exp, r=["dtp"], w=["dtp"])
                    kb.act(dt[:], dtp[:], AF.Ln, r=["dtp"], w=["dt"], bias=1.0)
                    kb.tt('dve', adt[:], dt[:], negA[:], ALU.mult, r=["dt", "negA"], w=["adt"])
                if '2' in _sect:
                    for j in range(2):
                        kb.mm(pb[3][:, j * 128:(j + 1) * 128], lhsT=xs_act[j][:, cs], rhs=ident_f, start=True, stop=True,
                              r=[f"xsa{j}", "C"], w=["pb3"])
                    kb.mm(pb[3][:, 256:384], lhsT=B_act[:, cs], rhs=ident_f, start=True, stop=True,
                          r=["B_act", "C"], w=["pb3"])
                    kb.cp('act', xs_sb[:], pb[3][:, 0:256], r=["pb3"], w=["xs_sb"])
                    kb.cp('dve', B_tm[:], pb[3][:, 256:384], r=["pb3"], w=["B_tm"])
                    for h in range(4):
                        kb.ts('pool', xdt_bf[:, h * 64:(h + 1) * 64], xs_sb[:, h * 64:(h + 1) * 64], dt[:, h:h + 1],
                              ALU.mult, r=["xs_sb", "dt"], w=["xdt_bf"])
                if '3' in _sect:
                    kb.mm(pb[4][:, 0:4], lhsT=U_f, rhs=adt[:], start=True, stop=True, r=["C", "adt"], w=["pb4"])
                    kb.cp('act', cum_sb[:], pb[4][:, 0:4], r=["pb4"], w=["cum_sb"])
                    for h in range(4):
                        kb.ts('pool', rh[:, h, :], U_f, adt[:, h:h + 1], ALU.mult, r=["C", "adt"], w=["rh"])
                    for h in range(4):
                        kb.mm(pb[5][:, h * 128:(h + 1) * 128], lhsT=ones_f, rhs=rh[:, h, :], start=True, stop=True,
                              r=["C", "rh"], w=["pb5"])
                    kb.mm(pb[4][:, 128:256], lhsT=BT_bf[:, cs], rhs=CT_bf[:, cs], start=True, stop=True,
                          r=["BT_bf", "CT_bf"], w=["pb4"])
                    kb.tt('dve', CBm[:], pb[4][:, 128:256], U_f, ALU.mult, r=["pb4", "C"], w=["CBm"])
                    for h in range(4):
                        kb.ts('dve', seg[:, h, :], pb[5][:, h * 128:(h + 1) * 128], cum_sb[:, h:h + 1], ALU.subtract,
                              r=["pb5", "cum_sb"], w=["seg"], s2=0.0, op1=ALU.min)
                    kb.act(dec[:], seg[:], AF.Exp, r=["seg"], w=["dec"])
                    for h in range(4):
                        kb.tt('pool', MT_bf[:, h, :], dec[:, h, :], CBm[:], ALU.mult, r=["dec", "CBm"], w=["MT_bf"])
                    kb.act(ecum[:], cum_sb[:], AF.Exp, r=["cum_sb"], w=["ecum"])
                    last = pb[5][:, :].rearrange("p (h i) -> p h i", h=4)[:, :, 127:128]
                    kb.tt('dve', tte[:].rearrange("p (h o) -> p h o", o=1), last,
                          cum_sb[:].rearrange("p (h o) -> p h o", o=1), ALU.subtract, r=["pb5", "cum_sb"], w=["tte"])
                    kb.act(tte[:], tte[:], AF.Exp, r=["tte"], w=["tte"])
                    kb.act(ecl[:].rearrange("p (h o) -> p h o", o=1), last, AF.Exp, r=["pb5"], w=["ecl"])
                if '4' in _sect:
                    for h in range(4):
                        kb.mm(pb[6][:, h * 64:(h + 1) * 64], lhsT=MT_bf[:, h, :], rhs=xdt_bf[:, h * 64:(h + 1) * 64],
                              start=True, stop=True, r=["MT_bf", "xdt_bf"], w=["pb6"])
                    kb.mm(pb[6][:, 256:512], lhsT=CT_bf[:, cs], rhs=state_bf[:], start=True, stop=True,
                          r=["CT_bf", "state_bf"], w=["pb6"])
                    kb.cp('act', yi_sb[:], pb[6][:, 0:256], r=["pb6"], w=["yi_sb"])
                    for h in range(4):
                        hs_ = slice(h * 64, (h + 1) * 64)
                        kb.stt('dve', y[:, hs_], pb[6][:, 256 + h * 64:256 + (h + 1) * 64], ecum[:, h:h + 1], yi_sb[:, hs_],
                               ALU.mult, ALU.add, r=["pb6", "ecum", "yi_sb"], w=["y"])
                if '5' in _sect:
                    for h in range(4):
                        hs_ = slice(h * 64, (h + 1) * 64)
                        kb.ts('pool', xte_bf[:, hs_], xdt_bf[:, hs_], tte[:, h:h + 1], ALU.mult, r=["xdt_bf", "tte"],
                              w=["xte_bf"])
                    kb.mm(pb[7][:, 0:256], lhsT=B_tm[:], rhs=xte_bf[:], start=True, stop=True, r=["B_tm", "xte_bf"],
                          w=["pb7"])
                    for h in range(4):
                        hs_ = slice(h * 64, (h + 1) * 64)
                        kb.stt('dve', state[:, hs_], state[:, hs_], ecl[:, h:h + 1], pb[7][:, hs_], ALU.mult, ALU.add,
                               r=["state", "ecl", "pb7"], w=["state"])
                    kb.cp('pool', state_bf[:], state[:], r=["state"], w=["state_bf"])
                if '6' in _sect:
                    kb.tt('pool', y2[:], xs_sb[:], dsk[:], ALU.mult, r=["xs_sb", "dsk"], w=["y2"])
                    kb.tt('pool', y[:], y[:], y2[:], ALU.add, r=["y", "y2"], w=["y"])
                    kb.tt('pool', y[:], y[:], zs[:], ALU.mult, r=["y", "zs"], w=["y"])
                    kb.act(mjunk[:], y[:], AF.Square, r=["y"], w=["mjunk", "mss"], accum=mss[:])
                    kb.act(mrstd[:], mss[:], AF.Sqrt, r=["mss"], w=["mrstd"], bias=1e-5, scale=1.0 / 256)
                    kb.recip(mrstd[:], mrstd[:], r=["mrstd"], w=["mrstd"])
                    kb.stt('dve', yo_bf[:], y[:], mrstd[:, 0:1], mng[:], ALU.mult, ALU.mult, r=["y", "mrstd", "mng"],
                           w=["yo_bf"])
                    for t in range(2):
                        kb.mm(pb[7][:, 256 + t * 128:256 + (t + 1) * 128], lhsT=yo_bf[:, t * 128:(t + 1) * 128],
                              rhs=ident_b, start=True, stop=True, r=["yo_bf", "Cb"], w=["pb7"])
                    kb.cp('act', ys[:, :, cs], pb[7][:, 256:512].rearrange("p (t s) -> p t s", t=2), r=["pb7"], w=[ysn])
            kb.dma(yT_ap(0, 256, m).rearrange("(t p) s -> p t s", p=128), ys[:], ysn, r=[ysn],
                   w=["yT_m"])
            after_y(0, m, "yT_m")
        sc.close()

    if 'ret' in parts:
        sc = Scope(kb)
        Wfm = load_cast_weights(kb, sc, w_rfm, 512, "Wfm")
        Wtm = load_cast_weights(kb, sc, w_rtm, 512, "Wtm")
        CB = sc.sb("CB", [128, 1024])
        kb.dma(CB[:], cstb, "CB", w=["CB"])
        invf = CB[:, 0:1]
        sgn = CB[:, 1:2]
        cdec = CB[:, 2:3]
        QD = CB[:, 128:256]
        KDEC = CB[:, 256:384]
        DMT = CB[:, 384:640]
        hT = [sc.sb(f"hT{i}", [128, 16, 512], BF16) for i in range(2)]
        pi_ = sc.sb("pi", [128, 512], I32)
        ang = sc.sb("ang", [128, 512])
        kq = sc.sb("kq", [128, 512])
        rr = sc.sb("rr", [128, 512])
        sinT = sc.sb("sinT", [128, 512])
        cosT = sc.sb("cosT", [128, 512])
        t1 = sc.sb("t1", [128, 512])
        t2 = sc.sb("t2", [128, 512])
        qr_bf = sc.sb("qr_bf", [128, 512], BF16)
        kr_bf = sc.sb("kr_bf", [128, 512], BF16)
        krz = [sc.sb(f"krz{h}", [128, 512], BF16) for h in range(2)]
        qdz = [sc.sb(f"qdz{h}", [128, 512], BF16) for h in range(2)]
        v_bf = sc.sb("v_bf", [128, 256], BF16)
        gs = sc.sb("gs", [128, 256])
        ST_bf = sc.sb("ST_bf", [128, 256], BF16)
        kd_bf = sc.sb("kd_bf", [128, 128], BF16)
        rstate = sc.sb("rstate", [128, 128])
        rstate_bf = sc.sb("rstate_bf", [128, 128], BF16)
        rjunk = sc.sb("rjunk", [128, 128])
        rss = sc.sb("rss", [128, 2])
        rrs = sc.sb("rrs", [128, 2])
        yo_bf = sc.sb("ryo_bf", [128, 256], BF16)
        yst = [sc.sb(f"yst{i}", [128, 2, 512], BF16) for i in range(2)]
        kb.memset('pool', rstate[:], 0.0, w=["rstate"])
        kb.memset('pool', rstate_bf[:], 0.0, w=["rstate_bf"])
        for h in range(2):
            kb.memset('pool', krz[h][:], 0.0, w=[f"krz{h}"])
            kb.memset('pool', qdz[h][:], 0.0, w=[f"qdz{h}"])
        for m in range(NM):
            h_ = hT[m % 2]
            hn = f"hT{m % 2}"
            load_hT(h_, m, hn)
            ys = yst[m % 2]
            ysn = f"yst{m % 2}"
            kb.dma(pi_[:], pos[0:1, m * 512:(m + 1) * 512].partition_broadcast(128), "pi", w=["pi"])
            kb.cp('dve', ang[:], pi_[:], r=["pi"], w=["ang"])
            kb.ts('dve', ang[:], ang[:], invf, ALU.mult, r=["ang", "CB"], w=["ang"])
            for which, dst, shift in (("s", sinT, 0.0), ("c", cosT, float(np.pi / 2))):
                if shift != 0.0:
                    kb.ts('pool', rr[:], ang[:], shift, ALU.add, r=["ang"], w=["rr"])
                    src, srn = rr, "rr"
                else:
                    src, srn = ang, "ang"
                kb.ts('dve', kq[:], src[:], float(1.0 / TWO_PI), ALU.mult, r=[srn], w=["kq"], s2=MAGIC, op1=ALU.add)
                kb.ts('pool', kq[:], kq[:], -MAGIC, ALU.add, r=["kq"], w=["kq"])
                kb.stt('dve', rr[:], kq[:], -TWO_PI, src[:], ALU.mult, ALU.add, r=["kq", srn], w=["rr"])
                kb.ts('pool', rr[:], rr[:], 3.14159, ALU.min, r=["rr"], w=["rr"], s2=-3.14159, op1=ALU.max)
                if which == "s":
                    kb.act(dst[:], rr[:], AF.Sin, r=["rr", "CB"], w=["sinT"], scale=sgn)
                else:
                    kb.act(dst[:], rr[:], AF.Sin, r=["rr"], w=["cosT"])
            for j in range(4):
                pr = f"pb{j % 2}"
                pa = pb[j % 2]
                for k in range(16):
                    kb.mm(pa[:, :], lhsT=Wfm[:, k, j * 128:(j + 1) * 128], rhs=h_[:, k, :], start=(k == 0),
                          stop=(k == 15), r=["Wfm", hn], w=[pr])
                if j % 2 == 0:
                    kb.tt('dve', t1[:], pa[:, :], cosT[:], ALU.mult, r=[pr, "cosT"], w=["t1"])
                else:
                    kb.tt('dve', t2[:], pa[:, :], sinT[:], ALU.mult, r=[pr, "sinT"], w=["t2"])
                    if j == 1:
                        kb.tt('pool', qr_bf[:], t1[:], t2[:], ALU.add, r=["t1", "t2"], w=["qr_bf"])
                    else:
                        kb.tt('pool', t1[:], t1[:], t2[:], ALU.add, r=["t1", "t2"], w=["t1"])
                        kb.ts('pool', kr_bf[:], t1[:], 0.125, ALU.mult, r=["t1"], w=["kr_bf"])
            for h in range(2):
                ph = slice(h * 64, (h + 1) * 64)
                kb.cp('pool', krz[h][ph, :], kr_bf[ph, :], r=["kr_bf"], w=[f"krz{h}"])
                for c in range(4):
                    cs = slice(c * 128, (c + 1) * 128)
                    kb.tt('dve' if c % 2 else 'pool', qdz[h][ph, cs], qr_bf[ph, cs], QD[ph, :], ALU.mult,
                          r=["qr_bf", "CB"], w=[f"qdz{h}"])
            for c in range(4):
                cs = slice(c * 128, (c + 1) * 128)
                for k in range(16):
                    kb.mm(pb[2][:, :], lhsT=h_[:, k, cs], rhs=Wtm[:, k, :], start=(k == 0), stop=(k == 15),
                          r=[hn, "Wtm"], w=["pb2"])
                kb.cp('act', v_bf[:], pb[2][:, 0:256], r=["pb2"], w=["v_bf"])
                kb.act(gs[:], pb[2][:, 256:512], AF.Silu, r=["pb2"], w=["gs"])
                for h in range(2):
                    kb.mm(pb[3][:, h * 128:(h + 1) * 128], lhsT=krz[h][:, cs], rhs=qr_bf[:, cs], start=True, stop=True,
                          r=[f"krz{h}", "qr_bf"], w=["pb3"])
                kb.tt('dve', ST_bf[:], pb[3][:, 0:256], DMT, ALU.mult, r=["pb3", "CB"], w=["ST_bf"])
                kb.mm(pb[4][:, 0:128], lhsT=kr_bf[:, cs], rhs=ident_b, start=True, stop=True, r=["kr_bf", "Cb"],
                      w=["pb4"])
                kb.tt('dve', kd_bf[:], pb[4][:, 0:128], KDEC, ALU.mult, r=["pb4", "CB"], w=["kd_bf"])
                for h in range(2):
                    hs_ = slice(h * 128, (h + 1) * 128)
                    kb.mm(pb[5][:, hs_], lhsT=ST_bf[:, hs_], rhs=v_bf[:, hs_], start=True, stop=False,
                          r=["ST_bf", "v_bf"], w=["pb5"])
                    kb.mm(pb[5][:, hs_], lhsT=qdz[h][:, cs], rhs=rstate_bf[:], start=False, stop=True,
                          r=[f"qdz{h}", "rstate_bf"], w=["pb5"])
                kb.mm(pb[6][:, 0:256], lhsT=kd_bf[:], rhs=v_bf[:], start=True, stop=True, r=["kd_bf", "v_bf"],
                      w=["pb6"])
                for h in range(2):
                    ph = slice(h * 64, (h + 1) * 64)
                    kb.stt('dve', rstate[ph, :], rstate[ph, :], cdec[ph, :], pb[6][ph, h * 128:(h + 1) * 128],
                           ALU.mult, ALU.add, r=["rstate", "CB", "pb6"], w=["rstate"])
                kb.cp('pool', rstate_bf[:], rstate[:], r=["rstate"], w=["rstate_bf"])
                for h in range(2):
                    kb.act(rjunk[:], pb[5][:, h * 128:(h + 1) * 128], AF.Square, r=["pb5"], w=["rjunk", "rss"],
                           accum=rss[:, h:h + 1])
                kb.act(rrs[:], rss[:], AF.Sqrt, r=["rss"], w=["rrs"], bias=1e-6, scale=1.0 / 128)
                kb.recip(rrs[:], rrs[:], r=["rrs"], w=["rrs"])
                for h in range(2):
                    hs_ = slice(h * 128, (h + 1) * 128)
                    kb.stt('dve', yo_bf[:, hs_], pb[5][:, hs_], rrs[:, h:h + 1], gs[:, hs_], ALU.mult, ALU.mult,
                           r=["pb5", "rrs", "gs"], w=["ryo_bf"])
                for t in range(2):
                    kb.mm(pb[7][:, 256 + t * 128:256 + (t + 1) * 128], lhsT=yo_bf[:, t * 128:(t + 1) * 128],
                          rhs=ident_b, start=True, stop=True, r=["ryo_bf", "Cb"], w=["pb7"])
                kb.cp('act', ys[:, :, cs], pb[7][:, 256:512].rearrange("p (t s) -> p t s", t=2), r=["pb7"], w=[ysn])
            kb.dma(yT_ap(512, 256, m).rearrange("(t p) s -> p t s", p=128), ys[:], ysn, r=[ysn],
                   w=["yT_r"])
            after_y(2, m, "yT_r")
        sc.close()

    if 'rwkv' in parts:
        sc = Scope(kb)
        Wr = load_cast_weights(kb, sc, w_wfm, 1216, "Wr")
        offs = [0, 128, 256, 384, 512, 640, 768, 864, 960, 1088]
        sizes = [128, 128, 128, 128, 128, 128, 96, 96, 128, 128]
        mu = sc.sb("mu", [128, 10])
        kb.dma(mu[:], rw_mu, "mu", w=["mu"])
        vec = sc.sb("vec", [128, 2, 7])
        kb.dma(vec[:], rw_vec, "vec", w=["vec"])
        omka = sc.sb("omka", [128, 2])
        kb.ts('dve', omka[:].rearrange("p (c o) -> p c o", o=1), vec[:, :, 3:4], -1.0, ALU.mult, r=["vec"], w=["omka"],
              s2=1.0, op1=ALU.add)
        lstg = sc.sb("lstg", [128, 2, 256])
        w2_bf = sc.sb("w2_bf", [96, 256], BF16)
        a2_bf = sc.sb("a2_bf", [96, 256], BF16)
        g2_bf = sc.sb("g2_bf", [128, 2, 256], BF16)
        kb.dma(lstg[0:96, 0, :], rw_w2, "lstg", w=["lstg"])
        kb.cp('dve', w2_bf[:], lstg[0:96, 0, :], r=["lstg"], w=["w2_bf"])
        kb.dma(lstg[0:96, 1, :], rw_a2, "lstg", r=["lstg"], w=["lstg"])
        kb.cp('dve', a2_bf[:], lstg[0:96, 1, :], r=["lstg"], w=["a2_bf"])
        kb.dma(lstg[:], rw_g2.rearrange("(k p) n -> p k n", p=128), "lstg", r=["lstg"], w=["lstg"])
        kb.cp('dve', g2_bf[:], lstg[:], r=["lstg"], w=["g2_bf"])
        M3 = sc.sb("M3", [128, 384])
        kb.cp('pool', M3[:, 0:128], SU_f, r=["C"], w=["M3"])
        kb.cp('pool', M3[:, 128:256], SL_f, r=["C"], w=["M3"])
        kb.cp('pool', M3[:, 256:384], SU_f, r=["C"], w=["M3"])
        M2 = sc.sb("M2", [128, 256])
        kb.cp('pool', M2[:, 0:128], UI_f, r=["C"], w=["M2"])
        kb.cp('pool', M2[:, 128:256], UI_f, r=["C"], w=["M2"])
        hT = sc.sb("hT0", [128, 16, 512], BF16)
        halo = sc.sb("halo", [128, 10])
        kb.memset('pool', halo[:], 0.0, w=["halo"])
        rawt = [sc.sb(f"rawt{i}", [128, 513]) for i in range(2)]
        dlt = [sc.sb(f"dlt{i}", [128, 512]) for i in range(2)]
        sh = [sc.sb(f"sh{q}", [128, 512]) for q in range(10)]
        tw = sc.sb("tw", [128, 512], BF16)
        ad_bf = sc.sb("ad_bf", [128, 512], BF16)
        sg = [sc.sb(f"sg{i}", [128, 512], BF16) for i in range(2)]
        names = ["a", "kkraw", "sq", "rn", "kk", "kmul", "kh", "bb", "lw", "cwm", "Winv", "Wexc", "tmpw"]
        T = {n: sc.sb("rw_" + n, [128, 512]) for n in names}
        for alias, base in (("prod", "tmpw"), ("yc", "kkraw"), ("sq2", "sq"), ("rs2", "rn"), ("yn", "kmul")):
            T[alias] = T[base]
        bdn = ["Abd", "Bbd", "Kbd", "Rbd", "Vbd"]
        BDc = [{n: sc.sb(f"{n}{ct}", [128, 8, 128], BF16) for n in bdn} for ct in range(2)]
        for ct in range(2):
            for n in bdn:
                kb.memset('pool', BDc[ct][n][:], 0.0, w=[f"{n}{ct}"])
        Tc = [{n: sc.sb(f"rw_{n}{ct}", [128, 512]) for n in ("Winc", "bv", "g_sb", "yraw")} for ct in range(2)]
        II = sc.sb("II", [128, 128], BF16)
        kb.cp('dve', II[:], ident_f, r=["C"], w=["II"])
        tm_bf_c = [sc.sb(f"tm_bf{ct}", [128, 384], BF16) for ct in range(2)]
        gr_bf_c = [sc.sb(f"gr_bf{ct}", [128, 384], BF16) for ct in range(2)]
        pr_bf_c = [sc.sb(f"pr_bf{ct}", [128, 256], BF16) for ct in range(2)]
        pw_bf_c = [[sc.sb(f"pw_bf{ct}_{i}", [128, 256], BF16) for i in range(5)] for ct in range(2)]
        G_bf_c = [[sc.sb(f"G_bf{ct}_{i}", [128, 128], BF16) for i in range(2)] for ct in range(2)]
        Xn_bf_c = [sc.sb(f"Xn_bf{ct}", [128, 128], BF16) for ct in range(2)]
        U_bf_c = [sc.sb(f"U_bf{ct}", [128, 128], BF16) for ct in range(2)]
        ST_f = [sc.sb(f"ST_f{ct}", [128, 128]) for ct in range(2)]
        STt_c = [sc.sb(f"STt{ct}", [128, 128]) for ct in range(2)]
        STb = [sc.sb(f"STb{ct}", [128, 128], BF16) for ct in range(2)]
        yst = [sc.sb(f"wyst{i}", [128, 512], BF16) for i in range(2)]
        for ct in range(2):
            kb.memset('pool', ST_f[ct][:], 0.0, w=[f"ST_f{ct}"])
            kb.memset('pool', STb[ct][:], 0.0, w=[f"STb{ct}"])
        v3 = lambda ap: ap.rearrange("p (c t) -> p c t", c=8)
        for m in range(NM):
            load_hT(hT, m, "hT0")
            for q in range(10):
                sz = sizes[q]
                pr = f"pb{q % 2}"
                pa = pb[q % 2]
                for k in range(16):
                    kb.mm(pa[0:sz, :], lhsT=Wr[:, k, offs[q]:offs[q] + sz], rhs=hT[:, k, :], start=(k == 0),
                          stop=(k == 15), r=["Wr", "hT0"], w=[pr])
                rw_ = rawt[q % 2]
                rn_ = f"rawt{q % 2}"
                kb.cp('pool', rw_[0:sz, 0:1], halo[0:sz, q:q + 1], r=["halo"], w=[rn_])
                kb.cp('act', rw_[0:sz, 1:513], pa[0:sz, :], r=[pr], w=[rn_])
                kb.cp('pool', halo[0:sz, q:q + 1], rw_[0:sz, 512:513], r=[rn_], w=["halo"])
                d_ = dlt[q % 2]
                dn_ = f"dlt{q % 2}"
                kb.tt('pool', d_[0:sz, :], rw_[0:sz, 0:512], rw_[0:sz, 1:513], ALU.subtract, r=[rn_], w=[dn_])
                kb.stt('dve', sh[q][0:sz, :], d_[0:sz, :], mu[0:sz, q:q + 1], rw_[0:sz, 1:513], ALU.mult, ALU.add,
                       r=[dn_, "mu", rn_], w=[f"sh{q}"])
            kb.act(tw[0:96, :], sh[6][0:96, :], AF.Tanh, r=["sh6"], w=["tw"])
            kb.cp('pool', ad_bf[0:96, :], sh[7][0:96, :], r=["sh7"], w=["ad_bf"])
            for i in range(2):
                kb.act(sg[i][:], sh[8 + i][:], AF.Sigmoid, r=[f"sh{8 + i}"], w=[f"sg{i}"])
            for ct in range(2):
                cts = slice(ct * 128, (ct + 1) * 128)
                r_, k_, v_ = sh[ct], sh[2 + ct], sh[4 + ct]
                rn_r, rn_k, rn_v = f"sh{ct}", f"sh{2 + ct}", f"sh{4 + ct}"
                kb.mm(pb[2][:, :], lhsT=w2_bf[:, cts], rhs=tw[0:96, :], start=True, stop=True, r=["w2_bf", "tw"],
                      w=["pb2"])
                kb.act(T["lw"][:], pb[2][:, :], AF.Sigmoid, r=["pb2", "vec"], w=["rw_lw"], bias=vec[:, ct, 0:1])
                kb.ts('dve', T["lw"][:], T["lw"][:], -0.6065306597126334, ALU.mult, r=["rw_lw"], w=["rw_lw"])
                kb.mm(pb[3][:, :], lhsT=a2_bf[:, cts], rhs=ad_bf[0:96, :], start=True, stop=True, r=["a2_bf", "ad_bf"],
                      w=["pb3"])
                kb.act(T["a"][:], pb[3][:, :], AF.Sigmoid, r=["pb3", "vec"], w=["rw_a"], bias=vec[:, ct, 1:2])
                for i in range(2):
                    kb.mm(pb[2][:, :], lhsT=g2_bf[:, i, cts], rhs=sg[i][:], start=(i == 0), stop=(i == 1),
                          r=["g2_bf", f"sg{i}"], w=["pb2"])
                kb.cp('act', Tc[ct]["g_sb"][:], pb[2][:, :], r=["pb2"], w=[f"rw_g_sb{ct}"])
                kb.ts('pool', T["kkraw"][:], k_[:], vec[:, ct, 2:3], ALU.mult, r=[rn_k, "vec"], w=["rw_kkraw"])
                kb.tt('pool', T["sq"][:], T["kkraw"][:], T["kkraw"][:], ALU.mult, r=["rw_kkraw"], w=["rw_sq"])
                kb.mm(pb[3][:, :], lhsT=blk_f, rhs=T["sq"][:], start=True, stop=True, r=["C", "rw_sq"], w=["pb3"])
                kb.act(T["rn"][:], pb[3][:, :], AF.Sqrt, r=["pb3"], w=["rw_rn"])
                kb.ts('dve', T["rn"][:], T["rn"][:], 1e-12, ALU.max, r=["rw_rn"], w=["rw_rn"])
                kb.recip(T["rn"][:], T["rn"][:], r=["rw_rn"], w=["rw_rn"])
                kb.tt('pool', T["kk"][:], T["kkraw"][:], T["rn"][:], ALU.mult, r=["rw_kkraw", "rw_rn"], w=["rw_kk"])
                kb.ts('dve', T["kmul"][:], T["a"][:], vec[:, ct, 3:4], ALU.mult, r=["rw_a", "vec", "omka"],
                      w=["rw_kmul"], s2=omka[:, ct:ct + 1], op1=ALU.add)
                kb.tt('pool', T["kh"][:], k_[:], T["kmul"][:], ALU.mult, r=[rn_k, "rw_kmul"], w=["rw_kh"])
                kb.tt('pool', T["bb"][:], T["kk"][:], T["a"][:], ALU.mult, r=["rw_kk", "rw_a"], w=["rw_bb"])
                kb.scan(T["cwm"][:], rmask, T["lw"][:], r=["C", "rw_lw"], w=["rw_cwm"])
                kb.act(Tc[ct]["Winc"][:], T["cwm"][:], AF.Exp, r=["rw_cwm"], w=[f"rw_Winc{ct}"])
                kb.act(T["Winv"][:], T["cwm"][:], AF.Exp, r=["rw_cwm"], w=["rw_Winv"], scale=-1.0)
                kb.tt('pool', T["tmpw"][:], T["cwm"][:], T["lw"][:], ALU.subtract, r=["rw_cwm", "rw_lw"], w=["rw_tmpw"])
                kb.act(T["Wexc"][:], T["tmpw"][:], AF.Exp, r=["rw_tmpw"], w=["rw_Wexc"])
                pairs = (("Abd", T["kk"], "rw_kk", T["Wexc"], "rw_Wexc"), ("Bbd", T["bb"], "rw_bb", T["Winv"], "rw_Winv"),
                         ("Kbd", T["kh"], "rw_kh", T["Winv"], "rw_Winv"), ("Rbd", r_, rn_r, Tc[ct]["Winc"], f"rw_Winc{ct}"))
                ei = 0
                for (bn, a0, an0, a1, an1) in pairs:
                    for hh in range(2):
                        ph = slice(hh * 64, (hh + 1) * 64)
                        kb.tt('dve' if ei % 2 == 0 else 'pool', BDc[ct][bn][ph, :, hh * 64:(hh + 1) * 64], v3(a0[ph, :]),
                              v3(a1[ph, :]), ALU.mult, r=[an0, an1, bn + str(ct)], w=[bn + str(ct)])
                        ei += 1
                for hh in range(2):
                    ph = slice(hh * 64, (hh + 1) * 64)
                    kb.cp('pool', BDc[ct]["Vbd"][ph, :, hh * 64:(hh + 1) * 64], v3(v_[ph, :]), r=[rn_v, f"Vbd{ct}"], w=[f"Vbd{ct}"])
                kb.stt('dve', T["prod"][:], r_[:], vec[:, ct, 4:5], T["kh"][:], ALU.mult, ALU.mult,
                       r=[rn_r, "vec", "rw_kh"], w=["rw_tmpw"])
                kb.mm(pb[3][:, :], lhsT=blk_f, rhs=T["prod"][:], start=True, stop=True, r=["C", "rw_tmpw"], w=["pb3"])
                kb.tt('dve', Tc[ct]["bv"][:], pb[3][:, :], v_[:], ALU.mult, r=["pb3", rn_v], w=[f"rw_bv{ct}"])
            caps = []
            for ct in range(2):
                kb.tr.begin_capture()
                PBK = pb[4:8] if ct == 0 else pb[0:4]
                PBN = [f"pb{4 + k_}" for k_ in range(4)] if ct == 0 else [f"pb{k_}" for k_ in range(4)]
                STf, STn = ST_f[ct], f"ST_f{ct}"
                STbf, STbn = STb[ct], f"STb{ct}"
                for c in range(8):
                    A_c, B_c, K_c, R_c, V_c = (BDc[ct][n][:, c, :] for n in bdn)
                    for i, (X_c, xn) in enumerate(((B_c, f"Bbd{ct}"), (K_c, f"Kbd{ct}"), (V_c, f"Vbd{ct}"))):
                        kb.mm(PBK[0][:, i * 128:(i + 1) * 128], lhsT=X_c, rhs=ident_b, start=True, stop=True,
                              r=[xn, "Cb"], w=[PBN[0]])
                    kb.cp('act', tm_bf_c[ct][:], PBK[0][:, 0:384], r=[PBN[0]], w=[f"tm_bf{ct}"])
                    btm, ktm, vtm = tm_bf_c[ct][:, 0:128], tm_bf_c[ct][:, 128:256], tm_bf_c[ct][:, 256:384]
                    kb.mm(PBK[1][:, 0:128], lhsT=B_c, rhs=A_c, start=True, stop=True, r=[f"Bbd{ct}", f"Abd{ct}"], w=[PBN[1]])
                    kb.mm(PBK[1][:, 128:256], lhsT=A_c, rhs=B_c, start=True, stop=True, r=[f"Bbd{ct}", f"Abd{ct}"], w=[PBN[1]])
                    kb.mm(PBK[1][:, 256:384], lhsT=K_c, rhs=A_c, start=True, stop=True, r=[f"Kbd{ct}", f"Abd{ct}"], w=[PBN[1]])
                    kb.tt('dve', gr_bf_c[ct][:], PBK[1][:, 0:384], M3[:], ALU.mult, r=[PBN[1], "M3"], w=[f"gr_bf{ct}"])
                    kb.mm(PBK[2][:, 0:128], lhsT=B_c, rhs=R_c, start=True, stop=True, r=[f"Bbd{ct}", f"Rbd{ct}"], w=[PBN[2]])
                    kb.mm(PBK[2][:, 128:256], lhsT=K_c, rhs=R_c, start=True, stop=True, r=[f"Kbd{ct}", f"Rbd{ct}"], w=[PBN[2]])
                    kb.tt('dve', pr_bf_c[ct][:], PBK[2][:, 0:256], M2[:], ALU.mult, r=[PBN[2], "M2"], w=[f"pr_bf{ct}"])
                    Nn, Tt, TakT = gr_bf_c[ct][:, 0:128], gr_bf_c[ct][:, 128:256], gr_bf_c[ct][:, 256:384]
                    PrbT, PrkT = pr_bf_c[ct][:, 0:128], pr_bf_c[ct][:, 128:256]
                    kb.tt('pool', G_bf_c[ct][0][:], II[:], Nn, ALU.subtract, r=["II", f"gr_bf{ct}"], w=[f"G_bf{ct}_0"])
                    gcur = 0
                    Ncur, Tcur, ncn = Nn, Tt, f"gr_bf{ct}"
                    for lv in range(5):
                        kb.mm(PBK[2][:, 256:384], lhsT=Tcur, rhs=Ncur, start=True, stop=True, r=[ncn], w=[PBN[2]])
                        kb.mm(PBK[2][:, 384:512], lhsT=Ncur, rhs=Tcur, start=True, stop=True, r=[ncn], w=[PBN[2]])
                        kb.cp('act', pw_bf_c[ct][lv][:], PBK[2][:, 256:512], r=[PBN[2]], w=[f"pw_bf{ct}_{lv}"])
                        Ncur, Tcur, ncn = pw_bf_c[ct][lv][:, 0:128], pw_bf_c[ct][lv][:, 128:256], f"pw_bf{ct}_{lv}"
                        kb.mm(PBK[0][:, 384:512], lhsT=Tcur, rhs=G_bf_c[ct][gcur][:], start=True, stop=True,
                              r=[ncn, f"G_bf{ct}_{gcur}"], w=[PBN[0]])
                        kb.tt('dve', G_bf_c[ct][1 - gcur][:], PBK[0][:, 384:512], G_bf_c[ct][gcur][:], ALU.add,
                              r=[PBN[0], f"G_bf{ct}_{gcur}"], w=[f"G_bf{ct}_{1 - gcur}"])
                        gcur = 1 - gcur
                    Gf, Gn = G_bf_c[ct][gcur], f"G_bf{ct}_{gcur}"
                    kb.mm(PBK[3][:, 0:128], lhsT=A_c, rhs=STbf[:], start=True, stop=False, r=[f"Abd{ct}", STbn], w=[PBN[3]])
                    kb.mm(PBK[3][:, 0:128], lhsT=TakT, rhs=vtm, start=False, stop=True, r=[f"gr_bf{ct}", f"tm_bf{ct}"], w=[PBN[3]])
                    kb.act(Xn_bf_c[ct][:], PBK[3][:, 0:128], AF.Copy, r=[PBN[3]], w=[f"Xn_bf{ct}"], scale=-1.0)
                    kb.mm(PBK[3][:, 128:256], lhsT=Gf[:], rhs=Xn_bf_c[ct][:], start=True, stop=True, r=[Gn, f"Xn_bf{ct}"], w=[PBN[3]])
                    kb.cp('act', U_bf_c[ct][:], PBK[3][:, 128:256], r=[PBN[3]], w=[f"U_bf{ct}"])
                    kb.mm(PBK[3][:, 256:384], lhsT=STbf[:], rhs=R_c, start=True, stop=False, r=[STbn, f"Rbd{ct}"], w=[PBN[3]])
                    kb.mm(PBK[3][:, 256:384], lhsT=U_bf_c[ct][:], rhs=PrbT, start=False, stop=False, r=[f"U_bf{ct}", f"pr_bf{ct}"],
                          w=[PBN[3]])
                    kb.mm(PBK[3][:, 256:384], lhsT=vtm, rhs=PrkT, start=False, stop=True, r=[f"tm_bf{ct}", f"pr_bf{ct}"], w=[PBN[3]])
                    for hh in range(2):
                        ph = slice(hh * 64, (hh + 1) * 64)
                        kb.cp('act', Tc[ct]["yraw"][ph, c * 64:(c + 1) * 64], PBK[3][ph, 256 + hh * 64:256 + (hh + 1) * 64],
                              r=[PBN[3]], w=[f"rw_yraw{ct}"])
                    kb.mm(PBK[3][:, 384:512], lhsT=btm, rhs=U_bf_c[ct][:], start=True, stop=False, r=[f"tm_bf{ct}", f"U_bf{ct}"], w=[PBN[3]])
                    kb.mm(PBK[3][:, 384:512], lhsT=ktm, rhs=vtm, start=False, stop=True, r=[f"tm_bf{ct}"], w=[PBN[3]])
                    WL = Tc[ct]["Winc"][:, c * 64 + 63:c * 64 + 64]
                    kb.ts('pool', STt_c[ct][:], STf[:], WL, ALU.mult, r=[STn, f"rw_Winc{ct}"], w=[f"STt{ct}"])
                    kb.stt('dve', STf[:], PBK[3][:, 384:512], WL, STt_c[ct][:], ALU.mult, ALU.add, r=[PBN[3], f"rw_Winc{ct}", f"STt{ct}"],
                           w=[STn])
                    kb.cp('pool', STbf[:], STf[:], r=[STn], w=[STbn])
                caps.append(kb.tr.end_capture())
            kb.tr.replay_interleaved(caps)
            for ct in range(2):
                kb.mm(pb[2][:, :], lhsT=blk64_f, rhs=Tc[ct]["yraw"][:], start=True, stop=True, r=["C", f"rw_yraw{ct}"], w=["pb2"])
                kb.tt('dve', T["yc"][:], Tc[ct]["yraw"][:], pb[2][:, :], ALU.subtract, r=[f"rw_yraw{ct}", "pb2"], w=["rw_kkraw"])
                kb.tt('pool', T["sq2"][:], T["yc"][:], T["yc"][:], ALU.mult, r=["rw_kkraw"], w=["rw_sq"])
                kb.mm(pb[3][:, :], lhsT=blk64_f, rhs=T["sq2"][:], start=True, stop=True, r=["C", "rw_sq"], w=["pb3"])
                kb.act(T["rs2"][:], pb[3][:, :], AF.Sqrt, r=["pb3"], w=["rw_rn"], bias=64e-5)
                kb.recip(T["rs2"][:], T["rs2"][:], r=["rw_rn"], w=["rw_rn"])
                kb.tt('pool', T["yn"][:], T["yc"][:], T["rs2"][:], ALU.mult, r=["rw_kkraw", "rw_rn"], w=["rw_kmul"])
                kb.ts('dve', T["yn"][:], T["yn"][:], vec[:, ct, 5:6], ALU.mult, r=["rw_kmul", "vec"], w=["rw_kmul"],
                      s2=vec[:, ct, 6:7], op1=ALU.add)
                kb.tt('pool', T["yn"][:], T["yn"][:], Tc[ct]["bv"][:], ALU.add, r=["rw_kmul", f"rw_bv{ct}"], w=["rw_kmul"])
                ys, ysn = yst[ct], f"wyst{ct}"
                kb.tt('dve', ys[:], T["yn"][:], Tc[ct]["g_sb"][:], ALU.mult, r=["rw_kmul", f"rw_g_sb{ct}"], w=[ysn])
                kb.dma(yT_ap(256 + ct * 128, 128, m), ys[:], ysn, r=[ysn], w=["yT_w"])
                if ct == 1:
                    after_y(1, m, "yT_w")
        sc.close()

    outs = ["yT_m", "yT_r", "yT_w"]
    if env is not None:
        return None
    return kb, outs


def finish_p1(kb, outs):
    return kb.finish(outs)


import numpy as np


FF = 5504
NJ = 43


P2_PARAM_SHAPES = {
    "w_ada": ([D, 6 * D], F32), "b_ada": ([1, 6 * D], F32), "ng_col": ([128, 64], F32), "norm_g": ([4, D], F32),
    "w_gate": ([3, D, D], F32), "w_branch": ([3, 1024, D], F32), "w_out": ([D, D], F32), "w_up": ([D, 2 * FF], F32),
    "f_cv": ([128, 86, 4], F32), "w_down": ([FF, D], F32),
}


def p2_scratch(kb, sfx=""):
    return {
        'Wg_d': kb.dscratch("Wg_d" + sfx, [3, 16, 128, 16, 128], BF16),
        'Wb_d': kb.dscratch("Wb_d" + sfx, [3, 16, 128, 8, 128], BF16),
        'Wo_d': kb.dscratch("Wo_d" + sfx, [16, 128, D], BF16),
        'Wu_d': kb.dscratch("Wu_d" + sfx, [NJ, 128, 16, 256], BF16),
        'Wd_d': kb.dscratch("Wd_d" + sfx, [NJ, 128, D], BF16),
    }


def build_p2(Tq, env=None):
    NT = 128 + Tq
    if env is None:
        kb = KB()
        e = {}
        xh = kb.din("xh", [NT, D])
        yTh = kb.din("yTh", [3, 1024, NT], BF16)
        e['hmask'] = kb.din("hmask", [128, 1])
        e['c_col'] = kb.din("c_col", [128, 16])
        for k, (shp, dt_) in P2_PARAM_SHAPES.items():
            e[k] = kb.din(k, shp, dt_)
        cst = kb.din("cst", [128, 2048])
        xo = kb.dout("xo", [Tq, D])
        e.update(p2_scratch(kb))
        G = setup_globals(kb, cst)

        def load_x(dst, rn, row0, is_halo):
            kb.dma(dst, xh[row0:row0 + 128, :], rn, w=[rn])

        def load_y(ybf, t0, ntok, is_halo):
            kb.dma(ybf[:, :, :, 0:ntok], yTh[:, :, t0:t0 + ntok].rearrange("i (k p) t -> p i k t", p=128), "ybf",
                   w=["ybf"])

        def store_x(src, rn, o0):
            kb.dma(xo[o0:o0 + 128, :], src, rn, r=[rn], w=["xo"])
        e['load_x'], e['load_y'], e['store_x'] = load_x, load_y, store_x
    else:
        kb = env['kb']
        e = env
        G = env['G']
    nc = kb.nc
    hmask, c_col, w_ada, b_ada, ng_col, norm_g = (e[k] for k in ('hmask', 'c_col', 'w_ada', 'b_ada', 'ng_col', 'norm_g'))
    w_gate, w_branch, w_out, w_up, f_cv, w_down = (e[k] for k in ('w_gate', 'w_branch', 'w_out', 'w_up', 'f_cv', 'w_down'))
    Wg_d, Wb_d, Wo_d, Wu_d, Wd_d = (e[k] for k in ('Wg_d', 'Wb_d', 'Wo_d', 'Wu_d', 'Wd_d'))
    load_x, load_y, store_x = e['load_x'], e['load_y'], e['store_x']
    pb = G['pb']
    C, Cb = G['C'], G['Cb']
    ident_f = C[:, 0:128]
    ones_row = C[0:1, 256:384]
    ident_b = Cb[:, 0:128]

    sc = Scope(kb)
    NSTG = 6
    stg = [sc.sb(f"stg{i}", [128, 2048]) for i in range(NSTG)]
    stb = [sc.sb(f"stb{i}", [128, 2048], BF16) for i in range(NSTG)]
    cnt = [0]

    def cast_block(src_ap, dst_ap, shape):
        i = cnt[0] % NSTG
        cnt[0] += 1
        n = int(np.prod(shape))
        sv = stg[i][:, 0:n]
        bv = stb[i][:, 0:n]
        if len(shape) == 2:
            sv = sv.rearrange("p (a b) -> p a b", a=shape[0])
            bv = bv.rearrange("p (a b) -> p a b", a=shape[0])
        kb.dma(sv, src_ap, f"stg{i}", w=[f"stg{i}"])
        eng = ('dve', 'act', 'dve', 'act', 'pool', 'dve')[i]
        kb.cp(eng, stb[i][:, 0:n], stg[i][:, 0:n], r=[f"stg{i}"], w=[f"stb{i}"])
        kb.dma(dst_ap, bv, f"stb{i}", r=[f"stb{i}"], w=[f"Wscr{cnt[0]}"])

    for i in range(3):
        wv = w_gate[i].rearrange("(k p) n -> p k n", p=128)
        for nt in range(16):
            cast_block(wv[:, :, nt * 128:(nt + 1) * 128], Wg_d[i, nt], (16, 128))
        wv = w_branch[i].rearrange("(k p) n -> p k n", p=128)
        for nt in range(16):
            cast_block(wv[:, :, nt * 128:(nt + 1) * 128], Wb_d[i, nt], (8, 128))
    for kn in range(16):
        cast_block(w_out[kn * 128:(kn + 1) * 128, :], Wo_d[kn], (2048,))
    wv = w_up.rearrange("(k p) n -> p k n", p=128)
    for j in range(NJ):
        cast_block(wv[:, :, j * 128:(j + 1) * 128], Wu_d[j, :, :, 0:128], (16, 128))
        cast_block(wv[:, :, FF + j * 128:FF + (j + 1) * 128], Wu_d[j, :, :, 128:256], (16, 128))
        cast_block(w_down[j * 128:(j + 1) * 128, :], Wd_d[j], (2048,))
    sc.close()

    scm = Scope(kb)
    GTm = scm.sb("GTm", [128, 2048])
    GTf = scm.sb("GTf", [128, 2048])
    cols = scm.sb("cols", [128, 64])
    ngc = scm.sb("ngc", [128, 64])
    kb.dma(ngc[:], ng_col, "ngc", w=["ngc"])
    sc = Scope(kb)
    rows = emit_mod(kb, sc, c_col, w_ada, b_ada, [0, 1, 2, 3, 4, 5], ones_row, pb, "m2")
    kb.dma(GTm[:], norm_g[1:2, :].partition_broadcast(128), "GTm", w=["GTm"])
    kb.dma(GTf[:], norm_g[3:4, :].partition_broadcast(128), "GTf", w=["GTf"])

    def mk_gt(dst, dn):
        def f(c, pa, pr):
            kb.tt('dve', dst[:, c * 512:(c + 1) * 512], pa, dst[:, c * 512:(c + 1) * 512], ALU.mult, r=[pr, dn], w=[dn])
        return f

    bcast_row(kb, rows[2][0], rows[2][1], ones_row, pb, mk_gt(GTm, "GTm"))
    bcast_row(kb, rows[5][0], rows[5][1], ones_row, pb, mk_gt(GTf, "GTf"))
    for slot, vid in enumerate((1, 0, 4, 3)):
        row, rn = rows[vid]
        for k in range(16):
            kb.mm(pb[3][:, slot * 16 + k:slot * 16 + k + 1], lhsT=row[0:1, k * 128:(k + 1) * 128],
                  rhs=ones_row[0:1, 0:1], start=True, stop=True, r=[rn, "C"], w=["pb3"])
    kb.cp('act', cols[:], pb[3][:, 0:64], r=["pb3"], w=["cols"])
    for slot, gsl in ((0, 0), (2, 2)):
        kb.stt('dve', cols[:, slot * 16:(slot + 1) * 16], cols[:, slot * 16:(slot + 1) * 16], 1.0,
               ngc[:, gsl * 16:(gsl + 1) * 16], ALU.add, ALU.mult, r=["cols", "ngc"], w=["cols"])
    sc.close()

    xres = [scm.sb(f"xres{i}", [128, 2048]) for i in range(2)]
    htmp = scm.sb("htmp", [128, 2048])
    hb = scm.sb("hb", [128, 2048], BF16)
    ss = scm.sb("ss", [128, 1])
    rstd = scm.sb("rstd", [128, 1])
    hT = scm.sb("hT", [128, 16, 256], BF16)
    ybf = scm.sb("ybf", [128, 3, 8, 256], BF16)
    mT = scm.sb("mT", [128, 16, 256], BF16)
    sig = [scm.sb(f"sig{i}", [128, 256]) for i in range(2)]
    macc = scm.sb("macc", [128, 256])
    mtmp = scm.sb("mtmp", [128, 256])
    Wg_s = [scm.sb(f"Wg_s{i}", [128, 16, 128], BF16) for i in range(3)]
    Wb_s = [scm.sb(f"Wb_s{i}", [128, 8, 128], BF16) for i in range(3)]
    Wo_s = [scm.sb(f"Wo_s{i}", [128, 2048], BF16) for i in range(2)]
    Wu_s = [scm.sb(f"Wu_s{i}", [128, 16, 256], BF16) for i in range(2)]
    Wd_s = [scm.sb(f"Wd_s{i}", [128, 2048], BF16) for i in range(2)]
    ymix = scm.sb("ymix", [128, 2048])
    rawf = [scm.sb(f"rawf{i}", [128, 258]) for i in range(2)]
    facc = [scm.sb(f"facc{i}", [128, 256]) for i in range(2)]
    gg = scm.sb("gg", [128, 256])
    aT = scm.sb("aT", [128, NJ, 256], BF16)
    fhalo = scm.sb("fhalo", [128, 86, 2])
    fc = scm.sb("fc", [128, 86, 4])
    hm = scm.sb("hm", [128, 1])
    kb.dma(fc[:], f_cv, "fc", w=["fc"])
    kb.dma(hm[:], hmask, "hm", w=["hm"])
    kb.memset('pool', fhalo[:], 0.0, w=["fhalo"])
    if 'extra_alloc' in e:
        e['extra_alloc'](scm, dict(ymix=ymix))
    slab_ctr = {"g": 0, "b": 0, "o": 0, "u": 0, "d": 0}

    def emit_h(nsub, ntok, gslot):
        for sub in range(nsub):
            xn = f"xres{sub}"
            kb.act(htmp[:], xres[sub][:], AF.Square, r=[xn], w=["htmp", "ss"], accum=ss[:])
            kb.act(rstd[:], ss[:], AF.Sqrt, r=["ss"], w=["rstd"], bias=1e-6, scale=1.0 / D)
            kb.recip(rstd[:], rstd[:], r=["rstd"], w=["rstd"])
            kb.ts('dve', hb[:], xres[sub][:], rstd[:, 0:1], ALU.mult, r=[xn, "rstd"], w=["hb"])
            for q in range(4):
                bank = 6 + (q % 2)
                for kk in range(4):
                    k = q * 4 + kk
                    kb.mm(pb[bank][:, kk * 128:(kk + 1) * 128], lhsT=hb[:, k * 128:(k + 1) * 128], rhs=ident_b,
                          start=True, stop=True, r=["hb", "Cb"], w=[f"pb{bank}"])
                for kk in range(4):
                    k = q * 4 + kk
                    gcol = cols[:, gslot * 16 + k:gslot * 16 + k + 1]
                    scol = cols[:, (gslot + 1) * 16 + k:(gslot + 1) * 16 + k + 1]
                    if kk % 2 == 0:
                        kb.act(hT[:, k, sub * 128:(sub + 1) * 128], pb[bank][:, kk * 128:(kk + 1) * 128], AF.Identity,
                               r=[f"pb{bank}", "cols"], w=["hT"], bias=scol, scale=gcol)
                    else:
                        kb.ts('dve', hT[:, k, sub * 128:(sub + 1) * 128], pb[bank][:, kk * 128:(kk + 1) * 128], gcol,
                              ALU.mult, r=[f"pb{bank}", "cols"], w=["hT"], s2=scol, op1=ALU.add)

    def post_norm(nsub_i, src_ps_banks, GT, gtn, sub, store_ap):
        xn = f"xres{sub}"
        kb.act(htmp[:], ymix[:], AF.Square, r=["ymix"], w=["htmp", "ss"], accum=ss[:])
        kb.act(rstd[:], ss[:], AF.Sqrt, r=["ss"], w=["rstd"], bias=1e-6, scale=1.0 / D)
        kb.recip(rstd[:], rstd[:], r=["rstd"], w=["rstd"])
        kb.stt('dve', htmp[:], ymix[:], rstd[:, 0:1], GT[:], ALU.mult, ALU.mult, r=["ymix", "rstd", gtn], w=["htmp"])
        kb.tt('pool', xres[sub][:], xres[sub][:], htmp[:], ALU.add, r=[xn, "htmp"], w=[xn])
        if store_ap is not None:
            store_x(xres[sub][:], xn, store_ap)

    def tile(t0, ntok, is_halo):
        nsub = ntok // 128
        tsl = slice(0, ntok)
        for sub in range(nsub):
            xn = f"xres{sub}"
            load_x(xres[sub][:], xn, t0 + sub * 128, is_halo)
        load_y(ybf, t0, ntok, is_halo)
        emit_h(nsub, ntok, 0)
        for nt in range(16):
            for i in range(3):
                gi = slab_ctr["g"] % 3
                slab_ctr["g"] += 1
                kb.dma(Wg_s[gi][:], Wg_d[i, nt], f"Wg_s{gi}", w=[f"Wg_s{gi}"])
                kb.dma(Wb_s[gi][:], Wb_d[i, nt], f"Wb_s{gi}", w=[f"Wb_s{gi}"])
                for k in range(16):
                    kb.mm(pb[4][:, tsl], lhsT=Wg_s[gi][:, k, :], rhs=hT[:, k, tsl], start=(k == 0), stop=(k == 15),
                          r=[f"Wg_s{gi}", "hT"], w=["pb4"])
                for k in range(8):
                    kb.mm(pb[5][:, tsl], lhsT=Wb_s[gi][:, k, :], rhs=ybf[:, i, k, tsl], start=(k == 0), stop=(k == 7),
                          r=[f"Wb_s{gi}", "ybf"], w=["pb5"])
                sg_ = sig[i % 2]
                sgn_ = f"sig{i % 2}"
                kb.act(sg_[:, tsl], pb[4][:, tsl], AF.Sigmoid, r=["pb4"], w=[sgn_])
                if i == 0:
                    kb.tt('dve', macc[:, tsl], pb[5][:, tsl], sg_[:, tsl], ALU.mult, r=["pb5", sgn_], w=["macc"])
                else:
                    kb.tt('dve', mtmp[:, tsl], pb[5][:, tsl], sg_[:, tsl], ALU.mult, r=["pb5", sgn_], w=["mtmp"])
                    if i == 1:
                        kb.tt('pool', macc[:, tsl], macc[:, tsl], mtmp[:, tsl], ALU.add, r=["macc", "mtmp"], w=["macc"])
                    else:
                        kb.tt('pool', mT[:, nt, tsl], macc[:, tsl], mtmp[:, tsl], ALU.add, r=["macc", "mtmp"],
                              w=["mT"])
        for sub in range(nsub):
            for kn in range(16):
                oi = slab_ctr["o"] % 2
                slab_ctr["o"] += 1
                kb.dma(Wo_s[oi][:], Wo_d[kn], f"Wo_s{oi}", w=[f"Wo_s{oi}"])
                for c in range(4):
                    kb.mm(pb[c][:, :], lhsT=mT[:, kn, sub * 128:(sub + 1) * 128], rhs=Wo_s[oi][:, c * 512:(c + 1) * 512],
                          start=(kn == 0), stop=(kn == 15), r=["mT", f"Wo_s{oi}"], w=[f"pb{c}"])
            for c in range(4):
                kb.cp('act', ymix[:, c * 512:(c + 1) * 512], pb[c][:, :], r=[f"pb{c}"], w=["ymix"])
            post_norm(nsub, None, GTm, "GTm", sub, None)
        emit_h(nsub, ntok, 2)
        for j in range(NJ):
            ui = slab_ctr["u"] % 2
            slab_ctr["u"] += 1
            kb.dma(Wu_s[ui][:], Wu_d[j], f"Wu_s{ui}", w=[f"Wu_s{ui}"])
            for half in range(2):
                bank = 4 + half
                jj = half * NJ + j
                for k in range(16):
                    kb.mm(pb[bank][:, tsl], lhsT=Wu_s[ui][:, k, half * 128:(half + 1) * 128], rhs=hT[:, k, tsl],
                          start=(k == 0), stop=(k == 15), r=[f"Wu_s{ui}", "hT"], w=[f"pb{bank}"])
                rw_ = rawf[half]
                rn_ = f"rawf{half}"
                kb.cp('pool', rw_[:, 0:2], fhalo[:, jj, :], r=["fhalo"], w=[rn_])
                kb.cp('act', rw_[:, 2:2 + ntok], pb[bank][:, tsl], r=[f"pb{bank}"], w=[rn_])
                if is_halo:
                    kb.ts('pool', fhalo[:, jj, :], rw_[:, ntok:ntok + 2], hm[:, 0:1], ALU.mult, r=[rn_, "hm"],
                          w=["fhalo"])
                else:
                    kb.cp('pool', fhalo[:, jj, :], rw_[:, ntok:ntok + 2], r=[rn_], w=["fhalo"])
                fa = facc[half]
                fan = f"facc{half}"
                kb.ts('dve', fa[:, tsl], rw_[:, 2:2 + ntok], fc[:, jj, 2:3], ALU.mult, r=[rn_, "fc"], w=[fan],
                      s2=fc[:, jj, 3:4], op1=ALU.add)
                kb.stt('dve', fa[:, tsl], rw_[:, 1:1 + ntok], fc[:, jj, 1:2], fa[:, tsl], ALU.mult, ALU.add,
                       r=[rn_, "fc", fan], w=[fan])
                kb.stt('dve', fa[:, tsl], rw_[:, 0:ntok], fc[:, jj, 0:1], fa[:, tsl], ALU.mult, ALU.add,
                       r=[rn_, "fc", fan], w=[fan])
            if not is_halo:
                kb.act(gg[:, tsl], facc[0][:, tsl], AF.Gelu_apprx_tanh, r=["facc0"], w=["gg"])
                kb.tt('pool', aT[:, j, tsl], gg[:, tsl], facc[1][:, tsl], ALU.mult, r=["gg", "facc1"], w=["aT"])
        if is_halo:
            return
        for sub in range(nsub):
            for j in range(NJ):
                di = slab_ctr["d"] % 2
                slab_ctr["d"] += 1
                kb.dma(Wd_s[di][:], Wd_d[j], f"Wd_s{di}", w=[f"Wd_s{di}"])
                for c in range(4):
                    kb.mm(pb[c][:, :], lhsT=aT[:, j, sub * 128:(sub + 1) * 128], rhs=Wd_s[di][:, c * 512:(c + 1) * 512],
                          start=(j == 0), stop=(j == NJ - 1), r=["aT", f"Wd_s{di}"], w=[f"pb{c}"])
            for c in range(4):
                kb.cp('act', ymix[:, c * 512:(c + 1) * 512], pb[c][:, :], r=[f"pb{c}"], w=["ymix"])
            o0 = t0 - 128 + sub * 128
            post_norm(nsub, None, GTf, "GTf", sub, o0)

    tile(0, 128, True)
    t = 128
    while t < NT:
        tile(t, 256, False)
        t += 256
    scm.close()
    if env is not None:
        return None
    return kb, ["xo"]


import numpy as np


RG = [[0, 1, 2, 3], [4, 5, 6, 7]]


def build_fused(S, depth=2):
    Tq = S // 4
    NM = S // 512
    NMq = NM // 4
    NT = 128 + Tq
    kb = KB()
    cst = kb.din("cst", [128, 2048])
    cstb = kb.din("cstb", [128, 1024])
    c_col = kb.din("c_col", [128, 16])
    pos = kb.din("pos", [1, S], I32)
    xh = kb.din("xh", [NT, D])
    hmask = kb.din("hmask", [128, 1])
    selv = kb.din("selv", [128, 8])
    xo = kb.dout("xo", [Tq, D])
    L = []
    for l in range(depth):
        e = {}
        for k, (shp, dt_) in P1_PARAM_SHAPES.items():
            e[k] = kb.din(f"{k}_{l}", shp, dt_)
        for k, (shp, dt_) in P2_PARAM_SHAPES.items():
            e[k] = kb.din(f"{k}_{l}", shp, dt_)
        L.append(e)
    G = setup_globals(kb, cst)
    CT = min(2048, Tq)
    NCH = S // CT
    MPC = CT // 512
    hT_own = kb.dscratch("hT_own", [NMq, 128, 16, 512], BF16)
    hTg = kb.dscratch("hTg", [2 * NMq, 4 * 64, 8192], BF16)
    ysc = [kb.dscratch(f"ysc{i}", [NCH, 256, CT], BF16) for i in range(3)]
    yall = [kb.dscratch(f"yall{i}", [NCH, 1024, CT], BF16) for i in range(3)]
    xs1 = kb.dscratch("xs1", [Tq, D])
    xlast = kb.dscratch("xlast", [128, D])
    xl_all = kb.dscratch("xl_all", [4 * 128, D])
    w2s = p2_scratch(kb)
    sel = kb.sb("sel", [128, 8])
    kb.dma(sel[:], selv, "sel", w=["sel"])

    def allgather(src2d, dst2d, key, reads, writes):
        kb.tr.dma('pool', lambda e_: e_.collective_compute("AllGather", ALU.bypass, replica_groups=RG,
                                                            ins=[src2d.opt()], outs=[dst2d.opt()]),
                  key, reads=reads, writes=writes, inc=1)

    for l in range(depth):
        P = L[l]
        x_own = xh[128:NT, :] if l == 0 else xs1
        def after_h(m):
            for half in range(2):
                allgather(hT_own[m, half * 64:(half + 1) * 64].rearrange("p k t -> p (k t)"), hTg[2 * m + half], "ag",
                          reads=[f"hTd{m}"], writes=[f"hTg{2 * m + half}"])

        def load_hT(dst, m, key):
            r_, ml = divmod(m, NMq)
            for half in range(2):
                kb.dma(dst[half * 64:(half + 1) * 64, :, :],
                       hTg[2 * ml + half, r_ * 64:(r_ + 1) * 64, :].rearrange("p (k t) -> p k t", k=16),
                       f"{key}_{half}", r=[f"hTg{2 * ml + half}"], w=[key])

        def yT_ap(row0, nrows, m):
            i, rr = divmod(row0, 256)
            c, mm = divmod(m, MPC)
            return ysc[i][c, rr:rr + nrows, mm * 512:(mm + 1) * 512]

        def after_y(i, m, res):
            if (m + 1) % MPC == 0:
                c = m // MPC
                allgather(ysc[i][c], yall[i][c], "ag", reads=[res], writes=[f"yall{i}_{c}"])

        env1 = dict(P)
        env1.update(kb=kb, G=G, x=x_own, NH=Tq // 128, hTo=hT_own, hTd=None, yT=None, c_col=c_col, pos=pos, cstb=cstb,
                    after_h=after_h, load_hT=load_hT, yT_ap=yT_ap, after_y=after_y)
        build_p1(S, parts=('h',), env=env1)
        build_p1(S, parts=('mamba', 'rwkv', 'ret'), env=env1)
        if l > 0:
            allgather(xlast, xl_all, "ag", reads=["xlast"], writes=["xlall"])

        X = {}

        def extra_alloc(scm, bufs, X=X):
            X['cand'] = scm.sb("ycand", [128, 3, 8, 256], BF16)
            X['ymix'] = bufs['ymix']

        def load_x(dst, rn, row0, is_halo, l=l, X=X):
            if l == 0:
                kb.dma(dst, xh[row0:row0 + 128, :], rn, w=[rn])
            elif not is_halo:
                kb.dma(dst, xs1[row0 - 128:row0, :], rn, r=["xs1"], w=[rn])
            else:
                for q in range(4):
                    kb.dma(X['ymix'][:], xl_all[q * 128:(q + 1) * 128, :], "ymix", r=["xlall"], w=["ymix"])
                    if q == 0:
                        kb.ts('dve', dst, X['ymix'][:], sel[:, 4:5], ALU.mult, r=["ymix", "sel"], w=[rn])
                    else:
                        kb.stt('dve', dst, X['ymix'][:], sel[:, 4 + q:5 + q], dst, ALU.mult, ALU.add,
                               r=["ymix", "sel", rn], w=[rn])

        def load_y(ybf, t0, ntok, is_halo, X=X):
            cand = X['cand']
            for q in range(4):
                gt0 = q * Tq + t0 - 128
                if gt0 < 0:
                    kb.memset('pool', cand[:, :, :, 0:ntok], 0.0, w=[f"ycand{g}" for g in range(3)])
                else:
                    c, off = divmod(gt0, CT)
                    for g in range(3):
                        kb.dma(cand[:, g, :, 0:ntok],
                               yall[g][c, :, off:off + ntok].rearrange("(k p) t -> p k t", p=128),
                               f"ycand{g}", r=[f"yall{g}_{c}"], w=[f"ycand{g}"])
                if q == 0:
                    kb.ts('dve', ybf[:, :, :, 0:ntok], cand[:, :, :, 0:ntok], sel[:, 0:1], ALU.mult,
                          r=["ycand0", "ycand1", "ycand2", "sel"], w=["ybf"])
                else:
                    kb.stt('dve', ybf[:, :, :, 0:ntok], cand[:, :, :, 0:ntok], sel[:, q:q + 1], ybf[:, :, :, 0:ntok],
                           ALU.mult, ALU.add, r=["ycand0", "ycand1", "ycand2", "sel", "ybf"], w=["ybf"])

        def store_x(src, rn, o0, l=l):
            if l < depth - 1:
                kb.dma(xs1[o0:o0 + 128, :], src, rn, r=[rn], w=["xs1"])
                if o0 == Tq - 128:
                    kb.dma(xlast, src, rn, r=[rn], w=["xlast"])
            else:
                kb.dma(xo[o0:o0 + 128, :], src, rn, r=[rn], w=["xo"])

        env2 = dict(P)
        env2.update(w2s)
        env2.update(kb=kb, G=G, hmask=hmask, c_col=c_col, load_x=load_x, load_y=load_y, store_x=store_x,
                    extra_alloc=extra_alloc)
        build_p2(Tq, env=env2)
    nc = kb.finish(["xo"])
    return kb, nc


def fused_core_inputs(inp, b, r, S, depth):
    Tq = S // 4
    o = {}
    for l in range(depth):
        p1 = p1_core_inputs(inp, l, b, r)
        for k in P1_PARAM_SHAPES:
            o[f"{k}_{l}"] = p1[k]
        p2 = p2_core_inputs(inp, l, b, r, Tq, None, None)
        for k in P2_PARAM_SHAPES:
            o[f"{k}_{l}"] = p2[k]
    o['cst'] = p1_consts()
    o['cstb'] = p1_core_consts(r)
    o['c_col'] = np.ascontiguousarray(inp['c'][b].reshape(16, 128).T)
    o['pos'] = np.ascontiguousarray(inp['positions'][b][None, :]).astype(np.int32)
    x_b = inp['x'][b]
    xh = np.zeros((128 + Tq, 2048), np.float32)
    if r == 0:
        xh[128:] = x_b[0:Tq]
    else:
        xh[:] = x_b[r * Tq - 128:(r + 1) * Tq]
    o['xh'] = xh
    o['hmask'] = np.full((128, 1), 0.0 if r == 0 else 1.0, np.float32)
    sv = np.zeros((128, 8), np.float32)
    sv[:, r] = 1.0
    if r > 0:
        sv[:, 4 + r - 1] = 1.0
    o['selv'] = sv
    return o


_CACHE = {}


def kernel(**inputs):
    inp = {k: np.asarray(v) for k, v in inputs.items()}
    inp['x'] = np.ascontiguousarray(inp['x'], dtype=np.float32)
    B, S, _ = inp['x'].shape
    depth = inp['w_in'].shape[0]
    Tq = S // 4
    key = (S, depth)
    if key not in _CACHE:
        _CACHE[key] = build_fused(S, depth)[1]
    nc = _CACHE[key]
    in_maps = []
    for core in range(8):
        b, r = divmod(core, 4)
        in_maps.append(fused_core_inputs(inp, b, r, S, depth))
    res = run_bass_kernel_spmd(nc, in_maps, core_ids=list(range(8)))
    out = np.zeros((B, S, 2048), np.float32)
    for core in range(8):
        b, r = divmod(core, 4)
        out[b, r * Tq:(r + 1) * Tq] = np.asarray(res.results[core]['xo'])
    return out
```

```python
import numpy as np
import concourse.bass as bass
import concourse.mybir as mybir
from concourse.bass_utils import run_bass_kernel_spmd

F32 = mybir.dt.float32
BF16 = mybir.dt.bfloat16
I32 = mybir.dt.int32
AF = mybir.ActivationFunctionType
ALU = mybir.AluOpType
AX = mybir.AxisListType
EPOCH = 30000
import os as _os
SAME_ENGINE_ORDERED = tuple(_os.environ.get('SEO', 'pe').split(','))
D = 2048


class Tracker:
    ENGS = ('pe', 'act', 'dve', 'pool', 'sp')

    def __init__(self, nc):
        self.nc = nc
        self.ops = {e: [] for e in self.ENGS}
        self.cur_sem = {}
        self.cnt = {}
        self.nsem = 0
        for e in self.ENGS:
            self.cur_sem[e] = self._new_sem(e)
            self.cnt[e] = 0
        self.known = {e: {} for e in self.ENGS}
        self.last_w = {}
        self.readers = {}
        self.dma_sems = {}
        self.dma_cnt = {}
        self.nops = 0
        self._old_epochs = []

    def _new_sem(self, tag):
        self.nsem += 1
        return self.nc.alloc_semaphore(f"s{self.nsem}_{tag}")

    def _waits_for(self, eng, reads, writes, is_dma):
        need = {}

        def add(ev):
            sem, val, src, src_dma = ev
            if src == eng and eng in SAME_ENGINE_ORDERED and not src_dma and not is_dma:
                return
            k = id(sem)
            if self.known[eng].get(k, 0) >= val:
                return
            if k not in need or need[k][1] < val:
                need[k] = (sem, val)

        for r in reads:
            ev = self.last_w.get(r)
            if ev is not None:
                add(ev)
        for w in writes:
            ev = self.last_w.get(w)
            if ev is not None:
                add(ev)
            rd = self.readers.get(w)
            if rd:
                for ev in rd.values():
                    add(ev)
        out = list(need.values())
        for sem, val in out:
            self.known[eng][id(sem)] = val
        return out

    def _commit(self, ev, reads, writes):
        for r in reads:
            self.readers.setdefault(r, {})[id(ev[0])] = ev
        for w in writes:
            self.last_w[w] = ev
            self.readers[w] = {}

    def begin_capture(self):
        self._cap = []

    def end_capture(self):
        c, self._cap = self._cap, None
        return c

    def replay_interleaved(self, caps):
        idx = [0] * len(caps)
        while True:
            done = True
            for j, c in enumerate(caps):
                if idx[j] < len(c):
                    kind, args = c[idx[j]]
                    idx[j] += 1
                    done = False
                    if kind == 'op':
                        self.op(*args)
                    else:
                        self.dma(*args)
            if done:
                break

    def op(self, eng, fn, reads=(), writes=()):
        if getattr(self, '_cap', None) is not None:
            self._cap.append(('op', (eng, fn, tuple(reads), tuple(writes))))
            return None
        pr = tuple(r for r in reads if isinstance(r, str) and r.startswith('pb'))
        if pr and eng != 'pe':
            writes = tuple(writes) + pr
        waits = self._waits_for(eng, reads, writes, False)
        if self.cnt[eng] >= EPOCH:
            self._old_epochs.append((self.cur_sem[eng], self.cnt[eng]))
            self.cur_sem[eng] = self._new_sem(eng)
            self.cnt[eng] = 0
        self.cnt[eng] += 1
        sem = self.cur_sem[eng]
        ev = (sem, self.cnt[eng], eng, False)
        self.ops[eng].append((waits, fn, sem, 1))
        self._commit(ev, reads, writes)
        self.nops += 1
        return ev

    def dma(self, eng, fn, key, reads=(), writes=(), inc=16):
        if getattr(self, '_cap', None) is not None:
            self._cap.append(('dma', (eng, fn, key, tuple(reads), tuple(writes), inc)))
            return None
        if key not in self.dma_sems:
            self.dma_sems[key] = self._new_sem('d' + str(key))
            self.dma_cnt[key] = 0
        sem = self.dma_sems[key]
        chan = ('__chan__', key)
        waits = self._waits_for(eng, tuple(reads), tuple(writes) + (chan,), True)
        self.dma_cnt[key] += inc
        ev = (sem, self.dma_cnt[key], eng, True)
        self.ops[eng].append((waits, fn, sem, inc))
        self._commit(ev, reads, tuple(writes) + (chan,))
        self.nops += 1
        return ev

    def barrier(self):
        latest = {}
        for e in self.ENGS:
            for (waits, fn, sem, inc) in ():
                pass
        for e in self.ENGS:
            if self.cnt[e] > 0:
                latest[id(self.cur_sem[e])] = (self.cur_sem[e], self.cnt[e])
        for k, sem in self.dma_sems.items():
            if self.dma_cnt[k] > 0:
                latest[id(sem)] = (sem, self.dma_cnt[k])
        for (sem, val) in self._old_epochs:
            latest[id(sem)] = (sem, val)
        for e in self.ENGS:
            waits = []
            for k, (sem, val) in latest.items():
                if self.known[e].get(k, 0) < val:
                    waits.append((sem, val))
                    self.known[e][k] = val
            if waits:
                self.ops[e].append((waits, None, None, 0))

    def wait_all(self, eng, resources):
        waits = self._waits_for(eng, resources, (), True)
        self.ops[eng].append((waits, None, None, 0))

    def emit(self):
        nc = self.nc
        ops = self.ops
        with nc.Block() as block:
            def run(e, lst):
                for waits, fn, sem, inc in lst:
                    for s, v in waits:
                        e.wait_ge(s, v)
                    if fn is not None:
                        fn(e).then_inc(sem, inc)

            @block.tensor
            def _(e):
                run(e, ops['pe'])

            @block.scalar
            def _(e):
                run(e, ops['act'])

            @block.vector
            def _(e):
                run(e, ops['dve'])

            @block.gpsimd
            def _(e):
                run(e, ops['pool'])

            @block.sync
            def _(e):
                run(e, ops['sp'])


class KB:
    def __init__(self, name="k"):
        self.nc = bass.Bass("TRN2", target_bir_lowering=False)
        self.nc.allow_low_precision("bf16 matmul operands with fp32 PSUM accumulation")
        self.tr = Tracker(self.nc)
        self._n = 0
        self.outs = []

    def din(self, name, shape, dt=F32):
        return self.nc.dram_tensor(name, list(shape), dt, kind="ExternalInput").ap()

    def dout(self, name, shape, dt=F32):
        self.outs.append(name)
        return self.nc.dram_tensor(name, list(shape), dt, kind="ExternalOutput").ap()

    def dscratch(self, name, shape, dt=F32):
        return self.nc.dram_tensor(name, list(shape), dt, kind="Internal").ap()

    def sb(self, name, shape, dt=F32):
        return self.nc.alloc_sbuf_tensor(name, list(shape), dt)

    def ps(self, name, shape, dt=F32):
        return self.nc.alloc_psum_tensor(name, list(shape), dt)

    def dma(self, out, in_, key, r=(), w=(), eng='sp'):
        self.tr.dma(eng, lambda e: e.dma_start(out=out, in_=in_), key, reads=r, writes=w)

    def mm(self, out, lhsT, rhs, start, stop, r, w):
        self.tr.op('pe', lambda e: e.matmul(out, lhsT=lhsT, rhs=rhs, start=start, stop=stop), reads=r, writes=w)

    def tp(self, out, in_, ident, r, w):
        self.tr.op('pe', lambda e: e.transpose(out=out, in_=in_, identity=ident), reads=r, writes=w)

    def act(self, out, in_, func, r, w, bias=None, scale=None, accum=None):
        kw = {}
        if bias is not None:
            kw['bias'] = bias
        if scale is not None:
            kw['scale'] = scale
        if accum is not None:
            kw['accum_out'] = accum
        self.tr.op('act', lambda e: e.activation(out=out, in_=in_, func=func, **kw), reads=r, writes=w)

    def tt(self, eng, out, a, b, op, r, w):
        self.tr.op(eng, lambda e: e.tensor_tensor(out=out, in0=a, in1=b, op=op), reads=r, writes=w)

    def ts(self, eng, out, a, s1, op0, r, w, s2=None, op1=None, accum=None):
        kw = {}
        if op1 is not None:
            kw['op1'] = op1
        if accum is not None:
            kw['accum_out'] = accum
        self.tr.op(eng, lambda e: e.tensor_scalar(out=out, in0=a, scalar1=s1, scalar2=s2, op0=op0, **kw),
                   reads=r, writes=w)

    def stt(self, eng, out, a, s, b, op0, op1, r, w):
        eng = 'dve'
        self.tr.op(eng, lambda e: e.scalar_tensor_tensor(out=out, in0=a, scalar=s, in1=b, op0=op0, op1=op1),
                   reads=r, writes=w)

    def cp(self, eng, out, in_, r, w):
        if eng == 'act':
            self.tr.op('act', lambda e: e.copy(out=out, in_=in_), reads=r, writes=w)
        else:
            self.tr.op(eng, lambda e: e.tensor_copy(out=out, in_=in_), reads=r, writes=w)

    def memset(self, eng, out, val, w):
        self.tr.op(eng, lambda e: e.memset(out, val), writes=w)

    def recip(self, out, in_, r, w):
        self.tr.op('dve', lambda e: e.reciprocal(out=out, in_=in_), reads=r, writes=w)

    def scan(self, out, d0, d1, r, w):
        self.tr.op('dve', lambda e: e.tensor_tensor_scan(out=out, data0=d0, data1=d1, initial=0.0,
                                                         op0=ALU.mult, op1=ALU.add), reads=r, writes=w)

    def finish(self, out_resources):
        self.tr.wait_all('sp', out_resources)
        self.tr.emit()
        return self.nc


import numpy as np


def p1_consts():
    C = np.zeros((128, 2048), np.float32)
    i = np.arange(128)
    C[:, 0:128] = np.eye(128)
    C[:, 128:256] = (i[None, :] >= i[:, None])
    C[:, 256:384] = 1.0
    blk = (i[:, None] // 64 == i[None, :] // 64).astype(np.float32)
    C[:, 384:512] = blk
    C[:, 512:640] = blk / 64.0
    C[:, 640:768] = blk * (i[:, None] < i[None, :])
    C[:, 768:896] = blk * (i[:, None] > i[None, :])
    C[:, 896:1024] = blk * (i[:, None] <= i[None, :])
    rm = np.ones(512, np.float32)
    rm[::64] = 0
    C[:, 1024:1536] = rm[None, :]
    return C


def p1_core_consts(g):
    Cb = np.zeros((128, 1024), np.float32)
    p = np.arange(128)
    hl = p // 64
    d = p % 64
    heads = 2 * g + hl
    lg = np.log1p(-np.exp2(-5.0 - heads.astype(np.float64)))
    inv_freq = 10000.0 ** (-(d % 32).astype(np.float64) / 32.0)
    Cb[:, 0] = inv_freq
    Cb[:, 1] = np.where(d < 32, -1.0, 1.0)
    Cb[:, 2] = np.exp(128.0 * lg)
    idx = np.arange(128, dtype=np.float64)
    Cb[:, 128:256] = np.exp((idx[None, :] + 1.0) * lg[:, None])
    Cb[:, 256:384] = np.exp((127.0 - idx)[:, None] * lg[None, :])
    for h2 in range(2):
        lgh = np.log1p(-np.exp2(-5.0 - (2 * g + h2)))
        rel = idx[None, :] - idx[:, None]
        Cb[:, 384 + h2 * 128:384 + (h2 + 1) * 128] = np.where(rel >= 0, np.exp(rel * lgh), 0.0)
    return Cb


M_COLS = 3088
RW_COLS = 3520


def p1_core_inputs(inp, l, b, g):
    w_in = inp['w_in'][l]
    o = {}
    zc = np.arange(256 * g, 256 * g + 256)
    xsc = 1024 + np.arange(256 * g, 256 * g + 256)
    Bc = 2048 + np.arange(128 * g, 128 * g + 128)
    Cc = 2560 + np.arange(128 * g, 128 * g + 128)
    dtc = 3072 + np.arange(4 * g, 4 * g + 4)
    o['w_mfm'] = np.ascontiguousarray(w_in[:, np.concatenate([xsc, Bc, Cc])])
    o['w_mtm'] = np.ascontiguousarray(w_in[:, np.concatenate([zc, dtc])])
    convc = np.concatenate([xsc, Bc, Cc]) - 1024
    cw = np.concatenate([inp['m_conv_w'][l][:, convc], inp['m_conv_b'][l][None, convc]], axis=0)
    o['m_cw'] = np.ascontiguousarray(cw.T.reshape(4, 128, 5).transpose(1, 0, 2))
    o['m_hd'] = np.ascontiguousarray(inp['m_head'][l][:, 4 * g:4 * g + 4].reshape(1, 12))
    o['m_ng'] = np.ascontiguousarray(inp['m_norm_g'][l][None, 256 * g:256 * g + 256])
    r0 = M_COLS
    ch = np.arange(256 * g, 256 * g + 256)
    cols = np.concatenate([r0 + ch, r0 + 1024 + ch, r0 + 2048 + ch, r0 + 3072 + np.arange(96),
                           r0 + 3168 + np.arange(96), r0 + 3264 + np.arange(256)])
    o['w_wfm'] = np.ascontiguousarray(w_in[:, cols])
    mu = inp['rwkv_mu'][l][cols - r0]
    mut = np.zeros((128, 10), np.float32)
    offs = [0, 128, 256, 384, 512, 640, 768, 864, 960, 1088]
    sizes = [128, 128, 128, 128, 128, 128, 96, 96, 128, 128]
    for q, (of, sz) in enumerate(zip(offs, sizes)):
        mut[:sz, q] = mu[of:of + sz]
    o['rw_mu'] = mut
    vec = inp['rwkv_vec'][l][:, ch]
    o['rw_vec'] = np.ascontiguousarray(vec.T.reshape(2, 128, 7).transpose(1, 0, 2))
    o['rw_w2'] = np.ascontiguousarray(inp['rwkv_w2'][l][:, ch])
    o['rw_a2'] = np.ascontiguousarray(inp['rwkv_a2'][l][:, ch])
    o['rw_g2'] = np.ascontiguousarray(inp['rwkv_g2'][l][:, ch])
    t0 = M_COLS + RW_COLS
    hd = np.arange(128 * g, 128 * g + 128)
    d = hd % 64
    partner = np.where(d < 32, hd + 32, hd - 32)
    cols = np.concatenate([t0 + hd, t0 + partner, t0 + 512 + hd, t0 + 512 + partner])
    o['w_rfm'] = np.ascontiguousarray(w_in[:, cols])
    cols = np.concatenate([t0 + 1024 + ch, t0 + 2048 + ch])
    o['w_rtm'] = np.ascontiguousarray(w_in[:, cols])
    o['c_col'] = np.ascontiguousarray(inp['c'][b].reshape(16, 128).T)
    o['w_ada'] = np.ascontiguousarray(inp['w_ada'][l][:, :4096])
    o['b_ada'] = np.ascontiguousarray(inp['b_ada'][l][None, :4096])
    o['norm_g'] = inp['norm_g'][l]
    o['pos'] = np.ascontiguousarray(inp['positions'][b][None, :]).astype(np.int32)
    o['cst'] = p1_consts()
    o['cstb'] = p1_core_consts(g)
    return o


def p2_core_inputs(inp, l, b, q, Tq, x_b, yT_b):
    o = {}
    t0 = q * Tq
    NT = 128 + Tq
    if x_b is not None:
        xh = np.zeros((NT, 2048), np.float32)
        yh = np.zeros((3, 1024, NT), yT_b.dtype)
        if q == 0:
            xh[128:] = x_b[0:Tq]
            yh[:, :, 128:] = yT_b[:, :, 0:Tq]
        else:
            xh[:] = x_b[t0 - 128:t0 + Tq]
            yh[:] = yT_b[:, :, t0 - 128:t0 + Tq]
        o['xh'] = xh
        o['yTh'] = yh
    o['hmask'] = np.full((128, 1), 0.0 if q == 0 else 1.0, np.float32)
    o['c_col'] = np.ascontiguousarray(inp['c'][b].reshape(16, 128).T)
    o['w_ada'] = inp['w_ada'][l]
    o['b_ada'] = inp['b_ada'][l][None, :]
    ng = inp['norm_g'][l]
    o['norm_g'] = ng
    o['ng_col'] = np.ascontiguousarray(ng.reshape(4, 16, 128).transpose(2, 0, 1).reshape(128, 64))
    o['w_gate'] = inp['w_gate'][l]
    o['w_branch'] = inp['w_branch'][l]
    o['w_out'] = inp['w_out'][l]
    o['w_up'] = inp['w_up'][l]
    fcv = np.concatenate([inp['f_conv_w'][l], inp['f_conv_b'][l][None, :]], axis=0)
    o['f_cv'] = np.ascontiguousarray(fcv.T.reshape(86, 128, 4).transpose(1, 0, 2))
    o['w_down'] = inp['w_down'][l]
    o['cst'] = p1_consts()
    return o


from contextlib import ExitStack
import numpy as np


MAGIC = 12582912.0
TWO_PI = float(2 * np.pi)


class Scope:
    _n = 0

    def __init__(self, kb):
        self.kb = kb
        self.st = ExitStack()
        Scope._n += 1
        self.tag = f"_sc{Scope._n}"

    def sb(self, name, shape, dt=F32):
        return self.st.enter_context(self.kb.nc.sbuf_tensor(name + self.tag, list(shape), dt))

    def close(self):
        self.kb.tr.barrier()
        self.st.close()


def load_cast_weights(kb, sc, wdram, ncols, name, stage_cols=128):
    W = sc.sb(name, [128, 16, ncols], BF16)
    stg = [sc.sb(f"{name}_stg{i}", [128, 16, stage_cols], F32) for i in range(2)]
    wv = wdram.rearrange("(k p) n -> p k n", p=128)
    i = 0
    for c0 in range(0, ncols, stage_cols):
        c1 = min(ncols, c0 + stage_cols)
        s = stg[i % 2]
        rn = f"{name}_stg{i % 2}"
        kb.dma(s[:, :, 0:c1 - c0], wv[:, :, c0:c1], rn, w=[rn])
        kb.cp('pool' if i % 2 else 'dve', W[:, :, c0:c1], s[:, :, 0:c1 - c0], r=[rn], w=[name])
        i += 1
    return W


def emit_mod(kb, sc, c_col, w_ada, b_ada, vec_ids, ones_row, pb, tag):
    cc = sc.sb(f"{tag}_cc", [128, 16])
    scl = sc.sb(f"{tag}_sc", [128, 16])
    kb.dma(cc[:], c_col, f"{tag}_cc", w=[f"{tag}_cc"])
    kb.act(scl[:], cc[:], AF.Silu, r=[f"{tag}_cc"], w=[f"{tag}_sc"])
    stg = [sc.sb(f"{tag}_wa{i}", [128, 16, 128], F32) for i in range(2)]
    wv = w_ada.rearrange("(k p) n -> p k n", p=128)
    rows = {}
    i = 0
    for vid in vec_ids:
        row = sc.sb(f"{tag}_row{vid}", [1, 2048])
        brow = sc.sb(f"{tag}_brow{vid}", [1, 2048])
        rn = f"{tag}_row{vid}"
        kb.dma(brow[:], b_ada[0:1, vid * 2048:(vid + 1) * 2048], f"{tag}_brow{vid}", w=[f"{tag}_brow{vid}"])
        for cch in range(16):
            c0 = vid * 2048 + cch * 128
            s = stg[i % 2]
            sn = f"{tag}_wa{i % 2}"
            kb.dma(s[:], wv[:, :, c0:c0 + 128], sn, w=[sn])
            pr = "pb7" if i % 2 == 0 else "pb4"
            pcols = (pb[7] if i % 2 == 0 else pb[4])[0:1, 0:128]
            for k in range(16):
                kb.mm(pcols, lhsT=scl[:, k:k + 1], rhs=s[:, k, :], start=(k == 0), stop=(k == 15),
                      r=[sn, f"{tag}_sc"], w=[pr])
            kb.tt('dve', row[0:1, cch * 128:(cch + 1) * 128], pcols, brow[0:1, cch * 128:(cch + 1) * 128], ALU.add,
                  r=[pr, f"{tag}_brow{vid}"], w=[rn])
            i += 1
        rows[vid] = (row, rn)
    return rows


def bcast_row(kb, row, rn, ones_row, pb, emit_chunk):
    for c in range(4):
        pr = "pb6" if c % 2 == 0 else "pb5"
        pa = pb[6][:, 0:512] if c % 2 == 0 else pb[5][:, 0:512]
        kb.mm(pa, lhsT=ones_row[0:1, 0:128], rhs=row[0:1, c * 512:(c + 1) * 512], start=True, stop=True,
              r=[rn, 'C'], w=[pr])
        emit_chunk(c, pa, pr)


P1_PARAM_SHAPES = {
    "w_mfm": ([D, 512], F32), "w_mtm": ([D, 260], F32), "m_cw": ([128, 4, 5], F32), "m_hd": ([1, 12], F32),
    "m_ng": ([1, 256], F32), "w_rfm": ([D, 512], F32), "w_rtm": ([D, 512], F32), "w_wfm": ([D, 1216], F32),
    "rw_mu": ([128, 10], F32), "rw_vec": ([128, 2, 7], F32), "rw_w2": ([96, 256], F32), "rw_a2": ([96, 256], F32),
    "rw_g2": ([256, 256], F32),
}


def setup_globals(kb, cst):
    G = {}
    G['pb'] = [kb.ps(f"pb{i}", [128, 512]) for i in range(8)]
    C = kb.sb("C", [128, 2048])
    kb.dma(C[:], cst, "C", w=["C"])
    Cb = kb.sb("Cb", [128, 256], BF16)
    kb.cp('dve', Cb[:, 0:128], C[:, 0:128], r=["C"], w=["Cb"])
    G['C'] = C
    G['Cb'] = Cb
    return G


def build_p1(S, parts=('h', 'mamba', 'rwkv', 'ret'), env=None):
    NM = S // 512
    if env is None:
        kb = KB()
        e = {}
        e['x'] = kb.din("x", [S, D])
        e['c_col'] = kb.din("c_col", [128, 16])
        e['w_ada'] = kb.din("w_ada", [D, 2 * D])
        e['b_ada'] = kb.din("b_ada", [1, 2 * D])
        e['norm_g'] = kb.din("norm_g", [4, D])
        e['pos'] = kb.din("pos", [1, S], I32)
        cst = kb.din("cst", [128, 2048])
        e['cstb'] = kb.din("cstb", [128, 1024])
        for k, (shp, dt_) in P1_PARAM_SHAPES.items():
            e[k] = kb.din(k, shp, dt_)
        e['yT'] = kb.dout("yT", [768, S], BF16)
        e['hTd'] = kb.dscratch("hTd", [NM, 128, 16, 512], BF16)
        e['hTo'] = e['hTd']
        e['NH'] = S // 128
        G = setup_globals(kb, cst)
    else:
        kb = env['kb']
        e = env
        G = env['G']
    nc = kb.nc
    x, c_col, w_ada, b_ada, norm_g, pos, cstb = (e[k] for k in ('x', 'c_col', 'w_ada', 'b_ada', 'norm_g', 'pos', 'cstb'))
    w_mfm, w_mtm, m_cw, m_hd, m_ng, w_rfm, w_rtm, w_wfm = (e[k] for k in ('w_mfm', 'w_mtm', 'm_cw', 'm_hd', 'm_ng', 'w_rfm', 'w_rtm', 'w_wfm'))
    rw_mu, rw_vec, rw_w2, rw_a2, rw_g2 = (e[k] for k in ('rw_mu', 'rw_vec', 'rw_w2', 'rw_a2', 'rw_g2'))
    yT, hTd, hTo, NH = e['yT'], e['hTd'], e['hTo'], e['NH']

    def _load_hT(dst, m, key):
        kb.dma(dst[:], hTd[m], key, r=[f"hTd{m}"], w=[key])

    def _yT_ap(row0, nrows, m):
        return yT[row0:row0 + nrows, m * 512:(m + 1) * 512]

    load_hT = e.get('load_hT', _load_hT)
    yT_ap = e.get('yT_ap', _yT_ap)
    after_y = e.get('after_y', lambda i, m, res: None)
    pb = G['pb']
    C, Cb = G['C'], G['Cb']
    ident_f = C[:, 0:128]
    U_f = C[:, 128:256]
    ones_f = C[:, 256:384]
    blk_f = C[:, 384:512]
    blk64_f = C[:, 512:640]
    SU_f = C[:, 640:768]
    SL_f = C[:, 768:896]
    UI_f = C[:, 896:1024]
    rmask = C[:, 1024:1536]
    ident_b = Cb[:, 0:128]
    ones_row = C[0:1, 256:384]

    if 'h' in parts:
        sc = Scope(kb)
        rows = emit_mod(kb, sc, c_col, w_ada, b_ada, [0, 1], ones_row, pb, "m0")
        Gbc = sc.sb("Gbc", [128, 2048])
        SHbc = sc.sb("SHbc", [128, 2048])
        kb.dma(Gbc[:], norm_g[0:1, :].partition_broadcast(128), "Gbc", w=["Gbc"])

        def g_chunk(c, pa, pr):
            kb.stt('dve', Gbc[:, c * 512:(c + 1) * 512], pa, 1.0, Gbc[:, c * 512:(c + 1) * 512], ALU.add, ALU.mult,
                   r=[pr, "Gbc"], w=["Gbc"])

        def sh_chunk(c, pa, pr):
            kb.cp('act', SHbc[:, c * 512:(c + 1) * 512], pa, r=[pr], w=["SHbc"])

        bcast_row(kb, rows[1][0], rows[1][1], ones_row, pb, g_chunk)
        bcast_row(kb, rows[0][0], rows[0][1], ones_row, pb, sh_chunk)
        xt = [sc.sb(f"xt{i}", [128, 2048]) for i in range(2)]
        junk = sc.sb("junk", [128, 2048], BF16)
        tmp = sc.sb("htmp", [128, 2048])
        hb = sc.sb("hb", [128, 2048], BF16)
        ss = sc.sb("ss", [128, 1])
        rstd = sc.sb("rstd", [128, 1])
        hst = [sc.sb(f"hst{i}", [128, 16, 512], BF16) for i in range(2)]
        for i in range(NH):
            m, sub = divmod(i, 4)
            xb = xt[i % 2]
            xn = f"xt{i % 2}"
            kb.dma(xb[:], x[i * 128:(i + 1) * 128, :], xn, w=[xn])
            kb.act(junk[:], xb[:], AF.Square, r=[xn], w=["junk", "ss"], accum=ss[:])
            kb.act(rstd[:], ss[:], AF.Sqrt, r=["ss"], w=["rstd"], bias=1e-6, scale=1.0 / D)
            kb.recip(rstd[:], rstd[:], r=["rstd"], w=["rstd"])
            kb.stt('dve', tmp[:], xb[:], rstd[:, 0:1], Gbc[:], ALU.mult, ALU.mult, r=[xn, "rstd", "Gbc"], w=["htmp"])
            kb.tt('pool', hb[:], tmp[:], SHbc[:], ALU.add, r=["htmp", "SHbc"], w=["hb"])
            hs = hst[m % 2]
            hn = f"hst{m % 2}"
            for q in range(4):
                pr = f"pb{q}"
                for kk in range(4):
                    k = q * 4 + kk
                    kb.mm(pb[q][:, kk * 128:(kk + 1) * 128], lhsT=hb[:, k * 128:(k + 1) * 128], rhs=ident_b,
                          start=True, stop=True, r=["hb", "Cb"], w=[pr])
                eng = 'act' if q % 2 == 0 else 'dve'
                kb.cp(eng, hs[:, q * 4:(q + 1) * 4, sub * 128:(sub + 1) * 128],
                      pb[q][:, :].rearrange("p (k t) -> p k t", k=4), r=[pr], w=[hn])
            if sub == 3:
                kb.dma(hTo[m], hs[:], hn, r=[hn], w=[f"hTd{m}"])
                if 'after_h' in e:
                    e['after_h'](m)
        sc.close()

    if 'mamba' in parts:
        sc = Scope(kb)
        Wfm = load_cast_weights(kb, sc, w_mfm, 512, "Wfm")
        Wtm = load_cast_weights(kb, sc, w_mtm, 260, "Wtm")
        cw = sc.sb("cw", [128, 4, 5])
        kb.dma(cw[:], m_cw, "cw", w=["cw"])
        mh = sc.sb("mh", [128, 12])
        kb.dma(mh[:], m_hd.partition_broadcast(128), "mh", w=["mh"])
        negA = sc.sb("negA", [128, 4])
        kb.act(negA[:], mh[:, 4:8], AF.Exp, r=["mh"], w=["negA"])
        kb.ts('dve', negA[:], negA[:], -1.0, ALU.mult, r=["negA"], w=["negA"])
        dsk = sc.sb("dsk", [128, 256])
        for h in range(4):
            kb.ts('dve', dsk[:, h * 64:(h + 1) * 64], ones_f[:, 0:64], mh[:, 8 + h:9 + h], ALU.mult,
                  r=["C", "mh"], w=["dsk"])
        mng = sc.sb("mng", [128, 256])
        kb.dma(mng[:], m_ng.partition_broadcast(128), "mng", w=["mng"])
        hT = [sc.sb(f"hT{i}", [128, 16, 512], BF16) for i in range(2)]
        raw = [sc.sb(f"raw{j}", [128, 515]) for j in range(4)]
        acc = [sc.sb(f"acc{j}", [128, 512]) for j in range(2)]
        xs_act = [sc.sb(f"xsa{j}", [128, 512]) for j in range(2)]
        B_act = sc.sb("B_act", [128, 512])
        BT_bf = sc.sb("BT_bf", [128, 512], BF16)
        CT_bf = sc.sb("CT_bf", [128, 512], BF16)
        zs = sc.sb("zs", [128, 256])
        dtp = sc.sb("dtp", [128, 4])
        dt = sc.sb("dt", [128, 4])
        adt = sc.sb("adt", [128, 4])
        xs_sb = sc.sb("xs_sb", [128, 256])
        B_tm = sc.sb("B_tm", [128, 128], BF16)
        xdt_bf = sc.sb("xdt_bf", [128, 256], BF16)
        xte_bf = sc.sb("xte_bf", [128, 256], BF16)
        cum_sb = sc.sb("cum_sb", [128, 4])
        rh = sc.sb("rh", [128, 4, 128])
        CBm = sc.sb("CBm", [128, 128])
        seg = sc.sb("seg", [128, 4, 128])
        dec = sc.sb("dec", [128, 4, 128])
        MT_bf = sc.sb("MT_bf", [128, 4, 128], BF16)
        ecum = sc.sb("ecum", [128, 4])
        tte = sc.sb("tte", [128, 4])
        ecl = sc.sb("ecl", [128, 4])
        yi_sb = sc.sb("yi_sb", [128, 256])
        y = sc.sb("y", [128, 256])
        y2 = sc.sb("y2", [128, 256])
        state = sc.sb("state", [128, 256])
        state_bf = sc.sb("state_bf", [128, 256], BF16)
        mjunk = sc.sb("mjunk", [128, 256])
        mss = sc.sb("mss", [128, 1])
        mrstd = sc.sb("mrstd", [128, 1])
        yo_bf = sc.sb("yo_bf", [128, 256], BF16)
        yst = [sc.sb(f"yst{i}", [128, 2, 512], BF16) for i in range(2)]
        kb.memset('pool', state[:], 0.0, w=["state"])
        kb.memset('pool', state_bf[:], 0.0, w=["state_bf"])
        for j in range(4):
            kb.memset('pool', raw[j][:], 0.0, w=[f"raw{j}"])
        for m in range(NM):
            h_ = hT[m % 2]
            hn = f"hT{m % 2}"
            load_hT(h_, m, hn)
            ys = yst[m % 2]
            ysn = f"yst{m % 2}"
            for j in range(4):
                pr = f"pb{j % 2}"
                pa = pb[j % 2]
                for k in range(16):
                    kb.mm(pa[:, :], lhsT=Wfm[:, k, j * 128:(j + 1) * 128], rhs=h_[:, k, :], start=(k == 0),
                          stop=(k == 15), r=["Wfm", hn], w=[pr])
                rj = f"raw{j}"
                kb.cp('pool', raw[j][:, 0:3], raw[j][:, 512:515], r=[rj], w=[rj])
                kb.cp('act', raw[j][:, 3:515], pa[:, :], r=[pr], w=[rj])
                a_ = acc[j % 2]
                an = f"acc{j % 2}"
                kb.ts('dve', a_[:], raw[j][:, 3:515], cw[:, j, 3:4], ALU.mult, r=[rj, "cw"], w=[an],
                      s2=cw[:, j, 4:5], op1=ALU.add)
                for tpi, eng in ((2, 'pool'), (1, 'dve'), (0, 'pool')):
                    kb.stt(eng, a_[:], raw[j][:, tpi:tpi + 512], cw[:, j, tpi:tpi + 1], a_[:], ALU.mult, ALU.add,
                           r=[rj, "cw", an], w=[an])
                if j < 2:
                    kb.act(xs_act[j][:], a_[:], AF.Silu, r=[an], w=[f"xsa{j}"])
                elif j == 2:
                    kb.act(B_act[:], a_[:], AF.Silu, r=[an], w=["B_act"])
                    kb.cp('pool', BT_bf[:], B_act[:], r=["B_act"], w=["BT_bf"])
                else:
                    kb.act(CT_bf[:], a_[:], AF.Silu, r=[an], w=["CT_bf"])
            import os as _os
            _nch = int(_os.environ.get("MB_NCH", "4"))
            for c in range(_nch):
                cs = slice(c * 128, (c + 1) * 128)
                _sect = _os.environ.get('MB_SECT', '123456')
                if '1' in _sect:
                    for k in range(16):
                        kb.mm(pb[2][:, 0:260], lhsT=h_[:, k, cs], rhs=Wtm[:, k, :], start=(k == 0), stop=(k == 15),
                              r=[hn, "Wtm"], w=["pb2"])
                    kb.act(zs[:], pb[2][:, 0:256], AF.Silu, r=["pb2"], w=["zs"])
                    kb.tt('dve', dtp[:], pb[2][:, 256:260], mh[:, 0:4], ALU.add, r=["pb2", "mh"], w=["dtp"])
                    kb.act(dtp[:], dtp[:], AF.Exp, r=["dtp"], w=["dtp"])
                    kb.act(dt[:], dtp[:], AF.Ln, r=["dtp"], w=["dt"], bias=1.0)
                    kb.tt('dve', adt[:], dt[:], negA[:], ALU.mult, r=["dt", "negA"], w=["adt"])
                if '2' in _sect:
                    for j in range(2):
                        kb.mm(pb[3][:, j * 128:(j + 1) * 128], lhsT=xs_act[j][:, cs], rhs=ident_f, start=True, stop=True,
                              r=[f"xsa{j}", "C"], w=["pb3"])
                    kb.mm(pb[3][:, 256:384], lhsT=B_act[:, cs], rhs=ident_f, start=True, stop=True,
                          r=["B_act", "C"], w=["pb3"])
                    kb.cp('act', xs_sb[:], pb[3][:, 0:256], r=["pb3"], w=["xs_sb"])
                    kb.cp('dve', B_tm[:], pb[3][:, 256:384], r=["pb3"], w=["B_tm"])
                    kb.tt('pool', xdt_bf[:].rearrange("p (h c) -> p h c", h=4), xs_sb[:].rearrange("p (h c) -> p h c", h=4),
                          dt[:].unsqueeze(2).to_broadcast([128, 4, 64]), ALU.mult, r=["xs_sb", "dt"], w=["xdt_bf"])
                if '3' in _sect:
                    kb.mm(pb[4][:, 0:4], lhsT=U_f, rhs=adt[:], start=True, stop=True, r=["C", "adt"], w=["pb4"])
                    kb.cp('act', cum_sb[:], pb[4][:, 0:4], r=["pb4"], w=["cum_sb"])
                    kb.tt('pool', rh[:], U_f.unsqueeze(1).to_broadcast([128, 4, 128]),
                          adt[:].unsqueeze(2).to_broadcast([128, 4, 128]), ALU.mult, r=["C", "adt"], w=["rh"])
                    for h in range(4):
                        kb.mm(pb[5][:, h * 128:(h + 1) * 128], lhsT=ones_f, rhs=rh[:, h, :], start=True, stop=True,
                              r=["C", "rh"], w=["pb5"])
                    kb.mm(pb[4][:, 128:256], lhsT=BT_bf[:, cs], rhs=CT_bf[:, cs], start=True, stop=True,
                          r=["BT_bf", "CT_bf"], w=["pb4"])
                    kb.tt('dve', CBm[:], pb[4][:, 128:256], U_f, ALU.mult, r=["pb4", "C"], w=["CBm"])
                    kb.tt('dve', seg[:], pb[5][:, :].rearrange("p (h i) -> p h i", h=4),
                          cum_sb[:].unsqueeze(2).to_broadcast([128, 4, 128]), ALU.subtract, r=["pb5", "cum_sb"], w=["seg"])
                    kb.ts('pool', seg[:], seg[:], 0.0, ALU.min, r=["seg"], w=["seg"])
                    kb.act(dec[:], seg[:], AF.Exp, r=["seg"], w=["dec"])
                    kb.tt('pool', MT_bf[:], dec[:], CBm[:].unsqueeze(1).to_broadcast([128, 4, 128]), ALU.mult,
                          r=["dec", "CBm"], w=["MT_bf"])
                    kb.act(ecum[:], cum_sb[:], AF.Exp, r=["cum_sb"], w=["ecum"])
                    last = pb[5][:, :].rearrange("p (h i) -> p h i", h=4)[:, :, 127:128]
                    kb.tt('dve', tte[:].rearrange("p (h o) -> p h o", o=1), last,
                          cum_sb[:].rearrange("p (h o) -> p h o", o=1), ALU.subtract, r=["pb5", "cum_sb"], w=["tte"])
                    kb.act(tte[:], tte[:], AF.Exp, r=["tte"], w=["tte"])
                    kb.act(ecl[:].rearrange("p (h o) -> p h o", o=1), last, AF.Exp, r=["pb5"], w=["ecl"])
                if '4' in _sect:
                    for h in range(4):
                        kb.mm(pb[6][:, h * 64:(h + 1) * 64], lhsT=MT_bf[:, h, :], rhs=xdt_bf[:, h * 64:(h + 1) * 64],
                              start=True, stop=True, r=["MT_bf", "xdt_bf"], w=["pb6"])
                    kb.mm(pb[6][:, 256:512], lhsT=CT_bf[:, cs], rhs=state_bf[:], start=True, stop=True,
                          r=["CT_bf", "state_bf"], w=["pb6"])
                    kb.tt('dve', yi_sb[:].rearrange("p (h c) -> p h c", h=4),
                          pb[6][:, 256:512].rearrange("p (h c) -> p h c", h=4),
                          ecum[:].unsqueeze(2).to_broadcast([128, 4, 64]), ALU.mult, r=["pb6", "ecum"], w=["yi_sb"])
                    kb.tt('dve', y[:], yi_sb[:], pb[6][:, 0:256], ALU.add, r=["pb6", "yi_sb"], w=["y"])
                if '5' in _sect:
                    kb.tt('pool', xte_bf[:].rearrange("p (h c) -> p h c", h=4), xdt_bf[:].rearrange("p (h c) -> p h c", h=4),
                          tte[:].unsqueeze(2).to_broadcast([128, 4, 64]), ALU.mult, r=["xdt_bf", "tte"], w=["xte_bf"])
                    kb.mm(pb[7][:, 0:256], lhsT=B_tm[:], rhs=xte_bf[:], start=True, stop=True, r=["B_tm", "xte_bf"],
                          w=["pb7"])
                    kb.tt('pool', state[:].rearrange("p (h c) -> p h c", h=4), state[:].rearrange("p (h c) -> p h c", h=4),
                          ecl[:].unsqueeze(2).to_broadcast([128, 4, 64]), ALU.mult, r=["state", "ecl"], w=["state"])
                    kb.tt('dve', state[:], state[:], pb[7][:, 0:256], ALU.add, r=["state", "pb7"], w=["state"])
                    kb.cp('pool', state_bf[:], state[:], r=["state"], w=["state_bf"])
                if '6' in _sect:
                    kb.tt('pool', y2[:], xs_sb[:], dsk[:], ALU.mult, r=["xs_sb", "dsk"], w=["y2"])
                    kb.tt('pool', y[:], y[:], y2[:], ALU.add, r=["y", "y2"], w=["y"])
                    kb.tt('pool', y[:], y[:], zs[:], ALU.mult, r=["y", "zs"], w=["y"])
                    kb.act(mjunk[:], y[:], AF.Square, r=["y"], w=["mjunk", "mss"], accum=mss[:])
                    kb.act(mrstd[:], mss[:], AF.Sqrt, r=["mss"], w=["mrstd"], bias=1e-5, scale=1.0 / 256)
                    kb.recip(mrstd[:], mrstd[:], r=["mrstd"], w=["mrstd"])
                    kb.stt('dve', yo_bf[:], y[:], mrstd[:, 0:1], mng[:], ALU.mult, ALU.mult, r=["y", "mrstd", "mng"],
                           w=["yo_bf"])
                    for t in range(2):
                        kb.mm(pb[7][:, 256 + t * 128:256 + (t + 1) * 128], lhsT=yo_bf[:, t * 128:(t + 1) * 128],
                              rhs=ident_b, start=True, stop=True, r=["yo_bf", "Cb"], w=["pb7"])
                    kb.cp('act', ys[:, :, cs], pb[7][:, 256:512].rearrange("p (t s) -> p t s", t=2), r=["pb7"], w=[ysn])
            kb.dma(yT_ap(0, 256, m).rearrange("(t p) s -> p t s", p=128), ys[:], ysn, r=[ysn],
                   w=["yT_m"])
            after_y(0, m, "yT_m")
        sc.close()

    if 'ret' in parts:
        sc = Scope(kb)
        Wfm = load_cast_weights(kb, sc, w_rfm, 512, "Wfm")
        Wtm = load_cast_weights(kb, sc, w_rtm, 512, "Wtm")
        CB = sc.sb("CB", [128, 1024])
        kb.dma(CB[:], cstb, "CB", w=["CB"])
        invf = CB[:, 0:1]
        sgn = CB[:, 1:2]
        cdec = CB[:, 2:3]
        QD = CB[:, 128:256]
        KDEC = CB[:, 256:384]
        DMT = CB[:, 384:640]
        hT = [sc.sb(f"hT{i}", [128, 16, 512], BF16) for i in range(2)]
        pi_ = sc.sb("pi", [128, 512], I32)
        ang = sc.sb("ang", [128, 512])
        kq = sc.sb("kq", [128, 512])
        rr = sc.sb("rr", [128, 512])
        sinT = sc.sb("sinT", [128, 512])
        cosT = sc.sb("cosT", [128, 512])
        t1 = sc.sb("t1", [128, 512])
        t2 = sc.sb("t2", [128, 512])
        qr_bf = sc.sb("qr_bf", [128, 512], BF16)
        kr_bf = sc.sb("kr_bf", [128, 512], BF16)
        krz = [sc.sb(f"krz{h}", [128, 512], BF16) for h in range(2)]
        qdz = [sc.sb(f"qdz{h}", [128, 512], BF16) for h in range(2)]
        v_bf = sc.sb("v_bf", [128, 256], BF16)
        gs = sc.sb("gs", [128, 256])
        ST_bf = sc.sb("ST_bf", [128, 256], BF16)
        kd_bf = sc.sb("kd_bf", [128, 128], BF16)
        rstate = sc.sb("rstate", [128, 128])
        rstate_bf = sc.sb("rstate_bf", [128, 128], BF16)
        rjunk = sc.sb("rjunk", [128, 128])
        rss = sc.sb("rss", [128, 2])
        rrs = sc.sb("rrs", [128, 2])
        yo_bf = sc.sb("ryo_bf", [128, 256], BF16)
        yst = [sc.sb(f"yst{i}", [128, 2, 512], BF16) for i in range(2)]
        kb.memset('pool', rstate[:], 0.0, w=["rstate"])
        kb.memset('pool', rstate_bf[:], 0.0, w=["rstate_bf"])
        for h in range(2):
            kb.memset('pool', krz[h][:], 0.0, w=[f"krz{h}"])
            kb.memset('pool', qdz[h][:], 0.0, w=[f"qdz{h}"])
        for m in range(NM):
            h_ = hT[m % 2]
            hn = f"hT{m % 2}"
            load_hT(h_, m, hn)
            ys = yst[m % 2]
            ysn = f"yst{m % 2}"
            kb.dma(pi_[:], pos[0:1, m * 512:(m + 1) * 512].partition_broadcast(128), "pi", w=["pi"])
            kb.cp('dve', ang[:], pi_[:], r=["pi"], w=["ang"])
            kb.ts('dve', ang[:], ang[:], invf, ALU.mult, r=["ang", "CB"], w=["ang"])
            for which, dst, shift in (("s", sinT, 0.0), ("c", cosT, float(np.pi / 2))):
                if shift != 0.0:
                    kb.ts('pool', rr[:], ang[:], shift, ALU.add, r=["ang"], w=["rr"])
                    src, srn = rr, "rr"
                else:
                    src, srn = ang, "ang"
                kb.ts('dve', kq[:], src[:], float(1.0 / TWO_PI), ALU.mult, r=[srn], w=["kq"], s2=MAGIC, op1=ALU.add)
                kb.ts('pool', kq[:], kq[:], -MAGIC, ALU.add, r=["kq"], w=["kq"])
                kb.stt('dve', rr[:], kq[:], -TWO_PI, src[:], ALU.mult, ALU.add, r=["kq", srn], w=["rr"])
                kb.ts('pool', rr[:], rr[:], 3.14159, ALU.min, r=["rr"], w=["rr"], s2=-3.14159, op1=ALU.max)
                if which == "s":
                    kb.act(dst[:], rr[:], AF.Sin, r=["rr", "CB"], w=["sinT"], scale=sgn)
                else:
                    kb.act(dst[:], rr[:], AF.Sin, r=["rr"], w=["cosT"])
            for j in range(4):
                pr = f"pb{j % 2}"
                pa = pb[j % 2]
                for k in range(16):
                    kb.mm(pa[:, :], lhsT=Wfm[:, k, j * 128:(j + 1) * 128], rhs=h_[:, k, :], start=(k == 0),
                          stop=(k == 15), r=["Wfm", hn], w=[pr])
                if j % 2 == 0:
                    kb.tt('dve', t1[:], pa[:, :], cosT[:], ALU.mult, r=[pr, "cosT"], w=["t1"])
                else:
                    kb.tt('dve', t2[:], pa[:, :], sinT[:], ALU.mult, r=[pr, "sinT"], w=["t2"])
                    if j == 1:
                        kb.tt('pool', qr_bf[:], t1[:], t2[:], ALU.add, r=["t1", "t2"], w=["qr_bf"])
                    else:
                        kb.tt('pool', t1[:], t1[:], t2[:], ALU.add, r=["t1", "t2"], w=["t1"])
                        kb.ts('pool', kr_bf[:], t1[:], 0.125, ALU.mult, r=["t1"], w=["kr_bf"])
            for h in range(2):
                ph = slice(h * 64, (h + 1) * 64)
                kb.cp('pool', krz[h][ph, :], kr_bf[ph, :], r=["kr_bf"], w=[f"krz{h}"])
                for c in range(4):
                    cs = slice(c * 128, (c + 1) * 128)
                    kb.tt('dve' if c % 2 else 'pool', qdz[h][ph, cs], qr_bf[ph, cs], QD[ph, :], ALU.mult,
                          r=["qr_bf", "CB"], w=[f"qdz{h}"])
            for c in range(4):
                cs = slice(c * 128, (c + 1) * 128)
                for k in range(16):
                    kb.mm(pb[2][:, :], lhsT=h_[:, k, cs], rhs=Wtm[:, k, :], start=(k == 0), stop=(k == 15),
                          r=[hn, "Wtm"], w=["pb2"])
                kb.cp('act', v_bf[:], pb[2][:, 0:256], r=["pb2"], w=["v_bf"])
                kb.act(gs[:], pb[2][:, 256:512], AF.Silu, r=["pb2"], w=["gs"])
                for h in range(2):
                    kb.mm(pb[3][:, h * 128:(h + 1) * 128], lhsT=krz[h][:, cs], rhs=qr_bf[:, cs], start=True, stop=True,
                          r=[f"krz{h}", "qr_bf"], w=["pb3"])
                kb.tt('dve', ST_bf[:], pb[3][:, 0:256], DMT, ALU.mult, r=["pb3", "CB"], w=["ST_bf"])
                kb.mm(pb[4][:, 0:128], lhsT=kr_bf[:, cs], rhs=ident_b, start=True, stop=True, r=["kr_bf", "Cb"],
                      w=["pb4"])
                kb.tt('dve', kd_bf[:], pb[4][:, 0:128], KDEC, ALU.mult, r=["pb4", "CB"], w=["kd_bf"])
                for h in range(2):
                    hs_ = slice(h * 128, (h + 1) * 128)
                    kb.mm(pb[5][:, hs_], lhsT=ST_bf[:, hs_], rhs=v_bf[:, hs_], start=True, stop=False,
                          r=["ST_bf", "v_bf"], w=["pb5"])
                    kb.mm(pb[5][:, hs_], lhsT=qdz[h][:, cs], rhs=rstate_bf[:], start=False, stop=True,
                          r=[f"qdz{h}", "rstate_bf"], w=["pb5"])
                kb.mm(pb[6][:, 0:256], lhsT=kd_bf[:], rhs=v_bf[:], start=True, stop=True, r=["kd_bf", "v_bf"],
                      w=["pb6"])
                for h in range(2):
                    ph = slice(h * 64, (h + 1) * 64)
                    kb.stt('dve', rstate[ph, :], rstate[ph, :], cdec[ph, :], pb[6][ph, h * 128:(h + 1) * 128],
                           ALU.mult, ALU.add, r=["rstate", "CB", "pb6"], w=["rstate"])
                kb.cp('pool', rstate_bf[:], rstate[:], r=["rstate"], w=["rstate_bf"])
                for h in range(2):
                    kb.act(rjunk[:], pb[5][:, h * 128:(h + 1) * 128], AF.Square, r=["pb5"], w=["rjunk", "rss"],
                           accum=rss[:, h:h + 1])
                kb.act(rrs[:], rss[:], AF.Sqrt, r=["rss"], w=["rrs"], bias=1e-6, scale=1.0 / 128)
                kb.recip(rrs[:], rrs[:], r=["rrs"], w=["rrs"])
                for h in range(2):
                    hs_ = slice(h * 128, (h + 1) * 128)
                    kb.stt('dve', yo_bf[:, hs_], pb[5][:, hs_], rrs[:, h:h + 1], gs[:, hs_], ALU.mult, ALU.mult,
                           r=["pb5", "rrs", "gs"], w=["ryo_bf"])
                for t in range(2):
                    kb.mm(pb[7][:, 256 + t * 128:256 + (t + 1) * 128], lhsT=yo_bf[:, t * 128:(t + 1) * 128],
                          rhs=ident_b, start=True, stop=True, r=["ryo_bf", "Cb"], w=["pb7"])
                kb.cp('act', ys[:, :, cs], pb[7][:, 256:512].rearrange("p (t s) -> p t s", t=2), r=["pb7"], w=[ysn])
            kb.dma(yT_ap(512, 256, m).rearrange("(t p) s -> p t s", p=128), ys[:], ysn, r=[ysn],
                   w=["yT_r"])
            after_y(2, m, "yT_r")
        sc.close()

    if 'rwkv' in parts:
        sc = Scope(kb)
        Wr = load_cast_weights(kb, sc, w_wfm, 1216, "Wr")
        offs = [0, 128, 256, 384, 512, 640, 768, 864, 960, 1088]
        sizes = [128, 128, 128, 128, 128, 128, 96, 96, 128, 128]
        mu = sc.sb("mu", [128, 10])
        kb.dma(mu[:], rw_mu, "mu", w=["mu"])
        vec = sc.sb("vec", [128, 2, 7])
        kb.dma(vec[:], rw_vec, "vec", w=["vec"])
        omka = sc.sb("omka", [128, 2])
        kb.ts('dve', omka[:].rearrange("p (c o) -> p c o", o=1), vec[:, :, 3:4], -1.0, ALU.mult, r=["vec"], w=["omka"],
              s2=1.0, op1=ALU.add)
        lstg = sc.sb("lstg", [128, 2, 256])
        w2_bf = sc.sb("w2_bf", [96, 256], BF16)
        a2_bf = sc.sb("a2_bf", [96, 256], BF16)
        g2_bf = sc.sb("g2_bf", [128, 2, 256], BF16)
        kb.dma(lstg[0:96, 0, :], rw_w2, "lstg", w=["lstg"])
        kb.cp('dve', w2_bf[:], lstg[0:96, 0, :], r=["lstg"], w=["w2_bf"])
        kb.dma(lstg[0:96, 1, :], rw_a2, "lstg", r=["lstg"], w=["lstg"])
        kb.cp('dve', a2_bf[:], lstg[0:96, 1, :], r=["lstg"], w=["a2_bf"])
        kb.dma(lstg[:], rw_g2.rearrange("(k p) n -> p k n", p=128), "lstg", r=["lstg"], w=["lstg"])
        kb.cp('dve', g2_bf[:], lstg[:], r=["lstg"], w=["g2_bf"])
        M3 = sc.sb("M3", [128, 384])
        kb.cp('pool', M3[:, 0:128], SU_f, r=["C"], w=["M3"])
        kb.cp('pool', M3[:, 128:256], SL_f, r=["C"], w=["M3"])
        kb.cp('pool', M3[:, 256:384], SU_f, r=["C"], w=["M3"])
        M2 = sc.sb("M2", [128, 256])
        kb.cp('pool', M2[:, 0:128], UI_f, r=["C"], w=["M2"])
        kb.cp('pool', M2[:, 128:256], UI_f, r=["C"], w=["M2"])
        hT = sc.sb("hT0", [128, 16, 512], BF16)
        halo = sc.sb("halo", [128, 10])
        kb.memset('pool', halo[:], 0.0, w=["halo"])
        rawt = [sc.sb(f"rawt{i}", [128, 513]) for i in range(2)]
        dlt = [sc.sb(f"dlt{i}", [128, 512]) for i in range(2)]
        sh = [sc.sb(f"sh{q}", [128, 512]) for q in range(10)]
        tw = sc.sb("tw", [128, 512], BF16)
        ad_bf = sc.sb("ad_bf", [128, 512], BF16)
        sg = [sc.sb(f"sg{i}", [128, 512], BF16) for i in range(2)]
        names = ["a", "kkraw", "sq", "rn", "kk", "kmul", "kh", "bb", "lw", "cwm", "Winv", "Wexc", "tmpw"]
        T = {n: sc.sb("rw_" + n, [128, 512]) for n in names}
        for alias, base in (("prod", "tmpw"), ("yc", "kkraw"), ("sq2", "sq"), ("rs2", "rn"), ("yn", "kmul")):
            T[alias] = T[base]
        bdn = ["Abd", "Bbd", "Kbd", "Rbd", "Vbd"]
        BDc = [{n: sc.sb(f"{n}{ct}", [128, 8, 128], BF16) for n in bdn} for ct in range(2)]
        for ct in range(2):
            for n in bdn:
                kb.memset('pool', BDc[ct][n][:], 0.0, w=[f"{n}{ct}"])
        Tc = [{n: sc.sb(f"rw_{n}{ct}", [128, 512]) for n in ("Winc", "bv", "g_sb", "yraw")} for ct in range(2)]
        II = sc.sb("II", [128, 128], BF16)
        kb.cp('dve', II[:], ident_f, r=["C"], w=["II"])
        tm_bf_c = [sc.sb(f"tm_bf{ct}", [128, 384], BF16) for ct in range(2)]
        gr_bf_c = [sc.sb(f"gr_bf{ct}", [128, 384], BF16) for ct in range(2)]
        pr_bf_c = [sc.sb(f"pr_bf{ct}", [128, 256], BF16) for ct in range(2)]
        pw_bf_c = [[sc.sb(f"pw_bf{ct}_{i}", [128, 256], BF16) for i in range(5)] for ct in range(2)]
        G_bf_c = [[sc.sb(f"G_bf{ct}_{i}", [128, 128], BF16) for i in range(2)] for ct in range(2)]
        Xn_bf_c = [sc.sb(f"Xn_bf{ct}", [128, 128], BF16) for ct in range(2)]
        U_bf_c = [sc.sb(f"U_bf{ct}", [128, 128], BF16) for ct in range(2)]
        ST_f = [sc.sb(f"ST_f{ct}", [128, 128]) for ct in range(2)]
        STt_c = [sc.sb(f"STt{ct}", [128, 128]) for ct in range(2)]
        STb = [sc.sb(f"STb{ct}", [128, 128], BF16) for ct in range(2)]
        yst = [sc.sb(f"wyst{i}", [128, 512], BF16) for i in range(2)]
        for ct in range(2):
            kb.memset('pool', ST_f[ct][:], 0.0, w=[f"ST_f{ct}"])
            kb.memset('pool', STb[ct][:], 0.0, w=[f"STb{ct}"])
        v3 = lambda ap: ap.rearrange("p (c t) -> p c t", c=8)
        for m in range(NM):
            load_hT(hT, m, "hT0")
            for q in range(10):
                sz = sizes[q]
                pr = f"pb{q % 2}"
                pa = pb[q % 2]
                for k in range(16):
                    kb.mm(pa[0:sz, :], lhsT=Wr[:, k, offs[q]:offs[q] + sz], rhs=hT[:, k, :], start=(k == 0),
                          stop=(k == 15), r=["Wr", "hT0"], w=[pr])
                rw_ = rawt[q % 2]
                rn_ = f"rawt{q % 2}"
                kb.cp('pool', rw_[0:sz, 0:1], halo[0:sz, q:q + 1], r=["halo"], w=[rn_])
                kb.cp('act', rw_[0:sz, 1:513], pa[0:sz, :], r=[pr], w=[rn_])
                kb.cp('pool', halo[0:sz, q:q + 1], rw_[0:sz, 512:513], r=[rn_], w=["halo"])
                d_ = dlt[q % 2]
                dn_ = f"dlt{q % 2}"
                kb.tt('pool', d_[0:sz, :], rw_[0:sz, 0:512], rw_[0:sz, 1:513], ALU.subtract, r=[rn_], w=[dn_])
                kb.stt('dve', sh[q][0:sz, :], d_[0:sz, :], mu[0:sz, q:q + 1], rw_[0:sz, 1:513], ALU.mult, ALU.add,
                       r=[dn_, "mu", rn_], w=[f"sh{q}"])
            kb.act(tw[0:96, :], sh[6][0:96, :], AF.Tanh, r=["sh6"], w=["tw"])
            kb.cp('pool', ad_bf[0:96, :], sh[7][0:96, :], r=["sh7"], w=["ad_bf"])
            for i in range(2):
                kb.act(sg[i][:], sh[8 + i][:], AF.Sigmoid, r=[f"sh{8 + i}"], w=[f"sg{i}"])
            for ct in range(2):
                cts = slice(ct * 128, (ct + 1) * 128)
                r_, k_, v_ = sh[ct], sh[2 + ct], sh[4 + ct]
                rn_r, rn_k, rn_v = f"sh{ct}", f"sh{2 + ct}", f"sh{4 + ct}"
                kb.mm(pb[2][:, :], lhsT=w2_bf[:, cts], rhs=tw[0:96, :], start=True, stop=True, r=["w2_bf", "tw"],
                      w=["pb2"])
                kb.act(T["lw"][:], pb[2][:, :], AF.Sigmoid, r=["pb2", "vec"], w=["rw_lw"], bias=vec[:, ct, 0:1])
                kb.ts('dve', T["lw"][:], T["lw"][:], -0.6065306597126334, ALU.mult, r=["rw_lw"], w=["rw_lw"])
                kb.mm(pb[3][:, :], lhsT=a2_bf[:, cts], rhs=ad_bf[0:96, :], start=True, stop=True, r=["a2_bf", "ad_bf"],
                      w=["pb3"])
                kb.act(T["a"][:], pb[3][:, :], AF.Sigmoid, r=["pb3", "vec"], w=["rw_a"], bias=vec[:, ct, 1:2])
                for i in range(2):
                    kb.mm(pb[2][:, :], lhsT=g2_bf[:, i, cts], rhs=sg[i][:], start=(i == 0), stop=(i == 1),
                          r=["g2_bf", f"sg{i}"], w=["pb2"])
                kb.cp('act', Tc[ct]["g_sb"][:], pb[2][:, :], r=["pb2"], w=[f"rw_g_sb{ct}"])
                kb.ts('pool', T["kkraw"][:], k_[:], vec[:, ct, 2:3], ALU.mult, r=[rn_k, "vec"], w=["rw_kkraw"])
                kb.tt('pool', T["sq"][:], T["kkraw"][:], T["kkraw"][:], ALU.mult, r=["rw_kkraw"], w=["rw_sq"])
                kb.mm(pb[3][:, :], lhsT=blk_f, rhs=T["sq"][:], start=True, stop=True, r=["C", "rw_sq"], w=["pb3"])
                kb.act(T["rn"][:], pb[3][:, :], AF.Sqrt, r=["pb3"], w=["rw_rn"])
                kb.ts('dve', T["rn"][:], T["rn"][:], 1e-12, ALU.max, r=["rw_rn"], w=["rw_rn"])
                kb.recip(T["rn"][:], T["rn"][:], r=["rw_rn"], w=["rw_rn"])
                kb.tt('pool', T["kk"][:], T["kkraw"][:], T["rn"][:], ALU.mult, r=["rw_kkraw", "rw_rn"], w=["rw_kk"])
                kb.ts('dve', T["kmul"][:], T["a"][:], vec[:, ct, 3:4], ALU.mult, r=["rw_a", "vec", "omka"],
                      w=["rw_kmul"], s2=omka[:, ct:ct + 1], op1=ALU.add)
                kb.tt('pool', T["kh"][:], k_[:], T["kmul"][:], ALU.mult, r=[rn_k, "rw_kmul"], w=["rw_kh"])
                kb.tt('pool', T["bb"][:], T["kk"][:], T["a"][:], ALU.mult, r=["rw_kk", "rw_a"], w=["rw_bb"])
                kb.scan(T["cwm"][:], rmask, T["lw"][:], r=["C", "rw_lw"], w=["rw_cwm"])
                kb.act(Tc[ct]["Winc"][:], T["cwm"][:], AF.Exp, r=["rw_cwm"], w=[f"rw_Winc{ct}"])
                kb.act(T["Winv"][:], T["cwm"][:], AF.Exp, r=["rw_cwm"], w=["rw_Winv"], scale=-1.0)
                kb.tt('pool', T["tmpw"][:], T["cwm"][:], T["lw"][:], ALU.subtract, r=["rw_cwm", "rw_lw"], w=["rw_tmpw"])
                kb.act(T["Wexc"][:], T["tmpw"][:], AF.Exp, r=["rw_tmpw"], w=["rw_Wexc"])
                pairs = (("Abd", T["kk"], "rw_kk", T["Wexc"], "rw_Wexc"), ("Bbd", T["bb"], "rw_bb", T["Winv"], "rw_Winv"),
                         ("Kbd", T["kh"], "rw_kh", T["Winv"], "rw_Winv"), ("Rbd", r_, rn_r, Tc[ct]["Winc"], f"rw_Winc{ct}"))
                ei = 0
                for (bn, a0, an0, a1, an1) in pairs:
                    for hh in range(2):
                        ph = slice(hh * 64, (hh + 1) * 64)
                        kb.tt('dve' if ei % 2 == 0 else 'pool', BDc[ct][bn][ph, :, hh * 64:(hh + 1) * 64], v3(a0[ph, :]),
                              v3(a1[ph, :]), ALU.mult, r=[an0, an1, bn + str(ct)], w=[bn + str(ct)])
                        ei += 1
                for hh in range(2):
                    ph = slice(hh * 64, (hh + 1) * 64)
                    kb.cp('pool', BDc[ct]["Vbd"][ph, :, hh * 64:(hh + 1) * 64], v3(v_[ph, :]), r=[rn_v, f"Vbd{ct}"], w=[f"Vbd{ct}"])
                kb.stt('dve', T["prod"][:], r_[:], vec[:, ct, 4:5], T["kh"][:], ALU.mult, ALU.mult,
                       r=[rn_r, "vec", "rw_kh"], w=["rw_tmpw"])
                kb.mm(pb[3][:, :], lhsT=blk_f, rhs=T["prod"][:], start=True, stop=True, r=["C", "rw_tmpw"], w=["pb3"])
                kb.tt('dve', Tc[ct]["bv"][:], pb[3][:, :], v_[:], ALU.mult, r=["pb3", rn_v], w=[f"rw_bv{ct}"])
            caps = []
            for ct in range(2):
                kb.tr.begin_capture()
                PBK = pb[4:8] if ct == 0 else pb[0:4]
                PBN = [f"pb{4 + k_}" for k_ in range(4)] if ct == 0 else [f"pb{k_}" for k_ in range(4)]
                STf, STn = ST_f[ct], f"ST_f{ct}"
                STbf, STbn = STb[ct], f"STb{ct}"
                for c in range(8):
                    A_c, B_c, K_c, R_c, V_c = (BDc[ct][n][:, c, :] for n in bdn)
                    for i, (X_c, xn) in enumerate(((B_c, f"Bbd{ct}"), (K_c, f"Kbd{ct}"), (V_c, f"Vbd{ct}"))):
                        kb.mm(PBK[0][:, i * 128:(i + 1) * 128], lhsT=X_c, rhs=ident_b, start=True, stop=True,
                              r=[xn, "Cb"], w=[PBN[0]])
                    kb.cp('act', tm_bf_c[ct][:], PBK[0][:, 0:384], r=[PBN[0]], w=[f"tm_bf{ct}"])
                    btm, ktm, vtm = tm_bf_c[ct][:, 0:128], tm_bf_c[ct][:, 128:256], tm_bf_c[ct][:, 256:384]
                    kb.mm(PBK[1][:, 0:128], lhsT=B_c, rhs=A_c, start=True, stop=True, r=[f"Bbd{ct}", f"Abd{ct}"], w=[PBN[1]])
                    kb.mm(PBK[1][:, 128:256], lhsT=A_c, rhs=B_c, start=True, stop=True, r=[f"Bbd{ct}", f"Abd{ct}"], w=[PBN[1]])
                    kb.mm(PBK[1][:, 256:384], lhsT=K_c, rhs=A_c, start=True, stop=True, r=[f"Kbd{ct}", f"Abd{ct}"], w=[PBN[1]])
                    kb.tt('dve', gr_bf_c[ct][:], PBK[1][:, 0:384], M3[:], ALU.mult, r=[PBN[1], "M3"], w=[f"gr_bf{ct}"])
                    kb.mm(PBK[2][:, 0:128], lhsT=B_c, rhs=R_c, start=True, stop=True, r=[f"Bbd{ct}", f"Rbd{ct}"], w=[PBN[2]])
                    kb.mm(PBK[2][:, 128:256], lhsT=K_c, rhs=R_c, start=True, stop=True, r=[f"Kbd{ct}", f"Rbd{ct}"], w=[PBN[2]])
                    kb.tt('dve', pr_bf_c[ct][:], PBK[2][:, 0:256], M2[:], ALU.mult, r=[PBN[2], "M2"], w=[f"pr_bf{ct}"])
                    Nn, Tt, TakT = gr_bf_c[ct][:, 0:128], gr_bf_c[ct][:, 128:256], gr_bf_c[ct][:, 256:384]
                    PrbT, PrkT = pr_bf_c[ct][:, 0:128], pr_bf_c[ct][:, 128:256]
                    kb.tt('pool', G_bf_c[ct][0][:], II[:], Nn, ALU.subtract, r=["II", f"gr_bf{ct}"], w=[f"G_bf{ct}_0"])
                    gcur = 0
                    Ncur, Tcur, ncn = Nn, Tt, f"gr_bf{ct}"
                    for lv in range(5):
                        kb.mm(PBK[2][:, 256:384], lhsT=Tcur, rhs=Ncur, start=True, stop=True, r=[ncn], w=[PBN[2]])
                        kb.mm(PBK[2][:, 384:512], lhsT=Ncur, rhs=Tcur, start=True, stop=True, r=[ncn], w=[PBN[2]])
                        kb.cp('act', pw_bf_c[ct][lv][:], PBK[2][:, 256:512], r=[PBN[2]], w=[f"pw_bf{ct}_{lv}"])
                        Ncur, Tcur, ncn = pw_bf_c[ct][lv][:, 0:128], pw_bf_c[ct][lv][:, 128:256], f"pw_bf{ct}_{lv}"
                        kb.mm(PBK[0][:, 384:512], lhsT=Tcur, rhs=G_bf_c[ct][gcur][:], start=True, stop=True,
                              r=[ncn, f"G_bf{ct}_{gcur}"], w=[PBN[0]])
                        kb.tt('dve', G_bf_c[ct][1 - gcur][:], PBK[0][:, 384:512], G_bf_c[ct][gcur][:], ALU.add,
                              r=[PBN[0], f"G_bf{ct}_{gcur}"], w=[f"G_bf{ct}_{1 - gcur}"])
                        gcur = 1 - gcur
                    Gf, Gn = G_bf_c[ct][gcur], f"G_bf{ct}_{gcur}"
                    kb.mm(PBK[3][:, 0:128], lhsT=A_c, rhs=STbf[:], start=True, stop=False, r=[f"Abd{ct}", STbn], w=[PBN[3]])
                    kb.mm(PBK[3][:, 0:128], lhsT=TakT, rhs=vtm, start=False, stop=True, r=[f"gr_bf{ct}", f"tm_bf{ct}"], w=[PBN[3]])
                    kb.act(Xn_bf_c[ct][:], PBK[3][:, 0:128], AF.Copy, r=[PBN[3]], w=[f"Xn_bf{ct}"], scale=-1.0)
                    kb.mm(PBK[3][:, 128:256], lhsT=Gf[:], rhs=Xn_bf_c[ct][:], start=True, stop=True, r=[Gn, f"Xn_bf{ct}"], w=[PBN[3]])
                    kb.cp('act', U_bf_c[ct][:], PBK[3][:, 128:256], r=[PBN[3]], w=[f"U_bf{ct}"])
                    kb.mm(PBK[3][:, 256:384], lhsT=STbf[:], rhs=R_c, start=True, stop=False, r=[STbn, f"Rbd{ct}"], w=[PBN[3]])
                    kb.mm(PBK[3][:, 256:384], lhsT=U_bf_c[ct][:], rhs=PrbT, start=False, stop=False, r=[f"U_bf{ct}", f"pr_bf{ct}"],
                          w=[PBN[3]])
                    kb.mm(PBK[3][:, 256:384], lhsT=vtm, rhs=PrkT, start=False, stop=True, r=[f"tm_bf{ct}", f"pr_bf{ct}"], w=[PBN[3]])
                    for hh in range(2):
                        ph = slice(hh * 64, (hh + 1) * 64)
                        kb.cp('act', Tc[ct]["yraw"][ph, c * 64:(c + 1) * 64], PBK[3][ph, 256 + hh * 64:256 + (hh + 1) * 64],
                              r=[PBN[3]], w=[f"rw_yraw{ct}"])
                    kb.mm(PBK[3][:, 384:512], lhsT=btm, rhs=U_bf_c[ct][:], start=True, stop=False, r=[f"tm_bf{ct}", f"U_bf{ct}"], w=[PBN[3]])
                    kb.mm(PBK[3][:, 384:512], lhsT=ktm, rhs=vtm, start=False, stop=True, r=[f"tm_bf{ct}"], w=[PBN[3]])
                    WL = Tc[ct]["Winc"][:, c * 64 + 63:c * 64 + 64]
                    kb.ts('pool', STt_c[ct][:], STf[:], WL, ALU.mult, r=[STn, f"rw_Winc{ct}"], w=[f"STt{ct}"])
                    kb.stt('dve', STf[:], PBK[3][:, 384:512], WL, STt_c[ct][:], ALU.mult, ALU.add, r=[PBN[3], f"rw_Winc{ct}", f"STt{ct}"],
                           w=[STn])
                    kb.cp('pool', STbf[:], STf[:], r=[STn], w=[STbn])
                caps.append(kb.tr.end_capture())
            kb.tr.replay_interleaved(caps)
            for ct in range(2):
                kb.mm(pb[2][:, :], lhsT=blk64_f, rhs=Tc[ct]["yraw"][:], start=True, stop=True, r=["C", f"rw_yraw{ct}"], w=["pb2"])
                kb.tt('dve', T["yc"][:], Tc[ct]["yraw"][:], pb[2][:, :], ALU.subtract, r=[f"rw_yraw{ct}", "pb2"], w=["rw_kkraw"])
                kb.tt('pool', T["sq2"][:], T["yc"][:], T["yc"][:], ALU.mult, r=["rw_kkraw"], w=["rw_sq"])
                kb.mm(pb[3][:, :], lhsT=blk64_f, rhs=T["sq2"][:], start=True, stop=True, r=["C", "rw_sq"], w=["pb3"])
                kb.act(T["rs2"][:], pb[3][:, :], AF.Sqrt, r=["pb3"], w=["rw_rn"], bias=64e-5)
                kb.recip(T["rs2"][:], T["rs2"][:], r=["rw_rn"], w=["rw_rn"])
                kb.tt('pool', T["yn"][:], T["yc"][:], T["rs2"][:], ALU.mult, r=["rw_kkraw", "rw_rn"], w=["rw_kmul"])
                kb.ts('dve', T["yn"][:], T["yn"][:], vec[:, ct, 5:6], ALU.mult, r=["rw_kmul", "vec"], w=["rw_kmul"],
                      s2=vec[:, ct, 6:7], op1=ALU.add)
                kb.tt('pool', T["yn"][:], T["yn"][:], Tc[ct]["bv"][:], ALU.add, r=["rw_kmul", f"rw_bv{ct}"], w=["rw_kmul"])
                ys, ysn = yst[ct], f"wyst{ct}"
                kb.tt('dve', ys[:], T["yn"][:], Tc[ct]["g_sb"][:], ALU.mult, r=["rw_kmul", f"rw_g_sb{ct}"], w=[ysn])
                kb.dma(yT_ap(256 + ct * 128, 128, m), ys[:], ysn, r=[ysn], w=["yT_w"])
                if ct == 1:
                    after_y(1, m, "yT_w")
        sc.close()

    outs = ["yT_m", "yT_r", "yT_w"]
    if env is not None:
        return None
    return kb, outs


def finish_p1(kb, outs):
    return kb.finish(outs)


import numpy as np


FF = 5504
NJ = 43


P2_PARAM_SHAPES = {
    "w_ada": ([D, 6 * D], F32), "b_ada": ([1, 6 * D], F32), "ng_col": ([128, 64], F32), "norm_g": ([4, D], F32),
    "w_gate": ([3, D, D], F32), "w_branch": ([3, 1024, D], F32), "w_out": ([D, D], F32), "w_up": ([D, 2 * FF], F32),
    "f_cv": ([128, 86, 4], F32), "w_down": ([FF, D], F32),
}


def p2_scratch(kb, sfx=""):
    return {
        'Wg_d': kb.dscratch("Wg_d" + sfx, [3, 16, 128, 16, 128], BF16),
        'Wb_d': kb.dscratch("Wb_d" + sfx, [3, 16, 128, 8, 128], BF16),
        'Wo_d': kb.dscratch("Wo_d" + sfx, [16, 128, D], BF16),
        'Wu_d': kb.dscratch("Wu_d" + sfx, [NJ, 128, 16, 256], BF16),
        'Wd_d': kb.dscratch("Wd_d" + sfx, [NJ, 128, D], BF16),
    }


def build_p2(Tq, env=None):
    NT = 128 + Tq
    if env is None:
        kb = KB()
        e = {}
        xh = kb.din("xh", [NT, D])
        yTh = kb.din("yTh", [3, 1024, NT], BF16)
        e['hmask'] = kb.din("hmask", [128, 1])
        e['c_col'] = kb.din("c_col", [128, 16])
        for k, (shp, dt_) in P2_PARAM_SHAPES.items():
            e[k] = kb.din(k, shp, dt_)
        cst = kb.din("cst", [128, 2048])
        xo = kb.dout("xo", [Tq, D])
        e.update(p2_scratch(kb))
        G = setup_globals(kb, cst)

        def load_x(dst, rn, row0, is_halo):
            kb.dma(dst, xh[row0:row0 + 128, :], rn, w=[rn])

        def load_y(ybf, t0, ntok, is_halo):
            kb.dma(ybf[:, :, :, 0:ntok], yTh[:, :, t0:t0 + ntok].rearrange("i (k p) t -> p i k t", p=128), "ybf",
                   w=["ybf"])

        def store_x(src, rn, o0):
            kb.dma(xo[o0:o0 + 128, :], src, rn, r=[rn], w=["xo"])
        e['load_x'], e['load_y'], e['store_x'] = load_x, load_y, store_x
    else:
        kb = env['kb']
        e = env
        G = env['G']
    nc = kb.nc
    hmask, c_col, w_ada, b_ada, ng_col, norm_g = (e[k] for k in ('hmask', 'c_col', 'w_ada', 'b_ada', 'ng_col', 'norm_g'))
    w_gate, w_branch, w_out, w_up, f_cv, w_down = (e[k] for k in ('w_gate', 'w_branch', 'w_out', 'w_up', 'f_cv', 'w_down'))
    Wg_d, Wb_d, Wo_d, Wu_d, Wd_d = (e[k] for k in ('Wg_d', 'Wb_d', 'Wo_d', 'Wu_d', 'Wd_d'))
    load_x, load_y, store_x = e['load_x'], e['load_y'], e['store_x']
    pb = G['pb']
    C, Cb = G['C'], G['Cb']
    ident_f = C[:, 0:128]
    ones_row = C[0:1, 256:384]
    ident_b = Cb[:, 0:128]

    sc = Scope(kb)
    NSTG = 6
    stg = [sc.sb(f"stg{i}", [128, 2048]) for i in range(NSTG)]
    stb = [sc.sb(f"stb{i}", [128, 2048], BF16) for i in range(NSTG)]
    cnt = [0]

    def cast_block(src_ap, dst_ap, shape):
        i = cnt[0] % NSTG
        cnt[0] += 1
        n = int(np.prod(shape))
        sv = stg[i][:, 0:n]
        bv = stb[i][:, 0:n]
        if len(shape) == 2:
            sv = sv.rearrange("p (a b) -> p a b", a=shape[0])
            bv = bv.rearrange("p (a b) -> p a b", a=shape[0])
        kb.dma(sv, src_ap, f"stg{i}", w=[f"stg{i}"])
        eng = ('dve', 'act', 'dve', 'act', 'pool', 'dve')[i]
        kb.cp(eng, stb[i][:, 0:n], stg[i][:, 0:n], r=[f"stg{i}"], w=[f"stb{i}"])
        kb.dma(dst_ap, bv, f"stb{i}", r=[f"stb{i}"], w=[f"Wscr{cnt[0]}"])

    for i in range(3):
        wv = w_gate[i].rearrange("(k p) n -> p k n", p=128)
        for nt in range(16):
            cast_block(wv[:, :, nt * 128:(nt + 1) * 128], Wg_d[i, nt], (16, 128))
        wv = w_branch[i].rearrange("(k p) n -> p k n", p=128)
        for nt in range(16):
            cast_block(wv[:, :, nt * 128:(nt + 1) * 128], Wb_d[i, nt], (8, 128))
    for kn in range(16):
        cast_block(w_out[kn * 128:(kn + 1) * 128, :], Wo_d[kn], (2048,))
    wv = w_up.rearrange("(k p) n -> p k n", p=128)
    for j in range(NJ):
        cast_block(wv[:, :, j * 128:(j + 1) * 128], Wu_d[j, :, :, 0:128], (16, 128))
        cast_block(wv[:, :, FF + j * 128:FF + (j + 1) * 128], Wu_d[j, :, :, 128:256], (16, 128))
        cast_block(w_down[j * 128:(j + 1) * 128, :], Wd_d[j], (2048,))
    sc.close()

    scm = Scope(kb)
    GTm = scm.sb("GTm", [128, 2048])
    GTf = scm.sb("GTf", [128, 2048])
    cols = scm.sb("cols", [128, 64])
    ngc = scm.sb("ngc", [128, 64])
    kb.dma(ngc[:], ng_col, "ngc", w=["ngc"])
    sc = Scope(kb)
    rows = emit_mod(kb, sc, c_col, w_ada, b_ada, [0, 1, 2, 3, 4, 5], ones_row, pb, "m2")
    kb.dma(GTm[:], norm_g[1:2, :].partition_broadcast(128), "GTm", w=["GTm"])
    kb.dma(GTf[:], norm_g[3:4, :].partition_broadcast(128), "GTf", w=["GTf"])

    def mk_gt(dst, dn):
        def f(c, pa, pr):
            kb.tt('dve', dst[:, c * 512:(c + 1) * 512], pa, dst[:, c * 512:(c + 1) * 512], ALU.mult, r=[pr, dn], w=[dn])
        return f

    bcast_row(kb, rows[2][0], rows[2][1], ones_row, pb, mk_gt(GTm, "GTm"))
    bcast_row(kb, rows[5][0], rows[5][1], ones_row, pb, mk_gt(GTf, "GTf"))
    for slot, vid in enumerate((1, 0, 4, 3)):
        row, rn = rows[vid]
        for k in range(16):
            kb.mm(pb[3][:, slot * 16 + k:slot * 16 + k + 1], lhsT=row[0:1, k * 128:(k + 1) * 128],
                  rhs=ones_row[0:1, 0:1], start=True, stop=True, r=[rn, "C"], w=["pb3"])
    kb.cp('act', cols[:], pb[3][:, 0:64], r=["pb3"], w=["cols"])
    for slot, gsl in ((0, 0), (2, 2)):
        kb.stt('dve', cols[:, slot * 16:(slot + 1) * 16], cols[:, slot * 16:(slot + 1) * 16], 1.0,
               ngc[:, gsl * 16:(gsl + 1) * 16], ALU.add, ALU.mult, r=["cols", "ngc"], w=["cols"])
    sc.close()

    xres = [scm.sb(f"xres{i}", [128, 2048]) for i in range(2)]
    htmp = scm.sb("htmp", [128, 2048])
    hb = scm.sb("hb", [128, 2048], BF16)
    ss = scm.sb("ss", [128, 1])
    rstd = scm.sb("rstd", [128, 1])
    hT = scm.sb("hT", [128, 16, 256], BF16)
    ybf = scm.sb("ybf", [128, 3, 8, 256], BF16)
    mT = scm.sb("mT", [128, 16, 256], BF16)
    sig = [scm.sb(f"sig{i}", [128, 256]) for i in range(2)]
    macc = scm.sb("macc", [128, 256])
    mtmp = scm.sb("mtmp", [128, 256])
    Wg_s = [scm.sb(f"Wg_s{i}", [128, 16, 128], BF16) for i in range(3)]
    Wb_s = [scm.sb(f"Wb_s{i}", [128, 8, 128], BF16) for i in range(3)]
    Wo_s = [scm.sb(f"Wo_s{i}", [128, 2048], BF16) for i in range(2)]
    Wu_s = [scm.sb(f"Wu_s{i}", [128, 16, 256], BF16) for i in range(2)]
    Wd_s = [scm.sb(f"Wd_s{i}", [128, 2048], BF16) for i in range(2)]
    ymix = scm.sb("ymix", [128, 2048])
    rawf = [scm.sb(f"rawf{i}", [128, 258]) for i in range(2)]
    facc = [scm.sb(f"facc{i}", [128, 256]) for i in range(2)]
    gg = scm.sb("gg", [128, 256])
    aT = scm.sb("aT", [128, NJ, 256], BF16)
    fhalo = scm.sb("fhalo", [128, 86, 2])
    fc = scm.sb("fc", [128, 86, 4])
    hm = scm.sb("hm", [128, 1])
    kb.dma(fc[:], f_cv, "fc", w=["fc"])
    kb.dma(hm[:], hmask, "hm", w=["hm"])
    kb.memset('pool', fhalo[:], 0.0, w=["fhalo"])
    if 'extra_alloc' in e:
        e['extra_alloc'](scm, dict(ymix=ymix))
    slab_ctr = {"g": 0, "b": 0, "o": 0, "u": 0, "d": 0}

    def emit_h(nsub, ntok, gslot):
        for sub in range(nsub):
            xn = f"xres{sub}"
            kb.act(htmp[:], xres[sub][:], AF.Square, r=[xn], w=["htmp", "ss"], accum=ss[:])
            kb.act(rstd[:], ss[:], AF.Sqrt, r=["ss"], w=["rstd"], bias=1e-6, scale=1.0 / D)
            kb.recip(rstd[:], rstd[:], r=["rstd"], w=["rstd"])
            kb.ts('dve', hb[:], xres[sub][:], rstd[:, 0:1], ALU.mult, r=[xn, "rstd"], w=["hb"])
            for q in range(4):
                bank = 6 + (q % 2)
                for kk in range(4):
                    k = q * 4 + kk
                    kb.mm(pb[bank][:, kk * 128:(kk + 1) * 128], lhsT=hb[:, k * 128:(k + 1) * 128], rhs=ident_b,
                          start=True, stop=True, r=["hb", "Cb"], w=[f"pb{bank}"])
                for kk in range(4):
                    k = q * 4 + kk
                    gcol = cols[:, gslot * 16 + k:gslot * 16 + k + 1]
                    scol = cols[:, (gslot + 1) * 16 + k:(gslot + 1) * 16 + k + 1]
                    if kk % 2 == 0:
                        kb.act(hT[:, k, sub * 128:(sub + 1) * 128], pb[bank][:, kk * 128:(kk + 1) * 128], AF.Identity,
                               r=[f"pb{bank}", "cols"], w=["hT"], bias=scol, scale=gcol)
                    else:
                        kb.ts('dve', hT[:, k, sub * 128:(sub + 1) * 128], pb[bank][:, kk * 128:(kk + 1) * 128], gcol,
                              ALU.mult, r=[f"pb{bank}", "cols"], w=["hT"], s2=scol, op1=ALU.add)

    def post_norm(nsub_i, src_ps_banks, GT, gtn, sub, store_ap):
        xn = f"xres{sub}"
        kb.act(htmp[:], ymix[:], AF.Square, r=["ymix"], w=["htmp", "ss"], accum=ss[:])
        kb.act(rstd[:], ss[:], AF.Sqrt, r=["ss"], w=["rstd"], bias=1e-6, scale=1.0 / D)
        kb.recip(rstd[:], rstd[:], r=["rstd"], w=["rstd"])
        kb.stt('dve', htmp[:], ymix[:], rstd[:, 0:1], GT[:], ALU.mult, ALU.mult, r=["ymix", "rstd", gtn], w=["htmp"])
        kb.tt('pool', xres[sub][:], xres[sub][:], htmp[:], ALU.add, r=[xn, "htmp"], w=[xn])
        if store_ap is not None:
            store_x(xres[sub][:], xn, store_ap)

    def tile(t0, ntok, is_halo):
        nsub = ntok // 128
        tsl = slice(0, ntok)
        for sub in range(nsub):
            xn = f"xres{sub}"
            load_x(xres[sub][:], xn, t0 + sub * 128, is_halo)
        load_y(ybf, t0, ntok, is_halo)
        emit_h(nsub, ntok, 0)
        for nt in range(16):
            for i in range(3):
                gi = slab_ctr["g"] % 3
                slab_ctr["g"] += 1
                kb.dma(Wg_s[gi][:], Wg_d[i, nt], f"Wg_s{gi}", w=[f"Wg_s{gi}"])
                kb.dma(Wb_s[gi][:], Wb_d[i, nt], f"Wb_s{gi}", w=[f"Wb_s{gi}"])
                for k in range(16):
                    kb.mm(pb[4][:, tsl], lhsT=Wg_s[gi][:, k, :], rhs=hT[:, k, tsl], start=(k == 0), stop=(k == 15),
                          r=[f"Wg_s{gi}", "hT"], w=["pb4"])
                for k in range(8):
                    kb.mm(pb[5][:, tsl], lhsT=Wb_s[gi][:, k, :], rhs=ybf[:, i, k, tsl], start=(k == 0), stop=(k == 7),
                          r=[f"Wb_s{gi}", "ybf"], w=["pb5"])
                sg_ = sig[i % 2]
                sgn_ = f"sig{i % 2}"
                kb.act(sg_[:, tsl], pb[4][:, tsl], AF.Sigmoid, r=["pb4"], w=[sgn_])
                if i == 0:
                    kb.tt('dve', macc[:, tsl], pb[5][:, tsl], sg_[:, tsl], ALU.mult, r=["pb5", sgn_], w=["macc"])
                else:
                    kb.tt('dve', mtmp[:, tsl], pb[5][:, tsl], sg_[:, tsl], ALU.mult, r=["pb5", sgn_], w=["mtmp"])
                    if i == 1:
                        kb.tt('pool', macc[:, tsl], macc[:, tsl], mtmp[:, tsl], ALU.add, r=["macc", "mtmp"], w=["macc"])
                    else:
                        kb.tt('pool', mT[:, nt, tsl], macc[:, tsl], mtmp[:, tsl], ALU.add, r=["macc", "mtmp"],
                              w=["mT"])
        for sub in range(nsub):
            for kn in range(16):
                oi = slab_ctr["o"] % 2
                slab_ctr["o"] += 1
                kb.dma(Wo_s[oi][:], Wo_d[kn], f"Wo_s{oi}", w=[f"Wo_s{oi}"])
                for c in range(4):
                    kb.mm(pb[c][:, :], lhsT=mT[:, kn, sub * 128:(sub + 1) * 128], rhs=Wo_s[oi][:, c * 512:(c + 1) * 512],
                          start=(kn == 0), stop=(kn == 15), r=["mT", f"Wo_s{oi}"], w=[f"pb{c}"])
            for c in range(4):
                kb.cp('act', ymix[:, c * 512:(c + 1) * 512], pb[c][:, :], r=[f"pb{c}"], w=["ymix"])
            post_norm(nsub, None, GTm, "GTm", sub, None)
        emit_h(nsub, ntok, 2)
        for j in range(NJ):
            ui = slab_ctr["u"] % 2
            slab_ctr["u"] += 1
            kb.dma(Wu_s[ui][:], Wu_d[j], f"Wu_s{ui}", w=[f"Wu_s{ui}"])
            for half in range(2):
                bank = 4 + half
                jj = half * NJ + j
                for k in range(16):
                    kb.mm(pb[bank][:, tsl], lhsT=Wu_s[ui][:, k, half * 128:(half + 1) * 128], rhs=hT[:, k, tsl],
                          start=(k == 0), stop=(k == 15), r=[f"Wu_s{ui}", "hT"], w=[f"pb{bank}"])
                rw_ = rawf[half]
                rn_ = f"rawf{half}"
                kb.cp('pool', rw_[:, 0:2], fhalo[:, jj, :], r=["fhalo"], w=[rn_])
                kb.cp('act', rw_[:, 2:2 + ntok], pb[bank][:, tsl], r=[f"pb{bank}"], w=[rn_])
                if is_halo:
                    kb.ts('pool', fhalo[:, jj, :], rw_[:, ntok:ntok + 2], hm[:, 0:1], ALU.mult, r=[rn_, "hm"],
                          w=["fhalo"])
                else:
                    kb.cp('pool', fhalo[:, jj, :], rw_[:, ntok:ntok + 2], r=[rn_], w=["fhalo"])
                fa = facc[half]
                fan = f"facc{half}"
                kb.ts('dve', fa[:, tsl], rw_[:, 2:2 + ntok], fc[:, jj, 2:3], ALU.mult, r=[rn_, "fc"], w=[fan],
                      s2=fc[:, jj, 3:4], op1=ALU.add)
                kb.stt('dve', fa[:, tsl], rw_[:, 1:1 + ntok], fc[:, jj, 1:2], fa[:, tsl], ALU.mult, ALU.add,
                       r=[rn_, "fc", fan], w=[fan])
                kb.stt('dve', fa[:, tsl], rw_[:, 0:ntok], fc[:, jj, 0:1], fa[:, tsl], ALU.mult, ALU.add,
                       r=[rn_, "fc", fan], w=[fan])
            if not is_halo:
                kb.act(gg[:, tsl], facc[0][:, tsl], AF.Gelu_apprx_tanh, r=["facc0"], w=["gg"])
                kb.tt('pool', aT[:, j, tsl], gg[:, tsl], facc[1][:, tsl], ALU.mult, r=["gg", "facc1"], w=["aT"])
        if is_halo:
            return
        for sub in range(nsub):
            for j in range(NJ):
                di = slab_ctr["d"] % 2
                slab_ctr["d"] += 1
                kb.dma(Wd_s[di][:], Wd_d[j], f"Wd_s{di}", w=[f"Wd_s{di}"])
                for c in range(4):
                    kb.mm(pb[c][:, :], lhsT=aT[:, j, sub * 128:(sub + 1) * 128], rhs=Wd_s[di][:, c * 512:(c + 1) * 512],
                          start=(j == 0), stop=(j == NJ - 1), r=["aT", f"Wd_s{di}"], w=[f"pb{c}"])
            for c in range(4):
                kb.cp('act', ymix[:, c * 512:(c + 1) * 512], pb[c][:, :], r=[f"pb{c}"], w=["ymix"])
            o0 = t0 - 128 + sub * 128
            post_norm(nsub, None, GTf, "GTf", sub, o0)

    tile(0, 128, True)
    t = 128
    while t < NT:
        tile(t, 256, False)
        t += 256
    scm.close()
    if env is not None:
        return None
    return kb, ["xo"]


import numpy as np


RG = [[0, 1, 2, 3], [4, 5, 6, 7]]


def build_fused(S, depth=2):
    Tq = S // 4
    NM = S // 512
    NMq = NM // 4
    NT = 128 + Tq
    kb = KB()
    cst = kb.din("cst", [128, 2048])
    cstb = kb.din("cstb", [128, 1024])
    c_col = kb.din("c_col", [128, 16])
    pos = kb.din("pos", [1, S], I32)
    xh = kb.din("xh", [NT, D])
    hmask = kb.din("hmask", [128, 1])
    selv = kb.din("selv", [128, 8])
    xo = kb.dout("xo", [Tq, D])
    L = []
    for l in range(depth):
        e = {}
        for k, (shp, dt_) in P1_PARAM_SHAPES.items():
            e[k] = kb.din(f"{k}_{l}", shp, dt_)
        for k, (shp, dt_) in P2_PARAM_SHAPES.items():
            e[k] = kb.din(f"{k}_{l}", shp, dt_)
        L.append(e)
    G = setup_globals(kb, cst)
    CT = min(2048, Tq)
    NCH = S // CT
    MPC = CT // 512
    hT_own = kb.dscratch("hT_own", [NMq, 128, 16, 512], BF16)
    hTg = kb.dscratch("hTg", [2 * NMq, 4 * 64, 8192], BF16)
    ysc = [kb.dscratch(f"ysc{i}", [NCH, 256, CT], BF16) for i in range(3)]
    yall = [kb.dscratch(f"yall{i}", [NCH, 1024, CT], BF16) for i in range(3)]
    xs1 = kb.dscratch("xs1", [Tq, D])
    xlast = kb.dscratch("xlast", [128, D])
    xl_all = kb.dscratch("xl_all", [4 * 128, D])
    w2s = p2_scratch(kb)
    sel = kb.sb("sel", [128, 8])
    kb.dma(sel[:], selv, "sel", w=["sel"])

    def allgather(src2d, dst2d, key, reads, writes):
        kb.tr.dma('pool', lambda e_: e_.collective_compute("AllGather", ALU.bypass, replica_groups=RG,
                                                            ins=[src2d.opt()], outs=[dst2d.opt()]),
                  key, reads=reads, writes=writes, inc=1)

    for l in range(depth):
        P = L[l]
        x_own = xh[128:NT, :] if l == 0 else xs1
        def after_h(m):
            for half in range(2):
                allgather(hT_own[m, half * 64:(half + 1) * 64].rearrange("p k t -> p (k t)"), hTg[2 * m + half], "ag",
                          reads=[f"hTd{m}"], writes=[f"hTg{2 * m + half}"])

        def load_hT(dst, m, key):
            r_, ml = divmod(m, NMq)
            for half in range(2):
                kb.dma(dst[half * 64:(half + 1) * 64, :, :],
                       hTg[2 * ml + half, r_ * 64:(r_ + 1) * 64, :].rearrange("p (k t) -> p k t", k=16),
                       f"{key}_{half}", r=[f"hTg{2 * ml + half}"], w=[key])

        def yT_ap(row0, nrows, m):
            i, rr = divmod(row0, 256)
            c, mm = divmod(m, MPC)
            return ysc[i][c, rr:rr + nrows, mm * 512:(mm + 1) * 512]

        def after_y(i, m, res):
            if (m + 1) % MPC == 0:
                c = m // MPC
                allgather(ysc[i][c], yall[i][c], "ag", reads=[res], writes=[f"yall{i}_{c}"])

        env1 = dict(P)
        env1.update(kb=kb, G=G, x=x_own, NH=Tq // 128, hTo=hT_own, hTd=None, yT=None, c_col=c_col, pos=pos, cstb=cstb,
                    after_h=after_h, load_hT=load_hT, yT_ap=yT_ap, after_y=after_y)
        build_p1(S, parts=('h',), env=env1)
        build_p1(S, parts=('mamba', 'rwkv', 'ret'), env=env1)
        if l > 0:
            allgather(xlast, xl_all, "ag", reads=["xlast"], writes=["xlall"])

        X = {}

        def extra_alloc(scm, bufs, X=X):
            X['cand'] = scm.sb("ycand", [128, 3, 8, 256], BF16)
            X['ymix'] = bufs['ymix']

        def load_x(dst, rn, row0, is_halo, l=l, X=X):
            if l == 0:
                kb.dma(dst, xh[row0:row0 + 128, :], rn, w=[rn])
            elif not is_halo:
                kb.dma(dst, xs1[row0 - 128:row0, :], rn, r=["xs1"], w=[rn])
            else:
                for q in range(4):
                    kb.dma(X['ymix'][:], xl_all[q * 128:(q + 1) * 128, :], "ymix", r=["xlall"], w=["ymix"])
                    if q == 0:
                        kb.ts('dve', dst, X['ymix'][:], sel[:, 4:5], ALU.mult, r=["ymix", "sel"], w=[rn])
                    else:
                        kb.stt('dve', dst, X['ymix'][:], sel[:, 4 + q:5 + q], dst, ALU.mult, ALU.add,
                               r=["ymix", "sel", rn], w=[rn])

        def load_y(ybf, t0, ntok, is_halo, X=X):
            cand = X['cand']
            for q in range(4):
                gt0 = q * Tq + t0 - 128
                if gt0 < 0:
                    kb.memset('pool', cand[:, :, :, 0:ntok], 0.0, w=[f"ycand{g}" for g in range(3)])
                else:
                    c, off = divmod(gt0, CT)
                    for g in range(3):
                        kb.dma(cand[:, g, :, 0:ntok],
                               yall[g][c, :, off:off + ntok].rearrange("(k p) t -> p k t", p=128),
                               f"ycand{g}", r=[f"yall{g}_{c}"], w=[f"ycand{g}"])
                if q == 0:
                    kb.ts('dve', ybf[:, :, :, 0:ntok], cand[:, :, :, 0:ntok], sel[:, 0:1], ALU.mult,
                          r=["ycand0", "ycand1", "ycand2", "sel"], w=["ybf"])
                else:
                    kb.stt('dve', ybf[:, :, :, 0:ntok], cand[:, :, :, 0:ntok], sel[:, q:q + 1], ybf[:, :, :, 0:ntok],
                           ALU.mult, ALU.add, r=["ycand0", "ycand1", "ycand2", "sel", "ybf"], w=["ybf"])

        def store_x(src, rn, o0, l=l):
            if l < depth - 1:
                kb.dma(xs1[o0:o0 + 128, :], src, rn, r=[rn], w=["xs1"])
                if o0 == Tq - 128:
                    kb.dma(xlast, src, rn, r=[rn], w=["xlast"])
            else:
                kb.dma(xo[o0:o0 + 128, :], src, rn, r=[rn], w=["xo"])

        env2 = dict(P)
        env2.update(w2s)
        env2.update(kb=kb, G=G, hmask=hmask, c_col=c_col, load_x=load_x, load_y=load_y, store_x=store_x,
                    extra_alloc=extra_alloc)
        build_p2(Tq, env=env2)
    nc = kb.finish(["xo"])
    return kb, nc


def fused_core_inputs(inp, b, r, S, depth):
    Tq = S // 4
    o = {}
    for l in range(depth):
        p1 = p1_core_inputs(inp, l, b, r)
        for k in P1_PARAM_SHAPES:
            o[f"{k}_{l}"] = p1[k]
        p2 = p2_core_inputs(inp, l, b, r, Tq, None, None)
        for k in P2_PARAM_SHAPES:
            o[f"{k}_{l}"] = p2[k]
    o['cst'] = p1_consts()
    o['cstb'] = p1_core_consts(r)
    o['c_col'] = np.ascontiguousarray(inp['c'][b].reshape(16, 128).T)
    o['pos'] = np.ascontiguousarray(inp['positions'][b][None, :]).astype(np.int32)
    x_b = inp['x'][b]
    xh = np.zeros((128 + Tq, 2048), np.float32)
    if r == 0:
        xh[128:] = x_b[0:Tq]
    else:
        xh[:] = x_b[r * Tq - 128:(r + 1) * Tq]
    o['xh'] = xh
    o['hmask'] = np.full((128, 1), 0.0 if r == 0 else 1.0, np.float32)
    sv = np.zeros((128, 8), np.float32)
    sv[:, r] = 1.0
    if r > 0:
        sv[:, 4 + r - 1] = 1.0
    o['selv'] = sv
    return o


_CACHE = {}


def kernel(**inputs):
    inp = {k: np.asarray(v) for k, v in inputs.items()}
    inp['x'] = np.ascontiguousarray(inp['x'], dtype=np.float32)
    B, S, _ = inp['x'].shape
    depth = inp['w_in'].shape[0]
    Tq = S // 4
    key = (S, depth)
    if key not in _CACHE:
        _CACHE[key] = build_fused(S, depth)[1]
    nc = _CACHE[key]
    in_maps = []
    for core in range(8):
        b, r = divmod(core, 4)
        in_maps.append(fused_core_inputs(inp, b, r, S, depth))
    res = run_bass_kernel_spmd(nc, in_maps, core_ids=list(range(8)))
    out = np.zeros((B, S, 2048), np.float32)
    for core in range(8):
        b, r = divmod(core, 4)
        out[b, r * Tq:(r + 1) * Tq] = np.asarray(res.results[core]['xo'])
    return out
```

```python
import numpy as np
import concourse.bass as bass
import concourse.mybir as mybir
from concourse.bass_utils import run_bass_kernel_spmd

F32 = mybir.dt.float32
BF16 = mybir.dt.bfloat16
I32 = mybir.dt.int32
AF = mybir.ActivationFunctionType
ALU = mybir.AluOpType
AX = mybir.AxisListType
EPOCH = 30000
import os as _os
SAME_ENGINE_ORDERED = tuple(_os.environ.get('SEO', 'pe').split(','))
D = 2048


class Tracker:
    ENGS = ('pe', 'act', 'dve', 'pool', 'sp')

    def __init__(self, nc):
        self.nc = nc
        self.ops = {e: [] for e in self.ENGS}
        self.cur_sem = {}
        self.cnt = {}
        self.nsem = 0
        for e in self.ENGS:
            self.cur_sem[e] = self._new_sem(e)
            self.cnt[e] = 0
        self.known = {e: {} for e in self.ENGS}
        self.last_w = {}
        self.readers = {}
        self.dma_sems = {}
        self.dma_cnt = {}
        self.nops = 0
        self._old_epochs = []

    def _new_sem(self, tag):
        self.nsem += 1
        return self.nc.alloc_semaphore(f"s{self.nsem}_{tag}")

    def _waits_for(self, eng, reads, writes, is_dma):
        need = {}

        def add(ev):
            sem, val, src, src_dma = ev
            if src == eng and eng in SAME_ENGINE_ORDERED and not src_dma and not is_dma:
                return
            k = id(sem)
            if self.known[eng].get(k, 0) >= val:
                return
            if k not in need or need[k][1] < val:
                need[k] = (sem, val)

        for r in reads:
            ev = self.last_w.get(r)
            if ev is not None:
                add(ev)
        for w in writes:
            ev = self.last_w.get(w)
            if ev is not None:
                add(ev)
            rd = self.readers.get(w)
            if rd:
                for ev in rd.values():
                    add(ev)
        out = list(need.values())
        for sem, val in out:
            self.known[eng][id(sem)] = val
        return out

    def _commit(self, ev, reads, writes):
        for r in reads:
            self.readers.setdefault(r, {})[id(ev[0])] = ev
        for w in writes:
            self.last_w[w] = ev
            self.readers[w] = {}

    def begin_capture(self):
        self._cap = []

    def end_capture(self):
        c, self._cap = self._cap, None
        return c

    def replay_interleaved(self, caps):
        idx = [0] * len(caps)
        while True:
            done = True
            for j, c in enumerate(caps):
                if idx[j] < len(c):
                    kind, args = c[idx[j]]
                    idx[j] += 1
                    done = False
                    if kind == 'op':
                        self.op(*args)
                    else:
                        self.dma(*args)
            if done:
                break

    def op(self, eng, fn, reads=(), writes=()):
        if getattr(self, '_cap', None) is not None:
            self._cap.append(('op', (eng, fn, tuple(reads), tuple(writes))))
            return None
        pr = tuple(r for r in reads if isinstance(r, str) and r.startswith('pb'))
        if pr and eng != 'pe':
            writes = tuple(writes) + pr
        waits = self._waits_for(eng, reads, writes, False)
        if self.cnt[eng] >= EPOCH:
            self._old_epochs.append((self.cur_sem[eng], self.cnt[eng]))
            self.cur_sem[eng] = self._new_sem(eng)
            self.cnt[eng] = 0
        self.cnt[eng] += 1
        sem = self.cur_sem[eng]
        ev = (sem, self.cnt[eng], eng, False)
        self.ops[eng].append((waits, fn, sem, 1))
        self._commit(ev, reads, writes)
        self.nops += 1
        return ev

    def dma(self, eng, fn, key, reads=(), writes=(), inc=16):
        if getattr(self, '_cap', None) is not None:
            self._cap.append(('dma', (eng, fn, key, tuple(reads), tuple(writes), inc)))
            return None
        if key not in self.dma_sems:
            self.dma_sems[key] = self._new_sem('d' + str(key))
            self.dma_cnt[key] = 0
        sem = self.dma_sems[key]
        chan = ('__chan__', key)
        waits = self._waits_for(eng, tuple(reads), tuple(writes) + (chan,), True)
        self.dma_cnt[key] += inc
        ev = (sem, self.dma_cnt[key], eng, True)
        self.ops[eng].append((waits, fn, sem, inc))
        self._commit(ev, reads, tuple(writes) + (chan,))
        self.nops += 1
        return ev

    def barrier(self):
        latest = {}
        for e in self.ENGS:
            for (waits, fn, sem, inc) in ():
                pass
        for e in self.ENGS:
            if self.cnt[e] > 0:
                latest[id(self.cur_sem[e])] = (self.cur_sem[e], self.cnt[e])
        for k, sem in self.dma_sems.items():
            if self.dma_cnt[k] > 0:
                latest[id(sem)] = (sem, self.dma_cnt[k])
        for (sem, val) in self._old_epochs:
            latest[id(sem)] = (sem, val)
        for e in self.ENGS:
            waits = []
            for k, (sem, val) in latest.items():
                if self.known[e].get(k, 0) < val:
                    waits.append((sem, val))
                    self.known[e][k] = val
            if waits:
                self.ops[e].append((waits, None, None, 0))

    def wait_all(self, eng, resources):
        waits = self._waits_for(eng, resources, (), True)
        self.ops[eng].append((waits, None, None, 0))

    def emit(self):
        nc = self.nc
        ops = self.ops
        with nc.Block() as block:
            def run(e, lst):
                for waits, fn, sem, inc in lst:
                    for s, v in waits:
                        e.wait_ge(s, v)
                    if fn is not None:
                        fn(e).then_inc(sem, inc)

            @block.tensor
            def _(e):
                run(e, ops['pe'])

            @block.scalar
            def _(e):
                run(e, ops['act'])

            @block.vector
            def _(e):
                run(e, ops['dve'])

            @block.gpsimd
            def _(e):
                run(e, ops['pool'])

            @block.sync
            def _(e):
                run(e, ops['sp'])


class KB:
    def __init__(self, name="k"):
        self.nc = bass.Bass("TRN2", target_bir_lowering=False)
        self.nc.allow_low_precision("bf16 matmul operands with fp32 PSUM accumulation")
        self.tr = Tracker(self.nc)
        self._n = 0
        self.outs = []

    def din(self, name, shape, dt=F32):
        return self.nc.dram_tensor(name, list(shape), dt, kind="ExternalInput").ap()

    def dout(self, name, shape, dt=F32):
        self.outs.append(name)
        return self.nc.dram_tensor(name, list(shape), dt, kind="ExternalOutput").ap()

    def dscratch(self, name, shape, dt=F32):
        return self.nc.dram_tensor(name, list(shape), dt, kind="Internal").ap()

    def sb(self, name, shape, dt=F32):
        return self.nc.alloc_sbuf_tensor(name, list(shape), dt)

    def ps(self, name, shape, dt=F32):
        return self.nc.alloc_psum_tensor(name, list(shape), dt)

    def dma(self, out, in_, key, r=(), w=(), eng='sp'):
        self.tr.dma(eng, lambda e: e.dma_start(out=out, in_=in_), key, reads=r, writes=w)

    def mm(self, out, lhsT, rhs, start, stop, r, w):
        self.tr.op('pe', lambda e: e.matmul(out, lhsT=lhsT, rhs=rhs, start=start, stop=stop), reads=r, writes=w)

    def tp(self, out, in_, ident, r, w):
        self.tr.op('pe', lambda e: e.transpose(out=out, in_=in_, identity=ident), reads=r, writes=w)

    def act(self, out, in_, func, r, w, bias=None, scale=None, accum=None):
        kw = {}
        if bias is not None:
            kw['bias'] = bias
        if scale is not None:
            kw['scale'] = scale
        if accum is not None:
            kw['accum_out'] = accum
        self.tr.op('act', lambda e: e.activation(out=out, in_=in_, func=func, **kw), reads=r, writes=w)

    def tt(self, eng, out, a, b, op, r, w):
        self.tr.op(eng, lambda e: e.tensor_tensor(out=out, in0=a, in1=b, op=op), reads=r, writes=w)

    def ts(self, eng, out, a, s1, op0, r, w, s2=None, op1=None, accum=None):
        kw = {}
        if op1 is not None:
            kw['op1'] = op1
        if accum is not None:
            kw['accum_out'] = accum
        self.tr.op(eng, lambda e: e.tensor_scalar(out=out, in0=a, scalar1=s1, scalar2=s2, op0=op0, **kw),
                   reads=r, writes=w)

    def stt(self, eng, out, a, s, b, op0, op1, r, w):
        eng = 'dve'
        self.tr.op(eng, lambda e: e.scalar_tensor_tensor(out=out, in0=a, scalar=s, in1=b, op0=op0, op1=op1),
                   reads=r, writes=w)

    def cp(self, eng, out, in_, r, w):
        if eng == 'act':
            self.tr.op('act', lambda e: e.copy(out=out, in_=in_), reads=r, writes=w)
        else:
            self.tr.op(eng, lambda e: e.tensor_copy(out=out, in_=in_), reads=r, writes=w)

    def memset(self, eng, out, val, w):
        self.tr.op(eng, lambda e: e.memset(out, val), writes=w)

    def recip(self, out, in_, r, w):
        self.tr.op('dve', lambda e: e.reciprocal(out=out, in_=in_), reads=r, writes=w)

    def scan(self, out, d0, d1, r, w):
        self.tr.op('dve', lambda e: e.tensor_tensor_scan(out=out, data0=d0, data1=d1, initial=0.0,
                                                         op0=ALU.mult, op1=ALU.add), reads=r, writes=w)

    def finish(self, out_resources):
        self.tr.wait_all('sp', out_resources)
        self.tr.emit()
        return self.nc


import numpy as np


def p1_consts():
    C = np.zeros((128, 2048), np.float32)
    i = np.arange(128)
    C[:, 0:128] = np.eye(128)
    C[:, 128:256] = (i[None, :] >= i[:, None])
    C[:, 256:384] = 1.0
    blk = (i[:, None] // 64 == i[None, :] // 64).astype(np.float32)
    C[:, 384:512] = blk
    C[:, 512:640] = blk / 64.0
    C[:, 640:768] = blk * (i[:, None] < i[None, :])
    C[:, 768:896] = blk * (i[:, None] > i[None, :])
    C[:, 896:1024] = blk * (i[:, None] <= i[None, :])
    rm = np.ones(512, np.float32)
    rm[::64] = 0
    C[:, 1024:1536] = rm[None, :]
    return C


def p1_core_consts(g):
    Cb = np.zeros((128, 1024), np.float32)
    p = np.arange(128)
    hl = p // 64
    d = p % 64
    heads = 2 * g + hl
    lg = np.log1p(-np.exp2(-5.0 - heads.astype(np.float64)))
    inv_freq = 10000.0 ** (-(d % 32).astype(np.float64) / 32.0)
    Cb[:, 0] = inv_freq
    Cb[:, 1] = np.where(d < 32, -1.0, 1.0)
    Cb[:, 2] = np.exp(128.0 * lg)
    idx = np.arange(128, dtype=np.float64)
    Cb[:, 128:256] = np.exp((idx[None, :] + 1.0) * lg[:, None])
    Cb[:, 256:384] = np.exp((127.0 - idx)[:, None] * lg[None, :])
    for h2 in range(2):
        lgh = np.log1p(-np.exp2(-5.0 - (2 * g + h2)))
        rel = idx[None, :] - idx[:, None]
        Cb[:, 384 + h2 * 128:384 + (h2 + 1) * 128] = np.where(rel >= 0, np.exp(rel * lgh), 0.0)
    return Cb


M_COLS = 3088
RW_COLS = 3520


def p1_core_inputs(inp, l, b, g):
    w_in = inp['w_in'][l]
    o = {}
    zc = np.arange(256 * g, 256 * g + 256)
    xsc = 1024 + np.arange(256 * g, 256 * g + 256)
    Bc = 2048 + np.arange(128 * g, 128 * g + 128)
    Cc = 2560 + np.arange(128 * g, 128 * g + 128)
    dtc = 3072 + np.arange(4 * g, 4 * g + 4)
    o['w_mfm'] = np.ascontiguousarray(w_in[:, np.concatenate([xsc, Bc, Cc])])
    o['w_mtm'] = np.ascontiguousarray(w_in[:, np.concatenate([zc, dtc])])
    convc = np.concatenate([xsc, Bc, Cc]) - 1024
    cw = np.concatenate([inp['m_conv_w'][l][:, convc], inp['m_conv_b'][l][None, convc]], axis=0)
    o['m_cw'] = np.ascontiguousarray(cw.T.reshape(4, 128, 5).transpose(1, 0, 2))
    o['m_hd'] = np.ascontiguousarray(inp['m_head'][l][:, 4 * g:4 * g + 4].reshape(1, 12))
    o['m_ng'] = np.ascontiguousarray(inp['m_norm_g'][l][None, 256 * g:256 * g + 256])
    r0 = M_COLS
    ch = np.arange(256 * g, 256 * g + 256)
    cols = np.concatenate([r0 + ch, r0 + 1024 + ch, r0 + 2048 + ch, r0 + 3072 + np.arange(96),
                           r0 + 3168 + np.arange(96), r0 + 3264 + np.arange(256)])
    o['w_wfm'] = np.ascontiguousarray(w_in[:, cols])
    mu = inp['rwkv_mu'][l][cols - r0]
    mut = np.zeros((128, 10), np.float32)
    offs = [0, 128, 256, 384, 512, 640, 768, 864, 960, 1088]
    sizes = [128, 128, 128, 128, 128, 128, 96, 96, 128, 128]
    for q, (of, sz) in enumerate(zip(offs, sizes)):
        mut[:sz, q] = mu[of:of + sz]
    o['rw_mu'] = mut
    vec = inp['rwkv_vec'][l][:, ch]
    o['rw_vec'] = np.ascontiguousarray(vec.T.reshape(2, 128, 7).transpose(1, 0, 2))
    o['rw_w2'] = np.ascontiguousarray(inp['rwkv_w2'][l][:, ch])
    o['rw_a2'] = np.ascontiguousarray(inp['rwkv_a2'][l][:, ch])
    o['rw_g2'] = np.ascontiguousarray(inp['rwkv_g2'][l][:, ch])
    t0 = M_COLS + RW_COLS
    hd = np.arange(128 * g, 128 * g + 128)
    d = hd % 64
    partner = np.where(d < 32, hd + 32, hd - 32)
    cols = np.concatenate([t0 + hd, t0 + partner, t0 + 512 + hd, t0 + 512 + partner])
    o['w_rfm'] = np.ascontiguousarray(w_in[:, cols])
    cols = np.concatenate([t0 + 1024 + ch, t0 + 2048 + ch])
    o['w_rtm'] = np.ascontiguousarray(w_in[:, cols])
    o['c_col'] = np.ascontiguousarray(inp['c'][b].reshape(16, 128).T)
    o['w_ada'] = np.ascontiguousarray(inp['w_ada'][l][:, :4096])
    o['b_ada'] = np.ascontiguousarray(inp['b_ada'][l][None, :4096])
    o['norm_g'] = inp['norm_g'][l]
    o['pos'] = np.ascontiguousarray(inp['positions'][b][None, :]).astype(np.int32)
    o['cst'] = p1_consts()
    o['cstb'] = p1_core_consts(g)
    return o


def p2_core_inputs(inp, l, b, q, Tq, x_b, yT_b):
    o = {}
    t0 = q * Tq
    NT = 128 + Tq
    if x_b is not None:
        xh = np.zeros((NT, 2048), np.float32)
        yh = np.zeros((3, 1024, NT), yT_b.dtype)
        if q == 0:
            xh[128:] = x_b[0:Tq]
            yh[:, :, 128:] = yT_b[:, :, 0:Tq]
        else:
            xh[:] = x_b[t0 - 128:t0 + Tq]
            yh[:] = yT_b[:, :, t0 - 128:t0 + Tq]
        o['xh'] = xh
        o['yTh'] = yh
    o['hmask'] = np.full((128, 1), 0.0 if q == 0 else 1.0, np.float32)
    o['c_col'] = np.ascontiguousarray(inp['c'][b].reshape(16, 128).T)
    o['w_ada'] = inp['w_ada'][l]
    o['b_ada'] = inp['b_ada'][l][None, :]
    ng = inp['norm_g'][l]
    o['norm_g'] = ng
    o['ng_col'] = np.ascontiguousarray(ng.reshape(4, 16, 128).transpose(2, 0, 1).reshape(128, 64))
    o['w_gate'] = inp['w_gate'][l]
    o['w_branch'] = inp['w_branch'][l]
    o['w_out'] = inp['w_out'][l]
    o['w_up'] = inp['w_up'][l]
    fcv = np.concatenate([inp['f_conv_w'][l], inp['f_conv_b'][l][None, :]], axis=0)
    o['f_cv'] = np.ascontiguousarray(fcv.T.reshape(86, 128, 4).transpose(1, 0, 2))
    o['w_down'] = inp['w_down'][l]
    o['cst'] = p1_consts()
    return o


from contextlib import ExitStack
import numpy as np


MAGIC = 12582912.0
TWO_PI = float(2 * np.pi)


class Scope:
    _n = 0

    def __init__(self, kb):
        self.kb = kb
        self.st = ExitStack()
        Scope._n += 1
        self.tag = f"_sc{Scope._n}"

    def sb(self, name, shape, dt=F32):
        return self.st.enter_context(self.kb.nc.sbuf_tensor(name + self.tag, list(shape), dt))

    def close(self):
        self.kb.tr.barrier()
        self.st.close()


def load_cast_weights(kb, sc, wdram, ncols, name, stage_cols=128):
    W = sc.sb(name, [128, 16, ncols], BF16)
    stg = [sc.sb(f"{name}_stg{i}", [128, 16, stage_cols], F32) for i in range(2)]
    wv = wdram.rearrange("(k p) n -> p k n", p=128)
    i = 0
    for c0 in range(0, ncols, stage_cols):
        c1 = min(ncols, c0 + stage_cols)
        s = stg[i % 2]
        rn = f"{name}_stg{i % 2}"
        kb.dma(s[:, :, 0:c1 - c0], wv[:, :, c0:c1], rn, w=[rn])
        kb.cp('pool' if i % 2 else 'dve', W[:, :, c0:c1], s[:, :, 0:c1 - c0], r=[rn], w=[name])
        i += 1
    return W


def emit_mod(kb, sc, c_col, w_ada, b_ada, vec_ids, ones_row, pb, tag):
    cc = sc.sb(f"{tag}_cc", [128, 16])
    scl = sc.sb(f"{tag}_sc", [128, 16])
    kb.dma(cc[:], c_col, f"{tag}_cc", w=[f"{tag}_cc"])
    kb.act(scl[:], cc[:], AF.Silu, r=[f"{tag}_cc"], w=[f"{tag}_sc"])
    stg = [sc.sb(f"{tag}_wa{i}", [128, 16, 128], F32) for i in range(2)]
    wv = w_ada.rearrange("(k p) n -> p k n", p=128)
    rows = {}
    i = 0
    for vid in vec_ids:
        row = sc.sb(f"{tag}_row{vid}", [1, 2048])
        brow = sc.sb(f"{tag}_brow{vid}", [1, 2048])
        rn = f"{tag}_row{vid}"
        kb.dma(brow[:], b_ada[0:1, vid * 2048:(vid + 1) * 2048], f"{tag}_brow{vid}", w=[f"{tag}_brow{vid}"])
        for cch in range(16):
            c0 = vid * 2048 + cch * 128
            s = stg[i % 2]
            sn = f"{tag}_wa{i % 2}"
            kb.dma(s[:], wv[:, :, c0:c0 + 128], sn, w=[sn])
            pr = "pb7" if i % 2 == 0 else "pb4"
            pcols = (pb[7] if i % 2 == 0 else pb[4])[0:1, 0:128]
            for k in range(16):
                kb.mm(pcols, lhsT=scl[:, k:k + 1], rhs=s[:, k, :], start=(k == 0), stop=(k == 15),
                      r=[sn, f"{tag}_sc"], w=[pr])
            kb.tt('dve', row[0:1, cch * 128:(cch + 1) * 128], pcols, brow[0:1, cch * 128:(cch + 1) * 128], ALU.add,
                  r=[pr, f"{tag}_brow{vid}"], w=[rn])
            i += 1
        rows[vid] = (row, rn)
    return rows


def bcast_row(kb, row, rn, ones_row, pb, emit_chunk):
    for c in range(4):
        pr = "pb6" if c % 2 == 0 else "pb5"
        pa = pb[6][:, 0:512] if c % 2 == 0 else pb[5][:, 0:512]
        kb.mm(pa, lhsT=ones_row[0:1, 0:128], rhs=row[0:1, c * 512:(c + 1) * 512], start=True, stop=True,
              r=[rn, 'C'], w=[pr])
        emit_chunk(c, pa, pr)


P1_PARAM_SHAPES = {
    "w_mfm": ([D, 512], F32), "w_mtm": ([D, 260], F32), "m_cw": ([128, 4, 5], F32), "m_hd": ([1, 12], F32),
    "m_ng": ([1, 256], F32), "w_rfm": ([D, 512], F32), "w_rtm": ([D, 512], F32), "w_wfm": ([D, 1216], F32),
    "rw_mu": ([128, 10], F32), "rw_vec": ([128, 2, 7], F32), "rw_w2": ([96, 256], F32), "rw_a2": ([96, 256], F32),
    "rw_g2": ([256, 256], F32),
}


def setup_globals(kb, cst):
    G = {}
    G['pb'] = [kb.ps(f"pb{i}", [128, 512]) for i in range(8)]
    C = kb.sb("C", [128, 2048])
    kb.dma(C[:], cst, "C", w=["C"])
    Cb = kb.sb("Cb", [128, 256], BF16)
    kb.cp('dve', Cb[:, 0:128], C[:, 0:128], r=["C"], w=["Cb"])
    G['C'] = C
    G['Cb'] = Cb
    return G


def build_p1(S, parts=('h', 'mamba', 'rwkv', 'ret'), env=None):
    NM = S // 512
    if env is None:
        kb = KB()
        e = {}
        e['x'] = kb.din("x", [S, D])
        e['c_col'] = kb.din("c_col", [128, 16])
        e['w_ada'] = kb.din("w_ada", [D, 2 * D])
        e['b_ada'] = kb.din("b_ada", [1, 2 * D])
        e['norm_g'] = kb.din("norm_g", [4, D])
        e['pos'] = kb.din("pos", [1, S], I32)
        cst = kb.din("cst", [128, 2048])
        e['cstb'] = kb.din("cstb", [128, 1024])
        for k, (shp, dt_) in P1_PARAM_SHAPES.items():
            e[k] = kb.din(k, shp, dt_)
        e['yT'] = kb.dout("yT", [768, S], BF16)
        e['hTd'] = kb.dscratch("hTd", [NM, 128, 16, 512], BF16)
        e['hTo'] = e['hTd']
        e['NH'] = S // 128
        G = setup_globals(kb, cst)
    else:
        kb = env['kb']
        e = env
        G = env['G']
    nc = kb.nc
    x, c_col, w_ada, b_ada, norm_g, pos, cstb = (e[k] for k in ('x', 'c_col', 'w_ada', 'b_ada', 'norm_g', 'pos', 'cstb'))
    w_mfm, w_mtm, m_cw, m_hd, m_ng, w_rfm, w_rtm, w_wfm = (e[k] for k in ('w_mfm', 'w_mtm', 'm_cw', 'm_hd', 'm_ng', 'w_rfm', 'w_rtm', 'w_wfm'))
    rw_mu, rw_vec, rw_w2, rw_a2, rw_g2 = (e[k] for k in ('rw_mu', 'rw_vec', 'rw_w2', 'rw_a2', 'rw_g2'))
    yT, hTd, hTo, NH = e['yT'], e['hTd'], e['hTo'], e['NH']

    def _load_hT(dst, m, key):
        kb.dma(dst[:], hTd[m], key, r=[f"hTd{m}"], w=[key])

    def _yT_ap(row0, nrows, m):
        return yT[row0:row0 + nrows, m * 512:(m + 1) * 512]

    load_hT = e.get('load_hT', _load_hT)
    yT_ap = e.get('yT_ap', _yT_ap)
    after_y = e.get('after_y', lambda i, m, res: None)
    pb = G['pb']
    C, Cb = G['C'], G['Cb']
    ident_f = C[:, 0:128]
    U_f = C[:, 128:256]
    ones_f = C[:, 256:384]
    blk_f = C[:, 384:512]
    blk64_f = C[:, 512:640]
    SU_f = C[:, 640:768]
    SL_f = C[:, 768:896]
    UI_f = C[:, 896:1024]
    rmask = C[:, 1024:1536]
    ident_b = Cb[:, 0:128]
    ones_row = C[0:1, 256:384]

    if 'h' in parts:
        sc = Scope(kb)
        rows = emit_mod(kb, sc, c_col, w_ada, b_ada, [0, 1], ones_row, pb, "m0")
        Gbc = sc.sb("Gbc", [128, 2048])
        SHbc = sc.sb("SHbc", [128, 2048])
        kb.dma(Gbc[:], norm_g[0:1, :].partition_broadcast(128), "Gbc", w=["Gbc"])

        def g_chunk(c, pa, pr):
            kb.stt('dve', Gbc[:, c * 512:(c + 1) * 512], pa, 1.0, Gbc[:, c * 512:(c + 1) * 512], ALU.add, ALU.mult,
                   r=[pr, "Gbc"], w=["Gbc"])

        def sh_chunk(c, pa, pr):
            kb.cp('act', SHbc[:, c * 512:(c + 1) * 512], pa, r=[pr], w=["SHbc"])

        bcast_row(kb, rows[1][0], rows[1][1], ones_row, pb, g_chunk)
        bcast_row(kb, rows[0][0], rows[0][1], ones_row, pb, sh_chunk)
        xt = [sc.sb(f"xt{i}", [128, 2048]) for i in range(2)]
        junk = sc.sb("junk", [128, 2048], BF16)
        tmp = sc.sb("htmp", [128, 2048])
        hb = sc.sb("hb", [128, 2048], BF16)
        ss = sc.sb("ss", [128, 1])
        rstd = sc.sb("rstd", [128, 1])
        hst = [sc.sb(f"hst{i}", [128, 16, 512], BF16) for i in range(2)]
        for i in range(NH):
            m, sub = divmod(i, 4)
            xb = xt[i % 2]
            xn = f"xt{i % 2}"
            kb.dma(xb[:], x[i * 128:(i + 1) * 128, :], xn, w=[xn])
            kb.act(junk[:], xb[:], AF.Square, r=[xn], w=["junk", "ss"], accum=ss[:])
            kb.act(rstd[:], ss[:], AF.Sqrt, r=["ss"], w=["rstd"], bias=1e-6, scale=1.0 / D)
            kb.recip(rstd[:], rstd[:], r=["rstd"], w=["rstd"])
            kb.stt('dve', tmp[:], xb[:], rstd[:, 0:1], Gbc[:], ALU.mult, ALU.mult, r=[xn, "rstd", "Gbc"], w=["htmp"])
            kb.tt('pool', hb[:], tmp[:], SHbc[:], ALU.add, r=["htmp", "SHbc"], w=["hb"])
            hs = hst[m % 2]
            hn = f"hst{m % 2}"
            for q in range(4):
                pr = f"pb{q}"
                for kk in range(4):
                    k = q * 4 + kk
                    kb.mm(pb[q][:, kk * 128:(kk + 1) * 128], lhsT=hb[:, k * 128:(k + 1) * 128], rhs=ident_b,
                          start=True, stop=True, r=["hb", "Cb"], w=[pr])
                eng = 'act' if q % 2 == 0 else 'dve'
                kb.cp(eng, hs[:, q * 4:(q + 1) * 4, sub * 128:(sub + 1) * 128],
                      pb[q][:, :].rearrange("p (k t) -> p k t", k=4), r=[pr], w=[hn])
            if sub == 3:
                kb.dma(hTo[m], hs[:], hn, r=[hn], w=[f"hTd{m}"])
                if 'after_h' in e:
                    e['after_h'](m)
        sc.close()

    if 'mamba' in parts:
        sc = Scope(kb)
        Wfm = load_cast_weights(kb, sc, w_mfm, 512, "Wfm")
        Wtm = load_cast_weights(kb, sc, w_mtm, 260, "Wtm")
        cw = sc.sb("cw", [128, 4, 5])
        kb.dma(cw[:], m_cw, "cw", w=["cw"])
        mh = sc.sb("mh", [128, 12])
        kb.dma(mh[:], m_hd.partition_broadcast(128), "mh", w=["mh"])
        negA = sc.sb("negA", [128, 4])
        kb.act(negA[:], mh[:, 4:8], AF.Exp, r=["mh"], w=["negA"])
        kb.ts('dve', negA[:], negA[:], -1.0, ALU.mult, r=["negA"], w=["negA"])
        dsk = sc.sb("dsk", [128, 256])
        for h in range(4):
            kb.ts('dve', dsk[:, h * 64:(h + 1) * 64], ones_f[:, 0:64], mh[:, 8 + h:9 + h], ALU.mult,
                  r=["C", "mh"], w=["dsk"])
        mng = sc.sb("mng", [128, 256])
        kb.dma(mng[:], m_ng.partition_broadcast(128), "mng", w=["mng"])
        hT = [sc.sb(f"hT{i}", [128, 16, 512], BF16) for i in range(2)]
        raw = [sc.sb(f"raw{j}", [128, 515]) for j in range(4)]
        acc = [sc.sb(f"acc{j}", [128, 512]) for j in range(2)]
        xs_act = [sc.sb(f"xsa{j}", [128, 512]) for j in range(2)]
        B_act = sc.sb("B_act", [128, 512])
        BT_bf = sc.sb("BT_bf", [128, 512], BF16)
        CT_bf = sc.sb("CT_bf", [128, 512], BF16)
        zs = sc.sb("zs", [128, 256])
        dtp = sc.sb("dtp", [128, 4])
        dt = sc.sb("dt", [128, 4])
        adt = sc.sb("adt", [128, 4])
        xs_sb = sc.sb("xs_sb", [128, 256])
        B_tm = sc.sb("B_tm", [128, 128], BF16)
        xdt_bf = sc.sb("xdt_bf", [128, 256], BF16)
        xte_bf = sc.sb("xte_bf", [128, 256], BF16)
        cum_sb = sc.sb("cum_sb", [128, 4])
        rh = sc.sb("rh", [128, 4, 128])
        CBm = sc.sb("CBm", [128, 128])
        seg = sc.sb("seg", [128, 4, 128])
        dec = sc.sb("dec", [128, 4, 128])
        MT_bf = sc.sb("MT_bf", [128, 4, 128], BF16)
        ecum = sc.sb("ecum", [128, 4])
        tte = sc.sb("tte", [128, 4])
        ecl = sc.sb("ecl", [128, 4])
        yi_sb = sc.sb("yi_sb", [128, 256])
        y = sc.sb("y", [128, 256])
        y2 = sc.sb("y2", [128, 256])
        state = sc.sb("state", [128, 256])
        state_bf = sc.sb("state_bf", [128, 256], BF16)
        mjunk = sc.sb("mjunk", [128, 256])
        mss = sc.sb("mss", [128, 1])
        mrstd = sc.sb("mrstd", [128, 1])
        yo_bf = sc.sb("yo_bf", [128, 256], BF16)
        yst = [sc.sb(f"yst{i}", [128, 2, 512], BF16) for i in range(2)]
        kb.memset('pool', state[:], 0.0, w=["state"])
        kb.memset('pool', state_bf[:], 0.0, w=["state_bf"])
        for j in range(4):
            kb.memset('pool', raw[j][:], 0.0, w=[f"raw{j}"])
        for m in range(NM):
            h_ = hT[m % 2]
            hn = f"hT{m % 2}"
            load_hT(h_, m, hn)
            ys = yst[m % 2]
            ysn = f"yst{m % 2}"
            for j in range(4):
                pr = f"pb{j % 2}"
                pa = pb[j % 2]
                for k in range(16):
                    kb.mm(pa[:, :], lhsT=Wfm[:, k, j * 128:(j + 1) * 128], rhs=h_[:, k, :], start=(k == 0),
                          stop=(k == 15), r=["Wfm", hn], w=[pr])
                rj = f"raw{j}"
                kb.cp('pool', raw[j][:, 0:3], raw[j][:, 512:515], r=[rj], w=[rj])
                kb.cp('act', raw[j][:, 3:515], pa[:, :], r=[pr], w=[rj])
                a_ = acc[j % 2]
                an = f"acc{j % 2}"
                kb.ts('dve', a_[:], raw[j][:, 3:515], cw[:, j, 3:4], ALU.mult, r=[rj, "cw"], w=[an],
                      s2=cw[:, j, 4:5], op1=ALU.add)
                for tpi, eng in ((2, 'pool'), (1, 'dve'), (0, 'pool')):
                    kb.stt(eng, a_[:], raw[j][:, tpi:tpi + 512], cw[:, j, tpi:tpi + 1], a_[:], ALU.mult, ALU.add,
                           r=[rj, "cw", an], w=[an])
                if j < 2:
                    kb.act(xs_act[j][:], a_[:], AF.Silu, r=[an], w=[f"xsa{j}"])
                elif j == 2:
                    kb.act(B_act[:], a_[:], AF.Silu, r=[an], w=["B_act"])
                    kb.cp('pool', BT_bf[:], B_act[:], r=["B_act"], w=["BT_bf"])
                else:
                    kb.act(CT_bf[:], a_[:], AF.Silu, r=[an], w=["CT_bf"])
            import os as _os
            _nch = int(_os.environ.get("MB_NCH", "4"))
            for c in range(_nch):
                cs = slice(c * 128, (c + 1) * 128)
                _sect = _os.environ.get('MB_SECT', '123456')
                if '1' in _sect:
                    for k in range(16):
                        kb.mm(pb[2][:, 0:260], lhsT=h_[:, k, cs], rhs=Wtm[:, k, :], start=(k == 0), stop=(k == 15),
                              r=[hn, "Wtm"], w=["pb2"])
                    kb.act(zs[:], pb[2][:, 0:256], AF.Silu, r=["pb2"], w=["zs"])
                    kb.tt('dve', dtp[:], pb[2][:, 256:260], mh[:, 0:4], ALU.add, r=["pb2", "mh"], w=["dtp"])
                    kb.act(dtp[:], dtp[:], AF.Exp, r=["dtp"], w=["dtp"])
                    kb.act(dt[:], dtp[:], AF.Ln, r=["dtp"], w=["dt"], bias=1.0)
                    kb.tt('dve', adt[:], dt[:], negA[:], ALU.mult, r=["dt", "negA"], w=["adt"])
                if '2' in _sect:
                    for j in range(2):
                        kb.mm(pb[3][:, j * 128:(j + 1) * 128], lhsT=xs_act[j][:, cs], rhs=ident_f, start=True, stop=True,
                              r=[f"xsa{j}", "C"], w=["pb3"])
                    kb.mm(pb[3][:, 256:384], lhsT=B_act[:, cs], rhs=ident_f, start=True, stop=True,
                          r=["B_act", "C"], w=["pb3"])
                    kb.cp('act', xs_sb[:], pb[3][:, 0:256], r=["pb3"], w=["xs_sb"])
                    kb.cp('dve', B_tm[:], pb[3][:, 256:384], r=["pb3"], w=["B_tm"])
                    kb.tt('pool', xdt_bf[:].rearrange("p (h c) -> p h c", h=4), xs_sb[:].rearrange("p (h c) -> p h c", h=4),
                          dt[:].unsqueeze(2).to_broadcast([128, 4, 64]), ALU.mult, r=["xs_sb", "dt"], w=["xdt_bf"])
                if '3' in _sect:
                    kb.mm(pb[4][:, 0:4], lhsT=U_f, rhs=adt[:], start=True, stop=True, r=["C", "adt"], w=["pb4"])
                    kb.cp('act', cum_sb[:], pb[4][:, 0:4], r=["pb4"], w=["cum_sb"])
                    kb.tt('pool', rh[:], U_f.unsqueeze(1).to_broadcast([128, 4, 128]),
                          adt[:].unsqueeze(2).to_broadcast([128, 4, 128]), ALU.mult, r=["C", "adt"], w=["rh"])
                    for h in range(4):
                        kb.mm(pb[5][:, h * 128:(h + 1) * 128], lhsT=ones_f, rhs=rh[:, h, :], start=True, stop=True,
                              r=["C", "rh"], w=["pb5"])
                    kb.mm(pb[4][:, 128:256], lhsT=BT_bf[:, cs], rhs=CT_bf[:, cs], start=True, stop=True,
                          r=["BT_bf", "CT_bf"], w=["pb4"])
                    kb.tt('dve', CBm[:], pb[4][:, 128:256], U_f, ALU.mult, r=["pb4", "C"], w=["CBm"])
                    kb.tt('dve', seg[:], pb[5][:, :].rearrange("p (h i) -> p h i", h=4),
                          cum_sb[:].unsqueeze(2).to_broadcast([128, 4, 128]), ALU.subtract, r=["pb5", "cum_sb"], w=["seg"])
                    kb.ts('pool', seg[:], seg[:], 0.0, ALU.min, r=["seg"], w=["seg"])
                    kb.act(dec[:], seg[:], AF.Exp, r=["seg"], w=["dec"])
                    kb.tt('pool', MT_bf[:], dec[:], CBm[:].unsqueeze(1).to_broadcast([128, 4, 128]), ALU.mult,
                          r=["dec", "CBm"], w=["MT_bf"])
                    kb.act(ecum[:], cum_sb[:], AF.Exp, r=["cum_sb"], w=["ecum"])
                    last = pb[5][:, :].rearrange("p (h i) -> p h i", h=4)[:, :, 127:128]
                    kb.tt('dve', tte[:].rearrange("p (h o) -> p h o", o=1), last,
                          cum_sb[:].rearrange("p (h o) -> p h o", o=1), ALU.subtract, r=["pb5", "cum_sb"], w=["tte"])
                    kb.act(tte[:], tte[:], AF.Exp, r=["tte"], w=["tte"])
                    kb.act(ecl[:].rearrange("p (h o) -> p h o", o=1), last, AF.Exp, r=["pb5"], w=["ecl"])
                if '4' in _sect:
                    for h in range(4):
                        kb.mm(pb[6][:, h * 64:(h + 1) * 64], lhsT=MT_bf[:, h, :], rhs=xdt_bf[:, h * 64:(h + 1) * 64],
                              start=True, stop=True, r=["MT_bf", "xdt_bf"], w=["pb6"])
                    kb.mm(pb[6][:, 256:512], lhsT=CT_bf[:, cs], rhs=state_bf[:], start=True, stop=True,
                          r=["CT_bf", "state_bf"], w=["pb6"])
                    kb.tt('dve', yi_sb[:].rearrange("p (h c) -> p h c", h=4),
                          pb[6][:, 256:512].rearrange("p (h c) -> p h c", h=4),
                          ecum[:].unsqueeze(2).to_broadcast([128, 4, 64]), ALU.mult, r=["pb6", "ecum"], w=["yi_sb"])
                    kb.tt('dve', y[:], yi_sb[:], pb[6][:, 0:256], ALU.add, r=["pb6", "yi_sb"], w=["y"])
                if '5' in _sect:
                    kb.tt('pool', xte_bf[:].rearrange("p (h c) -> p h c", h=4), xdt_bf[:].rearrange("p (h c) -> p h c", h=4),
                          tte[:].unsqueeze(2).to_broadcast([128, 4, 64]), ALU.mult, r=["xdt_bf", "tte"], w=["xte_bf"])
                    kb.mm(pb[7][:, 0:256], lhsT=B_tm[:], rhs=xte_bf[:], start=True, stop=True, r=["B_tm", "xte_bf"],
                          w=["pb7"])
                    kb.tt('pool', state[:].rearrange("p (h c) -> p h c", h=4), state[:].rearrange("p (h c) -> p h c", h=4),
                          ecl[:].unsqueeze(2).to_broadcast([128, 4, 64]), ALU.mult, r=["state", "ecl"], w=["state"])
                    kb.tt('dve', state[:], state[:], pb[7][:, 0:256], ALU.add, r=["state", "pb7"], w=["state"])
                    kb.cp('pool', state_bf[:], state[:], r=["state"], w=["state_bf"])
                if '6' in _sect:
                    kb.tt('pool', y2[:], xs_sb[:], dsk[:], ALU.mult, r=["xs_sb", "dsk"], w=["y2"])
                    kb.tt('pool', y[:], y[:], y2[:], ALU.add, r=["y", "y2"], w=["y"])
                    kb.tt('pool', y[:], y[:], zs[:], ALU.mult, r=["y", "zs"], w=["y"])
                    kb.act(mjunk[:], y[:], AF.Square, r=["y"], w=["mjunk", "mss"], accum=mss[:])
                    kb.act(mrstd[:], mss[:], AF.Sqrt, r=["mss"], w=["mrstd"], bias=1e-5, scale=1.0 / 256)
                    kb.recip(mrstd[:], mrstd[:], r=["mrstd"], w=["mrstd"])
                    kb.stt('dve', yo_bf[:], y[:], mrstd[:, 0:1], mng[:], ALU.mult, ALU.mult, r=["y", "mrstd", "mng"],
                           w=["yo_bf"])
                    for t in range(2):
                        kb.mm(pb[7][:, 256 + t * 128:256 + (t + 1) * 128], lhsT=yo_bf[:, t * 128:(t + 1) * 128],
                              rhs=ident_b, start=True, stop=True, r=["yo_bf", "Cb"], w=["pb7"])
                    kb.cp('act', ys[:, :, cs], pb[7][:, 256:512].rearrange("p (t s) -> p t s", t=2), r=["pb7"], w=[ysn])
            kb.dma(yT_ap(0, 256, m).rearrange("(t p) s -> p t s", p=128), ys[:], ysn, r=[ysn],
                   w=["yT_m"])
            after_y(0, m, "yT_m")
        sc.close()

    if 'ret' in parts:
        sc = Scope(kb)
        Wfm = load_cast_weights(kb, sc, w_rfm, 512, "Wfm")
        Wtm = load_cast_weights(kb, sc, w_rtm, 512, "Wtm")
        CB = sc.sb("CB", [128, 1024])
        kb.dma(CB[:], cstb, "CB", w=["CB"])
        invf = CB[:, 0:1]
        sgn = CB[:, 1:2]
        cdec = CB[:, 2:3]
        QD = CB[:, 128:256]
        KDEC = CB[:, 256:384]
        DMT = CB[:, 384:640]
        hT = [sc.sb(f"hT{i}", [128, 16, 512], BF16) for i in range(2)]
        pi_ = sc.sb("pi", [128, 512], I32)
        ang = sc.sb("ang", [128, 512])
        kq = sc.sb("kq", [128, 512])
        rr = sc.sb("rr", [128, 512])
        sinT = sc.sb("sinT", [128, 512])
        cosT = sc.sb("cosT", [128, 512])
        t1 = sc.sb("t1", [128, 512])
        t2 = sc.sb("t2", [128, 512])
        qr_bf = sc.sb("qr_bf", [128, 512], BF16)
        kr_bf = sc.sb("kr_bf", [128, 512], BF16)
        krz = [sc.sb(f"krz{h}", [128, 512], BF16) for h in range(2)]
        qdz = [sc.sb(f"qdz{h}", [128, 512], BF16) for h in range(2)]
        v_bf = sc.sb("v_bf", [128, 256], BF16)
        gs = sc.sb("gs", [128, 256])
        ST_bf = sc.sb("ST_bf", [128, 256], BF16)
        kd_bf = sc.sb("kd_bf", [128, 128], BF16)
        rstate = sc.sb("rstate", [128, 128])
        rstate_bf = sc.sb("rstate_bf", [128, 128], BF16)
        rjunk = sc.sb("rjunk", [128, 128])
        rss = sc.sb("rss", [128, 2])
        rrs = sc.sb("rrs", [128, 2])
        yo_bf = sc.sb("ryo_bf", [128, 256], BF16)
        yst = [sc.sb(f"yst{i}", [128, 2, 512], BF16) for i in range(2)]
        kb.memset('pool', rstate[:], 0.0, w=["rstate"])
        kb.memset('pool', rstate_bf[:], 0.0, w=["rstate_bf"])
        for h in range(2):
            kb.memset('pool', krz[h][:], 0.0, w=[f"krz{h}"])
            kb.memset('pool', qdz[h][:], 0.0, w=[f"qdz{h}"])
        for m in range(NM):
            h_ = hT[m % 2]
            hn = f"hT{m % 2}"
            load_hT(h_, m, hn)
            ys = yst[m % 2]
            ysn = f"yst{m % 2}"
            kb.dma(pi_[:], pos[0:1, m * 512:(m + 1) * 512].partition_broadcast(128), "pi", w=["pi"])
            kb.cp('dve', ang[:], pi_[:], r=["pi"], w=["ang"])
            kb.ts('dve', ang[:], ang[:], invf, ALU.mult, r=["ang", "CB"], w=["ang"])
            for which, dst, shift in (("s", sinT, 0.0), ("c", cosT, float(np.pi / 2))):
                if shift != 0.0:
                    kb.ts('pool', rr[:], ang[:], shift, ALU.add, r=["ang"], w=["rr"])
                    src, srn = rr, "rr"
                else:
                    src, srn = ang, "ang"
                kb.ts('dve', kq[:], src[:], float(1.0 / TWO_PI), ALU.mult, r=[srn], w=["kq"], s2=MAGIC, op1=ALU.add)
                kb.ts('pool', kq[:], kq[:], -MAGIC, ALU.add, r=["kq"], w=["kq"])
                kb.stt('dve', rr[:], kq[:], -TWO_PI, src[:], ALU.mult, ALU.add, r=["kq", srn], w=["rr"])
                kb.ts('pool', rr[:], rr[:], 3.14159, ALU.min, r=["rr"], w=["rr"], s2=-3.14159, op1=ALU.max)
                if which == "s":
                    kb.act(dst[:], rr[:], AF.Sin, r=["rr", "CB"], w=["sinT"], scale=sgn)
                else:
                    kb.act(dst[:], rr[:], AF.Sin, r=["rr"], w=["cosT"])
            for j in range(4):
                pr = f"pb{j % 2}"
                pa = pb[j % 2]
                for k in range(16):
                    kb.mm(pa[:, :], lhsT=Wfm[:, k, j * 128:(j + 1) * 128], rhs=h_[:, k, :], start=(k == 0),
                          stop=(k == 15), r=["Wfm", hn], w=[pr])
                if j % 2 == 0:
                    kb.tt('dve', t1[:], pa[:, :], cosT[:], ALU.mult, r=[pr, "cosT"], w=["t1"])
                else:
                    kb.tt('dve', t2[:], pa[:, :], sinT[:], ALU.mult, r=[pr, "sinT"], w=["t2"])
                    if j == 1:
                        kb.tt('pool', qr_bf[:], t1[:], t2[:], ALU.add, r=["t1", "t2"], w=["qr_bf"])
                    else:
                        kb.tt('pool', t1[:], t1[:], t2[:], ALU.add, r=["t1", "t2"], w=["t1"])
                        kb.ts('pool', kr_bf[:], t1[:], 0.125, ALU.mult, r=["t1"], w=["kr_bf"])
            for h in range(2):
                ph = slice(h * 64, (h + 1) * 64)
                kb.cp('pool', krz[h][ph, :], kr_bf[ph, :], r=["kr_bf"], w=[f"krz{h}"])
                for c in range(4):
                    cs = slice(c * 128, (c + 1) * 128)
                    kb.tt('dve' if c % 2 else 'pool', qdz[h][ph, cs], qr_bf[ph, cs], QD[ph, :], ALU.mult,
                          r=["qr_bf", "CB"], w=[f"qdz{h}"])
            for c in range(4):
                cs = slice(c * 128, (c + 1) * 128)
                for k in range(16):
                    kb.mm(pb[2][:, :], lhsT=h_[:, k, cs], rhs=Wtm[:, k, :], start=(k == 0), stop=(k == 15),
                          r=[hn, "Wtm"], w=["pb2"])
                kb.cp('act', v_bf[:], pb[2][:, 0:256], r=["pb2"], w=["v_bf"])
                kb.act(gs[:], pb[2][:, 256:512], AF.Silu, r=["pb2"], w=["gs"])
                for h in range(2):
                    kb.mm(pb[3][:, h * 128:(h + 1) * 128], lhsT=krz[h][:, cs], rhs=qr_bf[:, cs], start=True, stop=True,
                          r=[f"krz{h}", "qr_bf"], w=["pb3"])
                kb.tt('dve', ST_bf[:], pb[3][:, 0:256], DMT, ALU.mult, r=["pb3", "CB"], w=["ST_bf"])
                kb.mm(pb[4][:, 0:128], lhsT=kr_bf[:, cs], rhs=ident_b, start=True, stop=True, r=["kr_bf", "Cb"],
                      w=["pb4"])
                kb.tt('dve', kd_bf[:], pb[4][:, 0:128], KDEC, ALU.mult, r=["pb4", "CB"], w=["kd_bf"])
                for h in range(2):
                    hs_ = slice(h * 128, (h + 1) * 128)
                    kb.mm(pb[5][:, hs_], lhsT=ST_bf[:, hs_], rhs=v_bf[:, hs_], start=True, stop=False,
                          r=["ST_bf", "v_bf"], w=["pb5"])
                    kb.mm(pb[5][:, hs_], lhsT=qdz[h][:, cs], rhs=rstate_bf[:], start=False, stop=True,
                          r=[f"qdz{h}", "rstate_bf"], w=["pb5"])
                kb.mm(pb[6][:, 0:256], lhsT=kd_bf[:], rhs=v_bf[:], start=True, stop=True, r=["kd_bf", "v_bf"],
                      w=["pb6"])
                for h in range(2):
                    ph = slice(h * 64, (h + 1) * 64)
                    kb.stt('dve', rstate[ph, :], rstate[ph, :], cdec[ph, :], pb[6][ph, h * 128:(h + 1) * 128],
                           ALU.mult, ALU.add, r=["rstate", "CB", "pb6"], w=["rstate"])
                kb.cp('pool', rstate_bf[:], rstate[:], r=["rstate"], w=["rstate_bf"])
                for h in range(2):
                    kb.act(rjunk[:], pb[5][:, h * 128:(h + 1) * 128], AF.Square, r=["pb5"], w=["rjunk", "rss"],
                           accum=rss[:, h:h + 1])
                kb.act(rrs[:], rss[:], AF.Sqrt, r=["rss"], w=["rrs"], bias=1e-6, scale=1.0 / 128)
                kb.recip(rrs[:], rrs[:], r=["rrs"], w=["rrs"])
                for h in range(2):
                    hs_ = slice(h * 128, (h + 1) * 128)
                    kb.stt('dve', yo_bf[:, hs_], pb[5][:, hs_], rrs[:, h:h + 1], gs[:, hs_], ALU.mult, ALU.mult,
                           r=["pb5", "rrs", "gs"], w=["ryo_bf"])
                for t in range(2):
                    kb.mm(pb[7][:, 256 + t * 128:256 + (t + 1) * 128], lhsT=yo_bf[:, t * 128:(t + 1) * 128],
                          rhs=ident_b, start=True, stop=True, r=["ryo_bf", "Cb"], w=["pb7"])
                kb.cp('act', ys[:, :, cs], pb[7][:, 256:512].rearrange("p (t s) -> p t s", t=2), r=["pb7"], w=[ysn])
            kb.dma(yT_ap(512, 256, m).rearrange("(t p) s -> p t s", p=128), ys[:], ysn, r=[ysn],
                   w=["yT_r"])
            after_y(2, m, "yT_r")
        sc.close()

    if 'rwkv' in parts:
        sc = Scope(kb)
        Wr = load_cast_weights(kb, sc, w_wfm, 1216, "Wr")
        offs = [0, 128, 256, 384, 512, 640, 768, 864, 960, 1088]
        sizes = [128, 128, 128, 128, 128, 128, 96, 96, 128, 128]
        mu = sc.sb("mu", [128, 10])
        kb.dma(mu[:], rw_mu, "mu", w=["mu"])
        vec = sc.sb("vec", [128, 2, 7])
        kb.dma(vec[:], rw_vec, "vec", w=["vec"])
        omka = sc.sb("omka", [128, 2])
        kb.ts('dve', omka[:].rearrange("p (c o) -> p c o", o=1), vec[:, :, 3:4], -1.0, ALU.mult, r=["vec"], w=["omka"],
              s2=1.0, op1=ALU.add)
        lstg = sc.sb("lstg", [128, 2, 256])
        w2_bf = sc.sb("w2_bf", [96, 256], BF16)
        a2_bf = sc.sb("a2_bf", [96, 256], BF16)
        g2_bf = sc.sb("g2_bf", [128, 2, 256], BF16)
        kb.dma(lstg[0:96, 0, :], rw_w2, "lstg", w=["lstg"])
        kb.cp('dve', w2_bf[:], lstg[0:96, 0, :], r=["lstg"], w=["w2_bf"])
        kb.dma(lstg[0:96, 1, :], rw_a2, "lstg", r=["lstg"], w=["lstg"])
        kb.cp('dve', a2_bf[:], lstg[0:96, 1, :], r=["lstg"], w=["a2_bf"])
        kb.dma(lstg[:], rw_g2.rearrange("(k p) n -> p k n", p=128), "lstg", r=["lstg"], w=["lstg"])
        kb.cp('dve', g2_bf[:], lstg[:], r=["lstg"], w=["g2_bf"])
        M3 = sc.sb("M3", [128, 384])
        kb.cp('pool', M3[:, 0:128], SU_f, r=["C"], w=["M3"])
        kb.cp('pool', M3[:, 128:256], SL_f, r=["C"], w=["M3"])
        kb.cp('pool', M3[:, 256:384], SU_f, r=["C"], w=["M3"])
        M2 = sc.sb("M2", [128, 256])
        kb.cp('pool', M2[:, 0:128], UI_f, r=["C"], w=["M2"])
        kb.cp('pool', M2[:, 128:256], UI_f, r=["C"], w=["M2"])
        hT = sc.sb("hT0", [128, 16, 512], BF16)
        halo = sc.sb("halo", [128, 10])
        kb.memset('pool', halo[:], 0.0, w=["halo"])
        rawt = [sc.sb(f"rawt{i}", [128, 513]) for i in range(2)]
        dlt = [sc.sb(f"dlt{i}", [128, 512]) for i in range(2)]
        sh = [sc.sb(f"sh{q}", [128, 512]) for q in range(10)]
        tw = sc.sb("tw", [128, 512], BF16)
        ad_bf = sc.sb("ad_bf", [128, 512], BF16)
        sg = [sc.sb(f"sg{i}", [128, 512], BF16) for i in range(2)]
        names = ["a", "kkraw", "sq", "rn", "kk", "kmul", "kh", "bb", "lw", "cwm", "Winv", "Wexc", "tmpw"]
        T = {n: sc.sb("rw_" + n, [128, 512]) for n in names}
        for alias, base in (("prod", "tmpw"), ("yc", "kkraw"), ("sq2", "sq"), ("rs2", "rn"), ("yn", "kmul")):
            T[alias] = T[base]
        bdn = ["Abd", "Bbd", "Kbd", "Rbd", "Vbd"]
        BDc = [{n: sc.sb(f"{n}{ct}", [128, 8, 128], BF16) for n in bdn} for ct in range(2)]
        for ct in range(2):
            for n in bdn:
                kb.memset('pool', BDc[ct][n][:], 0.0, w=[f"{n}{ct}"])
        Tc = [{n: sc.sb(f"rw_{n}{ct}", [128, 512]) for n in ("Winc", "bv", "g_sb", "yraw")} for ct in range(2)]
        II = sc.sb("II", [128, 128], BF16)
        kb.cp('dve', II[:], ident_f, r=["C"], w=["II"])
        tm_bf_c = [sc.sb(f"tm_bf{ct}", [128, 384], BF16) for ct in range(2)]
        gr_bf_c = [sc.sb(f"gr_bf{ct}", [128, 384], BF16) for ct in range(2)]
        pr_bf_c = [sc.sb(f"pr_bf{ct}", [128, 256], BF16) for ct in range(2)]
        pw_bf_c = [[sc.sb(f"pw_bf{ct}_{i}", [128, 256], BF16) for i in range(5)] for ct in range(2)]
        G_bf_c = [[sc.sb(f"G_bf{ct}_{i}", [128, 128], BF16) for i in range(2)] for ct in range(2)]
        Xn_bf_c = [sc.sb(f"Xn_bf{ct}", [128, 128], BF16) for ct in range(2)]
        U_bf_c = [sc.sb(f"U_bf{ct}", [128, 128], BF16) for ct in range(2)]
        ST_f = [sc.sb(f"ST_f{ct}", [128, 128]) for ct in range(2)]
        STt_c = [sc.sb(f"STt{ct}", [128, 128]) for ct in range(2)]
        STb = [sc.sb(f"STb{ct}", [128, 128], BF16) for ct in range(2)]
        yst = [sc.sb(f"wyst{i}", [128, 512], BF16) for i in range(2)]
        for ct in range(2):
            kb.memset('pool', ST_f[ct][:], 0.0, w=[f"ST_f{ct}"])
            kb.memset('pool', STb[ct][:], 0.0, w=[f"STb{ct}"])
        v3 = lambda ap: ap.rearrange("p (c t) -> p c t", c=8)
        for m in range(NM):
            load_hT(hT, m, "hT0")
            for q in range(10):
                sz = sizes[q]
                pr = f"pb{q % 2}"
                pa = pb[q % 2]
                for k in range(16):
                    kb.mm(pa[0:sz, :], lhsT=Wr[:, k, offs[q]:offs[q] + sz], rhs=hT[:, k, :], start=(k == 0),
                          stop=(k == 15), r=["Wr", "hT0"], w=[pr])
                rw_ = rawt[q % 2]
                rn_ = f"rawt{q % 2}"
                kb.cp('pool', rw_[0:sz, 0:1], halo[0:sz, q:q + 1], r=["halo"], w=[rn_])
                kb.cp('act', rw_[0:sz, 1:513], pa[0:sz, :], r=[pr], w=[rn_])
                kb.cp('pool', halo[0:sz, q:q + 1], rw_[0:sz, 512:513], r=[rn_], w=["halo"])
                d_ = dlt[q % 2]
                dn_ = f"dlt{q % 2}"
                kb.tt('pool', d_[0:sz, :], rw_[0:sz, 0:512], rw_[0:sz, 1:513], ALU.subtract, r=[rn_], w=[dn_])
                kb.stt('dve', sh[q][0:sz, :], d_[0:sz, :], mu[0:sz, q:q + 1], rw_[0:sz, 1:513], ALU.mult, ALU.add,
                       r=[dn_, "mu", rn_], w=[f"sh{q}"])
            kb.act(tw[0:96, :], sh[6][0:96, :], AF.Tanh, r=["sh6"], w=["tw"])
            kb.cp('pool', ad_bf[0:96, :], sh[7][0:96, :], r=["sh7"], w=["ad_bf"])
            for i in range(2):
                kb.act(sg[i][:], sh[8 + i][:], AF.Sigmoid, r=[f"sh{8 + i}"], w=[f"sg{i}"])
            for ct in range(2):
                cts = slice(ct * 128, (ct + 1) * 128)
                r_, k_, v_ = sh[ct], sh[2 + ct], sh[4 + ct]
                rn_r, rn_k, rn_v = f"sh{ct}", f"sh{2 + ct}", f"sh{4 + ct}"
                kb.mm(pb[2][:, :], lhsT=w2_bf[:, cts], rhs=tw[0:96, :], start=True, stop=True, r=["w2_bf", "tw"],
                      w=["pb2"])
                kb.act(T["lw"][:], pb[2][:, :], AF.Sigmoid, r=["pb2", "vec"], w=["rw_lw"], bias=vec[:, ct, 0:1])
                kb.ts('dve', T["lw"][:], T["lw"][:], -0.6065306597126334, ALU.mult, r=["rw_lw"], w=["rw_lw"])
                kb.mm(pb[3][:, :], lhsT=a2_bf[:, cts], rhs=ad_bf[0:96, :], start=True, stop=True, r=["a2_bf", "ad_bf"],
                      w=["pb3"])
                kb.act(T["a"][:], pb[3][:, :], AF.Sigmoid, r=["pb3", "vec"], w=["rw_a"], bias=vec[:, ct, 1:2])
                for i in range(2):
                    kb.mm(pb[2][:, :], lhsT=g2_bf[:, i, cts], rhs=sg[i][:], start=(i == 0), stop=(i == 1),
                          r=["g2_bf", f"sg{i}"], w=["pb2"])
                kb.cp('act', Tc[ct]["g_sb"][:], pb[2][:, :], r=["pb2"], w=[f"rw_g_sb{ct}"])
                kb.ts('pool', T["kkraw"][:], k_[:], vec[:, ct, 2:3], ALU.mult, r=[rn_k, "vec"], w=["rw_kkraw"])
                kb.tt('pool', T["sq"][:], T["kkraw"][:], T["kkraw"][:], ALU.mult, r=["rw_kkraw"], w=["rw_sq"])
                kb.mm(pb[3][:, :], lhsT=blk_f, rhs=T["sq"][:], start=True, stop=True, r=["C", "rw_sq"], w=["pb3"])
                kb.act(T["rn"][:], pb[3][:, :], AF.Sqrt, r=["pb3"], w=["rw_rn"])
                kb.ts('dve', T["rn"][:], T["rn"][:], 1e-12, ALU.max, r=["rw_rn"], w=["rw_rn"])
                kb.recip(T["rn"][:], T["rn"][:], r=["rw_rn"], w=["rw_rn"])
                kb.tt('pool', T["kk"][:], T["kkraw"][:], T["rn"][:], ALU.mult, r=["rw_kkraw", "rw_rn"], w=["rw_kk"])
                kb.ts('dve', T["kmul"][:], T["a"][:], vec[:, ct, 3:4], ALU.mult, r=["rw_a", "vec", "omka"],
                      w=["rw_kmul"], s2=omka[:, ct:ct + 1], op1=ALU.add)
                kb.tt('pool', T["kh"][:], k_[:], T["kmul"][:], ALU.mult, r=[rn_k, "rw_kmul"], w=["rw_kh"])
                kb.tt('pool', T["bb"][:], T["kk"][:], T["a"][:], ALU.mult, r=["rw_kk", "rw_a"], w=["rw_bb"])
                kb.scan(T["cwm"][:], rmask, T["lw"][:], r=["C", "rw_lw"], w=["rw_cwm"])
                kb.act(Tc[ct]["Winc"][:], T["cwm"][:], AF.Exp, r=["rw_cwm"], w=[f"rw_Winc{ct}"])
                kb.act(T["Winv"][:], T["cwm"][:], AF.Exp, r=["rw_cwm"], w=["rw_Winv"], scale=-1.0)
                kb.tt('pool', T["tmpw"][:], T["cwm"][:], T["lw"][:], ALU.subtract, r=["rw_cwm", "rw_lw"], w=["rw_tmpw"])
                kb.act(T["Wexc"][:], T["tmpw"][:], AF.Exp, r=["rw_tmpw"], w=["rw_Wexc"])
                pairs = (("Abd", T["kk"], "rw_kk", T["Wexc"], "rw_Wexc"), ("Bbd", T["bb"], "rw_bb", T["Winv"], "rw_Winv"),
                         ("Kbd", T["kh"], "rw_kh", T["Winv"], "rw_Winv"), ("Rbd", r_, rn_r, Tc[ct]["Winc"], f"rw_Winc{ct}"))
                ei = 0
                for (bn, a0, an0, a1, an1) in pairs:
                    for hh in range(2):
                        ph = slice(hh * 64, (hh + 1) * 64)
                        kb.tt('dve' if ei % 2 == 0 else 'pool', BDc[ct][bn][ph, :, hh * 64:(hh + 1) * 64], v3(a0[ph, :]),
                              v3(a1[ph, :]), ALU.mult, r=[an0, an1, bn + str(ct)], w=[bn + str(ct)])
                        ei += 1
                for hh in range(2):
                    ph = slice(hh * 64, (hh + 1) * 64)
                    kb.cp('pool', BDc[ct]["Vbd"][ph, :, hh * 64:(hh + 1) * 64], v3(v_[ph, :]), r=[rn_v, f"Vbd{ct}"], w=[f"Vbd{ct}"])
                kb.stt('dve', T["prod"][:], r_[:], vec[:, ct, 4:5], T["kh"][:], ALU.mult, ALU.mult,
                       r=[rn_r, "vec", "rw_kh"], w=["rw_tmpw"])
                kb.mm(pb[3][:, :], lhsT=blk_f, rhs=T["prod"][:], start=True, stop=True, r=["C", "rw_tmpw"], w=["pb3"])
                kb.tt('dve', Tc[ct]["bv"][:], pb[3][:, :], v_[:], ALU.mult, r=["pb3", rn_v], w=[f"rw_bv{ct}"])
            caps = []
            for ct in range(2):
                kb.tr.begin_capture()
                PBK = pb[4:8] if ct == 0 else pb[0:4]
                PBN = [f"pb{4 + k_}" for k_ in range(4)] if ct == 0 else [f"pb{k_}" for k_ in range(4)]
                STf, STn = ST_f[ct], f"ST_f{ct}"
                STbf, STbn = STb[ct], f"STb{ct}"
                for c in range(8):
                    A_c, B_c, K_c, R_c, V_c = (BDc[ct][n][:, c, :] for n in bdn)
                    for i, (X_c, xn) in enumerate(((B_c, f"Bbd{ct}"), (K_c, f"Kbd{ct}"), (V_c, f"Vbd{ct}"))):
                        kb.mm(PBK[0][:, i * 128:(i + 1) * 128], lhsT=X_c, rhs=ident_b, start=True, stop=True,
                              r=[xn, "Cb"], w=[PBN[0]])
                    kb.cp('act', tm_bf_c[ct][:], PBK[0][:, 0:384], r=[PBN[0]], w=[f"tm_bf{ct}"])
                    btm, ktm, vtm = tm_bf_c[ct][:, 0:128], tm_bf_c[ct][:, 128:256], tm_bf_c[ct][:, 256:384]
                    kb.mm(PBK[1][:, 0:128], lhsT=B_c, rhs=A_c, start=True, stop=True, r=[f"Bbd{ct}", f"Abd{ct}"], w=[PBN[1]])
                    kb.mm(PBK[1][:, 128:256], lhsT=A_c, rhs=B_c, start=True, stop=True, r=[f"Bbd{ct}", f"Abd{ct}"], w=[PBN[1]])
                    kb.mm(PBK[1][:, 256:384], lhsT=K_c, rhs=A_c, start=True, stop=True, r=[f"Kbd{ct}", f"Abd{ct}"], w=[PBN[1]])
                    kb.tt('dve', gr_bf_c[ct][:], PBK[1][:, 0:384], M3[:], ALU.mult, r=[PBN[1], "M3"], w=[f"gr_bf{ct}"])
                    kb.mm(PBK[2][:, 0:128], lhsT=B_c, rhs=R_c, start=True, stop=True, r=[f"Bbd{ct}", f"Rbd{ct}"], w=[PBN[2]])
                    kb.mm(PBK[2][:, 128:256], lhsT=K_c, rhs=R_c, start=True, stop=True, r=[f"Kbd{ct}", f"Rbd{ct}"], w=[PBN[2]])
                    kb.tt('dve', pr_bf_c[ct][:], PBK[2][:, 0:256], M2[:], ALU.mult, r=[PBN[2], "M2"], w=[f"pr_bf{ct}"])
                    Nn, Tt, TakT = gr_bf_c[ct][:, 0:128], gr_bf_c[ct][:, 128:256], gr_bf_c[ct][:, 256:384]
                    PrbT, PrkT = pr_bf_c[ct][:, 0:128], pr_bf_c[ct][:, 128:256]
                    kb.tt('pool', G_bf_c[ct][0][:], II[:], Nn, ALU.subtract, r=["II", f"gr_bf{ct}"], w=[f"G_bf{ct}_0"])
                    gcur = 0
                    Ncur, Tcur, ncn = Nn, Tt, f"gr_bf{ct}"
                    for lv in range(5):
                        kb.mm(PBK[2][:, 256:384], lhsT=Tcur, rhs=Ncur, start=True, stop=True, r=[ncn], w=[PBN[2]])
                        kb.mm(PBK[2][:, 384:512], lhsT=Ncur, rhs=Tcur, start=True, stop=True, r=[ncn], w=[PBN[2]])
                        kb.cp('act', pw_bf_c[ct][lv][:], PBK[2][:, 256:512], r=[PBN[2]], w=[f"pw_bf{ct}_{lv}"])
                        Ncur, Tcur, ncn = pw_bf_c[ct][lv][:, 0:128], pw_bf_c[ct][lv][:, 128:256], f"pw_bf{ct}_{lv}"
                        kb.mm(PBK[0][:, 384:512], lhsT=Tcur, rhs=G_bf_c[ct][gcur][:], start=True, stop=True,
                              r=[ncn, f"G_bf{ct}_{gcur}"], w=[PBN[0]])
                        kb.tt('dve', G_bf_c[ct][1 - gcur][:], PBK[0][:, 384:512], G_bf_c[ct][gcur][:], ALU.add,
                              r=[PBN[0], f"G_bf{ct}_{gcur}"], w=[f"G_bf{ct}_{1 - gcur}"])
                        gcur = 1 - gcur
                    Gf, Gn = G_bf_c[ct][gcur], f"G_bf{ct}_{gcur}"
                    kb.mm(PBK[3][:, 0:128], lhsT=A_c, rhs=STbf[:], start=True, stop=False, r=[f"Abd{ct}", STbn], w=[PBN[3]])
                    kb.mm(PBK[3][:, 0:128], lhsT=TakT, rhs=vtm, start=False, stop=True, r=[f"gr_bf{ct}", f"tm_bf{ct}"], w=[PBN[3]])
                    kb.act(Xn_bf_c[ct][:], PBK[3][:, 0:128], AF.Copy, r=[PBN[3]], w=[f"Xn_bf{ct}"], scale=-1.0)
                    kb.mm(PBK[3][:, 128:256], lhsT=Gf[:], rhs=Xn_bf_c[ct][:], start=True, stop=True, r=[Gn, f"Xn_bf{ct}"], w=[PBN[3]])
                    kb.cp('act', U_bf_c[ct][:], PBK[3][:, 128:256], r=[PBN[3]], w=[f"U_bf{ct}"])
                    kb.mm(PBK[3][:, 256:384], lhsT=STbf[:], rhs=R_c, start=True, stop=False, r=[STbn, f"Rbd{ct}"], w=[PBN[3]])
                    kb.mm(PBK[3][:, 256:384], lhsT=U_bf_c[ct][:], rhs=PrbT, start=False, stop=False, r=[f"U_bf{ct}", f"pr_bf{ct}"],
                          w=[PBN[3]])
                    kb.mm(PBK[3][:, 256:384], lhsT=vtm, rhs=PrkT, start=False, stop=True, r=[f"tm_bf{ct}", f"pr_bf{ct}"], w=[PBN[3]])
                    for hh in range(2):
                        ph = slice(hh * 64, (hh + 1) * 64)
                        kb.cp('act', Tc[ct]["yraw"][ph, c * 64:(c + 1) * 64], PBK[3][ph, 256 + hh * 64:256 + (hh + 1) * 64],
                              r=[PBN[3]], w=[f"rw_yraw{ct}"])
                    kb.mm(PBK[3][:, 384:512], lhsT=btm, rhs=U_bf_c[ct][:], start=True, stop=False, r=[f"tm_bf{ct}", f"U_bf{ct}"], w=[PBN[3]])
                    kb.mm(PBK[3][:, 384:512], lhsT=ktm, rhs=vtm, start=False, stop=True, r=[f"tm_bf{ct}"], w=[PBN[3]])
                    WL = Tc[ct]["Winc"][:, c * 64 + 63:c * 64 + 64]
                    kb.ts('pool', STt_c[ct][:], STf[:], WL, ALU.mult, r=[STn, f"rw_Winc{ct}"], w=[f"STt{ct}"])
                    kb.stt('dve', STf[:], PBK[3][:, 384:512], WL, STt_c[ct][:], ALU.mult, ALU.add, r=[PBN[3], f"rw_Winc{ct}", f"STt{ct}"],
                           w=[STn])
                    kb.cp('pool', STbf[:], STf[:], r=[STn], w=[STbn])
                caps.append(kb.tr.end_capture())
            kb.tr.replay_interleaved(caps)
            for ct in range(2):
                kb.mm(pb[2][:, :], lhsT=blk64_f, rhs=Tc[ct]["yraw"][:], start=True, stop=True, r=["C", f"rw_yraw{ct}"], w=["pb2"])
                kb.tt('dve', T["yc"][:], Tc[ct]["yraw"][:], pb[2][:, :], ALU.subtract, r=[f"rw_yraw{ct}", "pb2"], w=["rw_kkraw"])
                kb.tt('pool', T["sq2"][:], T["yc"][:], T["yc"][:], ALU.mult, r=["rw_kkraw"], w=["rw_sq"])
                kb.mm(pb[3][:, :], lhsT=blk64_f, rhs=T["sq2"][:], start=True, stop=True, r=["C", "rw_sq"], w=["pb3"])
                kb.act(T["rs2"][:], pb[3][:, :], AF.Sqrt, r=["pb3"], w=["rw_rn"], bias=64e-5)
                kb.recip(T["rs2"][:], T["rs2"][:], r=["rw_rn"], w=["rw_rn"])
                kb.tt('pool', T["yn"][:], T["yc"][:], T["rs2"][:], ALU.mult, r=["rw_kkraw", "rw_rn"], w=["rw_kmul"])
                kb.ts('dve', T["yn"][:], T["yn"][:], vec[:, ct, 5:6], ALU.mult, r=["rw_kmul", "vec"], w=["rw_kmul"],
                      s2=vec[:, ct, 6:7], op1=ALU.add)
                kb.tt('pool', T["yn"][:], T["yn"][:], Tc[ct]["bv"][:], ALU.add, r=["rw_kmul", f"rw_bv{ct}"], w=["rw_kmul"])
                ys, ysn = yst[ct], f"wyst{ct}"
                kb.tt('dve', ys[:], T["yn"][:], Tc[ct]["g_sb"][:], ALU.mult, r=["rw_kmul", f"rw_g_sb{ct}"], w=[ysn])
                kb.dma(yT_ap(256 + ct * 128, 128, m), ys[:], ysn, r=[ysn], w=["yT_w"])
                if ct == 1:
                    after_y(1, m, "yT_w")
        sc.close()

    outs = ["yT_m", "yT_r", "yT_w"]
    if env is not None:
        return None
    return kb, outs


def finish_p1(kb, outs):
    return kb.finish(outs)


import numpy as np


FF = 5504
NJ = 43


P2_PARAM_SHAPES = {
    "w_ada": ([D, 6 * D], F32), "b_ada": ([1, 6 * D], F32), "ng_col": ([128, 64], F32), "norm_g": ([4, D], F32),
    "w_gate": ([3, D, D], F32), "w_branch": ([3, 1024, D], F32), "w_out": ([D, D], F32), "w_up": ([D, 2 * FF], F32),
    "f_cv": ([128, 86, 4], F32), "w_down": ([FF, D], F32),
}


def p2_scratch(kb, sfx=""):
    return {
        'Wg_d': kb.dscratch("Wg_d" + sfx, [3, 16, 128, 16, 128], BF16),
        'Wb_d': kb.dscratch("Wb_d" + sfx, [3, 16, 128, 8, 128], BF16),
        'Wo_d': kb.dscratch("Wo_d" + sfx, [16, 128, D], BF16),
        'Wu_d': kb.dscratch("Wu_d" + sfx, [NJ, 128, 16, 256], BF16),
        'Wd_d': kb.dscratch("Wd_d" + sfx, [NJ, 128, D], BF16),
    }


def build_p2(Tq, env=None):
    NT = 128 + Tq
    if env is None:
        kb = KB()
        e = {}
        xh = kb.din("xh", [NT, D])
        yTh = kb.din("yTh", [3, 1024, NT], BF16)
        e['hmask'] = kb.din("hmask", [128, 1])
        e['c_col'] = kb.din("c_col", [128, 16])
        for k, (shp, dt_) in P2_PARAM_SHAPES.items():
            e[k] = kb.din(k, shp, dt_)
        cst = kb.din("cst", [128, 2048])
        xo = kb.dout("xo", [Tq, D])
        e.update(p2_scratch(kb))
        G = setup_globals(kb, cst)

        def load_x(dst, rn, row0, is_halo):
            kb.dma(dst, xh[row0:row0 + 128, :], rn, w=[rn])

        def load_y(ybf, t0, ntok, is_halo):
            kb.dma(ybf[:, :, :, 0:ntok], yTh[:, :, t0:t0 + ntok].rearrange("i (k p) t -> p i k t", p=128), "ybf",
                   w=["ybf"])

        def store_x(src, rn, o0):
            kb.dma(xo[o0:o0 + 128, :], src, rn, r=[rn], w=["xo"])
        e['load_x'], e['load_y'], e['store_x'] = load_x, load_y, store_x
    else:
        kb = env['kb']
        e = env
        G = env['G']
    nc = kb.nc
    hmask, c_col, w_ada, b_ada, ng_col, norm_g = (e[k] for k in ('hmask', 'c_col', 'w_ada', 'b_ada', 'ng_col', 'norm_g'))
    w_gate, w_branch, w_out, w_up, f_cv, w_down = (e[k] for k in ('w_gate', 'w_branch', 'w_out', 'w_up', 'f_cv', 'w_down'))
    Wg_d, Wb_d, Wo_d, Wu_d, Wd_d = (e[k] for k in ('Wg_d', 'Wb_d', 'Wo_d', 'Wu_d', 'Wd_d'))
    load_x, load_y, store_x = e['load_x'], e['load_y'], e['store_x']
    pb = G['pb']
    C, Cb = G['C'], G['Cb']
    ident_f = C[:, 0:128]
    ones_row = C[0:1, 256:384]
    ident_b = Cb[:, 0:128]

    sc = Scope(kb)
    NSTG = 6
    stg = [sc.sb(f"stg{i}", [128, 2048]) for i in range(NSTG)]
    stb = [sc.sb(f"stb{i}", [128, 2048], BF16) for i in range(NSTG)]
    cnt = [0]

    def cast_block(src_ap, dst_ap, shape):
        i = cnt[0] % NSTG
        cnt[0] += 1
        n = int(np.prod(shape))
        sv = stg[i][:, 0:n]
        bv = stb[i][:, 0:n]
        if len(shape) == 2:
            sv = sv.rearrange("p (a b) -> p a b", a=shape[0])
            bv = bv.rearrange("p (a b) -> p a b", a=shape[0])
        kb.dma(sv, src_ap, f"stg{i}", w=[f"stg{i}"])
        eng = ('dve', 'act', 'dve', 'act', 'pool', 'dve')[i]
        kb.cp(eng, stb[i][:, 0:n], stg[i][:, 0:n], r=[f"stg{i}"], w=[f"stb{i}"])
        kb.dma(dst_ap, bv, f"stb{i}", r=[f"stb{i}"], w=[f"Wscr{cnt[0]}"])

    for i in range(3):
        wv = w_gate[i].rearrange("(k p) n -> p k n", p=128)
        for nt in range(16):
            cast_block(wv[:, :, nt * 128:(nt + 1) * 128], Wg_d[i, nt], (16, 128))
        wv = w_branch[i].rearrange("(k p) n -> p k n", p=128)
        for nt in range(16):
            cast_block(wv[:, :, nt * 128:(nt + 1) * 128], Wb_d[i, nt], (8, 128))
    for kn in range(16):
        cast_block(w_out[kn * 128:(kn + 1) * 128, :], Wo_d[kn], (2048,))
    wv = w_up.rearrange("(k p) n -> p k n", p=128)
    for j in range(NJ):
        cast_block(wv[:, :, j * 128:(j + 1) * 128], Wu_d[j, :, :, 0:128], (16, 128))
        cast_block(wv[:, :, FF + j * 128:FF + (j + 1) * 128], Wu_d[j, :, :, 128:256], (16, 128))
        cast_block(w_down[j * 128:(j + 1) * 128, :], Wd_d[j], (2048,))
    sc.close()

    scm = Scope(kb)
    GTm = scm.sb("GTm", [128, 2048])
    GTf = scm.sb("GTf", [128, 2048])
    cols = scm.sb("cols", [128, 64])
    ngc = scm.sb("ngc", [128, 64])
    kb.dma(ngc[:], ng_col, "ngc", w=["ngc"])
    sc = Scope(kb)
    rows = emit_mod(kb, sc, c_col, w_ada, b_ada, [0, 1, 2, 3, 4, 5], ones_row, pb, "m2")
    kb.dma(GTm[:], norm_g[1:2, :].partition_broadcast(128), "GTm", w=["GTm"])
    kb.dma(GTf[:], norm_g[3:4, :].partition_broadcast(128), "GTf", w=["GTf"])

    def mk_gt(dst, dn):
        def f(c, pa, pr):
            kb.tt('dve', dst[:, c * 512:(c + 1) * 512], pa, dst[:, c * 512:(c + 1) * 512], ALU.mult, r=[pr, dn], w=[dn])
        return f

    bcast_row(kb, rows[2][0], rows[2][1], ones_row, pb, mk_gt(GTm, "GTm"))
    bcast_row(kb, rows[5][0], rows[5][1], ones_row, pb, mk_gt(GTf, "GTf"))
    for slot, vid in enumerate((1, 0, 4, 3)):
        row, rn = rows[vid]
        for k in range(16):
            kb.mm(pb[3][:, slot * 16 + k:slot * 16 + k + 1], lhsT=row[0:1, k * 128:(k + 1) * 128],
                  rhs=ones_row[0:1, 0:1], start=True, stop=True, r=[rn, "C"], w=["pb3"])
    kb.cp('act', cols[:], pb[3][:, 0:64], r=["pb3"], w=["cols"])
    for slot, gsl in ((0, 0), (2, 2)):
        kb.stt('dve', cols[:, slot * 16:(slot + 1) * 16], cols[:, slot * 16:(slot + 1) * 16], 1.0,
               ngc[:, gsl * 16:(gsl + 1) * 16], ALU.add, ALU.mult, r=["cols", "ngc"], w=["cols"])
    sc.close()

    xres = [scm.sb(f"xres{i}", [128, 2048]) for i in range(2)]
    htmp = scm.sb("htmp", [128, 2048])
    hb = scm.sb("hb", [128, 2048], BF16)
    ss = scm.sb("ss", [128, 1])
    rstd = scm.sb("rstd", [128, 1])
    hT = scm.sb("hT", [128, 16, 256], BF16)
    ybf = scm.sb("ybf", [128, 3, 8, 256], BF16)
    mT = scm.sb("mT", [128, 16, 256], BF16)
    sig = [scm.sb(f"sig{i}", [128, 256]) for i in range(2)]
    macc = scm.sb("macc", [128, 256])
    mtmp = scm.sb("mtmp", [128, 256])
    Wg_s = [scm.sb(f"Wg_s{i}", [128, 16, 128], BF16) for i in range(3)]
    Wb_s = [scm.sb(f"Wb_s{i}", [128, 8, 128], BF16) for i in range(3)]
    Wo_s = [scm.sb(f"Wo_s{i}", [128, 2048], BF16) for i in range(2)]
    Wu_s = [scm.sb(f"Wu_s{i}", [128, 16, 256], BF16) for i in range(2)]
    Wd_s = [scm.sb(f"Wd_s{i}", [128, 2048], BF16) for i in range(2)]
    ymix = scm.sb("ymix", [128, 2048])
    rawf = [scm.sb(f"rawf{i}", [128, 258]) for i in range(2)]
    facc = [scm.sb(f"facc{i}", [128, 256]) for i in range(2)]
    gg = scm.sb("gg", [128, 256])
    aT = scm.sb("aT", [128, NJ, 256], BF16)
    fhalo = scm.sb("fhalo", [128, 86, 2])
    fc = scm.sb("fc", [128, 86, 4])
    hm = scm.sb("hm", [128, 1])
    kb.dma(fc[:], f_cv, "fc", w=["fc"])
    kb.dma(hm[:], hmask, "hm", w=["hm"])
    kb.memset('pool', fhalo[:], 0.0, w=["fhalo"])
    if 'extra_alloc' in e:
        e['extra_alloc'](scm, dict(ymix=ymix))
    slab_ctr = {"g": 0, "b": 0, "o": 0, "u": 0, "d": 0}

    def emit_h(nsub, ntok, gslot):
        for sub in range(nsub):
            xn = f"xres{sub}"
            kb.act(htmp[:], xres[sub][:], AF.Square, r=[xn], w=["htmp", "ss"], accum=ss[:])
            kb.act(rstd[:], ss[:], AF.Sqrt, r=["ss"], w=["rstd"], bias=1e-6, scale=1.0 / D)
            kb.recip(rstd[:], rstd[:], r=["rstd"], w=["rstd"])
            kb.ts('dve', hb[:], xres[sub][:], rstd[:, 0:1], ALU.mult, r=[xn, "rstd"], w=["hb"])
            for q in range(4):
                bank = 6 + (q % 2)
                for kk in range(4):
                    k = q * 4 + kk
                    kb.mm(pb[bank][:, kk * 128:(kk + 1) * 128], lhsT=hb[:, k * 128:(k + 1) * 128], rhs=ident_b,
                          start=True, stop=True, r=["hb", "Cb"], w=[f"pb{bank}"])
                for kk in range(4):
                    k = q * 4 + kk
                    gcol = cols[:, gslot * 16 + k:gslot * 16 + k + 1]
                    scol = cols[:, (gslot + 1) * 16 + k:(gslot + 1) * 16 + k + 1]
                    if kk % 2 == 0:
                        kb.act(hT[:, k, sub * 128:(sub + 1) * 128], pb[bank][:, kk * 128:(kk + 1) * 128], AF.Identity,
                               r=[f"pb{bank}", "cols"], w=["hT"], bias=scol, scale=gcol)
                    else:
                        kb.ts('dve', hT[:, k, sub * 128:(sub + 1) * 128], pb[bank][:, kk * 128:(kk + 1) * 128], gcol,
                              ALU.mult, r=[f"pb{bank}", "cols"], w=["hT"], s2=scol, op1=ALU.add)

    def post_norm(nsub_i, src_ps_banks, GT, gtn, sub, store_ap):
        xn = f"xres{sub}"
        kb.act(htmp[:], ymix[:], AF.Square, r=["ymix"], w=["htmp", "ss"], accum=ss[:])
        kb.act(rstd[:], ss[:], AF.Sqrt, r=["ss"], w=["rstd"], bias=1e-6, scale=1.0 / D)
        kb.recip(rstd[:], rstd[:], r=["rstd"], w=["rstd"])
        kb.stt('dve', htmp[:], ymix[:], rstd[:, 0:1], GT[:], ALU.mult, ALU.mult, r=["ymix", "rstd", gtn], w=["htmp"])
        kb.tt('pool', xres[sub][:], xres[sub][:], htmp[:], ALU.add, r=[xn, "htmp"], w=[xn])
        if store_ap is not None:
            store_x(xres[sub][:], xn, store_ap)

    def tile(t0, ntok, is_halo, nxt=None):
        nsub = ntok // 128
        tsl = slice(0, ntok)
        for sub in range(nsub):
            xn = f"xres{sub}"
            load_x(xres[sub][:], xn, t0 + sub * 128, is_halo)
        if is_halo:
            load_y(ybf, t0, ntok, is_halo)
        emit_h(nsub, ntok, 0)
        for nt in range(16):
            for i in range(3):
                gi = slab_ctr["g"] % 3
                slab_ctr["g"] += 1
                kb.dma(Wg_s[gi][:], Wg_d[i, nt], f"Wg_s{gi}", w=[f"Wg_s{gi}"])
                kb.dma(Wb_s[gi][:], Wb_d[i, nt], f"Wb_s{gi}", w=[f"Wb_s{gi}"])
                for k in range(16):
                    kb.mm(pb[4][:, tsl], lhsT=Wg_s[gi][:, k, :], rhs=hT[:, k, tsl], start=(k == 0), stop=(k == 15),
                          r=[f"Wg_s{gi}", "hT"], w=["pb4"])
                for k in range(8):
                    kb.mm(pb[5][:, tsl], lhsT=Wb_s[gi][:, k, :], rhs=ybf[:, i, k, tsl], start=(k == 0), stop=(k == 7),
                          r=[f"Wb_s{gi}", "ybf"], w=["pb5"])
                sg_ = sig[i % 2]
                sgn_ = f"sig{i % 2}"
                kb.act(sg_[:, tsl], pb[4][:, tsl], AF.Sigmoid, r=["pb4"], w=[sgn_])
                if i == 0:
                    kb.tt('dve', macc[:, tsl], pb[5][:, tsl], sg_[:, tsl], ALU.mult, r=["pb5", sgn_], w=["macc"])
                else:
                    kb.tt('dve', mtmp[:, tsl], pb[5][:, tsl], sg_[:, tsl], ALU.mult, r=["pb5", sgn_], w=["mtmp"])
                    if i == 1:
                        kb.tt('pool', macc[:, tsl], macc[:, tsl], mtmp[:, tsl], ALU.add, r=["macc", "mtmp"], w=["macc"])
                    else:
                        kb.tt('pool', mT[:, nt, tsl], macc[:, tsl], mtmp[:, tsl], ALU.add, r=["macc", "mtmp"],
                              w=["mT"])
        if nxt is not None:
            load_y(ybf, nxt[0], nxt[1], False)
        for sub in range(nsub):
            for kn in range(16):
                oi = slab_ctr["o"] % 2
                slab_ctr["o"] += 1
                kb.dma(Wo_s[oi][:], Wo_d[kn], f"Wo_s{oi}", w=[f"Wo_s{oi}"])
                for c in range(4):
                    kb.mm(pb[c][:, :], lhsT=mT[:, kn, sub * 128:(sub + 1) * 128], rhs=Wo_s[oi][:, c * 512:(c + 1) * 512],
                          start=(kn == 0), stop=(kn == 15), r=["mT", f"Wo_s{oi}"], w=[f"pb{c}"])
            for c in range(4):
                kb.cp('act', ymix[:, c * 512:(c + 1) * 512], pb[c][:, :], r=[f"pb{c}"], w=["ymix"])
            post_norm(nsub, None, GTm, "GTm", sub, None)
        emit_h(nsub, ntok, 2)
        for j in range(NJ):
            ui = slab_ctr["u"] % 2
            slab_ctr["u"] += 1
            kb.dma(Wu_s[ui][:], Wu_d[j], f"Wu_s{ui}", w=[f"Wu_s{ui}"])
            for half in range(2):
                bank = 4 + half
                jj = half * NJ + j
                for k in range(16):
                    kb.mm(pb[bank][:, tsl], lhsT=Wu_s[ui][:, k, half * 128:(half + 1) * 128], rhs=hT[:, k, tsl],
                          start=(k == 0), stop=(k == 15), r=[f"Wu_s{ui}", "hT"], w=[f"pb{bank}"])
                rw_ = rawf[half]
                rn_ = f"rawf{half}"
                kb.cp('pool', rw_[:, 0:2], fhalo[:, jj, :], r=["fhalo"], w=[rn_])
                kb.cp('act', rw_[:, 2:2 + ntok], pb[bank][:, tsl], r=[f"pb{bank}"], w=[rn_])
                if is_halo:
                    kb.ts('pool', fhalo[:, jj, :], rw_[:, ntok:ntok + 2], hm[:, 0:1], ALU.mult, r=[rn_, "hm"],
                          w=["fhalo"])
                else:
                    kb.cp('pool', fhalo[:, jj, :], rw_[:, ntok:ntok + 2], r=[rn_], w=["fhalo"])
                fa = facc[half]
                fan = f"facc{half}"
                kb.ts('dve', fa[:, tsl], rw_[:, 2:2 + ntok], fc[:, jj, 2:3], ALU.mult, r=[rn_, "fc"], w=[fan],
                      s2=fc[:, jj, 3:4], op1=ALU.add)
                kb.stt('dve', fa[:, tsl], rw_[:, 1:1 + ntok], fc[:, jj, 1:2], fa[:, tsl], ALU.mult, ALU.add,
                       r=[rn_, "fc", fan], w=[fan])
                kb.stt('dve', fa[:, tsl], rw_[:, 0:ntok], fc[:, jj, 0:1], fa[:, tsl], ALU.mult, ALU.add,
                       r=[rn_, "fc", fan], w=[fan])
            if not is_halo:
                kb.act(gg[:, tsl], facc[0][:, tsl], AF.Gelu_apprx_tanh, r=["facc0"], w=["gg"])
                kb.tt('pool', aT[:, j, tsl], gg[:, tsl], facc[1][:, tsl], ALU.mult, r=["gg", "facc1"], w=["aT"])
        if is_halo:
            return
        for sub in range(nsub):
            for j in range(NJ):
                di = slab_ctr["d"] % 2
                slab_ctr["d"] += 1
                kb.dma(Wd_s[di][:], Wd_d[j], f"Wd_s{di}", w=[f"Wd_s{di}"])
                for c in range(4):
                    kb.mm(pb[c][:, :], lhsT=aT[:, j, sub * 128:(sub + 1) * 128], rhs=Wd_s[di][:, c * 512:(c + 1) * 512],
                          start=(j == 0), stop=(j == NJ - 1), r=["aT", f"Wd_s{di}"], w=[f"pb{c}"])
            for c in range(4):
                kb.cp('act', ymix[:, c * 512:(c + 1) * 512], pb[c][:, :], r=[f"pb{c}"], w=["ymix"])
            o0 = t0 - 128 + sub * 128
            post_norm(nsub, None, GTf, "GTf", sub, o0)

    tiles = [(0, 128, True)] + [(t, 256, False) for t in range(128, NT, 256)]
    for ti, (t, ntk, hl) in enumerate(tiles):
        tile(t, ntk, hl, nxt=(tiles[ti + 1][:2] if ti + 1 < len(tiles) else None))
    scm.close()
    if env is not None:
        return None
    return kb, ["xo"]


import numpy as np


RG = [[0, 1, 2, 3], [4, 5, 6, 7]]


def build_fused(S, depth=2):
    Tq = S // 4
    NM = S // 512
    NMq = NM // 4
    NT = 128 + Tq
    kb = KB()
    cst = kb.din("cst", [128, 2048])
    cstb = kb.din("cstb", [128, 1024])
    c_col = kb.din("c_col", [128, 16])
    pos = kb.din("pos", [1, S], I32)
    xh = kb.din("xh", [NT, D])
    hmask = kb.din("hmask", [128, 1])
    selv = kb.din("selv", [128, 8])
    xo = kb.dout("xo", [Tq, D])
    L = []
    for l in range(depth):
        e = {}
        for k, (shp, dt_) in P1_PARAM_SHAPES.items():
            e[k] = kb.din(f"{k}_{l}", shp, dt_)
        for k, (shp, dt_) in P2_PARAM_SHAPES.items():
            e[k] = kb.din(f"{k}_{l}", shp, dt_)
        L.append(e)
    G = setup_globals(kb, cst)
    CT = min(2048, Tq)
    NCH = S // CT
    MPC = CT // 512
    hT_own = kb.dscratch("hT_own", [NMq, 128, 16, 512], BF16)
    hTg = kb.dscratch("hTg", [2 * NMq, 4 * 64, 8192], BF16)
    ysc = [kb.dscratch(f"ysc{i}", [NCH, 256, CT], BF16) for i in range(3)]
    yall = [kb.dscratch(f"yall{i}", [NCH, 1024, CT], BF16) for i in range(3)]
    xs1 = kb.dscratch("xs1", [Tq, D])
    xlast = kb.dscratch("xlast", [128, D])
    xl_all = kb.dscratch("xl_all", [4 * 128, D])
    w2s = p2_scratch(kb)
    sel = kb.sb("sel", [128, 8])
    kb.dma(sel[:], selv, "sel", w=["sel"])

    def allgather(src2d, dst2d, key, reads, writes):
        kb.tr.dma('pool', lambda e_: e_.collective_compute("AllGather", ALU.bypass, replica_groups=RG,
                                                            ins=[src2d.opt()], outs=[dst2d.opt()]),
                  key, reads=reads, writes=writes, inc=1)

    for l in range(depth):
        P = L[l]
        x_own = xh[128:NT, :] if l == 0 else xs1
        def after_h(m):
            for half in range(2):
                allgather(hT_own[m, half * 64:(half + 1) * 64].rearrange("p k t -> p (k t)"), hTg[2 * m + half], "ag",
                          reads=[f"hTd{m}"], writes=[f"hTg{2 * m + half}"])

        def load_hT(dst, m, key):
            r_, ml = divmod(m, NMq)
            for half in range(2):
                kb.dma(dst[half * 64:(half + 1) * 64, :, :],
                       hTg[2 * ml + half, r_ * 64:(r_ + 1) * 64, :].rearrange("p (k t) -> p k t", k=16),
                       f"{key}_{half}", r=[f"hTg{2 * ml + half}"], w=[key])

        def yT_ap(row0, nrows, m):
            i, rr = divmod(row0, 256)
            c, mm = divmod(m, MPC)
            return ysc[i][c, rr:rr + nrows, mm * 512:(mm + 1) * 512]

        def after_y(i, m, res):
            if (m + 1) % MPC == 0:
                c = m // MPC
                allgather(ysc[i][c], yall[i][c], "ag", reads=[res], writes=[f"yall{i}_{c}"])

        env1 = dict(P)
        env1.update(kb=kb, G=G, x=x_own, NH=Tq // 128, hTo=hT_own, hTd=None, yT=None, c_col=c_col, pos=pos, cstb=cstb,
                    after_h=after_h, load_hT=load_hT, yT_ap=yT_ap, after_y=after_y)
        build_p1(S, parts=('h',), env=env1)
        build_p1(S, parts=('mamba', 'rwkv', 'ret'), env=env1)
        if l > 0:
            allgather(xlast, xl_all, "ag", reads=["xlast"], writes=["xlall"])

        X = {}

        def extra_alloc(scm, bufs, X=X):
            X['cand'] = scm.sb("ycand", [128, 3, 8, 256], BF16)
            X['ymix'] = bufs['ymix']

        def load_x(dst, rn, row0, is_halo, l=l, X=X):
            if l == 0:
                kb.dma(dst, xh[row0:row0 + 128, :], rn, w=[rn])
            elif not is_halo:
                kb.dma(dst, xs1[row0 - 128:row0, :], rn, r=["xs1"], w=[rn])
            else:
                for q in range(4):
                    kb.dma(X['ymix'][:], xl_all[q * 128:(q + 1) * 128, :], "ymix", r=["xlall"], w=["ymix"])
                    if q == 0:
                        kb.ts('dve', dst, X['ymix'][:], sel[:, 4:5], ALU.mult, r=["ymix", "sel"], w=[rn])
                    else:
                        kb.stt('dve', dst, X['ymix'][:], sel[:, 4 + q:5 + q], dst, ALU.mult, ALU.add,
                               r=["ymix", "sel", rn], w=[rn])

        def load_y(ybf, t0, ntok, is_halo, X=X):
            cand = X['cand']
            for q in range(4):
                gt0 = q * Tq + t0 - 128
                if gt0 < 0:
                    kb.memset('pool', cand[:, :, :, 0:ntok], 0.0, w=[f"ycand{g}" for g in range(3)])
                else:
                    c, off = divmod(gt0, CT)
                    for g in range(3):
                        kb.dma(cand[:, g, :, 0:ntok],
                               yall[g][c, :, off:off + ntok].rearrange("(k p) t -> p k t", p=128),
                               f"ycand{g}", r=[f"yall{g}_{c}"], w=[f"ycand{g}"])
                if q == 0:
                    kb.ts('dve', ybf[:, :, :, 0:ntok], cand[:, :, :, 0:ntok], sel[:, 0:1], ALU.mult,
                          r=["ycand0", "ycand1", "ycand2", "sel"], w=["ybf"])
                else:
                    kb.stt('dve', ybf[:, :, :, 0:ntok], cand[:, :, :, 0:ntok], sel[:, q:q + 1], ybf[:, :, :, 0:ntok],
                           ALU.mult, ALU.add, r=["ycand0", "ycand1", "ycand2", "sel", "ybf"], w=["ybf"])

        def store_x(src, rn, o0, l=l):
            if l < depth - 1:
                kb.dma(xs1[o0:o0 + 128, :], src, rn, r=[rn], w=["xs1"])
                if o0 == Tq - 128:
                    kb.dma(xlast, src, rn, r=[rn], w=["xlast"])
            else:
                kb.dma(xo[o0:o0 + 128, :], src, rn, r=[rn], w=["xo"])

        env2 = dict(P)
        env2.update(w2s)
        env2.update(kb=kb, G=G, hmask=hmask, c_col=c_col, load_x=load_x, load_y=load_y, store_x=store_x,
                    extra_alloc=extra_alloc)
        build_p2(Tq, env=env2)
    nc = kb.finish(["xo"])
    return kb, nc


def fused_core_inputs(inp, b, r, S, depth):
    Tq = S // 4
    o = {}
    for l in range(depth):
        p1 = p1_core_inputs(inp, l, b, r)
        for k in P1_PARAM_SHAPES:
            o[f"{k}_{l}"] = p1[k]
        p2 = p2_core_inputs(inp, l, b, r, Tq, None, None)
        for k in P2_PARAM_SHAPES:
            o[f"{k}_{l}"] = p2[k]
    o['cst'] = p1_consts()
    o['cstb'] = p1_core_consts(r)
    o['c_col'] = np.ascontiguousarray(inp['c'][b].reshape(16, 128).T)
    o['pos'] = np.ascontiguousarray(inp['positions'][b][None, :]).astype(np.int32)
    x_b = inp['x'][b]
    xh = np.zeros((128 + Tq, 2048), np.float32)
    if r == 0:
        xh[128:] = x_b[0:Tq]
    else:
        xh[:] = x_b[r * Tq - 128:(r + 1) * Tq]
    o['xh'] = xh
    o['hmask'] = np.full((128, 1), 0.0 if r == 0 else 1.0, np.float32)
    sv = np.zeros((128, 8), np.float32)
    sv[:, r] = 1.0
    if r > 0:
        sv[:, 4 + r - 1] = 1.0
    o['selv'] = sv
    return o


_CACHE = {}


def kernel(**inputs):
    inp = {k: np.asarray(v) for k, v in inputs.items()}
    inp['x'] = np.ascontiguousarray(inp['x'], dtype=np.float32)
    B, S, _ = inp['x'].shape
    depth = inp['w_in'].shape[0]
    Tq = S // 4
    key = (S, depth)
    if key not in _CACHE:
        _CACHE[key] = build_fused(S, depth)[1]
    nc = _CACHE[key]
    in_maps = []
    for core in range(8):
        b, r = divmod(core, 4)
        in_maps.append(fused_core_inputs(inp, b, r, S, depth))
    res = run_bass_kernel_spmd(nc, in_maps, core_ids=list(range(8)))
    out = np.zeros((B, S, 2048), np.float32)
    for core in range(8):
        b, r = divmod(core, 4)
        out[b, r * Tq:(r + 1) * Tq] = np.asarray(res.results[core]['xo'])
    return out
```

```python
import numpy as np
import concourse.bass as bass
import concourse.mybir as mybir
from concourse.bass_utils import run_bass_kernel_spmd

F32 = mybir.dt.float32
BF16 = mybir.dt.bfloat16
I32 = mybir.dt.int32
AF = mybir.ActivationFunctionType
ALU = mybir.AluOpType
AX = mybir.AxisListType
EPOCH = 30000
import os as _os
SAME_ENGINE_ORDERED = tuple(_os.environ.get('SEO', 'pe').split(','))
D = 2048


class Tracker:
    ENGS = ('pe', 'act', 'dve', 'pool', 'sp')

    def __init__(self, nc):
        self.nc = nc
        self.ops = {e: [] for e in self.ENGS}
        self.cur_sem = {}
        self.cnt = {}
        self.nsem = 0
        for e in self.ENGS:
            self.cur_sem[e] = self._new_sem(e)
            self.cnt[e] = 0
        self.known = {e: {} for e in self.ENGS}
        self.last_w = {}
        self.readers = {}
        self.dma_sems = {}
        self.dma_cnt = {}
        self.nops = 0
        self._old_epochs = []

    def _new_sem(self, tag):
        self.nsem += 1
        return self.nc.alloc_semaphore(f"s{self.nsem}_{tag}")

    def _waits_for(self, eng, reads, writes, is_dma):
        need = {}

        def add(ev):
            sem, val, src, src_dma = ev
            if src == eng and eng in SAME_ENGINE_ORDERED and not src_dma and not is_dma:
                return
            k = id(sem)
            if self.known[eng].get(k, 0) >= val:
                return
            if k not in need or need[k][1] < val:
                need[k] = (sem, val)

        for r in reads:
            ev = self.last_w.get(r)
            if ev is not None:
                add(ev)
        for w in writes:
            ev = self.last_w.get(w)
            if ev is not None:
                add(ev)
            rd = self.readers.get(w)
            if rd:
                for ev in rd.values():
                    add(ev)
        out = list(need.values())
        for sem, val in out:
            self.known[eng][id(sem)] = val
        return out

    def _commit(self, ev, reads, writes):
        for r in reads:
            self.readers.setdefault(r, {})[id(ev[0])] = ev
        for w in writes:
            self.last_w[w] = ev
            self.readers[w] = {}

    def begin_capture(self):
        self._cap = []

    def end_capture(self):
        c, self._cap = self._cap, None
        return c

    def replay_interleaved(self, caps):
        idx = [0] * len(caps)
        while True:
            done = True
            for j, c in enumerate(caps):
                if idx[j] < len(c):
                    kind, args = c[idx[j]]
                    idx[j] += 1
                    done = False
                    if kind == 'op':
                        self.op(*args)
                    else:
                        self.dma(*args)
            if done:
                break

    def op(self, eng, fn, reads=(), writes=()):
        if getattr(self, '_cap', None) is not None:
            self._cap.append(('op', (eng, fn, tuple(reads), tuple(writes))))
            return None
        pr = tuple(r for r in reads if isinstance(r, str) and r.startswith('pb'))
        if pr and eng != 'pe':
            writes = tuple(writes) + pr
        waits = self._waits_for(eng, reads, writes, False)
        if self.cnt[eng] >= EPOCH:
            self._old_epochs.append((self.cur_sem[eng], self.cnt[eng]))
            self.cur_sem[eng] = self._new_sem(eng)
            self.cnt[eng] = 0
        self.cnt[eng] += 1
        sem = self.cur_sem[eng]
        ev = (sem, self.cnt[eng], eng, False)
        self.ops[eng].append((waits, fn, sem, 1))
        self._commit(ev, reads, writes)
        self.nops += 1
        return ev

    def dma(self, eng, fn, key, reads=(), writes=(), inc=16):
        if getattr(self, '_cap', None) is not None:
            self._cap.append(('dma', (eng, fn, key, tuple(reads), tuple(writes), inc)))
            return None
        if key not in self.dma_sems:
            self.dma_sems[key] = self._new_sem('d' + str(key))
            self.dma_cnt[key] = 0
        sem = self.dma_sems[key]
        chan = ('__chan__', key)
        waits = self._waits_for(eng, tuple(reads), tuple(writes) + (chan,), True)
        self.dma_cnt[key] += inc
        ev = (sem, self.dma_cnt[key], eng, True)
        self.ops[eng].append((waits, fn, sem, inc))
        self._commit(ev, reads, tuple(writes) + (chan,))
        self.nops += 1
        return ev

    def barrier(self):
        latest = {}
        for e in self.ENGS:
            for (waits, fn, sem, inc) in ():
                pass
        for e in self.ENGS:
            if self.cnt[e] > 0:
                latest[id(self.cur_sem[e])] = (self.cur_sem[e], self.cnt[e])
        for k, sem in self.dma_sems.items():
            if self.dma_cnt[k] > 0:
                latest[id(sem)] = (sem, self.dma_cnt[k])
        for (sem, val) in self._old_epochs:
            latest[id(sem)] = (sem, val)
        for e in self.ENGS:
            waits = []
            for k, (sem, val) in latest.items():
                if self.known[e].get(k, 0) < val:
                    waits.append((sem, val))
                    self.known[e][k] = val
            if waits:
                self.ops[e].append((waits, None, None, 0))

    def wait_all(self, eng, resources):
        waits = self._waits_for(eng, resources, (), True)
        self.ops[eng].append((waits, None, None, 0))

    def emit(self):
        nc = self.nc
        ops = self.ops
        with nc.Block() as block:
            def run(e, lst):
                for waits, fn, sem, inc in lst:
                    for s, v in waits:
                        e.wait_ge(s, v)
                    if fn is not None:
                        fn(e).then_inc(sem, inc)

            @block.tensor
            def _(e):
                run(e, ops['pe'])

            @block.scalar
            def _(e):
                run(e, ops['act'])

            @block.vector
            def _(e):
                run(e, ops['dve'])

            @block.gpsimd
            def _(e):
                run(e, ops['pool'])

            @block.sync
            def _(e):
                run(e, ops['sp'])


class KB:
    def __init__(self, name="k"):
        self.nc = bass.Bass("TRN2", target_bir_lowering=False)
        self.nc.allow_low_precision("bf16 matmul operands with fp32 PSUM accumulation")
        self.tr = Tracker(self.nc)
        self._n = 0
        self.outs = []

    def din(self, name, shape, dt=F32):
        return self.nc.dram_tensor(name, list(shape), dt, kind="ExternalInput").ap()

    def dout(self, name, shape, dt=F32):
        self.outs.append(name)
        return self.nc.dram_tensor(name, list(shape), dt, kind="ExternalOutput").ap()

    def dscratch(self, name, shape, dt=F32):
        return self.nc.dram_tensor(name, list(shape), dt, kind="Internal").ap()

    def sb(self, name, shape, dt=F32):
        return self.nc.alloc_sbuf_tensor(name, list(shape), dt)

    def ps(self, name, shape, dt=F32):
        return self.nc.alloc_psum_tensor(name, list(shape), dt)

    def dma(self, out, in_, key, r=(), w=(), eng='sp'):
        self.tr.dma(eng, lambda e: e.dma_start(out=out, in_=in_), key, reads=r, writes=w)

    def mm(self, out, lhsT, rhs, start, stop, r, w):
        self.tr.op('pe', lambda e: e.matmul(out, lhsT=lhsT, rhs=rhs, start=start, stop=stop), reads=r, writes=w)

    def tp(self, out, in_, ident, r, w):
        self.tr.op('pe', lambda e: e.transpose(out=out, in_=in_, identity=ident), reads=r, writes=w)

    def act(self, out, in_, func, r, w, bias=None, scale=None, accum=None):
        kw = {}
        if bias is not None:
            kw['bias'] = bias
        if scale is not None:
            kw['scale'] = scale
        if accum is not None:
            kw['accum_out'] = accum
        self.tr.op('act', lambda e: e.activation(out=out, in_=in_, func=func, **kw), reads=r, writes=w)

    def tt(self, eng, out, a, b, op, r, w):
        self.tr.op(eng, lambda e: e.tensor_tensor(out=out, in0=a, in1=b, op=op), reads=r, writes=w)

    def ts(self, eng, out, a, s1, op0, r, w, s2=None, op1=None, accum=None):
        kw = {}
        if op1 is not None:
            kw['op1'] = op1
        if accum is not None:
            kw['accum_out'] = accum
        self.tr.op(eng, lambda e: e.tensor_scalar(out=out, in0=a, scalar1=s1, scalar2=s2, op0=op0, **kw),
                   reads=r, writes=w)

    def stt(self, eng, out, a, s, b, op0, op1, r, w):
        eng = 'dve'
        self.tr.op(eng, lambda e: e.scalar_tensor_tensor(out=out, in0=a, scalar=s, in1=b, op0=op0, op1=op1),
                   reads=r, writes=w)

    def cp(self, eng, out, in_, r, w):
        if eng == 'act':
            self.tr.op('act', lambda e: e.copy(out=out, in_=in_), reads=r, writes=w)
        else:
            self.tr.op(eng, lambda e: e.tensor_copy(out=out, in_=in_), reads=r, writes=w)

    def memset(self, eng, out, val, w):
        self.tr.op(eng, lambda e: e.memset(out, val), writes=w)

    def recip(self, out, in_, r, w):
        self.tr.op('dve', lambda e: e.reciprocal(out=out, in_=in_), reads=r, writes=w)

    def scan(self, out, d0, d1, r, w):
        self.tr.op('dve', lambda e: e.tensor_tensor_scan(out=out, data0=d0, data1=d1, initial=0.0,
                                                         op0=ALU.mult, op1=ALU.add), reads=r, writes=w)

    def finish(self, out_resources):
        self.tr.wait_all('sp', out_resources)
        self.tr.emit()
        return self.nc


import numpy as np

def tile_w_ada(inp, l):
    cache = inp.setdefault('_wada_tiled', {})
    if l not in cache:
        cache[l] = np.ascontiguousarray(inp['w_ada'][l].reshape(16, 128, 96, 128).transpose(2, 1, 0, 3))
    return cache[l]


def p1_consts():
    C = np.zeros((128, 2048), np.float32)
    i = np.arange(128)
    C[:, 0:128] = np.eye(128)
    C[:, 128:256] = (i[None, :] >= i[:, None])
    C[:, 256:384] = 1.0
    blk = (i[:, None] // 64 == i[None, :] // 64).astype(np.float32)
    C[:, 384:512] = blk
    C[:, 512:640] = blk / 64.0
    C[:, 640:768] = blk * (i[:, None] < i[None, :])
    C[:, 768:896] = blk * (i[:, None] > i[None, :])
    C[:, 896:1024] = blk * (i[:, None] <= i[None, :])
    rm = np.ones(512, np.float32)
    rm[::64] = 0
    C[:, 1024:1536] = rm[None, :]
    return C


def p1_core_consts(g):
    Cb = np.zeros((128, 1024), np.float32)
    p = np.arange(128)
    hl = p // 64
    d = p % 64
    heads = 2 * g + hl
    lg = np.log1p(-np.exp2(-5.0 - heads.astype(np.float64)))
    inv_freq = 10000.0 ** (-(d % 32).astype(np.float64) / 32.0)
    Cb[:, 0] = inv_freq
    Cb[:, 1] = np.where(d < 32, -1.0, 1.0)
    Cb[:, 2] = np.exp(128.0 * lg)
    idx = np.arange(128, dtype=np.float64)
    Cb[:, 128:256] = np.exp((idx[None, :] + 1.0) * lg[:, None])
    Cb[:, 256:384] = np.exp((127.0 - idx)[:, None] * lg[None, :])
    for h2 in range(2):
        lgh = np.log1p(-np.exp2(-5.0 - (2 * g + h2)))
        rel = idx[None, :] - idx[:, None]
        Cb[:, 384 + h2 * 128:384 + (h2 + 1) * 128] = np.where(rel >= 0, np.exp(rel * lgh), 0.0)
    return Cb


M_COLS = 3088
RW_COLS = 3520


def p1_core_inputs(inp, l, b, g):
    w_in = inp['w_in'][l]
    o = {}
    zc = np.arange(256 * g, 256 * g + 256)
    xsc = 1024 + np.arange(256 * g, 256 * g + 256)
    Bc = 2048 + np.arange(128 * g, 128 * g + 128)
    Cc = 2560 + np.arange(128 * g, 128 * g + 128)
    dtc = 3072 + np.arange(4 * g, 4 * g + 4)
    o['w_mfm'] = np.ascontiguousarray(w_in[:, np.concatenate([xsc, Bc, Cc])])
    o['w_mtm'] = np.ascontiguousarray(w_in[:, np.concatenate([zc, dtc])])
    convc = np.concatenate([xsc, Bc, Cc]) - 1024
    cw = np.concatenate([inp['m_conv_w'][l][:, convc], inp['m_conv_b'][l][None, convc]], axis=0)
    o['m_cw'] = np.ascontiguousarray(cw.T.reshape(4, 128, 5).transpose(1, 0, 2))
    o['m_hd'] = np.ascontiguousarray(inp['m_head'][l][:, 4 * g:4 * g + 4].reshape(1, 12))
    o['m_ng'] = np.ascontiguousarray(inp['m_norm_g'][l][None, 256 * g:256 * g + 256])
    r0 = M_COLS
    ch = np.arange(256 * g, 256 * g + 256)
    cols = np.concatenate([r0 + ch, r0 + 1024 + ch, r0 + 2048 + ch, r0 + 3072 + np.arange(96),
                           r0 + 3168 + np.arange(96), r0 + 3264 + np.arange(256)])
    o['w_wfm'] = np.ascontiguousarray(w_in[:, cols])
    mu = inp['rwkv_mu'][l][cols - r0]
    mut = np.zeros((128, 10), np.float32)
    offs = [0, 128, 256, 384, 512, 640, 768, 864, 960, 1088]
    sizes = [128, 128, 128, 128, 128, 128, 96, 96, 128, 128]
    for q, (of, sz) in enumerate(zip(offs, sizes)):
        mut[:sz, q] = mu[of:of + sz]
    o['rw_mu'] = mut
    vec = inp['rwkv_vec'][l][:, ch]
    o['rw_vec'] = np.ascontiguousarray(vec.T.reshape(2, 128, 7).transpose(1, 0, 2))
    o['rw_w2'] = np.ascontiguousarray(inp['rwkv_w2'][l][:, ch])
    o['rw_a2'] = np.ascontiguousarray(inp['rwkv_a2'][l][:, ch])
    o['rw_g2'] = np.ascontiguousarray(inp['rwkv_g2'][l][:, ch])
    t0 = M_COLS + RW_COLS
    hd = np.arange(128 * g, 128 * g + 128)
    d = hd % 64
    partner = np.where(d < 32, hd + 32, hd - 32)
    cols = np.concatenate([t0 + hd, t0 + partner, t0 + 512 + hd, t0 + 512 + partner])
    o['w_rfm'] = np.ascontiguousarray(w_in[:, cols])
    cols = np.concatenate([t0 + 1024 + ch, t0 + 2048 + ch])
    o['w_rtm'] = np.ascontiguousarray(w_in[:, cols])
    o['c_col'] = np.ascontiguousarray(inp['c'][b].reshape(16, 128).T)
    o['w_ada'] = np.ascontiguousarray(tile_w_ada(inp, l)[:32])
    o['b_ada'] = np.ascontiguousarray(inp['b_ada'][l][None, :4096])
    o['norm_g'] = inp['norm_g'][l]
    o['pos'] = np.ascontiguousarray(inp['positions'][b][None, :]).astype(np.int32)
    o['cst'] = p1_consts()
    o['cstb'] = p1_core_consts(g)
    return o


def p2_core_inputs(inp, l, b, q, Tq, x_b, yT_b):
    o = {}
    t0 = q * Tq
    NT = 128 + Tq
    if x_b is not None:
        xh = np.zeros((NT, 2048), np.float32)
        yh = np.zeros((3, 1024, NT), yT_b.dtype)
        if q == 0:
            xh[128:] = x_b[0:Tq]
            yh[:, :, 128:] = yT_b[:, :, 0:Tq]
        else:
            xh[:] = x_b[t0 - 128:t0 + Tq]
            yh[:] = yT_b[:, :, t0 - 128:t0 + Tq]
        o['xh'] = xh
        o['yTh'] = yh
    o['hmask'] = np.full((128, 1), 0.0 if q == 0 else 1.0, np.float32)
    o['c_col'] = np.ascontiguousarray(inp['c'][b].reshape(16, 128).T)
    o['w_ada'] = tile_w_ada(inp, l)
    o['b_ada'] = inp['b_ada'][l][None, :]
    ng = inp['norm_g'][l]
    o['norm_g'] = ng
    o['ng_col'] = np.ascontiguousarray(ng.reshape(4, 16, 128).transpose(2, 0, 1).reshape(128, 64))
    o['w_gate'] = inp['w_gate'][l]
    o['w_branch'] = inp['w_branch'][l]
    o['w_out'] = inp['w_out'][l]
    o['w_up'] = inp['w_up'][l]
    fcv = np.concatenate([inp['f_conv_w'][l], inp['f_conv_b'][l][None, :]], axis=0)
    o['f_cv'] = np.ascontiguousarray(fcv.T.reshape(86, 128, 4).transpose(1, 0, 2))
    o['w_down'] = inp['w_down'][l]
    o['cst'] = p1_consts()
    return o


from contextlib import ExitStack
import numpy as np


MAGIC = 12582912.0
TWO_PI = float(2 * np.pi)


class Scope:
    _n = 0

    def __init__(self, kb):
        self.kb = kb
        self.st = ExitStack()
        Scope._n += 1
        self.tag = f"_sc{Scope._n}"

    def sb(self, name, shape, dt=F32):
        return self.st.enter_context(self.kb.nc.sbuf_tensor(name + self.tag, list(shape), dt))

    def close(self):
        self.kb.tr.barrier()
        self.st.close()


def load_cast_weights(kb, sc, wdram, ncols, name, stage_cols=128):
    W = sc.sb(name, [128, 16, ncols], BF16)
    stg = [sc.sb(f"{name}_stg{i}", [128, 16, stage_cols], F32) for i in range(2)]
    wv = wdram.rearrange("(k p) n -> p k n", p=128)
    i = 0
    for c0 in range(0, ncols, stage_cols):
        c1 = min(ncols, c0 + stage_cols)
        s = stg[i % 2]
        rn = f"{name}_stg{i % 2}"
        kb.dma(s[:, :, 0:c1 - c0], wv[:, :, c0:c1], rn, w=[rn])
        kb.cp('pool' if i % 2 else 'dve', W[:, :, c0:c1], s[:, :, 0:c1 - c0], r=[rn], w=[name])
        i += 1
    return W


def emit_mod(kb, sc, c_col, w_ada, b_ada, vec_ids, ones_row, pb, tag):
    cc = sc.sb(f"{tag}_cc", [128, 16])
    scl = sc.sb(f"{tag}_sc", [128, 16])
    kb.dma(cc[:], c_col, f"{tag}_cc", w=[f"{tag}_cc"])
    kb.act(scl[:], cc[:], AF.Silu, r=[f"{tag}_cc"], w=[f"{tag}_sc"])
    NB = 4
    stg = [sc.sb(f"{tag}_wa{i}", [128, 16, 128], F32) for i in range(NB)]
    banks = (7, 4, 2, 1)
    rows = {}
    i = 0
    for vid in vec_ids:
        row = sc.sb(f"{tag}_row{vid}", [1, 2048])
        brow = sc.sb(f"{tag}_brow{vid}", [1, 2048])
        rn = f"{tag}_row{vid}"
        kb.dma(brow[:], b_ada[0:1, vid * 2048:(vid + 1) * 2048], f"{tag}_brow{vid}", w=[f"{tag}_brow{vid}"])
        for cch in range(16):
            s = stg[i % NB]
            sn = f"{tag}_wa{i % NB}"
            kb.dma(s[:], w_ada[vid * 16 + cch], sn, w=[sn])
            bk = banks[i % NB]
            pr = f"pb{bk}"
            pcols = pb[bk][0:1, 0:128]
            for k in range(16):
                kb.mm(pcols, lhsT=scl[:, k:k + 1], rhs=s[:, k, :], start=(k == 0), stop=(k == 15),
                      r=[sn, f"{tag}_sc"], w=[pr])
            kb.tt('dve', row[0:1, cch * 128:(cch + 1) * 128], pcols, brow[0:1, cch * 128:(cch + 1) * 128], ALU.add,
                  r=[pr, f"{tag}_brow{vid}"], w=[rn])
            i += 1
        rows[vid] = (row, rn)
    return rows


def bcast_row(kb, row, rn, ones_row, pb, emit_chunk):
    for c in range(4):
        pr = "pb6" if c % 2 == 0 else "pb5"
        pa = pb[6][:, 0:512] if c % 2 == 0 else pb[5][:, 0:512]
        kb.mm(pa, lhsT=ones_row[0:1, 0:128], rhs=row[0:1, c * 512:(c + 1) * 512], start=True, stop=True,
              r=[rn, 'C'], w=[pr])
        emit_chunk(c, pa, pr)


P1_PARAM_SHAPES = {
    "w_mfm": ([D, 512], F32), "w_mtm": ([D, 260], F32), "m_cw": ([128, 4, 5], F32), "m_hd": ([1, 12], F32),
    "m_ng": ([1, 256], F32), "w_rfm": ([D, 512], F32), "w_rtm": ([D, 512], F32), "w_wfm": ([D, 1216], F32),
    "rw_mu": ([128, 10], F32), "rw_vec": ([128, 2, 7], F32), "rw_w2": ([96, 256], F32), "rw_a2": ([96, 256], F32),
    "rw_g2": ([256, 256], F32),
}


def setup_globals(kb, cst):
    G = {}
    G['pb'] = [kb.ps(f"pb{i}", [128, 512]) for i in range(8)]
    C = kb.sb("C", [128, 2048])
    kb.dma(C[:], cst, "C", w=["C"])
    Cb = kb.sb("Cb", [128, 256], BF16)
    kb.cp('dve', Cb[:, 0:128], C[:, 0:128], r=["C"], w=["Cb"])
    G['C'] = C
    G['Cb'] = Cb
    return G


def build_p1(S, parts=('h', 'mamba', 'rwkv', 'ret'), env=None):
    NM = S // 512
    if env is None:
        kb = KB()
        e = {}
        e['x'] = kb.din("x", [S, D])
        e['c_col'] = kb.din("c_col", [128, 16])
        e['w_ada'] = kb.din("w_ada", [32, 128, 16, 128])
        e['b_ada'] = kb.din("b_ada", [1, 2 * D])
        e['norm_g'] = kb.din("norm_g", [4, D])
        e['pos'] = kb.din("pos", [1, S], I32)
        cst = kb.din("cst", [128, 2048])
        e['cstb'] = kb.din("cstb", [128, 1024])
        for k, (shp, dt_) in P1_PARAM_SHAPES.items():
            e[k] = kb.din(k, shp, dt_)
        e['yT'] = kb.dout("yT", [768, S], BF16)
        e['hTd'] = kb.dscratch("hTd", [NM, 128, 16, 512], BF16)
        e['hTo'] = e['hTd']
        e['NH'] = S // 128
        G = setup_globals(kb, cst)
    else:
        kb = env['kb']
        e = env
        G = env['G']
    nc = kb.nc
    x, c_col, w_ada, b_ada, norm_g, pos, cstb = (e[k] for k in ('x', 'c_col', 'w_ada', 'b_ada', 'norm_g', 'pos', 'cstb'))
    w_mfm, w_mtm, m_cw, m_hd, m_ng, w_rfm, w_rtm, w_wfm = (e[k] for k in ('w_mfm', 'w_mtm', 'm_cw', 'm_hd', 'm_ng', 'w_rfm', 'w_rtm', 'w_wfm'))
    rw_mu, rw_vec, rw_w2, rw_a2, rw_g2 = (e[k] for k in ('rw_mu', 'rw_vec', 'rw_w2', 'rw_a2', 'rw_g2'))
    yT, hTd, hTo, NH = e['yT'], e['hTd'], e['hTo'], e['NH']

    def _load_hT(dst, m, key):
        kb.dma(dst[:], hTd[m], key, r=[f"hTd{m}"], w=[key])

    def _yT_ap(row0, nrows, m):
        return yT[row0:row0 + nrows, m * 512:(m + 1) * 512]

    load_hT = e.get('load_hT', _load_hT)
    yT_ap = e.get('yT_ap', _yT_ap)
    after_y = e.get('after_y', lambda i, m, res: None)
    pb = G['pb']
    C, Cb = G['C'], G['Cb']
    ident_f = C[:, 0:128]
    U_f = C[:, 128:256]
    ones_f = C[:, 256:384]
    blk_f = C[:, 384:512]
    blk64_f = C[:, 512:640]
    SU_f = C[:, 640:768]
    SL_f = C[:, 768:896]
    UI_f = C[:, 896:1024]
    rmask = C[:, 1024:1536]
    ident_b = Cb[:, 0:128]
    ones_row = C[0:1, 256:384]

    if 'h' in parts:
        sc = Scope(kb)
        rows = emit_mod(kb, sc, c_col, w_ada, b_ada, [0, 1], ones_row, pb, "m0")
        Gbc = sc.sb("Gbc", [128, 2048])
        SHbc = sc.sb("SHbc", [128, 2048])
        kb.dma(Gbc[:], norm_g[0:1, :].partition_broadcast(128), "Gbc", w=["Gbc"])

        def g_chunk(c, pa, pr):
            kb.stt('dve', Gbc[:, c * 512:(c + 1) * 512], pa, 1.0, Gbc[:, c * 512:(c + 1) * 512], ALU.add, ALU.mult,
                   r=[pr, "Gbc"], w=["Gbc"])

        def sh_chunk(c, pa, pr):
            kb.cp('act', SHbc[:, c * 512:(c + 1) * 512], pa, r=[pr], w=["SHbc"])

        bcast_row(kb, rows[1][0], rows[1][1], ones_row, pb, g_chunk)
        bcast_row(kb, rows[0][0], rows[0][1], ones_row, pb, sh_chunk)
        xt = [sc.sb(f"xt{i}", [128, 2048]) for i in range(2)]
        junk = sc.sb("junk", [128, 2048], BF16)
        tmp = sc.sb("htmp", [128, 2048])
        hb = sc.sb("hb", [128, 2048], BF16)
        ss = sc.sb("ss", [128, 1])
        rstd = sc.sb("rstd", [128, 1])
        hst = [sc.sb(f"hst{i}", [128, 16, 512], BF16) for i in range(2)]
        for i in range(NH):
            m, sub = divmod(i, 4)
            xb = xt[i % 2]
            xn = f"xt{i % 2}"
            kb.dma(xb[:], x[i * 128:(i + 1) * 128, :], xn, w=[xn])
            kb.act(junk[:], xb[:], AF.Square, r=[xn], w=["junk", "ss"], accum=ss[:])
            kb.act(rstd[:], ss[:], AF.Sqrt, r=["ss"], w=["rstd"], bias=1e-6, scale=1.0 / D)
            kb.recip(rstd[:], rstd[:], r=["rstd"], w=["rstd"])
            kb.stt('dve', tmp[:], xb[:], rstd[:, 0:1], Gbc[:], ALU.mult, ALU.mult, r=[xn, "rstd", "Gbc"], w=["htmp"])
            kb.tt('pool', hb[:], tmp[:], SHbc[:], ALU.add, r=["htmp", "SHbc"], w=["hb"])
            hs = hst[m % 2]
            hn = f"hst{m % 2}"
            for q in range(4):
                pr = f"pb{q}"
                for kk in range(4):
                    k = q * 4 + kk
                    kb.mm(pb[q][:, kk * 128:(kk + 1) * 128], lhsT=hb[:, k * 128:(k + 1) * 128], rhs=ident_b,
                          start=True, stop=True, r=["hb", "Cb"], w=[pr])
                eng = 'act' if q % 2 == 0 else 'dve'
                kb.cp(eng, hs[:, q * 4:(q + 1) * 4, sub * 128:(sub + 1) * 128],
                      pb[q][:, :].rearrange("p (k t) -> p k t", k=4), r=[pr], w=[hn])
            if sub == 3:
                kb.dma(hTo[m], hs[:], hn, r=[hn], w=[f"hTd{m}"])
                if 'after_h' in e:
                    e['after_h'](m)
        sc.close()

    if 'mamba' in parts:
        sc = Scope(kb)
        Wfm = load_cast_weights(kb, sc, w_mfm, 512, "Wfm")
        Wtm = load_cast_weights(kb, sc, w_mtm, 260, "Wtm")
        cw = sc.sb("cw", [128, 4, 5])
        kb.dma(cw[:], m_cw, "cw", w=["cw"])
        mh = sc.sb("mh", [128, 12])
        kb.dma(mh[:], m_hd.partition_broadcast(128), "mh", w=["mh"])
        negA = sc.sb("negA", [128, 4])
        kb.act(negA[:], mh[:, 4:8], AF.Exp, r=["mh"], w=["negA"])
        kb.ts('dve', negA[:], negA[:], -1.0, ALU.mult, r=["negA"], w=["negA"])
        dsk = sc.sb("dsk", [128, 256])
        for h in range(4):
            kb.ts('dve', dsk[:, h * 64:(h + 1) * 64], ones_f[:, 0:64], mh[:, 8 + h:9 + h], ALU.mult,
                  r=["C", "mh"], w=["dsk"])
        mng = sc.sb("mng", [128, 256])
        kb.dma(mng[:], m_ng.partition_broadcast(128), "mng", w=["mng"])
        hT = [sc.sb(f"hT{i}", [128, 16, 512], BF16) for i in range(2)]
        raw = [sc.sb(f"raw{j}", [128, 515]) for j in range(4)]
        acc = [sc.sb(f"acc{j}", [128, 512]) for j in range(2)]
        xs_act = [sc.sb(f"xsa{j}", [128, 512]) for j in range(2)]
        B_act = sc.sb("B_act", [128, 512])
        BT_bf = sc.sb("BT_bf", [128, 512], BF16)
        CT_bf = sc.sb("CT_bf", [128, 512], BF16)
        zs = sc.sb("zs", [128, 256])
        dtp = sc.sb("dtp", [128, 4])
        dt = sc.sb("dt", [128, 4])
        adt = sc.sb("adt", [128, 4])
        xs_sb = sc.sb("xs_sb", [128, 256])
        B_tm = sc.sb("B_tm", [128, 128], BF16)
        xdt_bf = sc.sb("xdt_bf", [128, 256], BF16)
        xte_bf = sc.sb("xte_bf", [128, 256], BF16)
        cum_sb = sc.sb("cum_sb", [128, 4])
        rh = sc.sb("rh", [128, 4, 128])
        CBm = sc.sb("CBm", [128, 128])
        seg = sc.sb("seg", [128, 4, 128])
        dec = sc.sb("dec", [128, 4, 128])
        MT_bf = sc.sb("MT_bf", [128, 4, 128], BF16)
        ecum = sc.sb("ecum", [128, 4])
        tte = sc.sb("tte", [128, 4])
        ecl = sc.sb("ecl", [128, 4])
        yi_sb = sc.sb("yi_sb", [128, 256])
        y = sc.sb("y", [128, 256])
        y2 = sc.sb("y2", [128, 256])
        state = sc.sb("state", [128, 256])
        state_bf = sc.sb("state_bf", [128, 256], BF16)
        mjunk = sc.sb("mjunk", [128, 256])
        mss = sc.sb("mss", [128, 1])
        mrstd = sc.sb("mrstd", [128, 1])
        yo_bf = sc.sb("yo_bf", [128, 256], BF16)
        yst = [sc.sb(f"yst{i}", [128, 2, 512], BF16) for i in range(2)]
        kb.memset('pool', state[:], 0.0, w=["state"])
        kb.memset('pool', state_bf[:], 0.0, w=["state_bf"])
        for j in range(4):
            kb.memset('pool', raw[j][:], 0.0, w=[f"raw{j}"])
        for m in range(NM):
            h_ = hT[m % 2]
            hn = f"hT{m % 2}"
            load_hT(h_, m, hn)
            ys = yst[m % 2]
            ysn = f"yst{m % 2}"
            for j in range(4):
                pr = f"pb{j % 2}"
                pa = pb[j % 2]
                for k in range(16):
                    kb.mm(pa[:, :], lhsT=Wfm[:, k, j * 128:(j + 1) * 128], rhs=h_[:, k, :], start=(k == 0),
                          stop=(k == 15), r=["Wfm", hn], w=[pr])
                rj = f"raw{j}"
                kb.cp('pool', raw[j][:, 0:3], raw[j][:, 512:515], r=[rj], w=[rj])
                kb.cp('act', raw[j][:, 3:515], pa[:, :], r=[pr], w=[rj])
                a_ = acc[j % 2]
                an = f"acc{j % 2}"
                kb.ts('dve', a_[:], raw[j][:, 3:515], cw[:, j, 3:4], ALU.mult, r=[rj, "cw"], w=[an],
                      s2=cw[:, j, 4:5], op1=ALU.add)
                for tpi, eng in ((2, 'pool'), (1, 'dve'), (0, 'pool')):
                    kb.stt(eng, a_[:], raw[j][:, tpi:tpi + 512], cw[:, j, tpi:tpi + 1], a_[:], ALU.mult, ALU.add,
                           r=[rj, "cw", an], w=[an])
                if j < 2:
                    kb.act(xs_act[j][:], a_[:], AF.Silu, r=[an], w=[f"xsa{j}"])
                elif j == 2:
                    kb.act(B_act[:], a_[:], AF.Silu, r=[an], w=["B_act"])
                    kb.cp('pool', BT_bf[:], B_act[:], r=["B_act"], w=["BT_bf"])
                else:
                    kb.act(CT_bf[:], a_[:], AF.Silu, r=[an], w=["CT_bf"])
            import os as _os
            _nch = int(_os.environ.get("MB_NCH", "4"))
            for c in range(_nch):
                cs = slice(c * 128, (c + 1) * 128)
                _sect = _os.environ.get('MB_SECT', '123456')
                if '1' in _sect:
                    for k in range(16):
                        kb.mm(pb[2][:, 0:260], lhsT=h_[:, k, cs], rhs=Wtm[:, k, :], start=(k == 0), stop=(k == 15),
                              r=[hn, "Wtm"], w=["pb2"])
                    kb.act(zs[:], pb[2][:, 0:256], AF.Silu, r=["pb2"], w=["zs"])
                    kb.tt('dve', dtp[:], pb[2][:, 256:260], mh[:, 0:4], ALU.add, r=["pb2", "mh"], w=["dtp"])
                    kb.act(dtp[:], dtp[:], AF.Exp, r=["dtp"], w=["dtp"])
                    kb.act(dt[:], dtp[:], AF.Ln, r=["dtp"], w=["dt"], bias=1.0)
                    kb.tt('dve', adt[:], dt[:], negA[:], ALU.mult, r=["dt", "negA"], w=["adt"])
                if '2' in _sect:
                    for j in range(2):
                        kb.mm(pb[3][:, j * 128:(j + 1) * 128], lhsT=xs_act[j][:, cs], rhs=ident_f, start=True, stop=True,
                              r=[f"xsa{j}", "C"], w=["pb3"])
                    kb.mm(pb[3][:, 256:384], lhsT=B_act[:, cs], rhs=ident_f, start=True, stop=True,
                          r=["B_act", "C"], w=["pb3"])
                    kb.cp('act', xs_sb[:], pb[3][:, 0:256], r=["pb3"], w=["xs_sb"])
                    kb.cp('dve', B_tm[:], pb[3][:, 256:384], r=["pb3"], w=["B_tm"])
                    kb.tt('pool', xdt_bf[:].rearrange("p (h c) -> p h c", h=4), xs_sb[:].rearrange("p (h c) -> p h c", h=4),
                          dt[:].unsqueeze(2).to_broadcast([128, 4, 64]), ALU.mult, r=["xs_sb", "dt"], w=["xdt_bf"])
                if '3' in _sect:
                    kb.mm(pb[4][:, 0:4], lhsT=U_f, rhs=adt[:], start=True, stop=True, r=["C", "adt"], w=["pb4"])
                    kb.cp('act', cum_sb[:], pb[4][:, 0:4], r=["pb4"], w=["cum_sb"])
                    kb.tt('pool', rh[:], U_f.unsqueeze(1).to_broadcast([128, 4, 128]),
                          adt[:].unsqueeze(2).to_broadcast([128, 4, 128]), ALU.mult, r=["C", "adt"], w=["rh"])
                    for h in range(4):
                        kb.mm(pb[5][:, h * 128:(h + 1) * 128], lhsT=ones_f, rhs=rh[:, h, :], start=True, stop=True,
                              r=["C", "rh"], w=["pb5"])
                    kb.mm(pb[4][:, 128:256], lhsT=BT_bf[:, cs], rhs=CT_bf[:, cs], start=True, stop=True,
                          r=["BT_bf", "CT_bf"], w=["pb4"])
                    kb.tt('dve', CBm[:], pb[4][:, 128:256], U_f, ALU.mult, r=["pb4", "C"], w=["CBm"])
                    kb.tt('dve', seg[:], pb[5][:, :].rearrange("p (h i) -> p h i", h=4),
                          cum_sb[:].unsqueeze(2).to_broadcast([128, 4, 128]), ALU.subtract, r=["pb5", "cum_sb"], w=["seg"])
                    kb.ts('pool', seg[:], seg[:], 0.0, ALU.min, r=["seg"], w=["seg"])
                    kb.act(dec[:], seg[:], AF.Exp, r=["seg"], w=["dec"])
                    kb.tt('pool', MT_bf[:], dec[:], CBm[:].unsqueeze(1).to_broadcast([128, 4, 128]), ALU.mult,
                          r=["dec", "CBm"], w=["MT_bf"])
                    kb.act(ecum[:], cum_sb[:], AF.Exp, r=["cum_sb"], w=["ecum"])
                    last = pb[5][:, :].rearrange("p (h i) -> p h i", h=4)[:, :, 127:128]
                    kb.tt('dve', tte[:].rearrange("p (h o) -> p h o", o=1), last,
                          cum_sb[:].rearrange("p (h o) -> p h o", o=1), ALU.subtract, r=["pb5", "cum_sb"], w=["tte"])
                    kb.act(tte[:], tte[:], AF.Exp, r=["tte"], w=["tte"])
                    kb.act(ecl[:].rearrange("p (h o) -> p h o", o=1), last, AF.Exp, r=["pb5"], w=["ecl"])
                if '4' in _sect:
                    for h in range(4):
                        kb.mm(pb[6][:, h * 64:(h + 1) * 64], lhsT=MT_bf[:, h, :], rhs=xdt_bf[:, h * 64:(h + 1) * 64],
                              start=True, stop=True, r=["MT_bf", "xdt_bf"], w=["pb6"])
                    kb.mm(pb[6][:, 256:512], lhsT=CT_bf[:, cs], rhs=state_bf[:], start=True, stop=True,
                          r=["CT_bf", "state_bf"], w=["pb6"])
                    kb.tt('dve', yi_sb[:].rearrange("p (h c) -> p h c", h=4),
                          pb[6][:, 256:512].rearrange("p (h c) -> p h c", h=4),
                          ecum[:].unsqueeze(2).to_broadcast([128, 4, 64]), ALU.mult, r=["pb6", "ecum"], w=["yi_sb"])
                    kb.tt('dve', y[:], yi_sb[:], pb[6][:, 0:256], ALU.add, r=["pb6", "yi_sb"], w=["y"])
                if '5' in _sect:
                    kb.tt('pool', xte_bf[:].rearrange("p (h c) -> p h c", h=4), xdt_bf[:].rearrange("p (h c) -> p h c", h=4),
                          tte[:].unsqueeze(2).to_broadcast([128, 4, 64]), ALU.mult, r=["xdt_bf", "tte"], w=["xte_bf"])
                    kb.mm(pb[7][:, 0:256], lhsT=B_tm[:], rhs=xte_bf[:], start=True, stop=True, r=["B_tm", "xte_bf"],
                          w=["pb7"])
                    kb.tt('pool', state[:].rearrange("p (h c) -> p h c", h=4), state[:].rearrange("p (h c) -> p h c", h=4),
                          ecl[:].unsqueeze(2).to_broadcast([128, 4, 64]), ALU.mult, r=["state", "ecl"], w=["state"])
                    kb.tt('dve', state[:], state[:], pb[7][:, 0:256], ALU.add, r=["state", "pb7"], w=["state"])
                    kb.cp('pool', state_bf[:], state[:], r=["state"], w=["state_bf"])
                if '6' in _sect:
                    kb.tt('pool', y2[:], xs_sb[:], dsk[:], ALU.mult, r=["xs_sb", "dsk"], w=["y2"])
                    kb.tt('pool', y[:], y[:], y2[:], ALU.add, r=["y", "y2"], w=["y"])
                    kb.tt('pool', y[:], y[:], zs[:], ALU.mult, r=["y", "zs"], w=["y"])
                    kb.act(mjunk[:], y[:], AF.Square, r=["y"], w=["mjunk", "mss"], accum=mss[:])
                    kb.act(mrstd[:], mss[:], AF.Sqrt, r=["mss"], w=["mrstd"], bias=1e-5, scale=1.0 / 256)
                    kb.recip(mrstd[:], mrstd[:], r=["mrstd"], w=["mrstd"])
                    kb.stt('dve', yo_bf[:], y[:], mrstd[:, 0:1], mng[:], ALU.mult, ALU.mult, r=["y", "mrstd", "mng"],
                           w=["yo_bf"])
                    for t in range(2):
                        kb.mm(pb[7][:, 256 + t * 128:256 + (t + 1) * 128], lhsT=yo_bf[:, t * 128:(t + 1) * 128],
                              rhs=ident_b, start=True, stop=True, r=["yo_bf", "Cb"], w=["pb7"])
                    kb.cp('act', ys[:, :, cs], pb[7][:, 256:512].rearrange("p (t s) -> p t s", t=2), r=["pb7"], w=[ysn])
            kb.dma(yT_ap(0, 256, m).rearrange("(t p) s -> p t s", p=128), ys[:], ysn, r=[ysn],
                   w=["yT_m"])
            after_y(0, m, "yT_m")
        sc.close()

    if 'ret' in parts:
        sc = Scope(kb)
        Wfm = load_cast_weights(kb, sc, w_rfm, 512, "Wfm")
        Wtm = load_cast_weights(kb, sc, w_rtm, 512, "Wtm")
        CB = sc.sb("CB", [128, 1024])
        kb.dma(CB[:], cstb, "CB", w=["CB"])
        invf = CB[:, 0:1]
        sgn = CB[:, 1:2]
        cdec = CB[:, 2:3]
        QD = CB[:, 128:256]
        KDEC = CB[:, 256:384]
        DMT = CB[:, 384:640]
        hT = [sc.sb(f"hT{i}", [128, 16, 512], BF16) for i in range(2)]
        pi_ = sc.sb("pi", [128, 512], I32)
        ang = sc.sb("ang", [128, 512])
        kq = sc.sb("kq", [128, 512])
        rr = sc.sb("rr", [128, 512])
        sinT = sc.sb("sinT", [128, 512])
        cosT = sc.sb("cosT", [128, 512])
        t1 = sc.sb("t1", [128, 512])
        t2 = sc.sb("t2", [128, 512])
        qr_bf = sc.sb("qr_bf", [128, 512], BF16)
        kr_bf = sc.sb("kr_bf", [128, 512], BF16)
        krz = [sc.sb(f"krz{h}", [128, 512], BF16) for h in range(2)]
        qdz = [sc.sb(f"qdz{h}", [128, 512], BF16) for h in range(2)]
        v_bf = sc.sb("v_bf", [128, 256], BF16)
        gs = sc.sb("gs", [128, 256])
        ST_bf = sc.sb("ST_bf", [128, 256], BF16)
        kd_bf = sc.sb("kd_bf", [128, 128], BF16)
        rstate = sc.sb("rstate", [128, 128])
        rstate_bf = sc.sb("rstate_bf", [128, 128], BF16)
        rjunk = sc.sb("rjunk", [128, 128])
        rss = sc.sb("rss", [128, 2])
        rrs = sc.sb("rrs", [128, 2])
        yo_bf = sc.sb("ryo_bf", [128, 256], BF16)
        yst = [sc.sb(f"yst{i}", [128, 2, 512], BF16) for i in range(2)]
        kb.memset('pool', rstate[:], 0.0, w=["rstate"])
        kb.memset('pool', rstate_bf[:], 0.0, w=["rstate_bf"])
        for h in range(2):
            kb.memset('pool', krz[h][:], 0.0, w=[f"krz{h}"])
            kb.memset('pool', qdz[h][:], 0.0, w=[f"qdz{h}"])
        for m in range(NM):
            h_ = hT[m % 2]
            hn = f"hT{m % 2}"
            load_hT(h_, m, hn)
            ys = yst[m % 2]
            ysn = f"yst{m % 2}"
            kb.dma(pi_[:], pos[0:1, m * 512:(m + 1) * 512].partition_broadcast(128), "pi", w=["pi"])
            kb.cp('dve', ang[:], pi_[:], r=["pi"], w=["ang"])
            kb.ts('dve', ang[:], ang[:], invf, ALU.mult, r=["ang", "CB"], w=["ang"])
            for which, dst, shift in (("s", sinT, 0.0), ("c", cosT, float(np.pi / 2))):
                if shift != 0.0:
                    kb.ts('pool', rr[:], ang[:], shift, ALU.add, r=["ang"], w=["rr"])
                    src, srn = rr, "rr"
                else:
                    src, srn = ang, "ang"
                kb.ts('dve', kq[:], src[:], float(1.0 / TWO_PI), ALU.mult, r=[srn], w=["kq"], s2=MAGIC, op1=ALU.add)
                kb.ts('pool', kq[:], kq[:], -MAGIC, ALU.add, r=["kq"], w=["kq"])
                kb.stt('dve', rr[:], kq[:], -TWO_PI, src[:], ALU.mult, ALU.add, r=["kq", srn], w=["rr"])
                kb.ts('pool', rr[:], rr[:], 3.14159, ALU.min, r=["rr"], w=["rr"], s2=-3.14159, op1=ALU.max)
                if which == "s":
                    kb.act(dst[:], rr[:], AF.Sin, r=["rr", "CB"], w=["sinT"], scale=sgn)
                else:
                    kb.act(dst[:], rr[:], AF.Sin, r=["rr"], w=["cosT"])
            for j in range(4):
                pr = f"pb{j % 2}"
                pa = pb[j % 2]
                for k in range(16):
                    kb.mm(pa[:, :], lhsT=Wfm[:, k, j * 128:(j + 1) * 128], rhs=h_[:, k, :], start=(k == 0),
                          stop=(k == 15), r=["Wfm", hn], w=[pr])
                if j % 2 == 0:
                    kb.tt('dve', t1[:], pa[:, :], cosT[:], ALU.mult, r=[pr, "cosT"], w=["t1"])
                else:
                    kb.tt('dve', t2[:], pa[:, :], sinT[:], ALU.mult, r=[pr, "sinT"], w=["t2"])
                    if j == 1:
                        kb.tt('pool', qr_bf[:], t1[:], t2[:], ALU.add, r=["t1", "t2"], w=["qr_bf"])
                    else:
                        kb.tt('pool', t1[:], t1[:], t2[:], ALU.add, r=["t1", "t2"], w=["t1"])
                        kb.ts('pool', kr_bf[:], t1[:], 0.125, ALU.mult, r=["t1"], w=["kr_bf"])
            for h in range(2):
                ph = slice(h * 64, (h + 1) * 64)
                kb.cp('pool', krz[h][ph, :], kr_bf[ph, :], r=["kr_bf"], w=[f"krz{h}"])
                for c in range(4):
                    cs = slice(c * 128, (c + 1) * 128)
                    kb.tt('dve' if c % 2 else 'pool', qdz[h][ph, cs], qr_bf[ph, cs], QD[ph, :], ALU.mult,
                          r=["qr_bf", "CB"], w=[f"qdz{h}"])
            for c in range(4):
                cs = slice(c * 128, (c + 1) * 128)
                for k in range(16):
                    kb.mm(pb[2][:, :], lhsT=h_[:, k, cs], rhs=Wtm[:, k, :], start=(k == 0), stop=(k == 15),
                          r=[hn, "Wtm"], w=["pb2"])
                kb.cp('act', v_bf[:], pb[2][:, 0:256], r=["pb2"], w=["v_bf"])
                kb.act(gs[:], pb[2][:, 256:512], AF.Silu, r=["pb2"], w=["gs"])
                for h in range(2):
                    kb.mm(pb[3][:, h * 128:(h + 1) * 128], lhsT=krz[h][:, cs], rhs=qr_bf[:, cs], start=True, stop=True,
                          r=[f"krz{h}", "qr_bf"], w=["pb3"])
                kb.tt('dve', ST_bf[:], pb[3][:, 0:256], DMT, ALU.mult, r=["pb3", "CB"], w=["ST_bf"])
                kb.mm(pb[4][:, 0:128], lhsT=kr_bf[:, cs], rhs=ident_b, start=True, stop=True, r=["kr_bf", "Cb"],
                      w=["pb4"])
                kb.tt('dve', kd_bf[:], pb[4][:, 0:128], KDEC, ALU.mult, r=["pb4", "CB"], w=["kd_bf"])
                for h in range(2):
                    hs_ = slice(h * 128, (h + 1) * 128)
                    kb.mm(pb[5][:, hs_], lhsT=ST_bf[:, hs_], rhs=v_bf[:, hs_], start=True, stop=False,
                          r=["ST_bf", "v_bf"], w=["pb5"])
                    kb.mm(pb[5][:, hs_], lhsT=qdz[h][:, cs], rhs=rstate_bf[:], start=False, stop=True,
                          r=[f"qdz{h}", "rstate_bf"], w=["pb5"])
                kb.mm(pb[6][:, 0:256], lhsT=kd_bf[:], rhs=v_bf[:], start=True, stop=True, r=["kd_bf", "v_bf"],
                      w=["pb6"])
                for h in range(2):
                    ph = slice(h * 64, (h + 1) * 64)
                    kb.stt('dve', rstate[ph, :], rstate[ph, :], cdec[ph, :], pb[6][ph, h * 128:(h + 1) * 128],
                           ALU.mult, ALU.add, r=["rstate", "CB", "pb6"], w=["rstate"])
                kb.cp('pool', rstate_bf[:], rstate[:], r=["rstate"], w=["rstate_bf"])
                for h in range(2):
                    kb.act(rjunk[:], pb[5][:, h * 128:(h + 1) * 128], AF.Square, r=["pb5"], w=["rjunk", "rss"],
                           accum=rss[:, h:h + 1])
                kb.act(rrs[:], rss[:], AF.Sqrt, r=["rss"], w=["rrs"], bias=1e-6, scale=1.0 / 128)
                kb.recip(rrs[:], rrs[:], r=["rrs"], w=["rrs"])
                for h in range(2):
                    hs_ = slice(h * 128, (h + 1) * 128)
                    kb.stt('dve', yo_bf[:, hs_], pb[5][:, hs_], rrs[:, h:h + 1], gs[:, hs_], ALU.mult, ALU.mult,
                           r=["pb5", "rrs", "gs"], w=["ryo_bf"])
                for t in range(2):
                    kb.mm(pb[7][:, 256 + t * 128:256 + (t + 1) * 128], lhsT=yo_bf[:, t * 128:(t + 1) * 128],
                          rhs=ident_b, start=True, stop=True, r=["ryo_bf", "Cb"], w=["pb7"])
                kb.cp('act', ys[:, :, cs], pb[7][:, 256:512].rearrange("p (t s) -> p t s", t=2), r=["pb7"], w=[ysn])
            kb.dma(yT_ap(512, 256, m).rearrange("(t p) s -> p t s", p=128), ys[:], ysn, r=[ysn],
                   w=["yT_r"])
            after_y(2, m, "yT_r")
        sc.close()

    if 'rwkv' in parts:
        sc = Scope(kb)
        Wr = load_cast_weights(kb, sc, w_wfm, 1216, "Wr")
        offs = [0, 128, 256, 384, 512, 640, 768, 864, 960, 1088]
        sizes = [128, 128, 128, 128, 128, 128, 96, 96, 128, 128]
        mu = sc.sb("mu", [128, 10])
        kb.dma(mu[:], rw_mu, "mu", w=["mu"])
        vec = sc.sb("vec", [128, 2, 7])
        kb.dma(vec[:], rw_vec, "vec", w=["vec"])
        omka = sc.sb("omka", [128, 2])
        kb.ts('dve', omka[:].rearrange("p (c o) -> p c o", o=1), vec[:, :, 3:4], -1.0, ALU.mult, r=["vec"], w=["omka"],
              s2=1.0, op1=ALU.add)
        lstg = sc.sb("lstg", [128, 2, 256])
        w2_bf = sc.sb("w2_bf", [96, 256], BF16)
        a2_bf = sc.sb("a2_bf", [96, 256], BF16)
        g2_bf = sc.sb("g2_bf", [128, 2, 256], BF16)
        kb.dma(lstg[0:96, 0, :], rw_w2, "lstg", w=["lstg"])
        kb.cp('dve', w2_bf[:], lstg[0:96, 0, :], r=["lstg"], w=["w2_bf"])
        kb.dma(lstg[0:96, 1, :], rw_a2, "lstg", r=["lstg"], w=["lstg"])
        kb.cp('dve', a2_bf[:], lstg[0:96, 1, :], r=["lstg"], w=["a2_bf"])
        kb.dma(lstg[:], rw_g2.rearrange("(k p) n -> p k n", p=128), "lstg", r=["lstg"], w=["lstg"])
        kb.cp('dve', g2_bf[:], lstg[:], r=["lstg"], w=["g2_bf"])
        M3 = sc.sb("M3", [128, 384])
        kb.cp('pool', M3[:, 0:128], SU_f, r=["C"], w=["M3"])
        kb.cp('pool', M3[:, 128:256], SL_f, r=["C"], w=["M3"])
        kb.cp('pool', M3[:, 256:384], SU_f, r=["C"], w=["M3"])
        M2 = sc.sb("M2", [128, 256])
        kb.cp('pool', M2[:, 0:128], UI_f, r=["C"], w=["M2"])
        kb.cp('pool', M2[:, 128:256], UI_f, r=["C"], w=["M2"])
        hT = sc.sb("hT0", [128, 16, 512], BF16)
        halo = sc.sb("halo", [128, 10])
        kb.memset('pool', halo[:], 0.0, w=["halo"])
        rawt = [sc.sb(f"rawt{i}", [128, 513]) for i in range(2)]
        dlt = [sc.sb(f"dlt{i}", [128, 512]) for i in range(2)]
        sh = [sc.sb(f"sh{q}", [128, 512]) for q in range(10)]
        tw = sc.sb("tw", [128, 512], BF16)
        ad_bf = sc.sb("ad_bf", [128, 512], BF16)
        sg = [sc.sb(f"sg{i}", [128, 512], BF16) for i in range(2)]
        names = ["a", "kkraw", "sq", "rn", "kk", "kmul", "kh", "bb", "lw", "cwm", "Winv", "Wexc", "tmpw"]
        T = {n: sc.sb("rw_" + n, [128, 512]) for n in names}
        for alias, base in (("prod", "tmpw"), ("yc", "kkraw"), ("sq2", "sq"), ("rs2", "rn"), ("yn", "kmul")):
            T[alias] = T[base]
        bdn = ["Abd", "Bbd", "Kbd", "Rbd", "Vbd"]
        BDc = [{n: sc.sb(f"{n}{ct}", [128, 8, 128], BF16) for n in bdn} for ct in range(2)]
        for ct in range(2):
            for n in bdn:
                kb.memset('pool', BDc[ct][n][:], 0.0, w=[f"{n}{ct}"])
        Tc = [{n: sc.sb(f"rw_{n}{ct}", [128, 512]) for n in ("Winc", "bv", "g_sb", "yraw")} for ct in range(2)]
        II = sc.sb("II", [128, 128], BF16)
        kb.cp('dve', II[:], ident_f, r=["C"], w=["II"])
        tm_bf_c = [sc.sb(f"tm_bf{ct}", [128, 384], BF16) for ct in range(2)]
        gr_bf_c = [sc.sb(f"gr_bf{ct}", [128, 384], BF16) for ct in range(2)]
        pr_bf_c = [sc.sb(f"pr_bf{ct}", [128, 256], BF16) for ct in range(2)]
        pw_bf_c = [[sc.sb(f"pw_bf{ct}_{i}", [128, 256], BF16) for i in range(5)] for ct in range(2)]
        G_bf_c = [[sc.sb(f"G_bf{ct}_{i}", [128, 128], BF16) for i in range(2)] for ct in range(2)]
        Xn_bf_c = [sc.sb(f"Xn_bf{ct}", [128, 128], BF16) for ct in range(2)]
        U_bf_c = [sc.sb(f"U_bf{ct}", [128, 128], BF16) for ct in range(2)]
        ST_f = [sc.sb(f"ST_f{ct}", [128, 128]) for ct in range(2)]
        STt_c = [sc.sb(f"STt{ct}", [128, 128]) for ct in range(2)]
        STb = [sc.sb(f"STb{ct}", [128, 128], BF16) for ct in range(2)]
        yst = [sc.sb(f"wyst{i}", [128, 512], BF16) for i in range(2)]
        for ct in range(2):
            kb.memset('pool', ST_f[ct][:], 0.0, w=[f"ST_f{ct}"])
            kb.memset('pool', STb[ct][:], 0.0, w=[f"STb{ct}"])
        v3 = lambda ap: ap.rearrange("p (c t) -> p c t", c=8)
        for m in range(NM):
            load_hT(hT, m, "hT0")
            for q in range(10):
                sz = sizes[q]
                pr = f"pb{q % 2}"
                pa = pb[q % 2]
                for k in range(16):
                    kb.mm(pa[0:sz, :], lhsT=Wr[:, k, offs[q]:offs[q] + sz], rhs=hT[:, k, :], start=(k == 0),
                          stop=(k == 15), r=["Wr", "hT0"], w=[pr])
                rw_ = rawt[q % 2]
                rn_ = f"rawt{q % 2}"
                kb.cp('pool', rw_[0:sz, 0:1], halo[0:sz, q:q + 1], r=["halo"], w=[rn_])
                kb.cp('act', rw_[0:sz, 1:513], pa[0:sz, :], r=[pr], w=[rn_])
                kb.cp('pool', halo[0:sz, q:q + 1], rw_[0:sz, 512:513], r=[rn_], w=["halo"])
                d_ = dlt[q % 2]
                dn_ = f"dlt{q % 2}"
                kb.tt('pool', d_[0:sz, :], rw_[0:sz, 0:512], rw_[0:sz, 1:513], ALU.subtract, r=[rn_], w=[dn_])
                kb.stt('dve', sh[q][0:sz, :], d_[0:sz, :], mu[0:sz, q:q + 1], rw_[0:sz, 1:513], ALU.mult, ALU.add,
                       r=[dn_, "mu", rn_], w=[f"sh{q}"])
            kb.act(tw[0:96, :], sh[6][0:96, :], AF.Tanh, r=["sh6"], w=["tw"])
            kb.cp('pool', ad_bf[0:96, :], sh[7][0:96, :], r=["sh7"], w=["ad_bf"])
            for i in range(2):
                kb.act(sg[i][:], sh[8 + i][:], AF.Sigmoid, r=[f"sh{8 + i}"], w=[f"sg{i}"])
            for ct in range(2):
                cts = slice(ct * 128, (ct + 1) * 128)
                r_, k_, v_ = sh[ct], sh[2 + ct], sh[4 + ct]
                rn_r, rn_k, rn_v = f"sh{ct}", f"sh{2 + ct}", f"sh{4 + ct}"
                kb.mm(pb[2][:, :], lhsT=w2_bf[:, cts], rhs=tw[0:96, :], start=True, stop=True, r=["w2_bf", "tw"],
                      w=["pb2"])
                kb.act(T["lw"][:], pb[2][:, :], AF.Sigmoid, r=["pb2", "vec"], w=["rw_lw"], bias=vec[:, ct, 0:1])
                kb.ts('dve', T["lw"][:], T["lw"][:], -0.6065306597126334, ALU.mult, r=["rw_lw"], w=["rw_lw"])
                kb.mm(pb[3][:, :], lhsT=a2_bf[:, cts], rhs=ad_bf[0:96, :], start=True, stop=True, r=["a2_bf", "ad_bf"],
                      w=["pb3"])
                kb.act(T["a"][:], pb[3][:, :], AF.Sigmoid, r=["pb3", "vec"], w=["rw_a"], bias=vec[:, ct, 1:2])
                for i in range(2):
                    kb.mm(pb[2][:, :], lhsT=g2_bf[:, i, cts], rhs=sg[i][:], start=(i == 0), stop=(i == 1),
                          r=["g2_bf", f"sg{i}"], w=["pb2"])
                kb.cp('act', Tc[ct]["g_sb"][:], pb[2][:, :], r=["pb2"], w=[f"rw_g_sb{ct}"])
                kb.ts('pool', T["kkraw"][:], k_[:], vec[:, ct, 2:3], ALU.mult, r=[rn_k, "vec"], w=["rw_kkraw"])
                kb.tt('pool', T["sq"][:], T["kkraw"][:], T["kkraw"][:], ALU.mult, r=["rw_kkraw"], w=["rw_sq"])
                kb.mm(pb[3][:, :], lhsT=blk_f, rhs=T["sq"][:], start=True, stop=True, r=["C", "rw_sq"], w=["pb3"])
                kb.act(T["rn"][:], pb[3][:, :], AF.Sqrt, r=["pb3"], w=["rw_rn"])
                kb.ts('dve', T["rn"][:], T["rn"][:], 1e-12, ALU.max, r=["rw_rn"], w=["rw_rn"])
                kb.recip(T["rn"][:], T["rn"][:], r=["rw_rn"], w=["rw_rn"])
                kb.tt('pool', T["kk"][:], T["kkraw"][:], T["rn"][:], ALU.mult, r=["rw_kkraw", "rw_rn"], w=["rw_kk"])
                kb.ts('dve', T["kmul"][:], T["a"][:], vec[:, ct, 3:4], ALU.mult, r=["rw_a", "vec", "omka"],
                      w=["rw_kmul"], s2=omka[:, ct:ct + 1], op1=ALU.add)
                kb.tt('pool', T["kh"][:], k_[:], T["kmul"][:], ALU.mult, r=[rn_k, "rw_kmul"], w=["rw_kh"])
                kb.tt('pool', T["bb"][:], T["kk"][:], T["a"][:], ALU.mult, r=["rw_kk", "rw_a"], w=["rw_bb"])
                kb.scan(T["cwm"][:], rmask, T["lw"][:], r=["C", "rw_lw"], w=["rw_cwm"])
                kb.act(Tc[ct]["Winc"][:], T["cwm"][:], AF.Exp, r=["rw_cwm"], w=[f"rw_Winc{ct}"])
                kb.act(T["Winv"][:], T["cwm"][:], AF.Exp, r=["rw_cwm"], w=["rw_Winv"], scale=-1.0)
                kb.tt('pool', T["tmpw"][:], T["cwm"][:], T["lw"][:], ALU.subtract, r=["rw_cwm", "rw_lw"], w=["rw_tmpw"])
                kb.act(T["Wexc"][:], T["tmpw"][:], AF.Exp, r=["rw_tmpw"], w=["rw_Wexc"])
                pairs = (("Abd", T["kk"], "rw_kk", T["Wexc"], "rw_Wexc"), ("Bbd", T["bb"], "rw_bb", T["Winv"], "rw_Winv"),
                         ("Kbd", T["kh"], "rw_kh", T["Winv"], "rw_Winv"), ("Rbd", r_, rn_r, Tc[ct]["Winc"], f"rw_Winc{ct}"))
                ei = 0
                for (bn, a0, an0, a1, an1) in pairs:
                    for hh in range(2):
                        ph = slice(hh * 64, (hh + 1) * 64)
                        kb.tt('dve' if ei % 2 == 0 else 'pool', BDc[ct][bn][ph, :, hh * 64:(hh + 1) * 64], v3(a0[ph, :]),
                              v3(a1[ph, :]), ALU.mult, r=[an0, an1, bn + str(ct)], w=[bn + str(ct)])
                        ei += 1
                for hh in range(2):
                    ph = slice(hh * 64, (hh + 1) * 64)
                    kb.cp('pool', BDc[ct]["Vbd"][ph, :, hh * 64:(hh + 1) * 64], v3(v_[ph, :]), r=[rn_v, f"Vbd{ct}"], w=[f"Vbd{ct}"])
                kb.stt('dve', T["prod"][:], r_[:], vec[:, ct, 4:5], T["kh"][:], ALU.mult, ALU.mult,
                       r=[rn_r, "vec", "rw_kh"], w=["rw_tmpw"])
                kb.mm(pb[3][:, :], lhsT=blk_f, rhs=T["prod"][:], start=True, stop=True, r=["C", "rw_tmpw"], w=["pb3"])
                kb.tt('dve', Tc[ct]["bv"][:], pb[3][:, :], v_[:], ALU.mult, r=["pb3", rn_v], w=[f"rw_bv{ct}"])
            caps = []
            for ct in range(2):
                kb.tr.begin_capture()
                PBK = pb[4:8] if ct == 0 else pb[0:4]
                PBN = [f"pb{4 + k_}" for k_ in range(4)] if ct == 0 else [f"pb{k_}" for k_ in range(4)]
                STf, STn = ST_f[ct], f"ST_f{ct}"
                STbf, STbn = STb[ct], f"STb{ct}"
                for c in range(8):
                    A_c, B_c, K_c, R_c, V_c = (BDc[ct][n][:, c, :] for n in bdn)
                    for i, (X_c, xn) in enumerate(((B_c, f"Bbd{ct}"), (K_c, f"Kbd{ct}"), (V_c, f"Vbd{ct}"))):
                        kb.mm(PBK[0][:, i * 128:(i + 1) * 128], lhsT=X_c, rhs=ident_b, start=True, stop=True,
                              r=[xn, "Cb"], w=[PBN[0]])
                    kb.cp('act', tm_bf_c[ct][:], PBK[0][:, 0:384], r=[PBN[0]], w=[f"tm_bf{ct}"])
                    btm, ktm, vtm = tm_bf_c[ct][:, 0:128], tm_bf_c[ct][:, 128:256], tm_bf_c[ct][:, 256:384]
                    kb.mm(PBK[1][:, 0:128], lhsT=B_c, rhs=A_c, start=True, stop=True, r=[f"Bbd{ct}", f"Abd{ct}"], w=[PBN[1]])
                    kb.mm(PBK[1][:, 128:256], lhsT=A_c, rhs=B_c, start=True, stop=True, r=[f"Bbd{ct}", f"Abd{ct}"], w=[PBN[1]])
                    kb.mm(PBK[1][:, 256:384], lhsT=K_c, rhs=A_c, start=True, stop=True, r=[f"Kbd{ct}", f"Abd{ct}"], w=[PBN[1]])
                    kb.tt('dve', gr_bf_c[ct][:], PBK[1][:, 0:384], M3[:], ALU.mult, r=[PBN[1], "M3"], w=[f"gr_bf{ct}"])
                    kb.mm(PBK[2][:, 0:128], lhsT=B_c, rhs=R_c, start=True, stop=True, r=[f"Bbd{ct}", f"Rbd{ct}"], w=[PBN[2]])
                    kb.mm(PBK[2][:, 128:256], lhsT=K_c, rhs=R_c, start=True, stop=True, r=[f"Kbd{ct}", f"Rbd{ct}"], w=[PBN[2]])
                    kb.tt('dve', pr_bf_c[ct][:], PBK[2][:, 0:256], M2[:], ALU.mult, r=[PBN[2], "M2"], w=[f"pr_bf{ct}"])
                    Nn, Tt, TakT = gr_bf_c[ct][:, 0:128], gr_bf_c[ct][:, 128:256], gr_bf_c[ct][:, 256:384]
                    PrbT, PrkT = pr_bf_c[ct][:, 0:128], pr_bf_c[ct][:, 128:256]
                    kb.tt('pool', G_bf_c[ct][0][:], II[:], Nn, ALU.subtract, r=["II", f"gr_bf{ct}"], w=[f"G_bf{ct}_0"])
                    gcur = 0
                    Ncur, Tcur, ncn = Nn, Tt, f"gr_bf{ct}"
                    for lv in range(5):
                        kb.mm(PBK[2][:, 256:384], lhsT=Tcur, rhs=Ncur, start=True, stop=True, r=[ncn], w=[PBN[2]])
                        kb.mm(PBK[2][:, 384:512], lhsT=Ncur, rhs=Tcur, start=True, stop=True, r=[ncn], w=[PBN[2]])
                        kb.cp('act', pw_bf_c[ct][lv][:], PBK[2][:, 256:512], r=[PBN[2]], w=[f"pw_bf{ct}_{lv}"])
                        Ncur, Tcur, ncn = pw_bf_c[ct][lv][:, 0:128], pw_bf_c[ct][lv][:, 128:256], f"pw_bf{ct}_{lv}"
                        kb.mm(PBK[0][:, 384:512], lhsT=Tcur, rhs=G_bf_c[ct][gcur][:], start=True, stop=True,
                              r=[ncn, f"G_bf{ct}_{gcur}"], w=[PBN[0]])
                        kb.tt('dve', G_bf_c[ct][1 - gcur][:], PBK[0][:, 384:512], G_bf_c[ct][gcur][:], ALU.add,
                              r=[PBN[0], f"G_bf{ct}_{gcur}"], w=[f"G_bf{ct}_{1 - gcur}"])
                        gcur = 1 - gcur
                    Gf, Gn = G_bf_c[ct][gcur], f"G_bf{ct}_{gcur}"
                    kb.mm(PBK[3][:, 0:128], lhsT=A_c, rhs=STbf[:], start=True, stop=False, r=[f"Abd{ct}", STbn], w=[PBN[3]])
                    kb.mm(PBK[3][:, 0:128], lhsT=TakT, rhs=vtm, start=False, stop=True, r=[f"gr_bf{ct}", f"tm_bf{ct}"], w=[PBN[3]])
                    kb.act(Xn_bf_c[ct][:], PBK[3][:, 0:128], AF.Copy, r=[PBN[3]], w=[f"Xn_bf{ct}"], scale=-1.0)
                    kb.mm(PBK[3][:, 128:256], lhsT=Gf[:], rhs=Xn_bf_c[ct][:], start=True, stop=True, r=[Gn, f"Xn_bf{ct}"], w=[PBN[3]])
                    kb.cp('act', U_bf_c[ct][:], PBK[3][:, 128:256], r=[PBN[3]], w=[f"U_bf{ct}"])
                    kb.mm(PBK[3][:, 256:384], lhsT=STbf[:], rhs=R_c, start=True, stop=False, r=[STbn, f"Rbd{ct}"], w=[PBN[3]])
                    kb.mm(PBK[3][:, 256:384], lhsT=U_bf_c[ct][:], rhs=PrbT, start=False, stop=False, r=[f"U_bf{ct}", f"pr_bf{ct}"],
                          w=[PBN[3]])
                    kb.mm(PBK[3][:, 256:384], lhsT=vtm, rhs=PrkT, start=False, stop=True, r=[f"tm_bf{ct}", f"pr_bf{ct}"], w=[PBN[3]])
                    for hh in range(2):
                        ph = slice(hh * 64, (hh + 1) * 64)
                        kb.cp('act', Tc[ct]["yraw"][ph, c * 64:(c + 1) * 64], PBK[3][ph, 256 + hh * 64:256 + (hh + 1) * 64],
                              r=[PBN[3]], w=[f"rw_yraw{ct}"])
                    kb.mm(PBK[3][:, 384:512], lhsT=btm, rhs=U_bf_c[ct][:], start=True, stop=False, r=[f"tm_bf{ct}", f"U_bf{ct}"], w=[PBN[3]])
                    kb.mm(PBK[3][:, 384:512], lhsT=ktm, rhs=vtm, start=False, stop=True, r=[f"tm_bf{ct}"], w=[PBN[3]])
                    WL = Tc[ct]["Winc"][:, c * 64 + 63:c * 64 + 64]
                    kb.ts('pool', STt_c[ct][:], STf[:], WL, ALU.mult, r=[STn, f"rw_Winc{ct}"], w=[f"STt{ct}"])
                    kb.stt('dve', STf[:], PBK[3][:, 384:512], WL, STt_c[ct][:], ALU.mult, ALU.add, r=[PBN[3], f"rw_Winc{ct}", f"STt{ct}"],
                           w=[STn])
                    kb.cp('pool', STbf[:], STf[:], r=[STn], w=[STbn])
                caps.append(kb.tr.end_capture())
            kb.tr.replay_interleaved(caps)
            for ct in range(2):
                kb.mm(pb[2][:, :], lhsT=blk64_f, rhs=Tc[ct]["yraw"][:], start=True, stop=True, r=["C", f"rw_yraw{ct}"], w=["pb2"])
                kb.tt('dve', T["yc"][:], Tc[ct]["yraw"][:], pb[2][:, :], ALU.subtract, r=[f"rw_yraw{ct}", "pb2"], w=["rw_kkraw"])
                kb.tt('pool', T["sq2"][:], T["yc"][:], T["yc"][:], ALU.mult, r=["rw_kkraw"], w=["rw_sq"])
                kb.mm(pb[3][:, :], lhsT=blk64_f, rhs=T["sq2"][:], start=True, stop=True, r=["C", "rw_sq"], w=["pb3"])
                kb.act(T["rs2"][:], pb[3][:, :], AF.Sqrt, r=["pb3"], w=["rw_rn"], bias=64e-5)
                kb.recip(T["rs2"][:], T["rs2"][:], r=["rw_rn"], w=["rw_rn"])
                kb.tt('pool', T["yn"][:], T["yc"][:], T["rs2"][:], ALU.mult, r=["rw_kkraw", "rw_rn"], w=["rw_kmul"])
                kb.ts('dve', T["yn"][:], T["yn"][:], vec[:, ct, 5:6], ALU.mult, r=["rw_kmul", "vec"], w=["rw_kmul"],
                      s2=vec[:, ct, 6:7], op1=ALU.add)
                kb.tt('pool', T["yn"][:], T["yn"][:], Tc[ct]["bv"][:], ALU.add, r=["rw_kmul", f"rw_bv{ct}"], w=["rw_kmul"])
                ys, ysn = yst[ct], f"wyst{ct}"
                kb.tt('dve', ys[:], T["yn"][:], Tc[ct]["g_sb"][:], ALU.mult, r=["rw_kmul", f"rw_g_sb{ct}"], w=[ysn])
                kb.dma(yT_ap(256 + ct * 128, 128, m), ys[:], ysn, r=[ysn], w=["yT_w"])
                if ct == 1:
                    after_y(1, m, "yT_w")
        sc.close()

    outs = ["yT_m", "yT_r", "yT_w"]
    if env is not None:
        return None
    return kb, outs


def finish_p1(kb, outs):
    return kb.finish(outs)


import numpy as np


FF = 5504
NJ = 43


P2_PARAM_SHAPES = {
    "w_ada": ([96, 128, 16, 128], F32), "b_ada": ([1, 6 * D], F32), "ng_col": ([128, 64], F32), "norm_g": ([4, D], F32),
    "w_gate": ([3, D, D], F32), "w_branch": ([3, 1024, D], F32), "w_out": ([D, D], F32), "w_up": ([D, 2 * FF], F32),
    "f_cv": ([128, 86, 4], F32), "w_down": ([FF, D], F32),
}


def p2_scratch(kb, sfx=""):
    return {
        'Wg_d': kb.dscratch("Wg_d" + sfx, [3, 16, 128, 16, 128], BF16),
        'Wb_d': kb.dscratch("Wb_d" + sfx, [3, 16, 128, 8, 128], BF16),
        'Wo_d': kb.dscratch("Wo_d" + sfx, [16, 128, D], BF16),
        'Wu_d': kb.dscratch("Wu_d" + sfx, [NJ, 128, 16, 256], BF16),
        'Wd_d': kb.dscratch("Wd_d" + sfx, [NJ, 128, D], BF16),
    }


def build_p2(Tq, env=None):
    NT = 128 + Tq
    if env is None:
        kb = KB()
        e = {}
        xh = kb.din("xh", [NT, D])
        yTh = kb.din("yTh", [3, 1024, NT], BF16)
        e['hmask'] = kb.din("hmask", [128, 1])
        e['c_col'] = kb.din("c_col", [128, 16])
        for k, (shp, dt_) in P2_PARAM_SHAPES.items():
            e[k] = kb.din(k, shp, dt_)
        cst = kb.din("cst", [128, 2048])
        xo = kb.dout("xo", [Tq, D])
        e.update(p2_scratch(kb))
        G = setup_globals(kb, cst)

        def load_x(dst, rn, row0, is_halo):
            kb.dma(dst, xh[row0:row0 + 128, :], rn, w=[rn])

        def load_y(ybf, t0, ntok, is_halo):
            kb.dma(ybf[:, :, :, 0:ntok], yTh[:, :, t0:t0 + ntok].rearrange("i (k p) t -> p i k t", p=128), "ybf",
                   w=["ybf"])

        def store_x(src, rn, o0):
            kb.dma(xo[o0:o0 + 128, :], src, rn, r=[rn], w=["xo"])
        e['load_x'], e['load_y'], e['store_x'] = load_x, load_y, store_x
    else:
        kb = env['kb']
        e = env
        G = env['G']
    nc = kb.nc
    hmask, c_col, w_ada, b_ada, ng_col, norm_g = (e[k] for k in ('hmask', 'c_col', 'w_ada', 'b_ada', 'ng_col', 'norm_g'))
    w_gate, w_branch, w_out, w_up, f_cv, w_down = (e[k] for k in ('w_gate', 'w_branch', 'w_out', 'w_up', 'f_cv', 'w_down'))
    Wg_d, Wb_d, Wo_d, Wu_d, Wd_d = (e[k] for k in ('Wg_d', 'Wb_d', 'Wo_d', 'Wu_d', 'Wd_d'))
    load_x, load_y, store_x = e['load_x'], e['load_y'], e['store_x']
    pb = G['pb']
    C, Cb = G['C'], G['Cb']
    ident_f = C[:, 0:128]
    ones_row = C[0:1, 256:384]
    ident_b = Cb[:, 0:128]

    sc = Scope(kb)
    NSTG = 6
    stg = [sc.sb(f"stg{i}", [128, 2048]) for i in range(NSTG)]
    stb = [sc.sb(f"stb{i}", [128, 2048], BF16) for i in range(NSTG)]
    cnt = [0]

    def cast_block(src_ap, dst_ap, shape):
        i = cnt[0] % NSTG
        cnt[0] += 1
        n = int(np.prod(shape))
        sv = stg[i][:, 0:n]
        bv = stb[i][:, 0:n]
        if len(shape) == 2:
            sv = sv.rearrange("p (a b) -> p a b", a=shape[0])
            bv = bv.rearrange("p (a b) -> p a b", a=shape[0])
        kb.dma(sv, src_ap, f"stg{i}", w=[f"stg{i}"])
        eng = ('dve', 'act', 'dve', 'act', 'pool', 'dve')[i]
        kb.cp(eng, stb[i][:, 0:n], stg[i][:, 0:n], r=[f"stg{i}"], w=[f"stb{i}"])
        kb.dma(dst_ap, bv, f"stb{i}", r=[f"stb{i}"], w=[f"Wscr{cnt[0]}"])

    for i in range(3):
        wv = w_gate[i].rearrange("(k p) n -> p k n", p=128)
        for nt in range(16):
            cast_block(wv[:, :, nt * 128:(nt + 1) * 128], Wg_d[i, nt], (16, 128))
        wv = w_branch[i].rearrange("(k p) n -> p k n", p=128)
        for nt in range(16):
            cast_block(wv[:, :, nt * 128:(nt + 1) * 128], Wb_d[i, nt], (8, 128))
    for kn in range(16):
        cast_block(w_out[kn * 128:(kn + 1) * 128, :], Wo_d[kn], (2048,))
    wv = w_up.rearrange("(k p) n -> p k n", p=128)
    for j in range(NJ):
        cast_block(wv[:, :, j * 128:(j + 1) * 128], Wu_d[j, :, :, 0:128], (16, 128))
        cast_block(wv[:, :, FF + j * 128:FF + (j + 1) * 128], Wu_d[j, :, :, 128:256], (16, 128))
        cast_block(w_down[j * 128:(j + 1) * 128, :], Wd_d[j], (2048,))
    sc.close()

    scm = Scope(kb)
    GTm = scm.sb("GTm", [128, 2048])
    GTf = scm.sb("GTf", [128, 2048])
    cols = scm.sb("cols", [128, 64])
    ngc = scm.sb("ngc", [128, 64])
    kb.dma(ngc[:], ng_col, "ngc", w=["ngc"])
    sc = Scope(kb)
    rows = emit_mod(kb, sc, c_col, w_ada, b_ada, [0, 1, 2, 3, 4, 5], ones_row, pb, "m2")
    kb.dma(GTm[:], norm_g[1:2, :].partition_broadcast(128), "GTm", w=["GTm"])
    kb.dma(GTf[:], norm_g[3:4, :].partition_broadcast(128), "GTf", w=["GTf"])

    def mk_gt(dst, dn):
        def f(c, pa, pr):
            kb.tt('dve', dst[:, c * 512:(c + 1) * 512], pa, dst[:, c * 512:(c + 1) * 512], ALU.mult, r=[pr, dn], w=[dn])
        return f

    bcast_row(kb, rows[2][0], rows[2][1], ones_row, pb, mk_gt(GTm, "GTm"))
    bcast_row(kb, rows[5][0], rows[5][1], ones_row, pb, mk_gt(GTf, "GTf"))
    for slot, vid in enumerate((1, 0, 4, 3)):
        row, rn = rows[vid]
        for k in range(16):
            kb.mm(pb[3][:, slot * 16 + k:slot * 16 + k + 1], lhsT=row[0:1, k * 128:(k + 1) * 128],
                  rhs=ones_row[0:1, 0:1], start=True, stop=True, r=[rn, "C"], w=["pb3"])
    kb.cp('act', cols[:], pb[3][:, 0:64], r=["pb3"], w=["cols"])
    for slot, gsl in ((0, 0), (2, 2)):
        kb.stt('dve', cols[:, slot * 16:(slot + 1) * 16], cols[:, slot * 16:(slot + 1) * 16], 1.0,
               ngc[:, gsl * 16:(gsl + 1) * 16], ALU.add, ALU.mult, r=["cols", "ngc"], w=["cols"])
    sc.close()

    xres = [scm.sb(f"xres{i}", [128, 2048]) for i in range(2)]
    htmp = scm.sb("htmp", [128, 2048])
    hb = scm.sb("hb", [128, 2048], BF16)
    ss = scm.sb("ss", [128, 1])
    rstd = scm.sb("rstd", [128, 1])
    hT = scm.sb("hT", [128, 16, 256], BF16)
    ybf = scm.sb("ybf", [128, 3, 8, 256], BF16)
    mT = scm.sb("mT", [128, 16, 256], BF16)
    sig = [scm.sb(f"sig{i}", [128, 256]) for i in range(2)]
    macc = scm.sb("macc", [128, 256])
    mtmp = scm.sb("mtmp", [128, 256])
    Wg_s = [scm.sb(f"Wg_s{i}", [128, 16, 128], BF16) for i in range(3)]
    Wb_s = [scm.sb(f"Wb_s{i}", [128, 8, 128], BF16) for i in range(3)]
    Wo_s = [scm.sb(f"Wo_s{i}", [128, 2048], BF16) for i in range(2)]
    Wu_s = [scm.sb(f"Wu_s{i}", [128, 16, 256], BF16) for i in range(2)]
    Wd_s = [scm.sb(f"Wd_s{i}", [128, 2048], BF16) for i in range(2)]
    ymix = scm.sb("ymix", [128, 2048])
    rawf = [scm.sb(f"rawf{i}", [128, 258]) for i in range(2)]
    facc = [scm.sb(f"facc{i}", [128, 256]) for i in range(2)]
    gg = scm.sb("gg", [128, 256])
    aT = scm.sb("aT", [128, NJ, 256], BF16)
    fhalo = scm.sb("fhalo", [128, 86, 2])
    fc = scm.sb("fc", [128, 86, 4])
    hm = scm.sb("hm", [128, 1])
    kb.dma(fc[:], f_cv, "fc", w=["fc"])
    kb.dma(hm[:], hmask, "hm", w=["hm"])
    kb.memset('pool', fhalo[:], 0.0, w=["fhalo"])
    if 'extra_alloc' in e:
        e['extra_alloc'](scm, dict(ymix=ymix))
    slab_ctr = {"g": 0, "b": 0, "o": 0, "u": 0, "d": 0}

    def emit_h(nsub, ntok, gslot):
        for sub in range(nsub):
            xn = f"xres{sub}"
            kb.act(htmp[:], xres[sub][:], AF.Square, r=[xn], w=["htmp", "ss"], accum=ss[:])
            kb.act(rstd[:], ss[:], AF.Sqrt, r=["ss"], w=["rstd"], bias=1e-6, scale=1.0 / D)
            kb.recip(rstd[:], rstd[:], r=["rstd"], w=["rstd"])
            kb.ts('dve', hb[:], xres[sub][:], rstd[:, 0:1], ALU.mult, r=[xn, "rstd"], w=["hb"])
            for q in range(4):
                bank = 6 + (q % 2)
                for kk in range(4):
                    k = q * 4 + kk
                    kb.mm(pb[bank][:, kk * 128:(kk + 1) * 128], lhsT=hb[:, k * 128:(k + 1) * 128], rhs=ident_b,
                          start=True, stop=True, r=["hb", "Cb"], w=[f"pb{bank}"])
                for kk in range(4):
                    k = q * 4 + kk
                    gcol = cols[:, gslot * 16 + k:gslot * 16 + k + 1]
                    scol = cols[:, (gslot + 1) * 16 + k:(gslot + 1) * 16 + k + 1]
                    if kk % 2 == 0:
                        kb.act(hT[:, k, sub * 128:(sub + 1) * 128], pb[bank][:, kk * 128:(kk + 1) * 128], AF.Identity,
                               r=[f"pb{bank}", "cols"], w=["hT"], bias=scol, scale=gcol)
                    else:
                        kb.ts('dve', hT[:, k, sub * 128:(sub + 1) * 128], pb[bank][:, kk * 128:(kk + 1) * 128], gcol,
                              ALU.mult, r=[f"pb{bank}", "cols"], w=["hT"], s2=scol, op1=ALU.add)

    def post_norm(nsub_i, src_ps_banks, GT, gtn, sub, store_ap):
        xn = f"xres{sub}"
        kb.act(htmp[:], ymix[:], AF.Square, r=["ymix"], w=["htmp", "ss"], accum=ss[:])
        kb.act(rstd[:], ss[:], AF.Sqrt, r=["ss"], w=["rstd"], bias=1e-6, scale=1.0 / D)
        kb.recip(rstd[:], rstd[:], r=["rstd"], w=["rstd"])
        kb.stt('dve', htmp[:], ymix[:], rstd[:, 0:1], GT[:], ALU.mult, ALU.mult, r=["ymix", "rstd", gtn], w=["htmp"])
        kb.tt('pool', xres[sub][:], xres[sub][:], htmp[:], ALU.add, r=[xn, "htmp"], w=[xn])
        if store_ap is not None:
            store_x(xres[sub][:], xn, store_ap)

    def tile(t0, ntok, is_halo, nxt=None):
        nsub = ntok // 128
        tsl = slice(0, ntok)
        for sub in range(nsub):
            xn = f"xres{sub}"
            load_x(xres[sub][:], xn, t0 + sub * 128, is_halo)
        if is_halo:
            load_y(ybf, t0, ntok, is_halo)
        emit_h(nsub, ntok, 0)
        for nt in range(16):
            for i in range(3):
                gi = slab_ctr["g"] % 3
                slab_ctr["g"] += 1
                kb.dma(Wg_s[gi][:], Wg_d[i, nt], f"Wg_s{gi}", w=[f"Wg_s{gi}"])
                kb.dma(Wb_s[gi][:], Wb_d[i, nt], f"Wb_s{gi}", w=[f"Wb_s{gi}"])
                for k in range(16):
                    kb.mm(pb[4][:, tsl], lhsT=Wg_s[gi][:, k, :], rhs=hT[:, k, tsl], start=(k == 0), stop=(k == 15),
                          r=[f"Wg_s{gi}", "hT"], w=["pb4"])
                for k in range(8):
                    kb.mm(pb[5][:, tsl], lhsT=Wb_s[gi][:, k, :], rhs=ybf[:, i, k, tsl], start=(k == 0), stop=(k == 7),
                          r=[f"Wb_s{gi}", "ybf"], w=["pb5"])
                sg_ = sig[i % 2]
                sgn_ = f"sig{i % 2}"
                kb.act(sg_[:, tsl], pb[4][:, tsl], AF.Sigmoid, r=["pb4"], w=[sgn_])
                if i == 0:
                    kb.tt('dve', macc[:, tsl], pb[5][:, tsl], sg_[:, tsl], ALU.mult, r=["pb5", sgn_], w=["macc"])
                else:
                    kb.tt('dve', mtmp[:, tsl], pb[5][:, tsl], sg_[:, tsl], ALU.mult, r=["pb5", sgn_], w=["mtmp"])
                    if i == 1:
                        kb.tt('pool', macc[:, tsl], macc[:, tsl], mtmp[:, tsl], ALU.add, r=["macc", "mtmp"], w=["macc"])
                    else:
                        kb.tt('pool', mT[:, nt, tsl], macc[:, tsl], mtmp[:, tsl], ALU.add, r=["macc", "mtmp"],
                              w=["mT"])
        if nxt is not None:
            load_y(ybf, nxt[0], nxt[1], False)
        for sub in range(nsub):
            for kn in range(16):
                oi = slab_ctr["o"] % 2
                slab_ctr["o"] += 1
                kb.dma(Wo_s[oi][:], Wo_d[kn], f"Wo_s{oi}", w=[f"Wo_s{oi}"])
                for c in range(4):
                    kb.mm(pb[c][:, :], lhsT=mT[:, kn, sub * 128:(sub + 1) * 128], rhs=Wo_s[oi][:, c * 512:(c + 1) * 512],
                          start=(kn == 0), stop=(kn == 15), r=["mT", f"Wo_s{oi}"], w=[f"pb{c}"])
            for c in range(4):
                kb.cp('act', ymix[:, c * 512:(c + 1) * 512], pb[c][:, :], r=[f"pb{c}"], w=["ymix"])
            post_norm(nsub, None, GTm, "GTm", sub, None)
        emit_h(nsub, ntok, 2)
        for j in range(NJ):
            ui = slab_ctr["u"] % 2
            slab_ctr["u"] += 1
            kb.dma(Wu_s[ui][:], Wu_d[j], f"Wu_s{ui}", w=[f"Wu_s{ui}"])
            for half in range(2):
                bank = 4 + half
                jj = half * NJ + j
                for k in range(16):
                    kb.mm(pb[bank][:, tsl], lhsT=Wu_s[ui][:, k, half * 128:(half + 1) * 128], rhs=hT[:, k, tsl],
                          start=(k == 0), stop=(k == 15), r=[f"Wu_s{ui}", "hT"], w=[f"pb{bank}"])
                rw_ = rawf[half]
                rn_ = f"rawf{half}"
                kb.cp('pool', rw_[:, 0:2], fhalo[:, jj, :], r=["fhalo"], w=[rn_])
                kb.cp('act', rw_[:, 2:2 + ntok], pb[bank][:, tsl], r=[f"pb{bank}"], w=[rn_])
                if is_halo:
                    kb.ts('pool', fhalo[:, jj, :], rw_[:, ntok:ntok + 2], hm[:, 0:1], ALU.mult, r=[rn_, "hm"],
                          w=["fhalo"])
                else:
                    kb.cp('pool', fhalo[:, jj, :], rw_[:, ntok:ntok + 2], r=[rn_], w=["fhalo"])
                fa = facc[half]
                fan = f"facc{half}"
                kb.ts('dve', fa[:, tsl], rw_[:, 2:2 + ntok], fc[:, jj, 2:3], ALU.mult, r=[rn_, "fc"], w=[fan],
                      s2=fc[:, jj, 3:4], op1=ALU.add)
                kb.stt('dve', fa[:, tsl], rw_[:, 1:1 + ntok], fc[:, jj, 1:2], fa[:, tsl], ALU.mult, ALU.add,
                       r=[rn_, "fc", fan], w=[fan])
                kb.stt('dve', fa[:, tsl], rw_[:, 0:ntok], fc[:, jj, 0:1], fa[:, tsl], ALU.mult, ALU.add,
                       r=[rn_, "fc", fan], w=[fan])
            if not is_halo:
                kb.act(gg[:, tsl], facc[0][:, tsl], AF.Gelu_apprx_tanh, r=["facc0"], w=["gg"])
                kb.tt('pool', aT[:, j, tsl], gg[:, tsl], facc[1][:, tsl], ALU.mult, r=["gg", "facc1"], w=["aT"])
        if is_halo:
            return
        for sub in range(nsub):
            for j in range(NJ):
                di = slab_ctr["d"] % 2
                slab_ctr["d"] += 1
                kb.dma(Wd_s[di][:], Wd_d[j], f"Wd_s{di}", w=[f"Wd_s{di}"])
                for c in range(4):
                    kb.mm(pb[c][:, :], lhsT=aT[:, j, sub * 128:(sub + 1) * 128], rhs=Wd_s[di][:, c * 512:(c + 1) * 512],
                          start=(j == 0), stop=(j == NJ - 1), r=["aT", f"Wd_s{di}"], w=[f"pb{c}"])
            for c in range(4):
                kb.cp('act', ymix[:, c * 512:(c + 1) * 512], pb[c][:, :], r=[f"pb{c}"], w=["ymix"])
            o0 = t0 - 128 + sub * 128
            post_norm(nsub, None, GTf, "GTf", sub, o0)

    tiles = [(0, 128, True)] + [(t, 256, False) for t in range(128, NT, 256)]
    for ti, (t, ntk, hl) in enumerate(tiles):
        tile(t, ntk, hl, nxt=(tiles[ti + 1][:2] if ti + 1 < len(tiles) else None))
    scm.close()
    if env is not None:
        return None
    return kb, ["xo"]


import numpy as np


RG = [[0, 1, 2, 3], [4, 5, 6, 7]]


def build_fused(S, depth=2):
    Tq = S // 4
    NM = S // 512
    NMq = NM // 4
    NT = 128 + Tq
    kb = KB()
    cst = kb.din("cst", [128, 2048])
    cstb = kb.din("cstb", [128, 1024])
    c_col = kb.din("c_col", [128, 16])
    pos = kb.din("pos", [1, S], I32)
    xh = kb.din("xh", [NT, D])
    hmask = kb.din("hmask", [128, 1])
    selv = kb.din("selv", [128, 8])
    xo = kb.dout("xo", [Tq, D])
    L = []
    for l in range(depth):
        e = {}
        for k, (shp, dt_) in P1_PARAM_SHAPES.items():
            e[k] = kb.din(f"{k}_{l}", shp, dt_)
        for k, (shp, dt_) in P2_PARAM_SHAPES.items():
            e[k] = kb.din(f"{k}_{l}", shp, dt_)
        L.append(e)
    G = setup_globals(kb, cst)
    CT = min(2048, Tq)
    NCH = S // CT
    MPC = CT // 512
    hT_own = kb.dscratch("hT_own", [NMq, 128, 16, 512], BF16)
    hTg = kb.dscratch("hTg", [2 * NMq, 4 * 64, 8192], BF16)
    ysc = [kb.dscratch(f"ysc{i}", [NCH, 256, CT], BF16) for i in range(3)]
    yall = [kb.dscratch(f"yall{i}", [NCH, 1024, CT], BF16) for i in range(3)]
    xs1 = kb.dscratch("xs1", [Tq, D])
    xlast = kb.dscratch("xlast", [128, D])
    xl_all = kb.dscratch("xl_all", [4 * 128, D])
    w2s = p2_scratch(kb)
    sel = kb.sb("sel", [128, 8])
    kb.dma(sel[:], selv, "sel", w=["sel"])

    def allgather(src2d, dst2d, key, reads, writes):
        kb.tr.dma('pool', lambda e_: e_.collective_compute("AllGather", ALU.bypass, replica_groups=RG,
                                                            ins=[src2d.opt()], outs=[dst2d.opt()]),
                  key, reads=reads, writes=writes, inc=1)

    for l in range(depth):
        P = L[l]
        x_own = xh[128:NT, :] if l == 0 else xs1
        def after_h(m):
            for half in range(2):
                allgather(hT_own[m, half * 64:(half + 1) * 64].rearrange("p k t -> p (k t)"), hTg[2 * m + half], "ag",
                          reads=[f"hTd{m}"], writes=[f"hTg{2 * m + half}"])

        def load_hT(dst, m, key):
            r_, ml = divmod(m, NMq)
            for half in range(2):
                kb.dma(dst[half * 64:(half + 1) * 64, :, :],
                       hTg[2 * ml + half, r_ * 64:(r_ + 1) * 64, :].rearrange("p (k t) -> p k t", k=16),
                       f"{key}_{half}", r=[f"hTg{2 * ml + half}"], w=[key])

        def yT_ap(row0, nrows, m):
            i, rr = divmod(row0, 256)
            c, mm = divmod(m, MPC)
            return ysc[i][c, rr:rr + nrows, mm * 512:(mm + 1) * 512]

        def after_y(i, m, res):
            if (m + 1) % MPC == 0:
                c = m // MPC
                allgather(ysc[i][c], yall[i][c], "ag", reads=[res], writes=[f"yall{i}_{c}"])

        env1 = dict(P)
        env1.update(kb=kb, G=G, x=x_own, NH=Tq // 128, hTo=hT_own, hTd=None, yT=None, c_col=c_col, pos=pos, cstb=cstb,
                    after_h=after_h, load_hT=load_hT, yT_ap=yT_ap, after_y=after_y)
        build_p1(S, parts=('h',), env=env1)
        build_p1(S, parts=('mamba', 'rwkv', 'ret'), env=env1)
        if l > 0:
            allgather(xlast, xl_all, "ag", reads=["xlast"], writes=["xlall"])

        X = {}

        def extra_alloc(scm, bufs, X=X):
            X['cand'] = scm.sb("ycand", [128, 3, 8, 256], BF16)
            X['ymix'] = bufs['ymix']

        def load_x(dst, rn, row0, is_halo, l=l, X=X):
            if l == 0:
                kb.dma(dst, xh[row0:row0 + 128, :], rn, w=[rn])
            elif not is_halo:
                kb.dma(dst, xs1[row0 - 128:row0, :], rn, r=["xs1"], w=[rn])
            else:
                for q in range(4):
                    kb.dma(X['ymix'][:], xl_all[q * 128:(q + 1) * 128, :], "ymix", r=["xlall"], w=["ymix"])
                    if q == 0:
                        kb.ts('dve', dst, X['ymix'][:], sel[:, 4:5], ALU.mult, r=["ymix", "sel"], w=[rn])
                    else:
                        kb.stt('dve', dst, X['ymix'][:], sel[:, 4 + q:5 + q], dst, ALU.mult, ALU.add,
                               r=["ymix", "sel", rn], w=[rn])

        def load_y(ybf, t0, ntok, is_halo, X=X):
            cand = X['cand']
            for q in range(4):
                gt0 = q * Tq + t0 - 128
                if gt0 < 0:
                    kb.memset('pool', cand[:, :, :, 0:ntok], 0.0, w=[f"ycand{g}" for g in range(3)])
                else:
                    c, off = divmod(gt0, CT)
                    for g in range(3):
                        kb.dma(cand[:, g, :, 0:ntok],
                               yall[g][c, :, off:off + ntok].rearrange("(k p) t -> p k t", p=128),
                               f"ycand{g}", r=[f"yall{g}_{c}"], w=[f"ycand{g}"])
                if q == 0:
                    kb.ts('dve', ybf[:, :, :, 0:ntok], cand[:, :, :, 0:ntok], sel[:, 0:1], ALU.mult,
                          r=["ycand0", "ycand1", "ycand2", "sel"], w=["ybf"])
                else:
                    kb.stt('dve', ybf[:, :, :, 0:ntok], cand[:, :, :, 0:ntok], sel[:, q:q + 1], ybf[:, :, :, 0:ntok],
                           ALU.mult, ALU.add, r=["ycand0", "ycand1", "ycand2", "sel", "ybf"], w=["ybf"])

        def store_x(src, rn, o0, l=l):
            if l < depth - 1:
                kb.dma(xs1[o0:o0 + 128, :], src, rn, r=[rn], w=["xs1"])
                if o0 == Tq - 128:
                    kb.dma(xlast, src, rn, r=[rn], w=["xlast"])
            else:
                kb.dma(xo[o0:o0 + 128, :], src, rn, r=[rn], w=["xo"])

        env2 = dict(P)
        env2.update(w2s)
        env2.update(kb=kb, G=G, hmask=hmask, c_col=c_col, load_x=load_x, load_y=load_y, store_x=store_x,
                    extra_alloc=extra_alloc)
        build_p2(Tq, env=env2)
    nc = kb.finish(["xo"])
    return kb, nc


def fused_core_inputs(inp, b, r, S, depth):
    Tq = S // 4
    o = {}
    for l in range(depth):
        p1 = p1_core_inputs(inp, l, b, r)
        for k in P1_PARAM_SHAPES:
            o[f"{k}_{l}"] = p1[k]
        p2 = p2_core_inputs(inp, l, b, r, Tq, None, None)
        for k in P2_PARAM_SHAPES:
            o[f"{k}_{l}"] = p2[k]
    o['cst'] = p1_consts()
    o['cstb'] = p1_core_consts(r)
    o['c_col'] = np.ascontiguousarray(inp['c'][b].reshape(16, 128).T)
    o['pos'] = np.ascontiguousarray(inp['positions'][b][None, :]).astype(np.int32)
    x_b = inp['x'][b]
    xh = np.zeros((128 + Tq, 2048), np.float32)
    if r == 0:
        xh[128:] = x_b[0:Tq]
    else:
        xh[:] = x_b[r * Tq - 128:(r + 1) * Tq]
    o['xh'] = xh
    o['hmask'] = np.full((128, 1), 0.0 if r == 0 else 1.0, np.float32)
    sv = np.zeros((128, 8), np.float32)
    sv[:, r] = 1.0
    if r > 0:
        sv[:, 4 + r - 1] = 1.0
    o['selv'] = sv
    return o


_CACHE = {}


def kernel(**inputs):
    inp = {k: np.asarray(v) for k, v in inputs.items()}
    inp['x'] = np.ascontiguousarray(inp['x'], dtype=np.float32)
    B, S, _ = inp['x'].shape
    depth = inp['w_in'].shape[0]
    Tq = S // 4
    key = (S, depth)
    if key not in _CACHE:
        _CACHE[key] = build_fused(S, depth)[1]
    nc = _CACHE[key]
    in_maps = []
    for core in range(8):
        b, r = divmod(core, 4)
        in_maps.append(fused_core_inputs(inp, b, r, S, depth))
    res = run_bass_kernel_spmd(nc, in_maps, core_ids=list(range(8)))
    out = np.zeros((B, S, 2048), np.float32)
    for core in range(8):
        b, r = divmod(core, 4)
        out[b, r * Tq:(r + 1) * Tq] = np.asarray(res.results[core]['xo'])
    return out
```

```python
import numpy as np
import concourse.bass as bass
import concourse.mybir as mybir
from concourse.bass_utils import run_bass_kernel_spmd

F32 = mybir.dt.float32
BF16 = mybir.dt.bfloat16
I32 = mybir.dt.int32
AF = mybir.ActivationFunctionType
ALU = mybir.AluOpType
AX = mybir.AxisListType
EPOCH = 30000
import os as _os
SAME_ENGINE_ORDERED = tuple(_os.environ.get('SEO', 'pe').split(','))
D = 2048


class Tracker:
    ENGS = ('pe', 'act', 'dve', 'pool', 'sp')

    def __init__(self, nc):
        self.nc = nc
        self.ops = {e: [] for e in self.ENGS}
        self.cur_sem = {}
        self.cnt = {}
        self.nsem = 0
        for e in self.ENGS:
            self.cur_sem[e] = self._new_sem(e)
            self.cnt[e] = 0
        self.known = {e: {} for e in self.ENGS}
        self.last_w = {}
        self.readers = {}
        self.dma_sems = {}
        self.dma_cnt = {}
        self.nops = 0
        self._old_epochs = []

    def _new_sem(self, tag):
        self.nsem += 1
        return self.nc.alloc_semaphore(f"s{self.nsem}_{tag}")

    def _waits_for(self, eng, reads, writes, is_dma):
        need = {}

        def add(ev):
            sem, val, src, src_dma = ev
            if src == eng and eng in SAME_ENGINE_ORDERED and not src_dma and not is_dma:
                return
            k = id(sem)
            if self.known[eng].get(k, 0) >= val:
                return
            if k not in need or need[k][1] < val:
                need[k] = (sem, val)

        for r in reads:
            ev = self.last_w.get(r)
            if ev is not None:
                add(ev)
        for w in writes:
            ev = self.last_w.get(w)
            if ev is not None:
                add(ev)
            rd = self.readers.get(w)
            if rd:
                for ev in rd.values():
                    add(ev)
        out = list(need.values())
        for sem, val in out:
            self.known[eng][id(sem)] = val
        return out

    def _commit(self, ev, reads, writes):
        for r in reads:
            self.readers.setdefault(r, {})[id(ev[0])] = ev
        for w in writes:
            self.last_w[w] = ev
            self.readers[w] = {}

    def begin_capture(self):
        self._cap = []

    def end_capture(self):
        c, self._cap = self._cap, None
        return c

    def replay_interleaved(self, caps):
        idx = [0] * len(caps)
        while True:
            done = True
            for j, c in enumerate(caps):
                if idx[j] < len(c):
                    kind, args = c[idx[j]]
                    idx[j] += 1
                    done = False
                    if kind == 'op':
                        self.op(*args)
                    else:
                        self.dma(*args)
            if done:
                break

    def op(self, eng, fn, reads=(), writes=()):
        if getattr(self, '_cap', None) is not None:
            self._cap.append(('op', (eng, fn, tuple(reads), tuple(writes))))
            return None
        pr = tuple(r for r in reads if isinstance(r, str) and r.startswith('pb'))
        if pr and eng != 'pe':
            writes = tuple(writes) + pr
        waits = self._waits_for(eng, reads, writes, False)
        if self.cnt[eng] >= EPOCH:
            self._old_epochs.append((self.cur_sem[eng], self.cnt[eng]))
            self.cur_sem[eng] = self._new_sem(eng)
            self.cnt[eng] = 0
        self.cnt[eng] += 1
        sem = self.cur_sem[eng]
        ev = (sem, self.cnt[eng], eng, False)
        self.ops[eng].append((waits, fn, sem, 1))
        self._commit(ev, reads, writes)
        self.nops += 1
        return ev

    def dma(self, eng, fn, key, reads=(), writes=(), inc=16):
        if getattr(self, '_cap', None) is not None:
            self._cap.append(('dma', (eng, fn, key, tuple(reads), tuple(writes), inc)))
            return None
        if key not in self.dma_sems:
            self.dma_sems[key] = self._new_sem('d' + str(key))
            self.dma_cnt[key] = 0
        sem = self.dma_sems[key]
        chan = ('__chan__', key)
        waits = self._waits_for(eng, tuple(reads), tuple(writes) + (chan,), True)
        self.dma_cnt[key] += inc
        ev = (sem, self.dma_cnt[key], eng, True)
        self.ops[eng].append((waits, fn, sem, inc))
        self._commit(ev, reads, tuple(writes) + (chan,))
        self.nops += 1
        return ev

    def barrier(self):
        latest = {}
        for e in self.ENGS:
            for (waits, fn, sem, inc) in ():
                pass
        for e in self.ENGS:
            if self.cnt[e] > 0:
                latest[id(self.cur_sem[e])] = (self.cur_sem[e], self.cnt[e])
        for k, sem in self.dma_sems.items():
            if self.dma_cnt[k] > 0:
                latest[id(sem)] = (sem, self.dma_cnt[k])
        for (sem, val) in self._old_epochs:
            latest[id(sem)] = (sem, val)
        for e in self.ENGS:
            waits = []
            for k, (sem, val) in latest.items():
                if self.known[e].get(k, 0) < val:
                    waits.append((sem, val))
                    self.known[e][k] = val
            if waits:
                self.ops[e].append((waits, None, None, 0))

    def wait_all(self, eng, resources):
        waits = self._waits_for(eng, resources, (), True)
        self.ops[eng].append((waits, None, None, 0))

    def emit(self):
        nc = self.nc
        ops = self.ops
        with nc.Block() as block:
            def run(e, lst):
                for waits, fn, sem, inc in lst:
                    for s, v in waits:
                        e.wait_ge(s, v)
                    if fn is not None:
                        fn(e).then_inc(sem, inc)

            @block.tensor
            def _(e):
                run(e, ops['pe'])

            @block.scalar
            def _(e):
                run(e, ops['act'])

            @block.vector
            def _(e):
                run(e, ops['dve'])

            @block.gpsimd
            def _(e):
                run(e, ops['pool'])

            @block.sync
            def _(e):
                run(e, ops['sp'])


class KB:
    def __init__(self, name="k"):
        self.nc = bass.Bass("TRN2", target_bir_lowering=False)
        self.nc.allow_low_precision("bf16 matmul operands with fp32 PSUM accumulation")
        self.tr = Tracker(self.nc)
        self._n = 0
        self.outs = []

    def din(self, name, shape, dt=F32):
        return self.nc.dram_tensor(name, list(shape), dt, kind="ExternalInput").ap()

    def dout(self, name, shape, dt=F32):
        self.outs.append(name)
        return self.nc.dram_tensor(name, list(shape), dt, kind="ExternalOutput").ap()

    def dscratch(self, name, shape, dt=F32):
        return self.nc.dram_tensor(name, list(shape), dt, kind="Internal").ap()

    def sb(self, name, shape, dt=F32):
        return self.nc.alloc_sbuf_tensor(name, list(shape), dt)

    def ps(self, name, shape, dt=F32):
        return self.nc.alloc_psum_tensor(name, list(shape), dt)

    def dma(self, out, in_, key, r=(), w=(), eng='sp'):
        self.tr.dma(eng, lambda e: e.dma_start(out=out, in_=in_), key, reads=r, writes=w)

    def mm(self, out, lhsT, rhs, start, stop, r, w):
        self.tr.op('pe', lambda e: e.matmul(out, lhsT=lhsT, rhs=rhs, start=start, stop=stop), reads=r, writes=w)

    def tp(self, out, in_, ident, r, w):
        self.tr.op('pe', lambda e: e.transpose(out=out, in_=in_, identity=ident), reads=r, writes=w)

    def act(self, out, in_, func, r, w, bias=None, scale=None, accum=None):
        kw = {}
        if bias is not None:
            kw['bias'] = bias
        if scale is not None:
            kw['scale'] = scale
        if accum is not None:
            kw['accum_out'] = accum
        self.tr.op('act', lambda e: e.activation(out=out, in_=in_, func=func, **kw), reads=r, writes=w)

    def tt(self, eng, out, a, b, op, r, w):
        self.tr.op(eng, lambda e: e.tensor_tensor(out=out, in0=a, in1=b, op=op), reads=r, writes=w)

    def ts(self, eng, out, a, s1, op0, r, w, s2=None, op1=None, accum=None):
        kw = {}
        if op1 is not None:
            kw['op1'] = op1
        if accum is not None:
            kw['accum_out'] = accum
        self.tr.op(eng, lambda e: e.tensor_scalar(out=out, in0=a, scalar1=s1, scalar2=s2, op0=op0, **kw),
                   reads=r, writes=w)

    def stt(self, eng, out, a, s, b, op0, op1, r, w):
        eng = 'dve'
        self.tr.op(eng, lambda e: e.scalar_tensor_tensor(out=out, in0=a, scalar=s, in1=b, op0=op0, op1=op1),
                   reads=r, writes=w)

    def cp(self, eng, out, in_, r, w):
        if eng == 'act':
            self.tr.op('act', lambda e: e.copy(out=out, in_=in_), reads=r, writes=w)
        else:
            self.tr.op(eng, lambda e: e.tensor_copy(out=out, in_=in_), reads=r, writes=w)

    def memset(self, eng, out, val, w):
        self.tr.op(eng, lambda e: e.memset(out, val), writes=w)

    def recip(self, out, in_, r, w):
        self.tr.op('dve', lambda e: e.reciprocal(out=out, in_=in_), reads=r, writes=w)

    def scan(self, out, d0, d1, r, w):
        self.tr.op('dve', lambda e: e.tensor_tensor_scan(out=out, data0=d0, data1=d1, initial=0.0,
                                                         op0=ALU.mult, op1=ALU.add), reads=r, writes=w)

    def finish(self, out_resources):
        self.tr.wait_all('sp', out_resources)
        self.tr.emit()
        return self.nc


import numpy as np

def tile_w_ada(inp, l):
    cache = inp.setdefault('_wada_tiled', {})
    if l not in cache:
        cache[l] = np.ascontiguousarray(inp['w_ada'][l].reshape(16, 128, 96, 128).transpose(2, 1, 0, 3))
    return cache[l]


def p1_consts():
    C = np.zeros((128, 2048), np.float32)
    i = np.arange(128)
    C[:, 0:128] = np.eye(128)
    C[:, 128:256] = (i[None, :] >= i[:, None])
    C[:, 256:384] = 1.0
    blk = (i[:, None] // 64 == i[None, :] // 64).astype(np.float32)
    C[:, 384:512] = blk
    C[:, 512:640] = blk / 64.0
    C[:, 640:768] = blk * (i[:, None] < i[None, :])
    C[:, 768:896] = blk * (i[:, None] > i[None, :])
    C[:, 896:1024] = blk * (i[:, None] <= i[None, :])
    rm = np.ones(512, np.float32)
    rm[::64] = 0
    C[:, 1024:1536] = rm[None, :]
    return C


def p1_core_consts(g):
    Cb = np.zeros((128, 1024), np.float32)
    p = np.arange(128)
    hl = p // 64
    d = p % 64
    heads = 2 * g + hl
    lg = np.log1p(-np.exp2(-5.0 - heads.astype(np.float64)))
    inv_freq = 10000.0 ** (-(d % 32).astype(np.float64) / 32.0)
    Cb[:, 0] = inv_freq
    Cb[:, 1] = np.where(d < 32, -1.0, 1.0)
    Cb[:, 2] = np.exp(128.0 * lg)
    idx = np.arange(128, dtype=np.float64)
    Cb[:, 128:256] = np.exp((idx[None, :] + 1.0) * lg[:, None])
    Cb[:, 256:384] = np.exp((127.0 - idx)[:, None] * lg[None, :])
    for h2 in range(2):
        lgh = np.log1p(-np.exp2(-5.0 - (2 * g + h2)))
        rel = idx[None, :] - idx[:, None]
        Cb[:, 384 + h2 * 128:384 + (h2 + 1) * 128] = np.where(rel >= 0, np.exp(rel * lgh), 0.0)
    return Cb


M_COLS = 3088
RW_COLS = 3520


def p1_core_inputs(inp, l, b, g):
    w_in = inp['w_in'][l]
    o = {}
    zc = np.arange(256 * g, 256 * g + 256)
    xsc = 1024 + np.arange(256 * g, 256 * g + 256)
    Bc = 2048 + np.arange(128 * g, 128 * g + 128)
    Cc = 2560 + np.arange(128 * g, 128 * g + 128)
    dtc = 3072 + np.arange(4 * g, 4 * g + 4)
    o['w_mfm'] = np.ascontiguousarray(w_in[:, np.concatenate([xsc, Bc, Cc])])
    o['w_mtm'] = np.ascontiguousarray(w_in[:, np.concatenate([zc, dtc])])
    convc = np.concatenate([xsc, Bc, Cc]) - 1024
    cw = np.concatenate([inp['m_conv_w'][l][:, convc], inp['m_conv_b'][l][None, convc]], axis=0)
    o['m_cw'] = np.ascontiguousarray(cw.T.reshape(4, 128, 5).transpose(1, 0, 2))
    o['m_hd'] = np.ascontiguousarray(inp['m_head'][l][:, 4 * g:4 * g + 4].reshape(1, 12))
    o['m_ng'] = np.ascontiguousarray(inp['m_norm_g'][l][None, 256 * g:256 * g + 256])
    r0 = M_COLS
    ch = np.arange(256 * g, 256 * g + 256)
    cols = np.concatenate([r0 + ch, r0 + 1024 + ch, r0 + 2048 + ch, r0 + 3072 + np.arange(96),
                           r0 + 3168 + np.arange(96), r0 + 3264 + np.arange(256)])
    o['w_wfm'] = np.ascontiguousarray(w_in[:, cols])
    mu = inp['rwkv_mu'][l][cols - r0]
    mut = np.zeros((128, 10), np.float32)
    offs = [0, 128, 256, 384, 512, 640, 768, 864, 960, 1088]
    sizes = [128, 128, 128, 128, 128, 128, 96, 96, 128, 128]
    for q, (of, sz) in enumerate(zip(offs, sizes)):
        mut[:sz, q] = mu[of:of + sz]
    o['rw_mu'] = mut
    vec = inp['rwkv_vec'][l][:, ch]
    o['rw_vec'] = np.ascontiguousarray(vec.T.reshape(2, 128, 7).transpose(1, 0, 2))
    o['rw_w2'] = np.ascontiguousarray(inp['rwkv_w2'][l][:, ch])
    o['rw_a2'] = np.ascontiguousarray(inp['rwkv_a2'][l][:, ch])
    o['rw_g2'] = np.ascontiguousarray(inp['rwkv_g2'][l][:, ch])
    t0 = M_COLS + RW_COLS
    hd = np.arange(128 * g, 128 * g + 128)
    d = hd % 64
    partner = np.where(d < 32, hd + 32, hd - 32)
    cols = np.concatenate([t0 + hd, t0 + partner, t0 + 512 + hd, t0 + 512 + partner])
    o['w_rfm'] = np.ascontiguousarray(w_in[:, cols])
    cols = np.concatenate([t0 + 1024 + ch, t0 + 2048 + ch])
    o['w_rtm'] = np.ascontiguousarray(w_in[:, cols])
    o['c_col'] = np.ascontiguousarray(inp['c'][b].reshape(16, 128).T)
    o['w_ada'] = np.ascontiguousarray(tile_w_ada(inp, l)[:32])
    o['b_ada'] = np.ascontiguousarray(inp['b_ada'][l][None, :4096])
    o['norm_g'] = inp['norm_g'][l]
    o['pos'] = np.ascontiguousarray(inp['positions'][b][None, :]).astype(np.int32)
    o['cst'] = p1_consts()
    o['cstb'] = p1_core_consts(g)
    return o


def p2_core_inputs(inp, l, b, q, Tq, x_b, yT_b):
    o = {}
    t0 = q * Tq
    NT = 128 + Tq
    if x_b is not None:
        xh = np.zeros((NT, 2048), np.float32)
        yh = np.zeros((3, 1024, NT), yT_b.dtype)
        if q == 0:
            xh[128:] = x_b[0:Tq]
            yh[:, :, 128:] = yT_b[:, :, 0:Tq]
        else:
            xh[:] = x_b[t0 - 128:t0 + Tq]
            yh[:] = yT_b[:, :, t0 - 128:t0 + Tq]
        o['xh'] = xh
        o['yTh'] = yh
    o['hmask'] = np.full((128, 1), 0.0 if q == 0 else 1.0, np.float32)
    o['c_col'] = np.ascontiguousarray(inp['c'][b].reshape(16, 128).T)
    o['w_ada'] = tile_w_ada(inp, l)
    o['b_ada'] = inp['b_ada'][l][None, :]
    ng = inp['norm_g'][l]
    o['norm_g'] = ng
    o['ng_col'] = np.ascontiguousarray(ng.reshape(4, 16, 128).transpose(2, 0, 1).reshape(128, 64))
    o['w_gate'] = inp['w_gate'][l]
    o['w_branch'] = inp['w_branch'][l]
    o['w_out'] = inp['w_out'][l]
    o['w_up'] = inp['w_up'][l]
    fcv = np.concatenate([inp['f_conv_w'][l], inp['f_conv_b'][l][None, :]], axis=0)
    o['f_cv'] = np.ascontiguousarray(fcv.T.reshape(86, 128, 4).transpose(1, 0, 2))
    o['w_down'] = inp['w_down'][l]
    o['cst'] = p1_consts()
    return o


from contextlib import ExitStack
import numpy as np


MAGIC = 12582912.0
TWO_PI = float(2 * np.pi)


class Scope:
    _n = 0

    def __init__(self, kb):
        self.kb = kb
        self.st = ExitStack()
        Scope._n += 1
        self.tag = f"_sc{Scope._n}"

    def sb(self, name, shape, dt=F32):
        return self.st.enter_context(self.kb.nc.sbuf_tensor(name + self.tag, list(shape), dt))

    def close(self):
        self.kb.tr.barrier()
        self.st.close()


def load_cast_weights(kb, sc, wdram, ncols, name, stage_cols=128):
    W = sc.sb(name, [128, 16, ncols], BF16)
    stg = [sc.sb(f"{name}_stg{i}", [128, 16, stage_cols], F32) for i in range(2)]
    wv = wdram.rearrange("(k p) n -> p k n", p=128)
    i = 0
    for c0 in range(0, ncols, stage_cols):
        c1 = min(ncols, c0 + stage_cols)
        s = stg[i % 2]
        rn = f"{name}_stg{i % 2}"
        kb.dma(s[:, :, 0:c1 - c0], wv[:, :, c0:c1], rn, w=[rn])
        kb.cp('pool' if i % 2 else 'dve', W[:, :, c0:c1], s[:, :, 0:c1 - c0], r=[rn], w=[name])
        i += 1
    return W


def emit_mod(kb, sc, c_col, w_ada, b_ada, vec_ids, ones_row, pb, tag):
    cc = sc.sb(f"{tag}_cc", [128, 16])
    scl = sc.sb(f"{tag}_sc", [128, 16])
    kb.dma(cc[:], c_col, f"{tag}_cc", w=[f"{tag}_cc"])
    kb.act(scl[:], cc[:], AF.Silu, r=[f"{tag}_cc"], w=[f"{tag}_sc"])
    NB = 4
    stg = [sc.sb(f"{tag}_wa{i}", [128, 16, 128], F32) for i in range(NB)]
    banks = (7, 4, 2, 1)
    rows = {}
    i = 0
    for vid in vec_ids:
        row = sc.sb(f"{tag}_row{vid}", [1, 2048])
        brow = sc.sb(f"{tag}_brow{vid}", [1, 2048])
        rn = f"{tag}_row{vid}"
        kb.dma(brow[:], b_ada[0:1, vid * 2048:(vid + 1) * 2048], f"{tag}_brow{vid}", w=[f"{tag}_brow{vid}"])
        for cch in range(16):
            s = stg[i % NB]
            sn = f"{tag}_wa{i % NB}"
            kb.dma(s[:], w_ada[vid * 16 + cch], sn, w=[sn])
            bk = banks[i % NB]
            pr = f"pb{bk}"
            pcols = pb[bk][0:1, 0:128]
            for k in range(16):
                kb.mm(pcols, lhsT=scl[:, k:k + 1], rhs=s[:, k, :], start=(k == 0), stop=(k == 15),
                      r=[sn, f"{tag}_sc"], w=[pr])
            kb.tt('dve', row[0:1, cch * 128:(cch + 1) * 128], pcols, brow[0:1, cch * 128:(cch + 1) * 128], ALU.add,
                  r=[pr, f"{tag}_brow{vid}"], w=[rn])
            i += 1
        rows[vid] = (row, rn)
    return rows


def bcast_row(kb, row, rn, ones_row, pb, emit_chunk):
    for c in range(4):
        pr = "pb6" if c % 2 == 0 else "pb5"
        pa = pb[6][:, 0:512] if c % 2 == 0 else pb[5][:, 0:512]
        kb.mm(pa, lhsT=ones_row[0:1, 0:128], rhs=row[0:1, c * 512:(c + 1) * 512], start=True, stop=True,
              r=[rn, 'C'], w=[pr])
        emit_chunk(c, pa, pr)


P1_PARAM_SHAPES = {
    "w_mfm": ([D, 512], F32), "w_mtm": ([D, 260], F32), "m_cw": ([128, 4, 5], F32), "m_hd": ([1, 12], F32),
    "m_ng": ([1, 256], F32), "w_rfm": ([D, 512], F32), "w_rtm": ([D, 512], F32), "w_wfm": ([D, 1216], F32),
    "rw_mu": ([128, 10], F32), "rw_vec": ([128, 2, 7], F32), "rw_w2": ([96, 256], F32), "rw_a2": ([96, 256], F32),
    "rw_g2": ([256, 256], F32),
}


def setup_globals(kb, cst):
    G = {}
    G['pb'] = [kb.ps(f"pb{i}", [128, 512]) for i in range(8)]
    C = kb.sb("C", [128, 2048])
    kb.dma(C[:], cst, "C", w=["C"])
    Cb = kb.sb("Cb", [128, 256], BF16)
    kb.cp('dve', Cb[:, 0:128], C[:, 0:128], r=["C"], w=["Cb"])
    G['C'] = C
    G['Cb'] = Cb
    return G


def build_p1(S, parts=('h', 'mamba', 'rwkv', 'ret'), env=None):
    NM = S // 512
    if env is None:
        kb = KB()
        e = {}
        e['x'] = kb.din("x", [S, D])
        e['c_col'] = kb.din("c_col", [128, 16])
        e['w_ada'] = kb.din("w_ada", [32, 128, 16, 128])
        e['b_ada'] = kb.din("b_ada", [1, 2 * D])
        e['norm_g'] = kb.din("norm_g", [4, D])
        e['pos'] = kb.din("pos", [1, S], I32)
        cst = kb.din("cst", [128, 2048])
        e['cstb'] = kb.din("cstb", [128, 1024])
        for k, (shp, dt_) in P1_PARAM_SHAPES.items():
            e[k] = kb.din(k, shp, dt_)
        e['yT'] = kb.dout("yT", [768, S], BF16)
        e['hTd'] = kb.dscratch("hTd", [NM, 128, 16, 512], BF16)
        e['hTo'] = e['hTd']
        e['NH'] = S // 128
        G = setup_globals(kb, cst)
    else:
        kb = env['kb']
        e = env
        G = env['G']
    nc = kb.nc
    x, c_col, w_ada, b_ada, norm_g, pos, cstb = (e[k] for k in ('x', 'c_col', 'w_ada', 'b_ada', 'norm_g', 'pos', 'cstb'))
    w_mfm, w_mtm, m_cw, m_hd, m_ng, w_rfm, w_rtm, w_wfm = (e[k] for k in ('w_mfm', 'w_mtm', 'm_cw', 'm_hd', 'm_ng', 'w_rfm', 'w_rtm', 'w_wfm'))
    rw_mu, rw_vec, rw_w2, rw_a2, rw_g2 = (e[k] for k in ('rw_mu', 'rw_vec', 'rw_w2', 'rw_a2', 'rw_g2'))
    yT, hTd, hTo, NH = e['yT'], e['hTd'], e['hTo'], e['NH']

    def _load_hT(dst, m, key):
        kb.dma(dst[:], hTd[m], key, r=[f"hTd{m}"], w=[key])

    def _yT_ap(row0, nrows, m):
        return yT[row0:row0 + nrows, m * 512:(m + 1) * 512]

    load_hT = e.get('load_hT', _load_hT)
    yT_ap = e.get('yT_ap', _yT_ap)
    after_y = e.get('after_y', lambda i, m, res: None)
    pb = G['pb']
    C, Cb = G['C'], G['Cb']
    ident_f = C[:, 0:128]
    U_f = C[:, 128:256]
    ones_f = C[:, 256:384]
    blk_f = C[:, 384:512]
    blk64_f = C[:, 512:640]
    SU_f = C[:, 640:768]
    SL_f = C[:, 768:896]
    UI_f = C[:, 896:1024]
    rmask = C[:, 1024:1536]
    ident_b = Cb[:, 0:128]
    ones_row = C[0:1, 256:384]

    if 'h' in parts:
        sc = Scope(kb)
        rows = emit_mod(kb, sc, c_col, w_ada, b_ada, [0, 1], ones_row, pb, "m0")
        Gbc = sc.sb("Gbc", [128, 2048])
        SHbc = sc.sb("SHbc", [128, 2048])
        kb.dma(Gbc[:], norm_g[0:1, :].partition_broadcast(128), "Gbc", w=["Gbc"])

        def g_chunk(c, pa, pr):
            kb.stt('dve', Gbc[:, c * 512:(c + 1) * 512], pa, 1.0, Gbc[:, c * 512:(c + 1) * 512], ALU.add, ALU.mult,
                   r=[pr, "Gbc"], w=["Gbc"])

        def sh_chunk(c, pa, pr):
            kb.cp('act', SHbc[:, c * 512:(c + 1) * 512], pa, r=[pr], w=["SHbc"])

        bcast_row(kb, rows[1][0], rows[1][1], ones_row, pb, g_chunk)
        bcast_row(kb, rows[0][0], rows[0][1], ones_row, pb, sh_chunk)
        xt = [sc.sb(f"xt{i}", [128, 2048]) for i in range(2)]
        junk = sc.sb("junk", [128, 2048], BF16)
        tmp = sc.sb("htmp", [128, 2048])
        hb = sc.sb("hb", [128, 2048], BF16)
        ss = sc.sb("ss", [128, 1])
        rstd = sc.sb("rstd", [128, 1])
        hst = [sc.sb(f"hst{i}", [128, 16, 512], BF16) for i in range(2)]
        for i in range(NH):
            m, sub = divmod(i, 4)
            xb = xt[i % 2]
            xn = f"xt{i % 2}"
            kb.dma(xb[:], x[i * 128:(i + 1) * 128, :], xn, w=[xn])
            kb.act(junk[:], xb[:], AF.Square, r=[xn], w=["junk", "ss"], accum=ss[:])
            kb.act(rstd[:], ss[:], AF.Sqrt, r=["ss"], w=["rstd"], bias=1e-6, scale=1.0 / D)
            kb.recip(rstd[:], rstd[:], r=["rstd"], w=["rstd"])
            kb.stt('dve', tmp[:], xb[:], rstd[:, 0:1], Gbc[:], ALU.mult, ALU.mult, r=[xn, "rstd", "Gbc"], w=["htmp"])
            kb.tt('pool', hb[:], tmp[:], SHbc[:], ALU.add, r=["htmp", "SHbc"], w=["hb"])
            hs = hst[m % 2]
            hn = f"hst{m % 2}"
            for q in range(4):
                pr = f"pb{q}"
                for kk in range(4):
                    k = q * 4 + kk
                    kb.mm(pb[q][:, kk * 128:(kk + 1) * 128], lhsT=hb[:, k * 128:(k + 1) * 128], rhs=ident_b,
                          start=True, stop=True, r=["hb", "Cb"], w=[pr])
                eng = 'act' if q % 2 == 0 else 'dve'
                kb.cp(eng, hs[:, q * 4:(q + 1) * 4, sub * 128:(sub + 1) * 128],
                      pb[q][:, :].rearrange("p (k t) -> p k t", k=4), r=[pr], w=[hn])
            if sub == 3:
                kb.dma(hTo[m], hs[:], hn, r=[hn], w=[f"hTd{m}"])
                if 'after_h' in e:
                    e['after_h'](m)
        sc.close()

    if 'mamba' in parts:
        sc = Scope(kb)
        Wfm = load_cast_weights(kb, sc, w_mfm, 512, "Wfm")
        Wtm = load_cast_weights(kb, sc, w_mtm, 260, "Wtm")
        cw = sc.sb("cw", [128, 4, 5])
        kb.dma(cw[:], m_cw, "cw", w=["cw"])
        mh = sc.sb("mh", [128, 12])
        kb.dma(mh[:], m_hd.partition_broadcast(128), "mh", w=["mh"])
        negA = sc.sb("negA", [128, 4])
        kb.act(negA[:], mh[:, 4:8], AF.Exp, r=["mh"], w=["negA"])
        kb.ts('dve', negA[:], negA[:], -1.0, ALU.mult, r=["negA"], w=["negA"])
        dsk = sc.sb("dsk", [128, 256])
        for h in range(4):
            kb.ts('dve', dsk[:, h * 64:(h + 1) * 64], ones_f[:, 0:64], mh[:, 8 + h:9 + h], ALU.mult,
                  r=["C", "mh"], w=["dsk"])
        mng = sc.sb("mng", [128, 256])
        kb.dma(mng[:], m_ng.partition_broadcast(128), "mng", w=["mng"])
        hT = [sc.sb(f"hT{i}", [128, 16, 512], BF16) for i in range(2)]
        raw = [sc.sb(f"raw{j}", [128, 515]) for j in range(4)]
        acc = [sc.sb(f"acc{j}", [128, 512]) for j in range(2)]
        xs_act = [sc.sb(f"xsa{j}", [128, 512]) for j in range(2)]
        B_act = sc.sb("B_act", [128, 512])
        BT_bf = sc.sb("BT_bf", [128, 512], BF16)
        CT_bf = sc.sb("CT_bf", [128, 512], BF16)
        zs = sc.sb("zs", [128, 256])
        dtp = sc.sb("dtp", [128, 4])
        dt = sc.sb("dt", [128, 4])
        adt = sc.sb("adt", [128, 4])
        xs_sb = sc.sb("xs_sb", [128, 256])
        B_tm = sc.sb("B_tm", [128, 128], BF16)
        xdt_bf = sc.sb("xdt_bf", [128, 256], BF16)
        xte_bf = sc.sb("xte_bf", [128, 256], BF16)
        cum_sb = sc.sb("cum_sb", [128, 4])
        rh = sc.sb("rh", [128, 4, 128])
        CBm = sc.sb("CBm", [128, 128])
        seg = sc.sb("seg", [128, 4, 128])
        dec = sc.sb("dec", [128, 4, 128])
        MT_bf = sc.sb("MT_bf", [128, 4, 128], BF16)
        ecum = sc.sb("ecum", [128, 4])
        tte = sc.sb("tte", [128, 4])
        ecl = sc.sb("ecl", [128, 4])
        yi_sb = sc.sb("yi_sb", [128, 256])
        y = sc.sb("y", [128, 256])
        y2 = sc.sb("y2", [128, 256])
        state = sc.sb("state", [128, 256])
        state_bf = sc.sb("state_bf", [128, 256], BF16)
        mjunk = sc.sb("mjunk", [128, 256])
        mss = sc.sb("mss", [128, 1])
        mrstd = sc.sb("mrstd", [128, 1])
        yo_bf = sc.sb("yo_bf", [128, 256], BF16)
        yst = [sc.sb(f"yst{i}", [128, 2, 512], BF16) for i in range(2)]
        kb.memset('pool', state[:], 0.0, w=["state"])
        kb.memset('pool', state_bf[:], 0.0, w=["state_bf"])
        for j in range(4):
            kb.memset('pool', raw[j][:], 0.0, w=[f"raw{j}"])
        for m in range(NM):
            h_ = hT[m % 2]
            hn = f"hT{m % 2}"
            load_hT(h_, m, hn)
            ys = yst[m % 2]
            ysn = f"yst{m % 2}"
            for j in range(4):
                pr = f"pb{j % 2}"
                pa = pb[j % 2]
                for k in range(16):
                    kb.mm(pa[:, :], lhsT=Wfm[:, k, j * 128:(j + 1) * 128], rhs=h_[:, k, :], start=(k == 0),
                          stop=(k == 15), r=["Wfm", hn], w=[pr])
                rj = f"raw{j}"
                kb.cp('pool', raw[j][:, 0:3], raw[j][:, 512:515], r=[rj], w=[rj])
                kb.cp('act', raw[j][:, 3:515], pa[:, :], r=[pr], w=[rj])
                a_ = acc[j % 2]
                an = f"acc{j % 2}"
                kb.ts('dve', a_[:], raw[j][:, 3:515], cw[:, j, 3:4], ALU.mult, r=[rj, "cw"], w=[an],
                      s2=cw[:, j, 4:5], op1=ALU.add)
                for tpi, eng in ((2, 'pool'), (1, 'dve'), (0, 'pool')):
                    kb.stt(eng, a_[:], raw[j][:, tpi:tpi + 512], cw[:, j, tpi:tpi + 1], a_[:], ALU.mult, ALU.add,
                           r=[rj, "cw", an], w=[an])
                if j < 2:
                    kb.act(xs_act[j][:], a_[:], AF.Silu, r=[an], w=[f"xsa{j}"])
                elif j == 2:
                    kb.act(B_act[:], a_[:], AF.Silu, r=[an], w=["B_act"])
                    kb.cp('pool', BT_bf[:], B_act[:], r=["B_act"], w=["BT_bf"])
                else:
                    kb.act(CT_bf[:], a_[:], AF.Silu, r=[an], w=["CT_bf"])
            import os as _os
            _nch = int(_os.environ.get("MB_NCH", "4"))
            for c in range(_nch):
                cs = slice(c * 128, (c + 1) * 128)
                _sect = _os.environ.get('MB_SECT', '123456')
                if '1' in _sect:
                    for k in range(16):
                        kb.mm(pb[2][:, 0:260], lhsT=h_[:, k, cs], rhs=Wtm[:, k, :], start=(k == 0), stop=(k == 15),
                              r=[hn, "Wtm"], w=["pb2"])
                    kb.act(zs[:], pb[2][:, 0:256], AF.Silu, r=["pb2"], w=["zs"])
                    kb.tt('dve', dtp[:], pb[2][:, 256:260], mh[:, 0:4], ALU.add, r=["pb2", "mh"], w=["dtp"])
                    kb.act(dtp[:], dtp[:], AF.Exp, r=["dtp"], w=["dtp"])
                    kb.act(dt[:], dtp[:], AF.Ln, r=["dtp"], w=["dt"], bias=1.0)
                    kb.tt('dve', adt[:], dt[:], negA[:], ALU.mult, r=["dt", "negA"], w=["adt"])
                if '2' in _sect:
                    for j in range(2):
                        kb.mm(pb[3][:, j * 128:(j + 1) * 128], lhsT=xs_act[j][:, cs], rhs=ident_f, start=True, stop=True,
                              r=[f"xsa{j}", "C"], w=["pb3"])
                    kb.mm(pb[3][:, 256:384], lhsT=B_act[:, cs], rhs=ident_f, start=True, stop=True,
                          r=["B_act", "C"], w=["pb3"])
                    kb.cp('act', xs_sb[:], pb[3][:, 0:256], r=["pb3"], w=["xs_sb"])
                    kb.cp('dve', B_tm[:], pb[3][:, 256:384], r=["pb3"], w=["B_tm"])
                    kb.tt('pool', xdt_bf[:].rearrange("p (h c) -> p h c", h=4), xs_sb[:].rearrange("p (h c) -> p h c", h=4),
                          dt[:].unsqueeze(2).to_broadcast([128, 4, 64]), ALU.mult, r=["xs_sb", "dt"], w=["xdt_bf"])
                if '3' in _sect:
                    kb.mm(pb[4][:, 0:4], lhsT=U_f, rhs=adt[:], start=True, stop=True, r=["C", "adt"], w=["pb4"])
                    kb.cp('act', cum_sb[:], pb[4][:, 0:4], r=["pb4"], w=["cum_sb"])
                    kb.tt('pool', rh[:], U_f.unsqueeze(1).to_broadcast([128, 4, 128]),
                          adt[:].unsqueeze(2).to_broadcast([128, 4, 128]), ALU.mult, r=["C", "adt"], w=["rh"])
                    for h in range(4):
                        kb.mm(pb[5][:, h * 128:(h + 1) * 128], lhsT=ones_f, rhs=rh[:, h, :], start=True, stop=True,
                              r=["C", "rh"], w=["pb5"])
                    kb.mm(pb[4][:, 128:256], lhsT=BT_bf[:, cs], rhs=CT_bf[:, cs], start=True, stop=True,
                          r=["BT_bf", "CT_bf"], w=["pb4"])
                    kb.tt('dve', CBm[:], pb[4][:, 128:256], U_f, ALU.mult, r=["pb4", "C"], w=["CBm"])
                    kb.tt('dve', seg[:], pb[5][:, :].rearrange("p (h i) -> p h i", h=4),
                          cum_sb[:].unsqueeze(2).to_broadcast([128, 4, 128]), ALU.subtract, r=["pb5", "cum_sb"], w=["seg"])
                    kb.ts('pool', seg[:], seg[:], 0.0, ALU.min, r=["seg"], w=["seg"])
                    kb.act(dec[:], seg[:], AF.Exp, r=["seg"], w=["dec"])
                    kb.tt('pool', MT_bf[:], dec[:], CBm[:].unsqueeze(1).to_broadcast([128, 4, 128]), ALU.mult,
                          r=["dec", "CBm"], w=["MT_bf"])
                    kb.act(ecum[:], cum_sb[:], AF.Exp, r=["cum_sb"], w=["ecum"])
                    last = pb[5][:, :].rearrange("p (h i) -> p h i", h=4)[:, :, 127:128]
                    kb.tt('dve', tte[:].rearrange("p (h o) -> p h o", o=1), last,
                          cum_sb[:].rearrange("p (h o) -> p h o", o=1), ALU.subtract, r=["pb5", "cum_sb"], w=["tte"])
                    kb.act(tte[:], tte[:], AF.Exp, r=["tte"], w=["tte"])
                    kb.act(ecl[:].rearrange("p (h o) -> p h o", o=1), last, AF.Exp, r=["pb5"], w=["ecl"])
                if '4' in _sect:
                    for h in range(4):
                        kb.mm(pb[6][:, h * 64:(h + 1) * 64], lhsT=MT_bf[:, h, :], rhs=xdt_bf[:, h * 64:(h + 1) * 64],
                              start=True, stop=True, r=["MT_bf", "xdt_bf"], w=["pb6"])
                    kb.mm(pb[6][:, 256:512], lhsT=CT_bf[:, cs], rhs=state_bf[:], start=True, stop=True,
                          r=["CT_bf", "state_bf"], w=["pb6"])
                    kb.tt('dve', yi_sb[:].rearrange("p (h c) -> p h c", h=4),
                          pb[6][:, 256:512].rearrange("p (h c) -> p h c", h=4),
                          ecum[:].unsqueeze(2).to_broadcast([128, 4, 64]), ALU.mult, r=["pb6", "ecum"], w=["yi_sb"])
                    kb.tt('dve', y[:], yi_sb[:], pb[6][:, 0:256], ALU.add, r=["pb6", "yi_sb"], w=["y"])
                if '5' in _sect:
                    kb.tt('pool', xte_bf[:].rearrange("p (h c) -> p h c", h=4), xdt_bf[:].rearrange("p (h c) -> p h c", h=4),
                          tte[:].unsqueeze(2).to_broadcast([128, 4, 64]), ALU.mult, r=["xdt_bf", "tte"], w=["xte_bf"])
                    kb.mm(pb[7][:, 0:256], lhsT=B_tm[:], rhs=xte_bf[:], start=True, stop=True, r=["B_tm", "xte_bf"],
                          w=["pb7"])
                    kb.tt('pool', state[:].rearrange("p (h c) -> p h c", h=4), state[:].rearrange("p (h c) -> p h c", h=4),
                          ecl[:].unsqueeze(2).to_broadcast([128, 4, 64]), ALU.mult, r=["state", "ecl"], w=["state"])
                    kb.tt('dve', state[:], state[:], pb[7][:, 0:256], ALU.add, r=["state", "pb7"], w=["state"])
                    kb.cp('pool', state_bf[:], state[:], r=["state"], w=["state_bf"])
                if '6' in _sect:
                    kb.tt('pool', y2[:], xs_sb[:], dsk[:], ALU.mult, r=["xs_sb", "dsk"], w=["y2"])
                    kb.tt('pool', y[:], y[:], y2[:], ALU.add, r=["y", "y2"], w=["y"])
                    kb.tt('pool', y[:], y[:], zs[:], ALU.mult, r=["y", "zs"], w=["y"])
                    kb.act(mjunk[:], y[:], AF.Square, r=["y"], w=["mjunk", "mss"], accum=mss[:])
                    kb.act(mrstd[:], mss[:], AF.Sqrt, r=["mss"], w=["mrstd"], bias=1e-5, scale=1.0 / 256)
                    kb.recip(mrstd[:], mrstd[:], r=["mrstd"], w=["mrstd"])
                    kb.stt('dve', yo_bf[:], y[:], mrstd[:, 0:1], mng[:], ALU.mult, ALU.mult, r=["y", "mrstd", "mng"],
                           w=["yo_bf"])
                    for t in range(2):
                        kb.mm(pb[7][:, 256 + t * 128:256 + (t + 1) * 128], lhsT=yo_bf[:, t * 128:(t + 1) * 128],
                              rhs=ident_b, start=True, stop=True, r=["yo_bf", "Cb"], w=["pb7"])
                    kb.cp('act', ys[:, :, cs], pb[7][:, 256:512].rearrange("p (t s) -> p t s", t=2), r=["pb7"], w=[ysn])
            kb.dma(yT_ap(0, 256, m).rearrange("(t p) s -> p t s", p=128), ys[:], ysn, r=[ysn],
                   w=["yT_m"])
            after_y(0, m, "yT_m")
        sc.close()

    if 'ret' in parts:
        sc = Scope(kb)
        Wfm = load_cast_weights(kb, sc, w_rfm, 512, "Wfm")
        Wtm = load_cast_weights(kb, sc, w_rtm, 512, "Wtm")
        CB = sc.sb("CB", [128, 1024])
        kb.dma(CB[:], cstb, "CB", w=["CB"])
        invf = CB[:, 0:1]
        sgn = CB[:, 1:2]
        cdec = CB[:, 2:3]
        QD = CB[:, 128:256]
        KDEC = CB[:, 256:384]
        DMT = CB[:, 384:640]
        hT = [sc.sb(f"hT{i}", [128, 16, 512], BF16) for i in range(2)]
        pi_ = sc.sb("pi", [128, 512], I32)
        ang = sc.sb("ang", [128, 512])
        kq = sc.sb("kq", [128, 512])
        rr = sc.sb("rr", [128, 512])
        sinT = sc.sb("sinT", [128, 512])
        cosT = sc.sb("cosT", [128, 512])
        t1 = sc.sb("t1", [128, 512])
        t2 = sc.sb("t2", [128, 512])
        qr_bf = sc.sb("qr_bf", [128, 512], BF16)
        kr_bf = sc.sb("kr_bf", [128, 512], BF16)
        krz = [sc.sb(f"krz{h}", [128, 512], BF16) for h in range(2)]
        qdz = [sc.sb(f"qdz{h}", [128, 512], BF16) for h in range(2)]
        v_bf = sc.sb("v_bf", [128, 256], BF16)
        gs = sc.sb("gs", [128, 256])
        ST_bf = sc.sb("ST_bf", [128, 256], BF16)
        kd_bf = sc.sb("kd_bf", [128, 128], BF16)
        rstate = sc.sb("rstate", [128, 128])
        rstate_bf = sc.sb("rstate_bf", [128, 128], BF16)
        rjunk = sc.sb("rjunk", [128, 128])
        rss = sc.sb("rss", [128, 2])
        rrs = sc.sb("rrs", [128, 2])
        yo_bf = sc.sb("ryo_bf", [128, 256], BF16)
        yst = [sc.sb(f"yst{i}", [128, 2, 512], BF16) for i in range(2)]
        kb.memset('pool', rstate[:], 0.0, w=["rstate"])
        kb.memset('pool', rstate_bf[:], 0.0, w=["rstate_bf"])
        for h in range(2):
            kb.memset('pool', krz[h][:], 0.0, w=[f"krz{h}"])
            kb.memset('pool', qdz[h][:], 0.0, w=[f"qdz{h}"])
        for m in range(NM):
            h_ = hT[m % 2]
            hn = f"hT{m % 2}"
            load_hT(h_, m, hn)
            ys = yst[m % 2]
            ysn = f"yst{m % 2}"
            kb.dma(pi_[:], pos[0:1, m * 512:(m + 1) * 512].partition_broadcast(128), "pi", w=["pi"])
            kb.cp('dve', ang[:], pi_[:], r=["pi"], w=["ang"])
            kb.ts('dve', ang[:], ang[:], invf, ALU.mult, r=["ang", "CB"], w=["ang"])
            for which, dst, shift in (("s", sinT, 0.0), ("c", cosT, float(np.pi / 2))):
                if shift != 0.0:
                    kb.ts('pool', rr[:], ang[:], shift, ALU.add, r=["ang"], w=["rr"])
                    src, srn = rr, "rr"
                else:
                    src, srn = ang, "ang"
                kb.ts('dve', kq[:], src[:], float(1.0 / TWO_PI), ALU.mult, r=[srn], w=["kq"], s2=MAGIC, op1=ALU.add)
                kb.ts('pool', kq[:], kq[:], -MAGIC, ALU.add, r=["kq"], w=["kq"])
                kb.stt('dve', rr[:], kq[:], -TWO_PI, src[:], ALU.mult, ALU.add, r=["kq", srn], w=["rr"])
                kb.ts('pool', rr[:], rr[:], 3.14159, ALU.min, r=["rr"], w=["rr"], s2=-3.14159, op1=ALU.max)
                if which == "s":
                    kb.act(dst[:], rr[:], AF.Sin, r=["rr", "CB"], w=["sinT"], scale=sgn)
                else:
                    kb.act(dst[:], rr[:], AF.Sin, r=["rr"], w=["cosT"])
            for j in range(4):
                pr = f"pb{j % 2}"
                pa = pb[j % 2]
                for k in range(16):
                    kb.mm(pa[:, :], lhsT=Wfm[:, k, j * 128:(j + 1) * 128], rhs=h_[:, k, :], start=(k == 0),
                          stop=(k == 15), r=["Wfm", hn], w=[pr])
                if j % 2 == 0:
                    kb.tt('dve', t1[:], pa[:, :], cosT[:], ALU.mult, r=[pr, "cosT"], w=["t1"])
                else:
                    kb.tt('dve', t2[:], pa[:, :], sinT[:], ALU.mult, r=[pr, "sinT"], w=["t2"])
                    if j == 1:
                        kb.tt('pool', qr_bf[:], t1[:], t2[:], ALU.add, r=["t1", "t2"], w=["qr_bf"])
                    else:
                        kb.tt('pool', t1[:], t1[:], t2[:], ALU.add, r=["t1", "t2"], w=["t1"])
                        kb.ts('pool', kr_bf[:], t1[:], 0.125, ALU.mult, r=["t1"], w=["kr_bf"])
            for h in range(2):
                ph = slice(h * 64, (h + 1) * 64)
                kb.cp('pool', krz[h][ph, :], kr_bf[ph, :], r=["kr_bf"], w=[f"krz{h}"])
                kb.tt('dve' if h else 'pool', qdz[h][ph, :].rearrange("p (c t) -> p c t", c=4),
                      qr_bf[ph, :].rearrange("p (c t) -> p c t", c=4), QD[ph, :].unsqueeze(1).to_broadcast([64, 4, 128]),
                      ALU.mult, r=["qr_bf", "CB"], w=[f"qdz{h}"])
            for c in range(4):
                cs = slice(c * 128, (c + 1) * 128)
                for k in range(16):
                    kb.mm(pb[2][:, :], lhsT=h_[:, k, cs], rhs=Wtm[:, k, :], start=(k == 0), stop=(k == 15),
                          r=[hn, "Wtm"], w=["pb2"])
                kb.cp('act', v_bf[:], pb[2][:, 0:256], r=["pb2"], w=["v_bf"])
                kb.act(gs[:], pb[2][:, 256:512], AF.Silu, r=["pb2"], w=["gs"])
                for h in range(2):
                    kb.mm(pb[3][:, h * 128:(h + 1) * 128], lhsT=krz[h][:, cs], rhs=qr_bf[:, cs], start=True, stop=True,
                          r=[f"krz{h}", "qr_bf"], w=["pb3"])
                kb.tt('dve', ST_bf[:], pb[3][:, 0:256], DMT, ALU.mult, r=["pb3", "CB"], w=["ST_bf"])
                kb.mm(pb[4][:, 0:128], lhsT=kr_bf[:, cs], rhs=ident_b, start=True, stop=True, r=["kr_bf", "Cb"],
                      w=["pb4"])
                kb.tt('dve', kd_bf[:], pb[4][:, 0:128], KDEC, ALU.mult, r=["pb4", "CB"], w=["kd_bf"])
                for h in range(2):
                    hs_ = slice(h * 128, (h + 1) * 128)
                    kb.mm(pb[5][:, hs_], lhsT=ST_bf[:, hs_], rhs=v_bf[:, hs_], start=True, stop=False,
                          r=["ST_bf", "v_bf"], w=["pb5"])
                    kb.mm(pb[5][:, hs_], lhsT=qdz[h][:, cs], rhs=rstate_bf[:], start=False, stop=True,
                          r=[f"qdz{h}", "rstate_bf"], w=["pb5"])
                kb.mm(pb[6][:, 0:256], lhsT=kd_bf[:], rhs=v_bf[:], start=True, stop=True, r=["kd_bf", "v_bf"],
                      w=["pb6"])
                for h in range(2):
                    ph = slice(h * 64, (h + 1) * 64)
                    kb.stt('dve', rstate[ph, :], rstate[ph, :], cdec[ph, :], pb[6][ph, h * 128:(h + 1) * 128],
                           ALU.mult, ALU.add, r=["rstate", "CB", "pb6"], w=["rstate"])
                kb.cp('pool', rstate_bf[:], rstate[:], r=["rstate"], w=["rstate_bf"])
                for h in range(2):
                    kb.act(rjunk[:], pb[5][:, h * 128:(h + 1) * 128], AF.Square, r=["pb5"], w=["rjunk", "rss"],
                           accum=rss[:, h:h + 1])
                kb.act(rrs[:], rss[:], AF.Sqrt, r=["rss"], w=["rrs"], bias=1e-6, scale=1.0 / 128)
                kb.recip(rrs[:], rrs[:], r=["rrs"], w=["rrs"])
                for h in range(2):
                    hs_ = slice(h * 128, (h + 1) * 128)
                    kb.stt('dve', yo_bf[:, hs_], pb[5][:, hs_], rrs[:, h:h + 1], gs[:, hs_], ALU.mult, ALU.mult,
                           r=["pb5", "rrs", "gs"], w=["ryo_bf"])
                for t in range(2):
                    kb.mm(pb[7][:, 256 + t * 128:256 + (t + 1) * 128], lhsT=yo_bf[:, t * 128:(t + 1) * 128],
                          rhs=ident_b, start=True, stop=True, r=["ryo_bf", "Cb"], w=["pb7"])
                kb.cp('act', ys[:, :, cs], pb[7][:, 256:512].rearrange("p (t s) -> p t s", t=2), r=["pb7"], w=[ysn])
            kb.dma(yT_ap(512, 256, m).rearrange("(t p) s -> p t s", p=128), ys[:], ysn, r=[ysn],
                   w=["yT_r"])
            after_y(2, m, "yT_r")
        sc.close()

    if 'rwkv' in parts:
        sc = Scope(kb)
        Wr = load_cast_weights(kb, sc, w_wfm, 1216, "Wr")
        offs = [0, 128, 256, 384, 512, 640, 768, 864, 960, 1088]
        sizes = [128, 128, 128, 128, 128, 128, 96, 96, 128, 128]
        mu = sc.sb("mu", [128, 10])
        kb.dma(mu[:], rw_mu, "mu", w=["mu"])
        vec = sc.sb("vec", [128, 2, 7])
        kb.dma(vec[:], rw_vec, "vec", w=["vec"])
        omka = sc.sb("omka", [128, 2])
        kb.ts('dve', omka[:].rearrange("p (c o) -> p c o", o=1), vec[:, :, 3:4], -1.0, ALU.mult, r=["vec"], w=["omka"],
              s2=1.0, op1=ALU.add)
        lstg = sc.sb("lstg", [128, 2, 256])
        w2_bf = sc.sb("w2_bf", [96, 256], BF16)
        a2_bf = sc.sb("a2_bf", [96, 256], BF16)
        g2_bf = sc.sb("g2_bf", [128, 2, 256], BF16)
        kb.dma(lstg[0:96, 0, :], rw_w2, "lstg", w=["lstg"])
        kb.cp('dve', w2_bf[:], lstg[0:96, 0, :], r=["lstg"], w=["w2_bf"])
        kb.dma(lstg[0:96, 1, :], rw_a2, "lstg", r=["lstg"], w=["lstg"])
        kb.cp('dve', a2_bf[:], lstg[0:96, 1, :], r=["lstg"], w=["a2_bf"])
        kb.dma(lstg[:], rw_g2.rearrange("(k p) n -> p k n", p=128), "lstg", r=["lstg"], w=["lstg"])
        kb.cp('dve', g2_bf[:], lstg[:], r=["lstg"], w=["g2_bf"])
        M3 = sc.sb("M3", [128, 384])
        kb.cp('pool', M3[:, 0:128], SU_f, r=["C"], w=["M3"])
        kb.cp('pool', M3[:, 128:256], SL_f, r=["C"], w=["M3"])
        kb.cp('pool', M3[:, 256:384], SU_f, r=["C"], w=["M3"])
        M2 = sc.sb("M2", [128, 256])
        kb.cp('pool', M2[:, 0:128], UI_f, r=["C"], w=["M2"])
        kb.cp('pool', M2[:, 128:256], UI_f, r=["C"], w=["M2"])
        hT = sc.sb("hT0", [128, 16, 512], BF16)
        halo = sc.sb("halo", [128, 10])
        kb.memset('pool', halo[:], 0.0, w=["halo"])
        rawt = [sc.sb(f"rawt{i}", [128, 513]) for i in range(2)]
        dlt = [sc.sb(f"dlt{i}", [128, 512]) for i in range(2)]
        sh = [sc.sb(f"sh{q}", [128, 512]) for q in range(10)]
        tw = sc.sb("tw", [128, 512], BF16)
        ad_bf = sc.sb("ad_bf", [128, 512], BF16)
        sg = [sc.sb(f"sg{i}", [128, 512], BF16) for i in range(2)]
        names = ["a", "kkraw", "sq", "rn", "kk", "kmul", "kh", "bb", "lw", "cwm", "Winv", "Wexc", "tmpw"]
        T = {n: sc.sb("rw_" + n, [128, 512]) for n in names}
        for alias, base in (("prod", "tmpw"), ("yc", "kkraw"), ("sq2", "sq"), ("rs2", "rn"), ("yn", "kmul")):
            T[alias] = T[base]
        bdn = ["Abd", "Bbd", "Kbd", "Rbd", "Vbd"]
        BDc = [{n: sc.sb(f"{n}{ct}", [128, 8, 128], BF16) for n in bdn} for ct in range(2)]
        for ct in range(2):
            for n in bdn:
                kb.memset('pool', BDc[ct][n][:], 0.0, w=[f"{n}{ct}"])
        Tc = [{n: sc.sb(f"rw_{n}{ct}", [128, 512]) for n in ("Winc", "bv", "g_sb", "yraw")} for ct in range(2)]
        II = sc.sb("II", [128, 128], BF16)
        kb.cp('dve', II[:], ident_f, r=["C"], w=["II"])
        tm_bf_c = [sc.sb(f"tm_bf{ct}", [128, 384], BF16) for ct in range(2)]
        gr_bf_c = [sc.sb(f"gr_bf{ct}", [128, 384], BF16) for ct in range(2)]
        pr_bf_c = [sc.sb(f"pr_bf{ct}", [128, 256], BF16) for ct in range(2)]
        pw_bf_c = [[sc.sb(f"pw_bf{ct}_{i}", [128, 256], BF16) for i in range(5)] for ct in range(2)]
        G_bf_c = [[sc.sb(f"G_bf{ct}_{i}", [128, 128], BF16) for i in range(2)] for ct in range(2)]
        Xn_bf_c = [sc.sb(f"Xn_bf{ct}", [128, 128], BF16) for ct in range(2)]
        U_bf_c = [sc.sb(f"U_bf{ct}", [128, 128], BF16) for ct in range(2)]
        ST_f = [sc.sb(f"ST_f{ct}", [128, 128]) for ct in range(2)]
        STt_c = [sc.sb(f"STt{ct}", [128, 128]) for ct in range(2)]
        STb = [sc.sb(f"STb{ct}", [128, 128], BF16) for ct in range(2)]
        yst = [sc.sb(f"wyst{i}", [128, 512], BF16) for i in range(2)]
        for ct in range(2):
            kb.memset('pool', ST_f[ct][:], 0.0, w=[f"ST_f{ct}"])
            kb.memset('pool', STb[ct][:], 0.0, w=[f"STb{ct}"])
        v3 = lambda ap: ap.rearrange("p (c t) -> p c t", c=8)
        for m in range(NM):
            load_hT(hT, m, "hT0")
            for q in range(10):
                sz = sizes[q]
                pr = f"pb{q % 2}"
                pa = pb[q % 2]
                for k in range(16):
                    kb.mm(pa[0:sz, :], lhsT=Wr[:, k, offs[q]:offs[q] + sz], rhs=hT[:, k, :], start=(k == 0),
                          stop=(k == 15), r=["Wr", "hT0"], w=[pr])
                rw_ = rawt[q % 2]
                rn_ = f"rawt{q % 2}"
                kb.cp('pool', rw_[0:sz, 0:1], halo[0:sz, q:q + 1], r=["halo"], w=[rn_])
                kb.cp('act', rw_[0:sz, 1:513], pa[0:sz, :], r=[pr], w=[rn_])
                kb.cp('pool', halo[0:sz, q:q + 1], rw_[0:sz, 512:513], r=[rn_], w=["halo"])
                d_ = dlt[q % 2]
                dn_ = f"dlt{q % 2}"
                kb.tt('pool', d_[0:sz, :], rw_[0:sz, 0:512], rw_[0:sz, 1:513], ALU.subtract, r=[rn_], w=[dn_])
                kb.stt('dve', sh[q][0:sz, :], d_[0:sz, :], mu[0:sz, q:q + 1], rw_[0:sz, 1:513], ALU.mult, ALU.add,
                       r=[dn_, "mu", rn_], w=[f"sh{q}"])
            kb.act(tw[0:96, :], sh[6][0:96, :], AF.Tanh, r=["sh6"], w=["tw"])
            kb.cp('pool', ad_bf[0:96, :], sh[7][0:96, :], r=["sh7"], w=["ad_bf"])
            for i in range(2):
                kb.act(sg[i][:], sh[8 + i][:], AF.Sigmoid, r=[f"sh{8 + i}"], w=[f"sg{i}"])
            for ct in range(2):
                cts = slice(ct * 128, (ct + 1) * 128)
                r_, k_, v_ = sh[ct], sh[2 + ct], sh[4 + ct]
                rn_r, rn_k, rn_v = f"sh{ct}", f"sh{2 + ct}", f"sh{4 + ct}"
                kb.mm(pb[2][:, :], lhsT=w2_bf[:, cts], rhs=tw[0:96, :], start=True, stop=True, r=["w2_bf", "tw"],
                      w=["pb2"])
                kb.act(T["lw"][:], pb[2][:, :], AF.Sigmoid, r=["pb2", "vec"], w=["rw_lw"], bias=vec[:, ct, 0:1])
                kb.ts('dve', T["lw"][:], T["lw"][:], -0.6065306597126334, ALU.mult, r=["rw_lw"], w=["rw_lw"])
                kb.mm(pb[3][:, :], lhsT=a2_bf[:, cts], rhs=ad_bf[0:96, :], start=True, stop=True, r=["a2_bf", "ad_bf"],
                      w=["pb3"])
                kb.act(T["a"][:], pb[3][:, :], AF.Sigmoid, r=["pb3", "vec"], w=["rw_a"], bias=vec[:, ct, 1:2])
                for i in range(2):
                    kb.mm(pb[2][:, :], lhsT=g2_bf[:, i, cts], rhs=sg[i][:], start=(i == 0), stop=(i == 1),
                          r=["g2_bf", f"sg{i}"], w=["pb2"])
                kb.cp('act', Tc[ct]["g_sb"][:], pb[2][:, :], r=["pb2"], w=[f"rw_g_sb{ct}"])
                kb.ts('pool', T["kkraw"][:], k_[:], vec[:, ct, 2:3], ALU.mult, r=[rn_k, "vec"], w=["rw_kkraw"])
                kb.tt('pool', T["sq"][:], T["kkraw"][:], T["kkraw"][:], ALU.mult, r=["rw_kkraw"], w=["rw_sq"])
                kb.mm(pb[3][:, :], lhsT=blk_f, rhs=T["sq"][:], start=True, stop=True, r=["C", "rw_sq"], w=["pb3"])
                kb.act(T["rn"][:], pb[3][:, :], AF.Sqrt, r=["pb3"], w=["rw_rn"])
                kb.ts('dve', T["rn"][:], T["rn"][:], 1e-12, ALU.max, r=["rw_rn"], w=["rw_rn"])
                kb.recip(T["rn"][:], T["rn"][:], r=["rw_rn"], w=["rw_rn"])
                kb.tt('pool', T["kk"][:], T["kkraw"][:], T["rn"][:], ALU.mult, r=["rw_kkraw", "rw_rn"], w=["rw_kk"])
                kb.ts('dve', T["kmul"][:], T["a"][:], vec[:, ct, 3:4], ALU.mult, r=["rw_a", "vec", "omka"],
                      w=["rw_kmul"], s2=omka[:, ct:ct + 1], op1=ALU.add)
                kb.tt('pool', T["kh"][:], k_[:], T["kmul"][:], ALU.mult, r=[rn_k, "rw_kmul"], w=["rw_kh"])
                kb.tt('pool', T["bb"][:], T["kk"][:], T["a"][:], ALU.mult, r=["rw_kk", "rw_a"], w=["rw_bb"])
                kb.scan(T["cwm"][:], rmask, T["lw"][:], r=["C", "rw_lw"], w=["rw_cwm"])
                kb.act(Tc[ct]["Winc"][:], T["cwm"][:], AF.Exp, r=["rw_cwm"], w=[f"rw_Winc{ct}"])
                kb.act(T["Winv"][:], T["cwm"][:], AF.Exp, r=["rw_cwm"], w=["rw_Winv"], scale=-1.0)
                kb.tt('pool', T["tmpw"][:], T["cwm"][:], T["lw"][:], ALU.subtract, r=["rw_cwm", "rw_lw"], w=["rw_tmpw"])
                kb.act(T["Wexc"][:], T["tmpw"][:], AF.Exp, r=["rw_tmpw"], w=["rw_Wexc"])
                pairs = (("Abd", T["kk"], "rw_kk", T["Wexc"], "rw_Wexc"), ("Bbd", T["bb"], "rw_bb", T["Winv"], "rw_Winv"),
                         ("Kbd", T["kh"], "rw_kh", T["Winv"], "rw_Winv"), ("Rbd", r_, rn_r, Tc[ct]["Winc"], f"rw_Winc{ct}"))
                ei = 0
                for (bn, a0, an0, a1, an1) in pairs:
                    for hh in range(2):
                        ph = slice(hh * 64, (hh + 1) * 64)
                        kb.tt('dve' if ei % 2 == 0 else 'pool', BDc[ct][bn][ph, :, hh * 64:(hh + 1) * 64], v3(a0[ph, :]),
                              v3(a1[ph, :]), ALU.mult, r=[an0, an1, bn + str(ct)], w=[bn + str(ct)])
                        ei += 1
                for hh in range(2):
                    ph = slice(hh * 64, (hh + 1) * 64)
                    kb.cp('pool', BDc[ct]["Vbd"][ph, :, hh * 64:(hh + 1) * 64], v3(v_[ph, :]), r=[rn_v, f"Vbd{ct}"], w=[f"Vbd{ct}"])
                kb.stt('dve', T["prod"][:], r_[:], vec[:, ct, 4:5], T["kh"][:], ALU.mult, ALU.mult,
                       r=[rn_r, "vec", "rw_kh"], w=["rw_tmpw"])
                kb.mm(pb[3][:, :], lhsT=blk_f, rhs=T["prod"][:], start=True, stop=True, r=["C", "rw_tmpw"], w=["pb3"])
                kb.tt('dve', Tc[ct]["bv"][:], pb[3][:, :], v_[:], ALU.mult, r=["pb3", rn_v], w=[f"rw_bv{ct}"])
            caps = []
            for ct in range(2):
                kb.tr.begin_capture()
                PBK = pb[4:8] if ct == 0 else pb[0:4]
                PBN = [f"pb{4 + k_}" for k_ in range(4)] if ct == 0 else [f"pb{k_}" for k_ in range(4)]
                STf, STn = ST_f[ct], f"ST_f{ct}"
                STbf, STbn = STb[ct], f"STb{ct}"
                for c in range(8):
                    A_c, B_c, K_c, R_c, V_c = (BDc[ct][n][:, c, :] for n in bdn)
                    for i, (X_c, xn) in enumerate(((B_c, f"Bbd{ct}"), (K_c, f"Kbd{ct}"), (V_c, f"Vbd{ct}"))):
                        kb.mm(PBK[0][:, i * 128:(i + 1) * 128], lhsT=X_c, rhs=ident_b, start=True, stop=True,
                              r=[xn, "Cb"], w=[PBN[0]])
                    kb.cp('act', tm_bf_c[ct][:], PBK[0][:, 0:384], r=[PBN[0]], w=[f"tm_bf{ct}"])
                    btm, ktm, vtm = tm_bf_c[ct][:, 0:128], tm_bf_c[ct][:, 128:256], tm_bf_c[ct][:, 256:384]
                    kb.mm(PBK[1][:, 0:128], lhsT=B_c, rhs=A_c, start=True, stop=True, r=[f"Bbd{ct}", f"Abd{ct}"], w=[PBN[1]])
                    kb.mm(PBK[1][:, 128:256], lhsT=A_c, rhs=B_c, start=True, stop=True, r=[f"Bbd{ct}", f"Abd{ct}"], w=[PBN[1]])
                    kb.mm(PBK[1][:, 256:384], lhsT=K_c, rhs=A_c, start=True, stop=True, r=[f"Kbd{ct}", f"Abd{ct}"], w=[PBN[1]])
                    kb.tt('dve', gr_bf_c[ct][:], PBK[1][:, 0:384], M3[:], ALU.mult, r=[PBN[1], "M3"], w=[f"gr_bf{ct}"])
                    kb.mm(PBK[2][:, 0:128], lhsT=B_c, rhs=R_c, start=True, stop=True, r=[f"Bbd{ct}", f"Rbd{ct}"], w=[PBN[2]])
                    kb.mm(PBK[2][:, 128:256], lhsT=K_c, rhs=R_c, start=True, stop=True, r=[f"Kbd{ct}", f"Rbd{ct}"], w=[PBN[2]])
                    kb.tt('dve', pr_bf_c[ct][:], PBK[2][:, 0:256], M2[:], ALU.mult, r=[PBN[2], "M2"], w=[f"pr_bf{ct}"])
                    Nn, Tt, TakT = gr_bf_c[ct][:, 0:128], gr_bf_c[ct][:, 128:256], gr_bf_c[ct][:, 256:384]
                    PrbT, PrkT = pr_bf_c[ct][:, 0:128], pr_bf_c[ct][:, 128:256]
                    kb.tt('pool', G_bf_c[ct][0][:], II[:], Nn, ALU.subtract, r=["II", f"gr_bf{ct}"], w=[f"G_bf{ct}_0"])
                    gcur = 0
                    Ncur, Tcur, ncn = Nn, Tt, f"gr_bf{ct}"
                    for lv in range(5):
                        kb.mm(PBK[2][:, 256:384], lhsT=Tcur, rhs=Ncur, start=True, stop=True, r=[ncn], w=[PBN[2]])
                        kb.mm(PBK[2][:, 384:512], lhsT=Ncur, rhs=Tcur, start=True, stop=True, r=[ncn], w=[PBN[2]])
                        kb.cp('act', pw_bf_c[ct][lv][:], PBK[2][:, 256:512], r=[PBN[2]], w=[f"pw_bf{ct}_{lv}"])
                        Ncur, Tcur, ncn = pw_bf_c[ct][lv][:, 0:128], pw_bf_c[ct][lv][:, 128:256], f"pw_bf{ct}_{lv}"
                        kb.mm(PBK[0][:, 384:512], lhsT=Tcur, rhs=G_bf_c[ct][gcur][:], start=True, stop=True,
                              r=[ncn, f"G_bf{ct}_{gcur}"], w=[PBN[0]])
                        kb.tt('dve', G_bf_c[ct][1 - gcur][:], PBK[0][:, 384:512], G_bf_c[ct][gcur][:], ALU.add,
                              r=[PBN[0], f"G_bf{ct}_{gcur}"], w=[f"G_bf{ct}_{1 - gcur}"])
                        gcur = 1 - gcur
                    Gf, Gn = G_bf_c[ct][gcur], f"G_bf{ct}_{gcur}"
                    kb.mm(PBK[3][:, 0:128], lhsT=A_c, rhs=STbf[:], start=True, stop=False, r=[f"Abd{ct}", STbn], w=[PBN[3]])
                    kb.mm(PBK[3][:, 0:128], lhsT=TakT, rhs=vtm, start=False, stop=True, r=[f"gr_bf{ct}", f"tm_bf{ct}"], w=[PBN[3]])
                    kb.act(Xn_bf_c[ct][:], PBK[3][:, 0:128], AF.Copy, r=[PBN[3]], w=[f"Xn_bf{ct}"], scale=-1.0)
                    kb.mm(PBK[3][:, 128:256], lhsT=Gf[:], rhs=Xn_bf_c[ct][:], start=True, stop=True, r=[Gn, f"Xn_bf{ct}"], w=[PBN[3]])
                    kb.cp('act', U_bf_c[ct][:], PBK[3][:, 128:256], r=[PBN[3]], w=[f"U_bf{ct}"])
                    kb.mm(PBK[3][:, 256:384], lhsT=STbf[:], rhs=R_c, start=True, stop=False, r=[STbn, f"Rbd{ct}"], w=[PBN[3]])
                    kb.mm(PBK[3][:, 256:384], lhsT=U_bf_c[ct][:], rhs=PrbT, start=False, stop=False, r=[f"U_bf{ct}", f"pr_bf{ct}"],
                          w=[PBN[3]])
                    kb.mm(PBK[3][:, 256:384], lhsT=vtm, rhs=PrkT, start=False, stop=True, r=[f"tm_bf{ct}", f"pr_bf{ct}"], w=[PBN[3]])
                    for hh in range(2):
                        ph = slice(hh * 64, (hh + 1) * 64)
                        kb.cp('act', Tc[ct]["yraw"][ph, c * 64:(c + 1) * 64], PBK[3][ph, 256 + hh * 64:256 + (hh + 1) * 64],
                              r=[PBN[3]], w=[f"rw_yraw{ct}"])
                    kb.mm(PBK[3][:, 384:512], lhsT=btm, rhs=U_bf_c[ct][:], start=True, stop=False, r=[f"tm_bf{ct}", f"U_bf{ct}"], w=[PBN[3]])
                    kb.mm(PBK[3][:, 384:512], lhsT=ktm, rhs=vtm, start=False, stop=True, r=[f"tm_bf{ct}"], w=[PBN[3]])
                    WL = Tc[ct]["Winc"][:, c * 64 + 63:c * 64 + 64]
                    kb.ts('pool', STt_c[ct][:], STf[:], WL, ALU.mult, r=[STn, f"rw_Winc{ct}"], w=[f"STt{ct}"])
                    kb.stt('dve', STf[:], PBK[3][:, 384:512], WL, STt_c[ct][:], ALU.mult, ALU.add, r=[PBN[3], f"rw_Winc{ct}", f"STt{ct}"],
                           w=[STn])
                    kb.cp('pool', STbf[:], STf[:], r=[STn], w=[STbn])
                caps.append(kb.tr.end_capture())
            kb.tr.replay_interleaved(caps)
            for ct in range(2):
                kb.mm(pb[2][:, :], lhsT=blk64_f, rhs=Tc[ct]["yraw"][:], start=True, stop=True, r=["C", f"rw_yraw{ct}"], w=["pb2"])
                kb.tt('dve', T["yc"][:], Tc[ct]["yraw"][:], pb[2][:, :], ALU.subtract, r=[f"rw_yraw{ct}", "pb2"], w=["rw_kkraw"])
                kb.tt('pool', T["sq2"][:], T["yc"][:], T["yc"][:], ALU.mult, r=["rw_kkraw"], w=["rw_sq"])
                kb.mm(pb[3][:, :], lhsT=blk64_f, rhs=T["sq2"][:], start=True, stop=True, r=["C", "rw_sq"], w=["pb3"])
                kb.act(T["rs2"][:], pb[3][:, :], AF.Sqrt, r=["pb3"], w=["rw_rn"], bias=64e-5)
                kb.recip(T["rs2"][:], T["rs2"][:], r=["rw_rn"], w=["rw_rn"])
                kb.tt('pool', T["yn"][:], T["yc"][:], T["rs2"][:], ALU.mult, r=["rw_kkraw", "rw_rn"], w=["rw_kmul"])
                kb.ts('dve', T["yn"][:], T["yn"][:], vec[:, ct, 5:6], ALU.mult, r=["rw_kmul", "vec"], w=["rw_kmul"],
                      s2=vec[:, ct, 6:7], op1=ALU.add)
                kb.tt('pool', T["yn"][:], T["yn"][:], Tc[ct]["bv"][:], ALU.add, r=["rw_kmul", f"rw_bv{ct}"], w=["rw_kmul"])
                ys, ysn = yst[ct], f"wyst{ct}"
                kb.tt('dve', ys[:], T["yn"][:], Tc[ct]["g_sb"][:], ALU.mult, r=["rw_kmul", f"rw_g_sb{ct}"], w=[ysn])
                kb.dma(yT_ap(256 + ct * 128, 128, m), ys[:], ysn, r=[ysn], w=["yT_w"])
                if ct == 1:
                    after_y(1, m, "yT_w")
        sc.close()

    outs = ["yT_m", "yT_r", "yT_w"]
    if env is not None:
        return None
    return kb, outs


def finish_p1(kb, outs):
    return kb.finish(outs)


import numpy as np


FF = 5504
NJ = 43


P2_PARAM_SHAPES = {
    "w_ada": ([96, 128, 16, 128], F32), "b_ada": ([1, 6 * D], F32), "ng_col": ([128, 64], F32), "norm_g": ([4, D], F32),
    "w_gate": ([3, D, D], F32), "w_branch": ([3, 1024, D], F32), "w_out": ([D, D], F32), "w_up": ([D, 2 * FF], F32),
    "f_cv": ([128, 86, 4], F32), "w_down": ([FF, D], F32),
}


def p2_scratch(kb, sfx=""):
    return {
        'Wg_d': kb.dscratch("Wg_d" + sfx, [3, 16, 128, 16, 128], BF16),
        'Wb_d': kb.dscratch("Wb_d" + sfx, [3, 16, 128, 8, 128], BF16),
        'Wo_d': kb.dscratch("Wo_d" + sfx, [16, 128, D], BF16),
        'Wu_d': kb.dscratch("Wu_d" + sfx, [NJ, 128, 16, 256], BF16),
        'Wd_d': kb.dscratch("Wd_d" + sfx, [NJ, 128, D], BF16),
    }


def build_p2(Tq, env=None):
    NT = 128 + Tq
    if env is None:
        kb = KB()
        e = {}
        xh = kb.din("xh", [NT, D])
        yTh = kb.din("yTh", [3, 1024, NT], BF16)
        e['hmask'] = kb.din("hmask", [128, 1])
        e['c_col'] = kb.din("c_col", [128, 16])
        for k, (shp, dt_) in P2_PARAM_SHAPES.items():
            e[k] = kb.din(k, shp, dt_)
        cst = kb.din("cst", [128, 2048])
        xo = kb.dout("xo", [Tq, D])
        e.update(p2_scratch(kb))
        G = setup_globals(kb, cst)

        def load_x(dst, rn, row0, is_halo):
            kb.dma(dst, xh[row0:row0 + 128, :], rn, w=[rn])

        def load_y(ybf, t0, ntok, is_halo):
            kb.dma(ybf[:, :, :, 0:ntok], yTh[:, :, t0:t0 + ntok].rearrange("i (k p) t -> p i k t", p=128), "ybf",
                   w=["ybf"])

        def store_x(src, rn, o0):
            kb.dma(xo[o0:o0 + 128, :], src, rn, r=[rn], w=["xo"])
        e['load_x'], e['load_y'], e['store_x'] = load_x, load_y, store_x
    else:
        kb = env['kb']
        e = env
        G = env['G']
    nc = kb.nc
    hmask, c_col, w_ada, b_ada, ng_col, norm_g = (e[k] for k in ('hmask', 'c_col', 'w_ada', 'b_ada', 'ng_col', 'norm_g'))
    w_gate, w_branch, w_out, w_up, f_cv, w_down = (e[k] for k in ('w_gate', 'w_branch', 'w_out', 'w_up', 'f_cv', 'w_down'))
    Wg_d, Wb_d, Wo_d, Wu_d, Wd_d = (e[k] for k in ('Wg_d', 'Wb_d', 'Wo_d', 'Wu_d', 'Wd_d'))
    load_x, load_y, store_x = e['load_x'], e['load_y'], e['store_x']
    pb = G['pb']
    C, Cb = G['C'], G['Cb']
    ident_f = C[:, 0:128]
    ones_row = C[0:1, 256:384]
    ident_b = Cb[:, 0:128]

    sc = Scope(kb)
    NSTG = 6
    stg = [sc.sb(f"stg{i}", [128, 2048]) for i in range(NSTG)]
    stb = [sc.sb(f"stb{i}", [128, 2048], BF16) for i in range(NSTG)]
    cnt = [0]

    def cast_block(src_ap, dst_ap, shape):
        i = cnt[0] % NSTG
        cnt[0] += 1
        n = int(np.prod(shape))
        sv = stg[i][:, 0:n]
        bv = stb[i][:, 0:n]
        if len(shape) == 2:
            sv = sv.rearrange("p (a b) -> p a b", a=shape[0])
            bv = bv.rearrange("p (a b) -> p a b", a=shape[0])
        kb.dma(sv, src_ap, f"stg{i}", w=[f"stg{i}"])
        eng = ('dve', 'act', 'dve', 'act', 'pool', 'dve')[i]
        kb.cp(eng, stb[i][:, 0:n], stg[i][:, 0:n], r=[f"stg{i}"], w=[f"stb{i}"])
        kb.dma(dst_ap, bv, f"stb{i}", r=[f"stb{i}"], w=[f"Wscr{cnt[0]}"])

    for i in range(3):
        wv = w_gate[i].rearrange("(k p) n -> p k n", p=128)
        for nt in range(16):
            cast_block(wv[:, :, nt * 128:(nt + 1) * 128], Wg_d[i, nt], (16, 128))
        wv = w_branch[i].rearrange("(k p) n -> p k n", p=128)
        for nt in range(16):
            cast_block(wv[:, :, nt * 128:(nt + 1) * 128], Wb_d[i, nt], (8, 128))
    for kn in range(16):
        cast_block(w_out[kn * 128:(kn + 1) * 128, :], Wo_d[kn], (2048,))
    wv = w_up.rearrange("(k p) n -> p k n", p=128)
    for j in range(NJ):
        cast_block(wv[:, :, j * 128:(j + 1) * 128], Wu_d[j, :, :, 0:128], (16, 128))
        cast_block(wv[:, :, FF + j * 128:FF + (j + 1) * 128], Wu_d[j, :, :, 128:256], (16, 128))
        cast_block(w_down[j * 128:(j + 1) * 128, :], Wd_d[j], (2048,))
    sc.close()

    scm = Scope(kb)
    GTm = scm.sb("GTm", [128, 2048])
    GTf = scm.sb("GTf", [128, 2048])
    cols = scm.sb("cols", [128, 64])
    ngc = scm.sb("ngc", [128, 64])
    kb.dma(ngc[:], ng_col, "ngc", w=["ngc"])
    sc = Scope(kb)
    rows = emit_mod(kb, sc, c_col, w_ada, b_ada, [0, 1, 2, 3, 4, 5], ones_row, pb, "m2")
    kb.dma(GTm[:], norm_g[1:2, :].partition_broadcast(128), "GTm", w=["GTm"])
    kb.dma(GTf[:], norm_g[3:4, :].partition_broadcast(128), "GTf", w=["GTf"])

    def mk_gt(dst, dn):
        def f(c, pa, pr):
            kb.tt('dve', dst[:, c * 512:(c + 1) * 512], pa, dst[:, c * 512:(c + 1) * 512], ALU.mult, r=[pr, dn], w=[dn])
        return f

    bcast_row(kb, rows[2][0], rows[2][1], ones_row, pb, mk_gt(GTm, "GTm"))
    bcast_row(kb, rows[5][0], rows[5][1], ones_row, pb, mk_gt(GTf, "GTf"))
    for slot, vid in enumerate((1, 0, 4, 3)):
        row, rn = rows[vid]
        for k in range(16):
            kb.mm(pb[3][:, slot * 16 + k:slot * 16 + k + 1], lhsT=row[0:1, k * 128:(k + 1) * 128],
                  rhs=ones_row[0:1, 0:1], start=True, stop=True, r=[rn, "C"], w=["pb3"])
    kb.cp('act', cols[:], pb[3][:, 0:64], r=["pb3"], w=["cols"])
    for slot, gsl in ((0, 0), (2, 2)):
        kb.stt('dve', cols[:, slot * 16:(slot + 1) * 16], cols[:, slot * 16:(slot + 1) * 16], 1.0,
               ngc[:, gsl * 16:(gsl + 1) * 16], ALU.add, ALU.mult, r=["cols", "ngc"], w=["cols"])
    sc.close()

    xres = [scm.sb(f"xres{i}", [128, 2048]) for i in range(2)]
    htmp = scm.sb("htmp", [128, 2048])
    hb = scm.sb("hb", [128, 2048], BF16)
    ss = scm.sb("ss", [128, 1])
    rstd = scm.sb("rstd", [128, 1])
    hT = scm.sb("hT", [128, 16, 256], BF16)
    ybf = scm.sb("ybf", [128, 3, 8, 256], BF16)
    mT = scm.sb("mT", [128, 16, 256], BF16)
    sig = [scm.sb(f"sig{i}", [128, 256]) for i in range(2)]
    macc = scm.sb("macc", [128, 256])
    mtmp = scm.sb("mtmp", [128, 256])
    Wg_s = [scm.sb(f"Wg_s{i}", [128, 16, 128], BF16) for i in range(3)]
    Wb_s = [scm.sb(f"Wb_s{i}", [128, 8, 128], BF16) for i in range(3)]
    Wo_s = [scm.sb(f"Wo_s{i}", [128, 2048], BF16) for i in range(2)]
    Wu_s = [scm.sb(f"Wu_s{i}", [128, 16, 256], BF16) for i in range(2)]
    Wd_s = [scm.sb(f"Wd_s{i}", [128, 2048], BF16) for i in range(2)]
    ymix = scm.sb("ymix", [128, 2048])
    rawf = [scm.sb(f"rawf{i}", [128, 258]) for i in range(2)]
    facc = [scm.sb(f"facc{i}", [128, 256]) for i in range(2)]
    gg = scm.sb("gg", [128, 256])
    aT = scm.sb("aT", [128, NJ, 256], BF16)
    fhalo = scm.sb("fhalo", [128, 86, 2])
    fc = scm.sb("fc", [128, 86, 4])
    hm = scm.sb("hm", [128, 1])
    kb.dma(fc[:], f_cv, "fc", w=["fc"])
    kb.dma(hm[:], hmask, "hm", w=["hm"])
    kb.memset('pool', fhalo[:], 0.0, w=["fhalo"])
    if 'extra_alloc' in e:
        e['extra_alloc'](scm, dict(ymix=ymix))
    slab_ctr = {"g": 0, "b": 0, "o": 0, "u": 0, "d": 0}

    def emit_h(nsub, ntok, gslot):
        for sub in range(nsub):
            xn = f"xres{sub}"
            kb.act(htmp[:], xres[sub][:], AF.Square, r=[xn], w=["htmp", "ss"], accum=ss[:])
            kb.act(rstd[:], ss[:], AF.Sqrt, r=["ss"], w=["rstd"], bias=1e-6, scale=1.0 / D)
            kb.recip(rstd[:], rstd[:], r=["rstd"], w=["rstd"])
            kb.ts('dve', hb[:], xres[sub][:], rstd[:, 0:1], ALU.mult, r=[xn, "rstd"], w=["hb"])
            for q in range(4):
                bank = 6 + (q % 2)
                for kk in range(4):
                    k = q * 4 + kk
                    kb.mm(pb[bank][:, kk * 128:(kk + 1) * 128], lhsT=hb[:, k * 128:(k + 1) * 128], rhs=ident_b,
                          start=True, stop=True, r=["hb", "Cb"], w=[f"pb{bank}"])
                for kk in range(4):
                    k = q * 4 + kk
                    gcol = cols[:, gslot * 16 + k:gslot * 16 + k + 1]
                    scol = cols[:, (gslot + 1) * 16 + k:(gslot + 1) * 16 + k + 1]
                    if kk % 2 == 0:
                        kb.act(hT[:, k, sub * 128:(sub + 1) * 128], pb[bank][:, kk * 128:(kk + 1) * 128], AF.Identity,
                               r=[f"pb{bank}", "cols"], w=["hT"], bias=scol, scale=gcol)
                    else:
                        kb.ts('dve', hT[:, k, sub * 128:(sub + 1) * 128], pb[bank][:, kk * 128:(kk + 1) * 128], gcol,
                              ALU.mult, r=[f"pb{bank}", "cols"], w=["hT"], s2=scol, op1=ALU.add)

    def post_norm(nsub_i, src_ps_banks, GT, gtn, sub, store_ap):
        xn = f"xres{sub}"
        kb.act(htmp[:], ymix[:], AF.Square, r=["ymix"], w=["htmp", "ss"], accum=ss[:])
        kb.act(rstd[:], ss[:], AF.Sqrt, r=["ss"], w=["rstd"], bias=1e-6, scale=1.0 / D)
        kb.recip(rstd[:], rstd[:], r=["rstd"], w=["rstd"])
        kb.stt('dve', htmp[:], ymix[:], rstd[:, 0:1], GT[:], ALU.mult, ALU.mult, r=["ymix", "rstd", gtn], w=["htmp"])
        kb.tt('pool', xres[sub][:], xres[sub][:], htmp[:], ALU.add, r=[xn, "htmp"], w=[xn])
        if store_ap is not None:
            store_x(xres[sub][:], xn, store_ap)

    def tile(t0, ntok, is_halo, nxt=None):
        nsub = ntok // 128
        tsl = slice(0, ntok)
        for sub in range(nsub):
            xn = f"xres{sub}"
            load_x(xres[sub][:], xn, t0 + sub * 128, is_halo)
        if is_halo:
            load_y(ybf, t0, ntok, is_halo)
        emit_h(nsub, ntok, 0)
        for nt in range(16):
            for i in range(3):
                gi = slab_ctr["g"] % 3
                slab_ctr["g"] += 1
                kb.dma(Wg_s[gi][:], Wg_d[i, nt], f"Wg_s{gi}", w=[f"Wg_s{gi}"])
                kb.dma(Wb_s[gi][:], Wb_d[i, nt], f"Wb_s{gi}", w=[f"Wb_s{gi}"])
                for k in range(16):
                    kb.mm(pb[4][:, tsl], lhsT=Wg_s[gi][:, k, :], rhs=hT[:, k, tsl], start=(k == 0), stop=(k == 15),
                          r=[f"Wg_s{gi}", "hT"], w=["pb4"])
                for k in range(8):
                    kb.mm(pb[5][:, tsl], lhsT=Wb_s[gi][:, k, :], rhs=ybf[:, i, k, tsl], start=(k == 0), stop=(k == 7),
                          r=[f"Wb_s{gi}", "ybf"], w=["pb5"])
                sg_ = sig[i % 2]
                sgn_ = f"sig{i % 2}"
                kb.act(sg_[:, tsl], pb[4][:, tsl], AF.Sigmoid, r=["pb4"], w=[sgn_])
                if i == 0:
                    kb.tt('dve', macc[:, tsl], pb[5][:, tsl], sg_[:, tsl], ALU.mult, r=["pb5", sgn_], w=["macc"])
                else:
                    kb.tt('dve', mtmp[:, tsl], pb[5][:, tsl], sg_[:, tsl], ALU.mult, r=["pb5", sgn_], w=["mtmp"])
                    if i == 1:
                        kb.tt('pool', macc[:, tsl], macc[:, tsl], mtmp[:, tsl], ALU.add, r=["macc", "mtmp"], w=["macc"])
                    else:
                        kb.tt('pool', mT[:, nt, tsl], macc[:, tsl], mtmp[:, tsl], ALU.add, r=["macc", "mtmp"],
                              w=["mT"])
        if nxt is not None:
            load_y(ybf, nxt[0], nxt[1], False)
        for sub in range(nsub):
            for kn in range(16):
                oi = slab_ctr["o"] % 2
                slab_ctr["o"] += 1
                kb.dma(Wo_s[oi][:], Wo_d[kn], f"Wo_s{oi}", w=[f"Wo_s{oi}"])
                for c in range(4):
                    kb.mm(pb[c][:, :], lhsT=mT[:, kn, sub * 128:(sub + 1) * 128], rhs=Wo_s[oi][:, c * 512:(c + 1) * 512],
                          start=(kn == 0), stop=(kn == 15), r=["mT", f"Wo_s{oi}"], w=[f"pb{c}"])
            for c in range(4):
                kb.cp('act', ymix[:, c * 512:(c + 1) * 512], pb[c][:, :], r=[f"pb{c}"], w=["ymix"])
            post_norm(nsub, None, GTm, "GTm", sub, None)
        emit_h(nsub, ntok, 2)
        for j in range(NJ):
            ui = slab_ctr["u"] % 2
            slab_ctr["u"] += 1
            kb.dma(Wu_s[ui][:], Wu_d[j], f"Wu_s{ui}", w=[f"Wu_s{ui}"])
            for half in range(2):
                bank = 4 + half
                jj = half * NJ + j
                for k in range(16):
                    kb.mm(pb[bank][:, tsl], lhsT=Wu_s[ui][:, k, half * 128:(half + 1) * 128], rhs=hT[:, k, tsl],
                          start=(k == 0), stop=(k == 15), r=[f"Wu_s{ui}", "hT"], w=[f"pb{bank}"])
                rw_ = rawf[half]
                rn_ = f"rawf{half}"
                kb.cp('pool', rw_[:, 0:2], fhalo[:, jj, :], r=["fhalo"], w=[rn_])
                kb.cp('act', rw_[:, 2:2 + ntok], pb[bank][:, tsl], r=[f"pb{bank}"], w=[rn_])
                if is_halo:
                    kb.ts('pool', fhalo[:, jj, :], rw_[:, ntok:ntok + 2], hm[:, 0:1], ALU.mult, r=[rn_, "hm"],
                          w=["fhalo"])
                else:
                    kb.cp('pool', fhalo[:, jj, :], rw_[:, ntok:ntok + 2], r=[rn_], w=["fhalo"])
                fa = facc[half]
                fan = f"facc{half}"
                kb.ts('dve', fa[:, tsl], rw_[:, 2:2 + ntok], fc[:, jj, 2:3], ALU.mult, r=[rn_, "fc"], w=[fan],
                      s2=fc[:, jj, 3:4], op1=ALU.add)
                kb.stt('dve', fa[:, tsl], rw_[:, 1:1 + ntok], fc[:, jj, 1:2], fa[:, tsl], ALU.mult, ALU.add,
                       r=[rn_, "fc", fan], w=[fan])
                kb.stt('dve', fa[:, tsl], rw_[:, 0:ntok], fc[:, jj, 0:1], fa[:, tsl], ALU.mult, ALU.add,
                       r=[rn_, "fc", fan], w=[fan])
            if not is_halo:
                kb.act(gg[:, tsl], facc[0][:, tsl], AF.Gelu_apprx_tanh, r=["facc0"], w=["gg"])
                kb.tt('pool', aT[:, j, tsl], gg[:, tsl], facc[1][:, tsl], ALU.mult, r=["gg", "facc1"], w=["aT"])
        if is_halo:
            return
        for sub in range(nsub):
            for j in range(NJ):
                di = slab_ctr["d"] % 2
                slab_ctr["d"] += 1
                kb.dma(Wd_s[di][:], Wd_d[j], f"Wd_s{di}", w=[f"Wd_s{di}"])
                for c in range(4):
                    kb.mm(pb[c][:, :], lhsT=aT[:, j, sub * 128:(sub + 1) * 128], rhs=Wd_s[di][:, c * 512:(c + 1) * 512],
                          start=(j == 0), stop=(j == NJ - 1), r=["aT", f"Wd_s{di}"], w=[f"pb{c}"])
            for c in range(4):
                kb.cp('act', ymix[:, c * 512:(c + 1) * 512], pb[c][:, :], r=[f"pb{c}"], w=["ymix"])
            o0 = t0 - 128 + sub * 128
            post_norm(nsub, None, GTf, "GTf", sub, o0)

    tiles = [(0, 128, True)] + [(t, 256, False) for t in range(128, NT, 256)]
    for ti, (t, ntk, hl) in enumerate(tiles):
        tile(t, ntk, hl, nxt=(tiles[ti + 1][:2] if ti + 1 < len(tiles) else None))
    scm.close()
    if env is not None:
        return None
    return kb, ["xo"]


import numpy as np


RG = [[0, 1, 2, 3], [4, 5, 6, 7]]


def build_fused(S, depth=2):
    Tq = S // 4
    NM = S // 512
    NMq = NM // 4
    NT = 128 + Tq
    kb = KB()
    cst = kb.din("cst", [128, 2048])
    cstb = kb.din("cstb", [128, 1024])
    c_col = kb.din("c_col", [128, 16])
    pos = kb.din("pos", [1, S], I32)
    xh = kb.din("xh", [NT, D])
    hmask = kb.din("hmask", [128, 1])
    selv = kb.din("selv", [128, 8])
    xo = kb.dout("xo", [Tq, D])
    L = []
    for l in range(depth):
        e = {}
        for k, (shp, dt_) in P1_PARAM_SHAPES.items():
            e[k] = kb.din(f"{k}_{l}", shp, dt_)
        for k, (shp, dt_) in P2_PARAM_SHAPES.items():
            e[k] = kb.din(f"{k}_{l}", shp, dt_)
        L.append(e)
    G = setup_globals(kb, cst)
    CT = min(2048, Tq)
    NCH = S // CT
    MPC = CT // 512
    hT_own = kb.dscratch("hT_own", [NMq, 128, 16, 512], BF16)
    hTg = kb.dscratch("hTg", [2 * NMq, 4 * 64, 8192], BF16)
    ysc = [kb.dscratch(f"ysc{i}", [NCH, 256, CT], BF16) for i in range(3)]
    yall = [kb.dscratch(f"yall{i}", [NCH, 1024, CT], BF16) for i in range(3)]
    xs1 = kb.dscratch("xs1", [Tq, D])
    xlast = kb.dscratch("xlast", [128, D])
    xl_all = kb.dscratch("xl_all", [4 * 128, D])
    w2s = p2_scratch(kb)
    sel = kb.sb("sel", [128, 8])
    kb.dma(sel[:], selv, "sel", w=["sel"])

    def allgather(src2d, dst2d, key, reads, writes):
        kb.tr.dma('pool', lambda e_: e_.collective_compute("AllGather", ALU.bypass, replica_groups=RG,
                                                            ins=[src2d.opt()], outs=[dst2d.opt()]),
                  key, reads=reads, writes=writes, inc=1)

    for l in range(depth):
        P = L[l]
        x_own = xh[128:NT, :] if l == 0 else xs1
        def after_h(m):
            for half in range(2):
                allgather(hT_own[m, half * 64:(half + 1) * 64].rearrange("p k t -> p (k t)"), hTg[2 * m + half], "ag",
                          reads=[f"hTd{m}"], writes=[f"hTg{2 * m + half}"])

        def load_hT(dst, m, key):
            r_, ml = divmod(m, NMq)
            for half in range(2):
                kb.dma(dst[half * 64:(half + 1) * 64, :, :],
                       hTg[2 * ml + half, r_ * 64:(r_ + 1) * 64, :].rearrange("p (k t) -> p k t", k=16),
                       f"{key}_{half}", r=[f"hTg{2 * ml + half}"], w=[key])

        def yT_ap(row0, nrows, m):
            i, rr = divmod(row0, 256)
            c, mm = divmod(m, MPC)
            return ysc[i][c, rr:rr + nrows, mm * 512:(mm + 1) * 512]

        def after_y(i, m, res):
            if (m + 1) % MPC == 0:
                c = m // MPC
                allgather(ysc[i][c], yall[i][c], "ag", reads=[res], writes=[f"yall{i}_{c}"])

        env1 = dict(P)
        env1.update(kb=kb, G=G, x=x_own, NH=Tq // 128, hTo=hT_own, hTd=None, yT=None, c_col=c_col, pos=pos, cstb=cstb,
                    after_h=after_h, load_hT=load_hT, yT_ap=yT_ap, after_y=after_y)
        build_p1(S, parts=('h',), env=env1)
        build_p1(S, parts=('mamba', 'rwkv', 'ret'), env=env1)
        if l > 0:
            allgather(xlast, xl_all, "ag", reads=["xlast"], writes=["xlall"])

        X = {}

        def extra_alloc(scm, bufs, X=X):
            X['cand'] = scm.sb("ycand", [128, 3, 8, 256], BF16)
            X['ymix'] = bufs['ymix']

        def load_x(dst, rn, row0, is_halo, l=l, X=X):
            if l == 0:
                kb.dma(dst, xh[row0:row0 + 128, :], rn, w=[rn])
            elif not is_halo:
                kb.dma(dst, xs1[row0 - 128:row0, :], rn, r=["xs1"], w=[rn])
            else:
                for q in range(4):
                    kb.dma(X['ymix'][:], xl_all[q * 128:(q + 1) * 128, :], "ymix", r=["xlall"], w=["ymix"])
                    if q == 0:
                        kb.ts('dve', dst, X['ymix'][:], sel[:, 4:5], ALU.mult, r=["ymix", "sel"], w=[rn])
                    else:
                        kb.stt('dve', dst, X['ymix'][:], sel[:, 4 + q:5 + q], dst, ALU.mult, ALU.add,
                               r=["ymix", "sel", rn], w=[rn])

        def load_y(ybf, t0, ntok, is_halo, X=X):
            cand = X['cand']
            for q in range(4):
                gt0 = q * Tq + t0 - 128
                if gt0 < 0:
                    kb.memset('pool', cand[:, :, :, 0:ntok], 0.0, w=[f"ycand{g}" for g in range(3)])
                else:
                    c, off = divmod(gt0, CT)
                    for g in range(3):
                        kb.dma(cand[:, g, :, 0:ntok],
                               yall[g][c, :, off:off + ntok].rearrange("(k p) t -> p k t", p=128),
                               f"ycand{g}", r=[f"yall{g}_{c}"], w=[f"ycand{g}"])
                if q == 0:
                    kb.ts('dve', ybf[:, :, :, 0:ntok], cand[:, :, :, 0:ntok], sel[:, 0:1], ALU.mult,
                          r=["ycand0", "ycand1", "ycand2", "sel"], w=["ybf"])
                else:
                    kb.stt('dve', ybf[:, :, :, 0:ntok], cand[:, :, :, 0:ntok], sel[:, q:q + 1], ybf[:, :, :, 0:ntok],
                           ALU.mult, ALU.add, r=["ycand0", "ycand1", "ycand2", "sel", "ybf"], w=["ybf"])

        def store_x(src, rn, o0, l=l):
            if l < depth - 1:
                kb.dma(xs1[o0:o0 + 128, :], src, rn, r=[rn], w=["xs1"])
                if o0 == Tq - 128:
                    kb.dma(xlast, src, rn, r=[rn], w=["xlast"])
            else:
                kb.dma(xo[o0:o0 + 128, :], src, rn, r=[rn], w=["xo"])

        env2 = dict(P)
        env2.update(w2s)
        env2.update(kb=kb, G=G, hmask=hmask, c_col=c_col, load_x=load_x, load_y=load_y, store_x=store_x,
                    extra_alloc=extra_alloc)
        build_p2(Tq, env=env2)
    nc = kb.finish(["xo"])
    return kb, nc


def fused_core_inputs(inp, b, r, S, depth):
    Tq = S // 4
    o = {}
    for l in range(depth):
        p1 = p1_core_inputs(inp, l, b, r)
        for k in P1_PARAM_SHAPES:
            o[f"{k}_{l}"] = p1[k]
        p2 = p2_core_inputs(inp, l, b, r, Tq, None, None)
        for k in P2_PARAM_SHAPES:
            o[f"{k}_{l}"] = p2[k]
    o['cst'] = p1_consts()
    o['cstb'] = p1_core_consts(r)
    o['c_col'] = np.ascontiguousarray(inp['c'][b].reshape(16, 128).T)
    o['pos'] = np.ascontiguousarray(inp['positions'][b][None, :]).astype(np.int32)
    x_b = inp['x'][b]
    xh = np.zeros((128 + Tq, 2048), np.float32)
    if r == 0:
        xh[128:] = x_b[0:Tq]
    else:
        xh[:] = x_b[r * Tq - 128:(r + 1) * Tq]
    o['xh'] = xh
    o['hmask'] = np.full((128, 1), 0.0 if r == 0 else 1.0, np.float32)
    sv = np.zeros((128, 8), np.float32)
    sv[:, r] = 1.0
    if r > 0:
        sv[:, 4 + r - 1] = 1.0
    o['selv'] = sv
    return o


_CACHE = {}


def kernel(**inputs):
    inp = {k: np.asarray(v) for k, v in inputs.items()}
    inp['x'] = np.ascontiguousarray(inp['x'], dtype=np.float32)
    B, S, _ = inp['x'].shape
    depth = inp['w_in'].shape[0]
    Tq = S // 4
    key = (S, depth)
    if key not in _CACHE:
        _CACHE[key] = build_fused(S, depth)[1]
    nc = _CACHE[key]
    in_maps = []
    for core in range(8):
        b, r = divmod(core, 4)
        in_maps.append(fused_core_inputs(inp, b, r, S, depth))
    res = run_bass_kernel_spmd(nc, in_maps, core_ids=list(range(8)))
    out = np.zeros((B, S, 2048), np.float32)
    for core in range(8):
        b, r = divmod(core, 4)
        out[b, r * Tq:(r + 1) * Tq] = np.asarray(res.results[core]['xo'])
    return out
```
